# Optimizing a Trainium2 kernel written in Bass

```python
import jax, jax.numpy as jnp
from jax import lax
import numpy as np

D_MODEL = 1024
BATCH = 8
SEQ = 2048
DEPTH = 2
DEC_BATCH = 32
DEC_SEQ = 8
PAST_LEN = 16384
PAGE_SIZE = 128

EPS = 1e-6
POOL_WINDOWS = (2, 4, 8, 16)
POOL_GROUPS = 4
POOL_GROUP_DIM = D_MODEL // 8
POOL_WIDTH = POOL_GROUPS * POOL_GROUP_DIM
POOL_BUF = 15
MLA_HEADS = 8
QK_NOPE = 64
QK_ROPE = 32
V_HEAD = 64
Q_LORA = 384
KV_LORA = 256
MLA_WIDTH = MLA_HEADS * V_HEAD
MLA_SCALE = (QK_NOPE + QK_ROPE) ** -0.5
ROPE_THETA = 10000.0
Q_BLOCK = 128
DN_HEADS = 8
DN_DK = 64
DN_DV = 64
DN_KEY_WIDTH = DN_HEADS * DN_DK
DN_WIDTH = DN_HEADS * DN_DV
CONV_W = 4
CONV_CH = 2 * DN_KEY_WIDTH + DN_WIDTH
DN_CHUNK = 64
N_BRANCH = 3
IN_SPLITS = (POOL_WIDTH, POOL_WIDTH,
             Q_LORA, KV_LORA, QK_ROPE, MLA_WIDTH,
             CONV_CH, DN_WIDTH, DN_HEADS, DN_HEADS,
             N_BRANCH * D_MODEL)
IN_WIDTH = sum(IN_SPLITS)

kernel_name = 'hybrid_pool_mla_gdn_step'


def _rmsnorm(x, g):
    xf = x.astype(jnp.float32)
    y = xf * lax.rsqrt(jnp.mean(xf * xf, axis=-1, keepdims=True) + EPS)
    return (y * g.astype(jnp.float32)).astype(x.dtype)


def _l2norm(x):
    xf = x.astype(jnp.float32)
    return xf * lax.rsqrt(jnp.sum(xf * xf, axis=-1, keepdims=True) + EPS)


def _rope(x, pos):
    half = QK_ROPE // 2
    inv = jnp.power(ROPE_THETA, -jnp.arange(half, dtype=jnp.float32) / half)
    ang = pos.astype(jnp.float32)[:, None] * inv[None, :]
    cos = jnp.cos(ang)[None, :, None, :]
    sin = jnp.sin(ang)[None, :, None, :]
    xf = x.astype(jnp.float32)
    x1, x2 = xf[..., :half], xf[..., half:]
    return jnp.concatenate([x1 * cos - x2 * sin, x2 * cos + x1 * sin], axis=-1).astype(x.dtype)


def _pool_mixer(u, buf, pos, mix, scale):
    B, L, _ = u.shape
    xx = jnp.concatenate([buf.astype(u.dtype), u], axis=1)
    xf = xx.astype(jnp.float32)
    prefix = jnp.concatenate([jnp.zeros_like(xf[:, :1]), lax.cumsum(xf, axis=1)], axis=1)
    hi = prefix[:, POOL_BUF + 1:]
    outs = []
    for gi, w in enumerate(POOL_WINDOWS):
        c0 = gi * POOL_GROUP_DIM
        c1 = c0 + POOL_GROUP_DIM
        lo = prefix[:, POOL_BUF + 1 - w:POOL_BUF + 1 - w + L, c0:c1]
        cnt = jnp.minimum(pos + 1, w).astype(jnp.float32)[None, :, None]
        outs.append((hi[..., c0:c1] - lo) / cnt)
    d = jnp.concatenate(outs, axis=-1) - u.astype(jnp.float32)
    d = d.reshape(B, L, POOL_GROUPS, POOL_GROUP_DIM).astype(u.dtype)
    y = jnp.einsum('blgc,gcd->blgd', d, mix).reshape(B, L, POOL_WIDTH) * scale
    return y, xx[:, -POOL_BUF:]


def _mla_attend(q_lat, q_rope, c_kv, k_rope, q_pos, k_pos):
    B, Lq, H, C = q_lat.shape
    qb = min(Q_BLOCK, Lq)
    nb = -(-Lq // qb)
    pad = nb * qb - Lq
    q_lat = jnp.pad(q_lat, ((0, 0), (0, pad), (0, 0), (0, 0)))
    q_rope = jnp.pad(q_rope, ((0, 0), (0, pad), (0, 0), (0, 0)))
    q_pos = jnp.pad(q_pos, (0, pad))
    to_blocks = lambda t: jnp.moveaxis(t.reshape(B, nb, qb, *t.shape[2:]), 1, 0)
    ql_b, qr_b = to_blocks(q_lat), to_blocks(q_rope)
    qp_b = q_pos.reshape(nb, qb)

    def block(args):
        ql, qr, qp = args
        s = (jnp.einsum('bqhc,bkc->bhqk', ql, c_kv, preferred_element_type=jnp.float32)
             + jnp.einsum('bqhr,bkr->bhqk', qr, k_rope, preferred_element_type=jnp.float32)) * MLA_SCALE
        mask = k_pos[None, :] <= qp[:, None]
        p = jax.nn.softmax(jnp.where(mask[None, None], s, -jnp.inf), axis=-1)
        o = jnp.einsum('bhqk,bkc->bqhc', p.astype(c_kv.dtype), c_kv, preferred_element_type=jnp.float32)
        return o.astype(c_kv.dtype)

    o = lax.map(block, (ql_b, qr_b, qp_b))
    return jnp.moveaxis(o, 0, 1).reshape(B, nb * qb, H, C)[:, :Lq]


def _short_conv(u, buf, w):
    L = u.shape[1]
    xx = jnp.concatenate([buf.astype(u.dtype), u], axis=1)
    y = xx[:, 0:L] * w[0]
    for j in range(1, CONV_W):
        y = y + xx[:, j:j + L] * w[j]
    return jax.nn.silu(y), xx[:, -(CONV_W - 1):]


def _gated_delta(q, k, v, g, beta, s0):
    B, L, H, DK = q.shape
    C = min(DN_CHUNK, L)
    n = -(-L // C)
    pad = n * C - L
    padf = lambda t: jnp.pad(t, [(0, 0), (0, pad)] + [(0, 0)] * (t.ndim - 2))
    chunks = lambda t: jnp.moveaxis(padf(t).reshape(B, n, C, *t.shape[2:]), (1, 3), (0, 2))
    q = chunks(q * (DK ** -0.5))
    k, v, g, beta = chunks(k), chunks(v), chunks(g), chunks(beta)
    gc = jnp.cumsum(g, axis=-1)
    tri_incl = jnp.tril(jnp.ones((C, C), dtype=bool))
    tri_strict = jnp.tril(jnp.ones((C, C), dtype=bool), -1)
    diff = gc[..., :, None] - gc[..., None, :]
    decay = jnp.where(tri_incl, jnp.exp(jnp.where(tri_incl, diff, 0.0)), 0.0)
    kb = k * beta[..., None]
    a_mat = jnp.where(tri_strict, jnp.einsum('...id,...jd->...ij', kb, k) * decay, 0.0)
    eye = jnp.eye(C, dtype=a_mat.dtype)
    t_inv = lax.linalg.triangular_solve(eye + a_mat, jnp.broadcast_to(eye, a_mat.shape),
                                        left_side=True, lower=True, unit_diagonal=True)
    u = t_inv @ (v * beta[..., None])
    w = t_inv @ (kb * jnp.exp(gc)[..., None])
    qk = jnp.where(tri_incl, jnp.einsum('...id,...jd->...ij', q, k) * decay, 0.0)

    def step(s, xs):
        q_i, k_i, u_i, w_i, gc_i, qk_i = xs
        v_new = u_i - w_i @ s
        o = (q_i * jnp.exp(gc_i)[..., None]) @ s + qk_i @ v_new
        g_last = gc_i[..., -1]
        s = s * jnp.exp(g_last)[..., None, None] + jnp.einsum(
            'bhcd,bhce->bhde', k_i * jnp.exp(g_last[..., None] - gc_i)[..., None], v_new)
        return s, o

    s_fin, o = lax.scan(step, s0, (q, k, u, w, gc, qk))
    o = jnp.moveaxis(o, (0, 2), (1, 3)).reshape(B, n * C, H, -1)[:, :L]
    return o, s_fin


def _layer(x, start, kv_past, kr_past, pool_buf, conv_buf, s0,
           norm_g, w_in, pool_mix, pool_scale, q_norm_g, w_uq, kv_norm_g, w_uk, w_uv,
           conv_w, a_log, dt_bias, dn_norm_g, w_br_pool, w_br_mla, w_br_dn, w_out):
    B, L, _ = x.shape
    f32 = jnp.float32
    pos = start + jnp.arange(L, dtype=jnp.int32)
    xn = _rmsnorm(x, norm_g)
    h = xn @ w_in
    split_pts = [int(c) for c in np.cumsum(IN_SPLITS)[:-1]]
    (h_pool, z_pool, h_q, h_kv, h_kr, z_mla, h_qkv, z_dn, h_beta, h_alpha, h_gate) = jnp.split(h, split_pts, axis=-1)

    y_pool, pool_buf_new = _pool_mixer(h_pool, pool_buf, pos, pool_mix, pool_scale)
    y_a = y_pool * jax.nn.silu(z_pool)

    c_q = _rmsnorm(h_q, q_norm_g)
    q = (c_q @ w_uq).reshape(B, L, MLA_HEADS, QK_NOPE + QK_ROPE)
    q_nope = q[..., :QK_NOPE]
    q_rope = _rope(q[..., QK_NOPE:], pos)
    c_kv = _rmsnorm(h_kv, kv_norm_g)
    k_r = _rope(h_kr[:, :, None, :], pos)[:, :, 0]
    q_lat = jnp.einsum('blhd,chd->blhc', q_nope, w_uk)
    keys_lat = jnp.concatenate([kv_past.astype(c_kv.dtype), c_kv], axis=1)
    keys_rope = jnp.concatenate([kr_past.astype(k_r.dtype), k_r], axis=1)
    k_pos = jnp.arange(keys_lat.shape[1], dtype=jnp.int32)
    o_lat = _mla_attend(q_lat, q_rope, keys_lat, keys_rope, pos, k_pos)
    o_mla = jnp.einsum('blhc,chd->blhd', o_lat, w_uv).reshape(B, L, MLA_WIDTH)
    y_b = o_mla * jax.nn.silu(z_mla)

    qkv, conv_buf_new = _short_conv(h_qkv, conv_buf, conv_w)
    q_d = _l2norm(qkv[..., :DN_KEY_WIDTH].reshape(B, L, DN_HEADS, DN_DK))
    k_d = _l2norm(qkv[..., DN_KEY_WIDTH:2 * DN_KEY_WIDTH].reshape(B, L, DN_HEADS, DN_DK))
    v_d = qkv[..., 2 * DN_KEY_WIDTH:].reshape(B, L, DN_HEADS, DN_DV).astype(f32)
    beta = jax.nn.sigmoid(h_beta.astype(f32))
    g = -jnp.exp(a_log.astype(f32)) * jax.nn.softplus(h_alpha.astype(f32) + dt_bias.astype(f32))
    o_dn, s_new = _gated_delta(q_d, k_d, v_d, g, beta, s0.astype(f32))
    y_c = _rmsnorm(o_dn, dn_norm_g).astype(x.dtype).reshape(B, L, DN_WIDTH) * jax.nn.silu(z_dn)

    gates = jax.nn.sigmoid(h_gate.astype(f32)).astype(x.dtype).reshape(B, L, N_BRANCH, D_MODEL)
    merged = (gates[:, :, 0] * (y_a @ w_br_pool) + gates[:, :, 1] * (y_b @ w_br_mla)
              + gates[:, :, 2] * (y_c @ w_br_dn))
    x = x + merged @ w_out
    return x, c_kv, k_r, pool_buf_new, conv_buf_new, s_new.astype(x.dtype)


def setup_inputs(seed: int = 0) -> dict:
    key = jax.random.key(seed)
    ks = jax.random.split(key, 32)
    f32 = jnp.float32
    nrm = lambda i, shape, scale: jax.random.normal(ks[i], shape, f32) * scale
    n_pages = PAST_LEN // PAGE_SIZE
    n_used = DEC_BATCH * n_pages
    n_pool = n_used + max(1, n_used // 4)
    perm = jax.random.permutation(ks[7], n_pool).astype(jnp.int32)
    page_table = perm[:n_used].reshape(DEC_BATCH, n_pages)
    dt = jnp.exp(jax.random.uniform(ks[19], (DEPTH, DN_HEADS), f32, np.log(1e-3), np.log(1e-1)))
    return {
        'x_prompt': nrm(0, (BATCH, SEQ, D_MODEL), 1.0),
        'x_sample': nrm(1, (DEC_BATCH, DEC_SEQ, D_MODEL), 1.0),
        'cache_kv_latent': nrm(2, (DEPTH, n_pool, PAGE_SIZE, KV_LORA), 1.0),
        'cache_k_rope': nrm(3, (DEPTH, n_pool, PAGE_SIZE, QK_ROPE), 1.0),
        'state_pool': nrm(4, (DEPTH, DEC_BATCH, POOL_BUF, POOL_WIDTH), 1.0),
        'state_conv': nrm(5, (DEPTH, DEC_BATCH, CONV_W - 1, CONV_CH), 1.0),
        'state_delta': nrm(6, (DEPTH, DEC_BATCH, DN_HEADS, DN_DK, DN_DV), 0.1),
        'page_table': page_table,
        'norm_g': 1.0 + nrm(8, (DEPTH, D_MODEL), 0.02),
        'w_in': nrm(9, (DEPTH, D_MODEL, IN_WIDTH), D_MODEL ** -0.5),
        'pool_mix': nrm(10, (DEPTH, POOL_GROUPS, POOL_GROUP_DIM, POOL_GROUP_DIM), POOL_GROUP_DIM ** -0.5),
        'pool_scale': 1.0 + nrm(11, (DEPTH, POOL_WIDTH), 0.1),
        'q_norm_g': 1.0 + nrm(12, (DEPTH, Q_LORA), 0.02),
        'w_uq': nrm(13, (DEPTH, Q_LORA, MLA_HEADS * (QK_NOPE + QK_ROPE)), Q_LORA ** -0.5),
        'kv_norm_g': 1.0 + nrm(14, (DEPTH, KV_LORA), 0.02),
        'w_uk': nrm(15, (DEPTH, KV_LORA, MLA_HEADS, QK_NOPE), KV_LORA ** -0.5),
        'w_uv': nrm(16, (DEPTH, KV_LORA, MLA_HEADS, V_HEAD), KV_LORA ** -0.5),
        'conv_w': nrm(17, (DEPTH, CONV_W, CONV_CH), CONV_W ** -0.5),
        'a_log': jnp.log(jax.random.uniform(ks[18], (DEPTH, DN_HEADS), f32, 1.0, 16.0)),
        'dt_bias': jnp.log(jnp.expm1(dt)),
        'dn_norm_g': 1.0 + nrm(20, (DEPTH, DN_DV), 0.02),
        'w_br_pool': nrm(21, (DEPTH, POOL_WIDTH, D_MODEL), POOL_WIDTH ** -0.5),
        'w_br_mla': nrm(22, (DEPTH, MLA_WIDTH, D_MODEL), MLA_WIDTH ** -0.5),
        'w_br_dn': nrm(23, (DEPTH, DN_WIDTH, D_MODEL), DN_WIDTH ** -0.5),
        'w_out': nrm(24, (DEPTH, D_MODEL, D_MODEL), D_MODEL ** -0.5),
        'final_norm_g': 1.0 + nrm(25, (D_MODEL,), 0.02),
    }


def reference(x_prompt, x_sample, cache_kv_latent, cache_k_rope, state_pool, state_conv, state_delta,
              page_table, norm_g, w_in, pool_mix, pool_scale, q_norm_g, w_uq, kv_norm_g, w_uk, w_uv,
              conv_w, a_log, dt_bias, dn_norm_g, w_br_pool, w_br_mla, w_br_dn, w_out, final_norm_g):
    bp = x_prompt.shape[0]
    db, n_pages = page_table.shape
    past_len = n_pages * PAGE_SIZE
    dt_ = x_prompt.dtype
    xp, xs = x_prompt, x_sample
    p_kv, p_kr, p_pool, p_conv, p_delta = [], [], [], [], []
    s_kv, s_kr, s_pool, s_conv, s_delta = [], [], [], [], []
    for l in range(DEPTH):
        params = (norm_g[l], w_in[l], pool_mix[l], pool_scale[l], q_norm_g[l], w_uq[l], kv_norm_g[l],
                  w_uk[l], w_uv[l], conv_w[l], a_log[l], dt_bias[l], dn_norm_g[l],
                  w_br_pool[l], w_br_mla[l], w_br_dn[l], w_out[l])
        xp, ckv, kr, pb, cb, sd = _layer(
            xp, 0,
            jnp.zeros((bp, 0, KV_LORA), dt_), jnp.zeros((bp, 0, QK_ROPE), dt_),
            jnp.zeros((bp, POOL_BUF, POOL_WIDTH), dt_), jnp.zeros((bp, CONV_W - 1, CONV_CH), dt_),
            jnp.zeros((bp, DN_HEADS, DN_DK, DN_DV), jnp.float32), *params)
        p_kv.append(ckv); p_kr.append(kr); p_pool.append(pb); p_conv.append(cb); p_delta.append(sd)
        kv_past = cache_kv_latent[l][page_table].reshape(db, past_len, KV_LORA)
        kr_past = cache_k_rope[l][page_table].reshape(db, past_len, QK_ROPE)
        xs, ckv, kr, pb, cb, sd = _layer(
            xs, past_len, kv_past, kr_past, state_pool[l], state_conv[l], state_delta[l], *params)
        s_kv.append(ckv); s_kr.append(kr); s_pool.append(pb); s_conv.append(cb); s_delta.append(sd)
    y_prompt = _rmsnorm(xp, final_norm_g)
    y_sample = _rmsnorm(xs, final_norm_g)
    return (y_prompt, y_sample,
            jnp.stack(p_kv), jnp.stack(p_kr), jnp.stack(p_pool), jnp.stack(p_conv), jnp.stack(p_delta),
            jnp.stack(s_kv), jnp.stack(s_kr), jnp.stack(s_pool), jnp.stack(s_conv), jnp.stack(s_delta))
```

```python
import contextlib
import numpy as np
import concourse.bass as bass
import concourse.mybir as mybir
from concourse.bass_utils import run_bass_kernel_spmd

F32 = mybir.dt.float32
BF16 = mybir.dt.bfloat16
I32 = mybir.dt.int32
AF = mybir.ActivationFunctionType
ALU = mybir.AluOpType

D = 1024
SEQ = 2048
DEPTH = 2
EPS = 1e-6
NPAGE = 128
H = 8
MLA_SCALE = 96 ** -0.5
NCORE = 8
CH = 512
NCH = SEQ // CH
C_POOL, C_ZPOOL, C_Q, C_KV, C_KR, C_ZMLA, C_QKV, C_ZDN, C_BETA, C_ALPHA, C_GATE = (
    0, 512, 1024, 1408, 1664, 1696, 2208, 3744, 4256, 4264, 4272)
INW = 7344


class Tk:
    __slots__ = ("w", "r", "name")

    def __init__(self, name=""):
        self.w = {}
        self.r = {}
        self.name = name


class Op:
    __slots__ = ("fn", "waits", "signal", "dma")

    def __init__(self, fn, dma=None):
        self.fn = fn
        self.waits = []
        self.signal = False
        self.dma = dma


STREAMS = ("pe", "act", "dve", "pool", "sp")
NSLOT = {"sp": 28, "act": 8, "pool": 24}


class Prog:
    def __init__(self, nc):
        self.nc = nc
        self.ops = {s: [] for s in STREAMS}
        self.seen_c = {s: {} for s in STREAMS}
        self.seen_d = {s: {} for s in STREAMS}
        self.slot_next = {s: 0 for s in NSLOT}
        self.slot_val = {}
        self.out_dma_events = []
        self.pending_dma = {}
        self.last_c = {s: -1 for s in STREAMS}

    def _need(self, stream, ev, waits, force_same=False):
        if ev[0] == "c":
            _, e2, idx = ev
            if idx < 0:
                return
            if e2 == stream and stream == "pe" and not force_same:
                return
            if self.seen_c[stream].get(e2, -1) >= idx:
                return
            self.seen_c[stream][e2] = idx
            self.ops[e2][idx].signal = True
            waits.append(ev)
        else:
            _, slot, val = ev
            if self.seen_d[stream].get(slot, 0) >= val:
                return
            self.seen_d[stream][slot] = val
            waits.append(ev)

    def _deps(self, stream, reads, writes, force_same=False):
        waits = []
        for t in reads:
            for ev in t.w.values():
                self._need(stream, ev, waits, force_same)
        for t in writes:
            for ev in t.w.values():
                self._need(stream, ev, waits, force_same)
            for ev in t.r.values():
                self._need(stream, ev, waits, force_same)
        return waits

    def _commit(self, ev, reads, writes):
        key = ev[:2]
        for t in reads:
            t.r[key] = ev
        for t in writes:
            t.w = {key: ev}
            t.r = {}

    def op(self, stream, fn, reads=(), writes=()):
        o = Op(fn)
        o.waits = self._deps(stream, reads, writes)
        idx = len(self.ops[stream])
        self.ops[stream].append(o)
        self.last_c[stream] = idx
        self._commit(("c", stream, idx), reads, writes)
        return o

    def dma(self, stream, out, in_, reads=(), writes=(), is_output=False, **kw):
        n = NSLOT[stream]
        k = self.slot_next[stream]
        self.slot_next[stream] = k + 1
        slot = (stream, k % n)
        prev = self.slot_val.get(slot, 0)
        val = prev + 16
        self.slot_val[slot] = val
        o = Op(lambda e: e.dma_start(out=out, in_=in_, **kw), dma=(slot, val))
        o.waits = self._deps(stream, reads, writes, force_same=True)
        if prev > 0:
            self._need(stream, ("d", slot, prev), o.waits)
        self.ops[stream].append(o)
        ev = ("d", slot, val)
        self._commit(ev, reads, writes)
        self.pending_dma[slot] = ev
        if is_output:
            self.out_dma_events.append(ev)
        return o

    def idma(self, out, in_, idx_ap, reads=(), writes=()):
        stream = "pool"
        n = NSLOT[stream]
        k = self.slot_next[stream]
        self.slot_next[stream] = k + 1
        slot = (stream, k % n)
        prev = self.slot_val.get(slot, 0)
        val = prev + 16
        self.slot_val[slot] = val
        o = Op(lambda e: e.indirect_dma_start(out=out, out_offset=None, in_=in_,
                                              in_offset=bass.IndirectOffsetOnAxis(ap=idx_ap, axis=0)), dma=(slot, val))
        o.waits = self._deps(stream, reads, writes, force_same=True)
        if prev > 0:
            self._need(stream, ("d", slot, prev), o.waits)
        self.ops[stream].append(o)
        ev = ("d", slot, val)
        self._commit(ev, reads, writes)
        self.pending_dma[slot] = ev
        return o

    def fence(self):
        last = dict(self.last_c)
        pend = list(self.pending_dma.values())
        self.pending_dma = {}
        self._fence_waits = {}
        for a in STREAMS:
            waits = []
            for b in STREAMS:
                if b != a:
                    self._need(a, ("c", b, last[b]), waits)
            for ev in pend:
                self._need(a, ev, waits)
            if waits:
                o = Op(None)
                o.waits = waits
                self.ops[a].append(o)

    def mm(self, out, lhsT, rhs, start=True, stop=True, reads=(), writes=(), **kw):
        return self.op("pe", lambda e: e.matmul(out, lhsT, rhs, start=start, stop=stop, **kw), reads, writes)

    def tr(self, out, in_, ident, reads=(), writes=()):
        return self.op("pe", lambda e: e.transpose(out, in_, ident), reads, writes)

    def act(self, out, in_, func, reads=(), writes=(), **kw):
        return self.op("act", lambda e: e.activation(out=out, in_=in_, func=func, **kw), reads, writes)

    def tt(self, stream, out, in0, in1, op, reads=(), writes=()):
        return self.op(stream, lambda e: e.tensor_tensor(out=out, in0=in0, in1=in1, op=op), reads, writes)

    def ts(self, stream, out, in0, s1, s2, op0, op1=None, reads=(), writes=(), **kw):
        if op1 is None:
            return self.op(stream, lambda e: e.tensor_scalar(out=out, in0=in0, scalar1=s1, scalar2=None, op0=op0, **kw), reads, writes)
        return self.op(stream, lambda e: e.tensor_scalar(out=out, in0=in0, scalar1=s1, scalar2=s2, op0=op0, op1=op1, **kw), reads, writes)

    def stt(self, out, in0, scalar, in1, op0, op1, reads=(), writes=(), **kw):
        return self.op("dve", lambda e: e.scalar_tensor_tensor(out=out, in0=in0, scalar=scalar, in1=in1, op0=op0, op1=op1, **kw), reads, writes)

    def copy(self, stream, out, in_, reads=(), writes=()):
        if stream == "act":
            return self.op("act", lambda e: e.copy(out=out, in_=in_), reads, writes)
        return self.op(stream, lambda e: e.tensor_copy(out=out, in_=in_), reads, writes)

    def memset(self, stream, ap, val, writes=()):
        return self.op(stream, lambda e: e.memset(ap, val), (), writes)

    def emit(self, sems, slot_sems):
        nc = self.nc
        cum = {}
        for s in STREAMS:
            c = 0
            arr = []
            for o in self.ops[s]:
                if o.signal and o.dma is None and o.fn is not None:
                    c += 1
                arr.append(c)
            cum[s] = arr
        final_waits = []
        for ev in self.out_dma_events:
            self._need("sp", ev, final_waits)
        self.n_instr = {s: len(self.ops[s]) for s in STREAMS}

        def run(stream, eng):
            for o in self.ops[stream]:
                for ev in o.waits:
                    if ev[0] == "c":
                        eng.wait_ge(sems[ev[1]], cum[ev[1]][ev[2]])
                    else:
                        eng.wait_ge(slot_sems[ev[1]], ev[2])
                if o.fn is None:
                    continue
                ins = o.fn(eng)
                if o.dma is not None:
                    ins.then_inc(slot_sems[o.dma[0]], 16)
                elif o.signal:
                    ins.then_inc(sems[stream], 1)
            if stream == "sp":
                for ev in final_waits:
                    eng.wait_ge(slot_sems[ev[1]], ev[2])

        with nc.Block() as block:
            @block.tensor
            def _(e):
                run("pe", e)

            @block.scalar
            def _(e):
                run("act", e)

            @block.vector
            def _(e):
                run("dve", e)

            @block.gpsimd
            def _(e):
                run("pool", e)

            @block.sync
            def _(e):
                run("sp", e)


class Buf:
    def __init__(self, t, name):
        self.t = t
        self.k = Tk(name)

    def __getitem__(self, key):
        return self.t[key]


def _consts():
    c = {}
    half = 16
    inv = np.power(10000.0, -np.arange(half, dtype=np.float32) / half).astype(np.float32)

    def tabs(pos):
        ang = pos.astype(np.float32)[:, None] * inv[None, :]
        return np.cos(ang).astype(np.float32), np.sin(ang).astype(np.float32)

    posp = np.arange(SEQ)
    poss = 16384 + np.arange(8)
    cp, sp_ = tabs(posp)
    cs, ss = tabs(poss)
    ropeq = np.zeros((32, 2, SEQ + 32), np.float32)
    ropeq[:, 0, :SEQ] = np.concatenate([cp.T, cp.T], 0)
    ropeq[:, 1, :SEQ] = np.concatenate([-sp_.T, sp_.T], 0)
    cs4 = np.tile(cs, (4, 1))
    ss4 = np.tile(ss, (4, 1))
    ropeq[:, 0, SEQ:] = np.concatenate([cs4.T, cs4.T], 0)
    ropeq[:, 1, SEQ:] = np.concatenate([-ss4.T, ss4.T], 0)
    c["ropeq"] = ropeq
    ropek = np.zeros((128, 17, 32), np.float32)
    ropek[:, :16, :16] = cp.reshape(16, 128, 16).transpose(1, 0, 2)
    ropek[:, :16, 16:] = sp_.reshape(16, 128, 16).transpose(1, 0, 2)
    ropek[:32, 16, :16] = cs4
    ropek[:32, 16, 16:] = ss4
    c["ropek"] = ropek
    f = np.zeros((128, 1024), np.float32)
    f[:, 0:128] = np.eye(128)
    f[:, 128:256] = 1.0
    ii = np.arange(64)
    f[:64, 256:320] = (ii[None, :] >= ii[:, None])
    f[:64, 320:384] = -(ii[None, :] > ii[:, None]).astype(np.float32)
    jj = np.arange(128)
    f[:, 384:512] = (jj[:, None] <= jj[None, :])
    t15 = np.arange(15)
    for gi, w in enumerate((2, 4, 8, 16)):
        f[:, 512 + gi * 15: 512 + (gi + 1) * 15] = 1.0 / np.minimum(t15 + 1, w)
    f[:8, 576:584] = (np.arange(8)[:, None] <= np.arange(8)[None, :])
    f[:, 600] = np.arange(128)
    f[:, 601] = np.arange(128) + 5120 * 128
    c["cf"] = f
    return c


_CONST = None


DBG_S = 0
DN_NSUB = None
DN_CUT = 99


def build(sample_only=False, prompt_chunks=NCH, dbg=None, stages=("pool", "mla", "dn"), npool=5120, skip=(), prompt_only=False):
    nc = bass.Bass("TRN2", target_bir_lowering=False)
    es = contextlib.ExitStack()

    def din(name, shape, dt=F32):
        return nc.dram_tensor(name, list(shape), dt, kind="ExternalInput").ap()

    def dout(name, shape, dt=F32):
        return nc.dram_tensor(name, list(shape), dt, kind="ExternalOutput").ap()

    xp = din("xp", [SEQ, D])
    xs = din("xs", [32, D])
    ckv = din("ckv", [DEPTH, npool, 128, 256])
    ckr = din("ckr", [DEPTH, npool, 128, 32])
    spool = din("spool", [DEPTH, 4, 15, 512])
    sconv = din("sconv", [DEPTH, 4, 3, 1536])
    sdelta = din("sdelta", [DEPTH, 4, 8, 64, 64])
    ptab = din("ptab", [4, 128], I32)
    norm_g = din("norm_g", [DEPTH, D])
    w_in = din("w_in", [DEPTH, D, INW])
    pool_mix = din("pool_mix", [DEPTH, 128, 4, 128])
    pool_scale = din("pool_scale", [DEPTH, 128, 4])
    q_norm_g = din("q_norm_g", [DEPTH, 384])
    w_uq = din("w_uq", [DEPTH, 384, H * 128])
    kv_norm_g = din("kv_norm_g", [DEPTH, 256])
    w_ukT = din("w_ukT", [DEPTH, 64, H, 256])
    w_uv = din("w_uv", [DEPTH, 256, H * 64])
    conv_w = din("conv_w", [DEPTH, 64, 24, 4])
    a_log = din("a_log", [DEPTH, H])
    dt_bias = din("dt_bias", [DEPTH, H])
    dn_norm_g = din("dn_norm_g", [DEPTH, 64, 1])
    w_br_pool = din("w_br_pool", [DEPTH, 512, D])
    w_br_mla = din("w_br_mla", [DEPTH, 512, D])
    w_br_dn = din("w_br_dn", [DEPTH, 512, D])
    w_out = din("w_out", [DEPTH, D, D])
    final_norm_g = din("final_norm_g", [D])
    cf_d = din("cf", [128, 1024])
    ropeq_d = din("ropeq", [32, 2, SEQ + 32])
    ropek_d = din("ropek", [128, 17, 32])

    y_p = dout("y_p", [SEQ, D])
    y_s = dout("y_s", [32, D])
    o_pkv = dout("o_pkv", [DEPTH, SEQ, 256])
    o_pkr = dout("o_pkr", [DEPTH, SEQ, 32])
    o_ppool = dout("o_ppool", [DEPTH, 15, 512])
    o_pconv = dout("o_pconv", [DEPTH, 3, 1536])
    o_pdelta = dout("o_pdelta", [DEPTH, H, 64, 64])
    o_skv = dout("o_skv", [DEPTH, 32, 256])
    o_skr = dout("o_skr", [DEPTH, 32, 32])
    o_spool = dout("o_spool", [DEPTH, 4, 15, 512])
    o_sconv = dout("o_sconv", [DEPTH, 4, 3, 1536])
    o_sdelta = dout("o_sdelta", [DEPTH, 4, H, 64, 64])
    dbg_out = {}
    if dbg:
        for name, shape in dbg.items():
            dbg_out[name] = dout("dbg_" + name, shape)

    def sb(name, shape, dt=F32):
        return Buf(es.enter_context(nc.sbuf_tensor(name, list(shape), dt)), name)

    def pstile(name, shape, dt=F32):
        return Buf(es.enter_context(nc.psum_tensor(name, list(shape), dt)), name)

    P = Prog(nc)

    x_sb = sb("x_sb", [128, 4, D])
    xnT = sb("xnT", [128, 8, CH], BF16)
    mrg = sb("mrg", [128, 8, CH])
    kTc = [sb(f"kTc{l}", [128, 3, SEQ], BF16) for l in range(DEPTH)]
    Vc = [sb(f"Vc{l}", [128, 16, 256], BF16) for l in range(DEPTH)]
    hist_pool = [sb(f"hpool{l}", [128, 4, 15]) for l in range(DEPTH)]
    hist_conv = [sb(f"hconv{l}", [64, 24, 3]) for l in range(DEPTH)]
    S_p = [sb(f"S_p{l}", [64, H, 64]) for l in range(DEPTH)]
    WA = sb("WA", [128, 8, 672], BF16)
    WB = sb("WB", [128, 8, 512], BF16)
    WBR = sb("WBR", [128, 8 * D], BF16)
    mixw = sb("mixw", [128, 4, 128], BF16)
    wuq = sb("wuq", [128, 3, H * 128], BF16)
    wuk = sb("wuk", [64, H, 256], BF16)
    wuv = sb("wuv", [128, 2, H * 64], BF16)
    cw = sb("cw", [64, 24, 4])
    psc = sb("psc", [128, 4])
    gdn = sb("gdn", [64, 1])
    gq_bc = sb("gq_bc", [128, 384])
    gkv_bc = sb("gkv_bc", [128, 256])
    gn_bc = sb("gn_bc", [128, D])
    a_bc = sb("a_bc", [64, H])
    dtb_bc = sb("dtb_bc", [64, H])
    cf = sb("cf_sb", [128, 1024])
    identb = sb("identb", [128, 128], BF16)
    onesb = sb("onesb", [128, 128], BF16)
    ropeq = sb("ropeq_sb", [32, 2, CH])
    ropek = sb("ropek_sb", [128, 17, 32])
    ptb = sb("ptb", [128, 128], I32)
    ridx = sb("ridx", [128, 128], I32)
    AF_N = 6784
    AB_N = 11264
    arena_f = sb("arena_f", [128, AF_N])
    arena_b = sb("arena_b", [128, AB_N], BF16)
    banks = [pstile(f"psf{i}", [128, 512]) for i in range(7)]
    bankb = pstile("psb", [128, 1024], BF16)

    sems = {s: es.enter_context(nc.semaphore("sem_" + s)) for s in ("pe", "act", "dve", "pool", "sp")}
    slot_sems = {}
    for s, n in NSLOT.items():
        for i in range(n):
            slot_sems[(s, i)] = es.enter_context(nc.semaphore(f"ds_{s}_{i}"))

    identf = cf[:, 0:128]
    onesf = cf[:, 128:256]
    m_incl = cf[0:64, 256:320]
    m_nstrict = cf[0:64, 320:384]
    m_causal = cf[:, 384:512]
    rc15 = cf[:, 512:572]
    m_causal8 = cf[0:8, 576:584]
    iota_p = cf[:, 600:601]

    st = {"af": 0, "ab": 0, "n": 0, "rot": list(range(7)), "ri": 0}

    class AB:
        pass

    def _arena(ar, key, cap, shape, name, even):
        n = int(np.prod(shape[1:]))
        na = (n + 1) // 2 * 2 if even else n
        off = st[key]
        st[key] = off + na
        assert st[key] <= cap, ("arena overflow", key, name, st[key], cap)
        st["n"] += 1
        b = AB()
        b.k = Tk(name or f"{key}{st['n']}")
        b.shape = list(shape)
        flat = ar.t[0:shape[0], off:off + n]
        b.ap = flat
        sh = shape
        if len(sh) == 2:
            b.v = flat
        elif len(sh) == 3:
            b.v = flat.rearrange("p (a b) -> p a b", b=sh[2])
        elif len(sh) == 4:
            b.v = flat.rearrange("p (a b c) -> p a b c", b=sh[2], c=sh[3])
        else:
            raise ValueError
        return b

    def af(shape, name=None):
        return _arena(arena_f, "af", AF_N, shape, name, False)

    def ab(shape, name=None):
        return _arena(arena_b, "ab", AB_N, shape, name, True)

    def new_stage(reset_b=True):
        P.fence()
        st["af"] = 0
        if reset_b:
            st["ab"] = 0

    def set_rot(lst):
        st["rot"] = list(lst)
        st["ri"] = 0

    def bank():
        b = banks[st["rot"][st["ri"] % len(st["rot"])]]
        st["ri"] += 1
        return b

    def hv(ap, n, t=None):
        return ap.rearrange("p (h t) -> p h t", h=n)

    P.dma("sp", cf[:, :], cf_d, writes=[cf.k])
    P.dma("sp", ropek[:, :, :], ropek_d, writes=[ropek.k])
    P.copy("dve", identb[:, :], cf[:, 0:128], reads=[cf.k], writes=[identb.k])
    P.copy("dve", onesb[:, :], cf[:, 128:256], reads=[cf.k], writes=[onesb.k])
    for l in range(DEPTH):
        P.memset("pool", hist_pool[l][:, :, :], 0.0, writes=[hist_pool[l].k])
        P.memset("pool", hist_conv[l][:, :, :], 0.0, writes=[hist_conv[l].k])
        P.memset("pool", S_p[l][:, :, :], 0.0, writes=[S_p[l].k])

    def dbg_store(name, ap, reads):
        if name in dbg_out:
            P.dma("pool", dbg_out[name], ap, reads=reads, is_output=True)

    w_in_v = [w_in[l].rearrange("(kt p) n -> p kt n", p=128) for l in range(DEPTH)]

    def load_w_in(dst, l, c0, ncol, dcol=0):
        P.dma("pool", dst[:, :, dcol:dcol + ncol], w_in_v[l][:, :, c0:c0 + ncol], writes=[dst.k])

    def fm_proj(ps_ap, Wb, wcol, M, NT, wk, psk):
        for kt in range(8):
            P.mm(ps_ap, Wb[:, kt, wcol:wcol + M], xnT[:, kt, 0:NT], start=(kt == 0), stop=(kt == 7),
                 reads=[wk, xnT.k], writes=[psk])

    def rstd_inplace(a, mult, keys):
        P.ts("dve", a, a, mult, EPS, ALU.mult, ALU.add, reads=keys, writes=keys)
        P.act(a, a, AF.Ln, reads=keys, writes=keys)
        P.act(a, a, AF.Exp, scale=-0.5, reads=keys, writes=keys)

    def stage_norm(cfg, l):
        new_stage()
        NT, TT, NTI = cfg["NT"], cfg["TT"], cfg["NTI"]
        P.dma("sp", gn_bc[:, :], norm_g[l].partition_broadcast(128), writes=[gn_bc.k])
        junk = af([128, D], "junk")
        ssq = af([128, 4], "ssq")
        xn = ab([128, D], "xn")
        for ti in range(NTI):
            xt = x_sb[0:TT, ti, :]
            P.act(junk.ap[0:TT, :], xt, AF.Square, accum_out=ssq.ap[0:TT, ti:ti + 1],
                  reads=[x_sb.k], writes=[junk.k, ssq.k])
            rstd_inplace(ssq.ap[0:TT, ti:ti + 1], 1.0 / D, [ssq.k])
            P.stt(xn.ap[0:TT, :], xt, ssq.ap[0:TT, ti:ti + 1], gn_bc[0:TT, :], ALU.mult, ALU.mult,
                  reads=[x_sb.k, ssq.k, gn_bc.k], writes=[xn.k])
            for kt in range(8):
                P.tr(bankb[:, kt * 128: kt * 128 + TT], xn.ap[0:TT, kt * 128:(kt + 1) * 128], identb[0:TT, 0:TT],
                     reads=[xn.k, identb.k], writes=[bankb.k])
            P.copy("act", xnT[:, :, ti * TT:(ti + 1) * TT], hv(bankb[:, :], 8)[:, :, 0:TT],
                   reads=[bankb.k], writes=[xnT.k])
        P.memset("pool", mrg[:, :, 0:NT], 0.0, writes=[mrg.k])
        dbg_store("xnT", xnT[:, :, 0:NT], [xnT.k])

    def merge_branch(cfg, l, bi, yv, yk, w_br, per_head):
        NT = cfg["NT"]
        if per_head:
            wv = WBR[0:64, :].rearrange("p (h n) -> p h n", h=8)
            P.dma("pool", wv, w_br[l].rearrange("(h p) n -> p h n", p=64), writes=[WBR.k])
        else:
            wv = WBR[:, 0:4 * D].rearrange("p (h n) -> p h n", h=4)
            P.dma("pool", wv, w_br[l].rearrange("(kt p) n -> p kt n", p=128), writes=[WBR.k])
        gs = af([128, CH], "gsig")
        for half in range(2):
            load_w_in(WB, l, C_GATE + bi * D + half * 512, 512)
            for jj in range(4):
                j = half * 4 + jj
                pg = bank()
                fm_proj(pg[:, 0:NT], WB, jj * 128, 128, NT, WB.k, pg.k)
                P.act(gs.ap[:, 0:NT], pg[:, 0:NT], AF.Sigmoid, reads=[pg.k], writes=[gs.k])
                pb = bank()
                nk = 8 if per_head else 4
                for kk in range(nk):
                    P.mm(pb[:, 0:NT], wv[:, kk, j * 128:(j + 1) * 128], yv[:, kk, 0:NT],
                         start=(kk == 0), stop=(kk == nk - 1), reads=[WBR.k, yk], writes=[pb.k])
                P.tt("dve", gs.ap[:, 0:NT], gs.ap[:, 0:NT], pb[:, 0:NT], ALU.mult, reads=[gs.k, pb.k], writes=[gs.k])
                P.tt("pool", mrg[:, j, 0:NT], mrg[:, j, 0:NT], gs.ap[:, 0:NT], ALU.add, reads=[gs.k, mrg.k], writes=[mrg.k])

    def stage_out(cfg, l):
        new_stage()
        NT, TT, NTI = cfg["NT"], cfg["TT"], cfg["NTI"]
        dbg_store(f"mrg{l}", mrg[:, :, 0:NT], [mrg.k])
        mb = ab([128, 8, NT], "mrgb")
        P.copy("dve", mb.v, mrg[:, :, 0:NT], reads=[mrg.k], writes=[mb.k])
        wo = w_out[l].rearrange("(kt p) n -> p kt n", p=128)
        for half in range(2):
            P.dma("pool", WB[:, :, :], wo[:, :, half * 512:(half + 1) * 512], writes=[WB.k])
            for ti in range(NTI):
                pb = bank()
                for kt in range(8):
                    P.mm(pb[0:TT, :], mb.v[:, kt, ti * TT:(ti + 1) * TT], WB[:, kt, :], start=(kt == 0), stop=(kt == 7),
                         reads=[mb.k, WB.k], writes=[pb.k])
                xsl = x_sb[0:TT, ti, half * 512:(half + 1) * 512]
                P.tt("dve", xsl, xsl, pb[0:TT, :], ALU.add, reads=[pb.k, x_sb.k], writes=[x_sb.k])

    def stage_pool(cfg, l):
        new_stage()
        NT, B, Ls, prompt, ck = cfg["NT"], cfg["B"], cfg["Ls"], cfg["prompt"], cfg["ck"]
        W = 15 + Ls
        load_w_in(WA, l, C_POOL, 512)
        load_w_in(WB, l, C_ZPOOL, 512)
        P.dma("pool", mixw[:, :, :], pool_mix[l], writes=[mixw.k])
        P.dma("sp", psc[:, :], pool_scale[l], writes=[psc.k])
        ext = af([128, 4, B, W], "ext")
        if prompt:
            P.copy("pool", ext.v[:, :, 0, 0:15], hist_pool[l][:, :, :], reads=[hist_pool[l].k], writes=[ext.k])
        else:
            stg = af([15, 4 * 512], "stg")
            for b in range(4):
                P.dma("sp", stg.ap[:, b * 512:(b + 1) * 512], spool[l, b], writes=[stg.k])
            pt = bank()
            for b in range(4):
                for g in range(4):
                    P.tr(pt[:, (b * 4 + g) * 15:(b * 4 + g + 1) * 15], stg.ap[0:15, b * 512 + g * 128: b * 512 + (g + 1) * 128],
                         identf[0:15, 0:15], reads=[stg.k, cf.k], writes=[pt.k])
            P.copy("act", ext.v[:, :, :, 0:15], pt[:, 0:240].rearrange("p (b g t) -> p g b t", b=4, g=4),
                   reads=[pt.k], writes=[ext.k])
        for g in range(4):
            pu = bank()
            fm_proj(pu[:, 0:NT], WA, g * 128, 128, NT, WA.k, pu.k)
            P.copy("act", ext.v[:, g, :, 15:W], pu[:, 0:NT].rearrange("p (b t) -> p b t", b=B), reads=[pu.k], writes=[ext.k])
        if prompt:
            P.copy("pool", hist_pool[l][:, :, :], ext.v[:, :, 0, Ls:Ls + 15], reads=[ext.k], writes=[hist_pool[l].k])
        if (not prompt) or ck == NCH - 1:
            ostg = af([15, 512], "ostg")
            for b in range(B):
                pt = bank()
                for g in range(4):
                    P.tr(pt[0:15, g * 128:(g + 1) * 128], ext.v[:, g, b, Ls:Ls + 15], identf[:, :],
                         reads=[ext.k, cf.k], writes=[pt.k])
                P.copy("act", ostg.ap[0:15, :], pt[0:15, 0:512], reads=[pt.k], writes=[ostg.k])
                dst = o_ppool[l] if prompt else o_spool[l, b]
                P.dma("sp", dst, ostg.ap[0:15, :], reads=[ostg.k], is_output=True)
        wa = af([128, B, W], "wa")
        wb_ = af([128, B, W], "wb")
        dT = ab([128, 4, B, Ls], "dT")
        ya = ab([128, 4, NT], "ya")
        zs = af([128, CH], "zs")
        fx = af([128, 15], "fx")
        for g, wdw in enumerate((2, 4, 8, 16)):
            cur, curk, n = ext.v[:, g, :, :], ext.k, W
            sh = 1
            bufs = [wa, wb_]
            bi = 0
            while sh < wdw:
                o = bufs[bi]
                P.tt("pool", o.v[:, :, 0:n - sh], cur[:, :, sh:n], cur[:, :, 0:n - sh], ALU.add, reads=[curk], writes=[o.k])
                cur, curk, n = o.v, o.k, n - sh
                sh *= 2
                bi ^= 1
            o0 = n - Ls
            P.stt(dT.v[:, g, :, :], cur[:, :, o0:o0 + Ls], 1.0 / wdw, ext.v[:, g, :, 15:W], ALU.mult, ALU.subtract,
                  reads=[curk, ext.k], writes=[dT.k])
            if prompt and ck == 0:
                P.tt("pool", fx.ap, cur[:, 0, o0:o0 + 15], rc15[:, g * 15:(g + 1) * 15], ALU.mult,
                     reads=[curk, cf.k], writes=[fx.k])
                P.tt("dve", dT.v[:, g, 0, 0:15], fx.ap, ext.v[:, g, 0, 15:30], ALU.subtract,
                     reads=[fx.k, ext.k, dT.k], writes=[dT.k])
        for g in range(4):
            p1 = bank()
            P.mm(p1[:, 0:NT], mixw[:, g, :], dT.ap[:, g * NT:(g + 1) * NT], reads=[mixw.k, dT.k], writes=[p1.k])
            p2 = bank()
            fm_proj(p2[:, 0:NT], WB, g * 128, 128, NT, WB.k, p2.k)
            P.act(zs.ap[:, 0:NT], p2[:, 0:NT], AF.Silu, reads=[p2.k], writes=[zs.k])
            P.stt(ya.v[:, g, 0:NT], p1[:, 0:NT], psc[:, g:g + 1], zs.ap[:, 0:NT], ALU.mult, ALU.mult,
                  reads=[p1.k, psc.k, zs.k], writes=[ya.k])
        dbg_store(f"ya{l}", ya.v, [ya.k])
        merge_branch(cfg, l, 0, ya.v, ya.k, w_br_pool, per_head=False)

    def stage_mla(cfg, l):
        new_stage()
        NT, TT, NTI, B, Ls, prompt, ck = cfg["NT"], cfg["TT"], cfg["NTI"], cfg["B"], cfg["Ls"], cfg["prompt"], cfg["ck"]
        tok0 = ck * CH if prompt else 0
        load_w_in(WA, l, C_Q, 672)
        load_w_in(WB, l, C_ZMLA, 512)
        P.dma("pool", wuq[:, :, :], w_uq[l].rearrange("(kt p) n -> p kt n", p=128), writes=[wuq.k])
        P.dma("pool", wuk[:, :, :], w_ukT[l], writes=[wuk.k])
        P.dma("pool", wuv[:, :, :], w_uv[l].rearrange("(kt p) n -> p kt n", p=128), writes=[wuv.k])
        P.dma("sp", gq_bc[:, :], q_norm_g[l].partition_broadcast(128), writes=[gq_bc.k])
        P.dma("sp", gkv_bc[:, :], kv_norm_g[l].partition_broadcast(128), writes=[gkv_bc.k])
        rq0 = tok0 if prompt else SEQ
        P.dma("sp", ropeq[:, :, 0:NT], ropeq_d[:, :, rq0:rq0 + NT], writes=[ropeq.k])
        yb = ab([64, 8, NT], "yb")
        for h in range(8):
            pz = bank()
            fm_proj(pz[0:64, 0:NT], WB, h * 64, 64, NT, WB.k, pz.k)
            P.act(yb.v[:, h, 0:NT], pz[0:64, 0:NT], AF.Silu, reads=[pz.k], writes=[yb.k])
        ckvf = af([128, 288], "ckvf")
        ssq = af([128, 2], "ssq2")
        junk = af([128, 384], "junk2")
        t1 = af([128, 64], "ropetmp")
        qr = af([32, 2, 8, TT], "qr")
        rden = af([128, 4 * TT], "rden")
        ckvb = ab([128, 288], "ckvb")
        cqb = ab([128, 384], "cqb")
        cqT = ab([128, 3, TT], "cqT")
        qn = ab([64, 8, TT], "qn")
        qrT = ab([32, 8, TT], "qrT")
        qlT = ab([128, 2, 8, TT], "qlT")
        if prompt:
            pT = [ab([128, 4, TT], f"pT{i}") for i in range(2)]
        else:
            knT = ab([128, 3, 32], "knT")
            vn = ab([32, 256], "vn")
            pg_b = [ab([128, 288], f"pgb{i}") for i in range(3)]
            pg_T = [ab([128, 3, 128], f"pgT{i}") for i in range(2)]
            pts = [ab([128, 64], f"pts{i}") for i in range(2)]
            vb8 = ab([8, 256], "vb8")
            qc = ab([128, 2, 64], "qc")
            qrc = ab([32, 64], "qrc")
            ols = ab([128, 2, 64], "ols")
        for ti in range(NTI):
            set_rot(range(7))
            ktile = (tok0 // 128 + ti) if prompt else 16
            tsl = slice(ti * TT, (ti + 1) * TT)
            pk = bank()
            for kt in range(8):
                P.mm(pk[0:TT, 0:288], xnT[:, kt, tsl], WA[:, kt, 384:672], start=(kt == 0), stop=(kt == 7),
                     reads=[xnT.k, WA.k], writes=[pk.k])
            P.act(junk.ap[0:TT, 0:256], pk[0:TT, 0:256], AF.Square, accum_out=ssq.ap[0:TT, 0:1],
                  reads=[pk.k], writes=[junk.k, ssq.k])
            rstd_inplace(ssq.ap[0:TT, 0:1], 1.0 / 256, [ssq.k])
            P.stt(ckvf.ap[0:TT, 0:256], pk[0:TT, 0:256], ssq.ap[0:TT, 0:1], gkv_bc[0:TT, :], ALU.mult, ALU.mult,
                  reads=[pk.k, ssq.k, gkv_bc.k], writes=[ckvf.k])
            cosk = ropek[0:TT, ktile, 0:16]
            sink = ropek[0:TT, ktile, 16:32]
            P.tt("dve", t1.ap[0:TT, 0:16], pk[0:TT, 256:272], cosk, ALU.mult, reads=[pk.k, ropek.k], writes=[t1.k])
            P.tt("dve", t1.ap[0:TT, 16:32], pk[0:TT, 272:288], sink, ALU.mult, reads=[pk.k, ropek.k], writes=[t1.k])
            P.tt("dve", t1.ap[0:TT, 32:48], pk[0:TT, 272:288], cosk, ALU.mult, reads=[pk.k, ropek.k], writes=[t1.k])
            P.tt("dve", t1.ap[0:TT, 48:64], pk[0:TT, 256:272], sink, ALU.mult, reads=[pk.k, ropek.k], writes=[t1.k])
            P.tt("dve", ckvf.ap[0:TT, 256:272], t1.ap[0:TT, 0:16], t1.ap[0:TT, 16:32], ALU.subtract, reads=[t1.k], writes=[ckvf.k])
            P.tt("dve", ckvf.ap[0:TT, 272:288], t1.ap[0:TT, 32:48], t1.ap[0:TT, 48:64], ALU.add, reads=[t1.k], writes=[ckvf.k])
            if prompt:
                r0 = tok0 + ti * 128
                P.dma("sp", o_pkv[l, r0:r0 + 128, :], ckvf.ap[0:128, 0:256], reads=[ckvf.k], is_output=True)
                P.dma("sp", o_pkr[l, r0:r0 + 128, :], ckvf.ap[0:128, 256:288], reads=[ckvf.k], is_output=True)
            else:
                P.dma("sp", o_skv[l], ckvf.ap[0:32, 0:256], reads=[ckvf.k], is_output=True)
                P.dma("sp", o_skr[l], ckvf.ap[0:32, 256:288], reads=[ckvf.k], is_output=True)
            P.copy("pool", ckvb.ap[0:TT, :], ckvf.ap[0:TT, :], reads=[ckvf.k], writes=[ckvb.k])
            for j, (c0, cn) in enumerate(((0, 128), (128, 128), (256, 32))):
                P.tr(bankb[0:cn, j * 128: j * 128 + TT], ckvb.ap[0:TT, c0:c0 + cn], identb[0:TT, 0:TT],
                     reads=[ckvb.k, identb.k], writes=[bankb.k])
            if prompt:
                P.copy("pool", Vc[l][:, ktile, :], ckvb.ap[:, 0:256], reads=[ckvb.k], writes=[Vc[l].k])
                P.copy("act", kTc[l][:, 0:2, ktile * 128:(ktile + 1) * 128], hv(bankb[:, 0:256], 2),
                       reads=[bankb.k], writes=[kTc[l].k])
                P.copy("act", kTc[l][0:32, 2, ktile * 128:(ktile + 1) * 128], bankb[0:32, 256:384],
                       reads=[bankb.k], writes=[kTc[l].k])
            else:
                P.copy("pool", vn.ap[0:32, :], ckvb.ap[0:32, 0:256], reads=[ckvb.k], writes=[vn.k])
                P.copy("act", knT.v[:, 0:2, :], hv(bankb[:, 0:256], 2)[:, :, 0:32], reads=[bankb.k], writes=[knT.k])
                P.copy("act", knT.v[0:32, 2, :], bankb[0:32, 256:288], reads=[bankb.k], writes=[knT.k])
            pq = bank()
            for kt in range(8):
                P.mm(pq[0:TT, 0:384], xnT[:, kt, tsl], WA[:, kt, 0:384], start=(kt == 0), stop=(kt == 7),
                     reads=[xnT.k, WA.k], writes=[pq.k])
            P.act(junk.ap[0:TT, 0:384], pq[0:TT, 0:384], AF.Square, accum_out=ssq.ap[0:TT, 1:2],
                  reads=[pq.k], writes=[junk.k, ssq.k])
            rstd_inplace(ssq.ap[0:TT, 1:2], 1.0 / 384, [ssq.k])
            P.stt(cqb.ap[0:TT, :], pq[0:TT, 0:384], ssq.ap[0:TT, 1:2], gq_bc[0:TT, :], ALU.mult, ALU.mult,
                  reads=[pq.k, ssq.k, gq_bc.k], writes=[cqb.k])
            for j in range(3):
                P.tr(bankb[:, 384 + j * 128: 384 + j * 128 + TT], cqb.ap[0:TT, j * 128:(j + 1) * 128], identb[0:TT, 0:TT],
                     reads=[cqb.k, identb.k], writes=[bankb.k])
            P.copy("act", cqT.v[:, :, 0:TT], hv(bankb[:, 384:768], 3)[:, :, 0:TT], reads=[bankb.k], writes=[cqT.k])
            for hg in range(2):
                pn = bank()
                for hh in range(4):
                    h = hg * 4 + hh
                    for j in range(3):
                        P.mm(pn[0:64, hh * 128: hh * 128 + TT], wuq[:, j, h * 128: h * 128 + 64], cqT.v[:, j, 0:TT],
                             start=(j == 0), stop=(j == 2), reads=[wuq.k, cqT.k], writes=[pn.k])
                P.copy("act", qn.v[:, hg * 4:(hg + 1) * 4, :], hv(pn[0:64, :], 4)[:, :, 0:TT], reads=[pn.k], writes=[qn.k])
            for v in range(2):
                for hg in range(2):
                    pr = bank()
                    for hh in range(4):
                        h = hg * 4 + hh
                        c0 = h * 128 + 64 + v * 32
                        for j in range(3):
                            P.mm(pr[0:32, hh * 128: hh * 128 + TT], wuq[:, j, c0:c0 + 32],
                                 cqT.v[:, j, 0:TT], start=(j == 0), stop=(j == 2), reads=[wuq.k, cqT.k], writes=[pr.k])
                    tab = ropeq[:, v, tsl]
                    P.tt("dve", qr.v[:, v, hg * 4:(hg + 1) * 4, :], hv(pr[0:32, :], 4)[:, :, 0:TT],
                         tab.unsqueeze(1).to_broadcast([32, 4, TT]), ALU.mult, reads=[pr.k, ropeq.k], writes=[qr.k])
            P.tt("pool", qrT.v, qr.v[:, 0, :, :], qr.v[:, 1, :, :], ALU.add, reads=[qr.k], writes=[qrT.k])
            for j in range(2):
                for hg in range(2):
                    pl = bank()
                    for hh in range(4):
                        h = hg * 4 + hh
                        P.mm(pl[:, hh * 128: hh * 128 + TT], wuk[:, h, j * 128:(j + 1) * 128], qn.v[:, h, :],
                             reads=[wuk.k, qn.k], writes=[pl.k])
                    P.copy("act" if hg == 0 else "dve", qlT.v[:, j, hg * 4:(hg + 1) * 4, :], hv(pl[:, :], 4)[:, :, 0:TT],
                           reads=[pl.k], writes=[qlT.k])
            dbg_store(f"qlT{l}", qlT.v, [qlT.k])
            dbg_store(f"qrT{l}", qrT.v, [qrT.k])
            po = [banks[0], banks[1]]
            pd = banks[2]
            set_rot([3, 4, 5, 6])
            if prompt:
                nkt = ktile + 1
                for hg in range(2):
                    qsl = slice(hg * 4, (hg + 1) * 4)
                    for kt in range(nkt):
                        pscr = bank()
                        ksl = slice(kt * 128, (kt + 1) * 128)
                        P.mm(pscr[:, :], kTc[l][:, 0, ksl], qlT.v[:, 0, qsl, :], start=True, stop=False,
                             reads=[kTc[l].k, qlT.k], writes=[pscr.k])
                        P.mm(pscr[:, :], kTc[l][:, 1, ksl], qlT.v[:, 1, qsl, :], start=False, stop=False,
                             reads=[kTc[l].k, qlT.k], writes=[pscr.k])
                        P.mm(pscr[:, :], kTc[l][0:32, 2, ksl], qrT.v[0:32, qsl, :], start=False, stop=True,
                             reads=[kTc[l].k, qrT.k], writes=[pscr.k])
                        pt_ = pT[kt % 2]
                        P.act(pt_.ap[:, :], pscr[:, :], AF.Exp, scale=MLA_SCALE, reads=[pscr.k], writes=[pt_.k])
                        if kt == ktile:
                            P.tt("pool", pt_.v, pt_.v, m_causal.unsqueeze(1).to_broadcast([128, 4, 128]), ALU.mult,
                                 reads=[cf.k, pt_.k], writes=[pt_.k])
                        for j in range(2):
                            P.mm(po[j][:, :], Vc[l][:, kt, j * 128:(j + 1) * 128], pt_.ap[:, :], start=(kt == 0), stop=(kt == nkt - 1),
                                 reads=[Vc[l].k, pt_.k], writes=[po[j].k])
                        P.mm(pd[:, :], onesb[:, :], pt_.ap[:, :], start=(kt == 0), stop=(kt == nkt - 1),
                             reads=[onesb.k, pt_.k], writes=[pd.k])
                    P.act(rden.ap[:, :], pd[:, :], AF.Ln, reads=[pd.k], writes=[rden.k])
                    P.act(rden.ap[:, :], rden.ap[:, :], AF.Exp, scale=-1.0, reads=[rden.k], writes=[rden.k])
                    for j in range(2):
                        P.tt("dve", qlT.v[:, j, qsl, :], hv(po[j][:, :], 4), hv(rden.ap[:, :], 4), ALU.mult,
                             reads=[po[j].k, rden.k, qlT.k], writes=[qlT.k])
                for hg in range(2):
                    pm = bank()
                    for hh in range(4):
                        h = hg * 4 + hh
                        for j in range(2):
                            P.mm(pm[0:64, hh * 128:(hh + 1) * 128], wuv[:, j, h * 64:(h + 1) * 64], qlT.v[:, j, h, :],
                                 start=(j == 0), stop=(j == 1), reads=[wuv.k, qlT.k], writes=[pm.k])
                    P.tt("dve", yb.v[:, hg * 4:(hg + 1) * 4, tsl], hv(pm[0:64, :], 4), yb.v[:, hg * 4:(hg + 1) * 4, tsl], ALU.mult,
                         reads=[pm.k, yb.k], writes=[yb.k])
            else:
                ckv_rows = ckv.rearrange("l n t c -> (l n t) c")
                ckr_rows = ckr.rearrange("l n t c -> (l n t) c")
                for b in range(4):
                    P.dma("sp", ptb[:, :], ptab[b].partition_broadcast(128), writes=[ptb.k])
                    P.ts("dve", ridx[:, :], ptb[:, :], 128.0, cf[:, 600 + l:601 + l], ALU.mult, ALU.add, reads=[ptb.k, cf.k], writes=[ridx.k])
                    P.copy("dve", qc.v.rearrange("p j (h t) -> p j h t", t=8), qlT.v[:, :, :, b * 8:(b + 1) * 8],
                           reads=[qlT.k], writes=[qc.k])
                    P.copy("dve", qrc.ap.rearrange("p (h t) -> p h t", t=8), qrT.v[:, :, b * 8:(b + 1) * 8],
                           reads=[qrT.k], writes=[qrc.k])
                    for page in range(NPAGE + 1):
                        pscr = bank()
                        ptsb = pts[page % 2]
                        first = (page == 0)
                        last = (page == NPAGE)
                        if page < NPAGE:
                            bb = pg_b[page % 3]
                            tb = pg_T[page % 2]
                            P.idma(bb.ap[:, 0:256], ckv_rows, ridx[:, page:page + 1], reads=[ridx.k], writes=[bb.k])
                            P.idma(bb.ap[:, 256:288], ckr_rows, ridx[:, page:page + 1], reads=[ridx.k], writes=[bb.k])
                            for j, (c0, cn) in enumerate(((0, 128), (128, 128), (256, 32))):
                                P.tr(bankb[0:cn, j * 128:(j + 1) * 128], bb.ap[:, c0:c0 + cn], identb[:, :],
                                     reads=[bb.k, identb.k], writes=[bankb.k])
                            P.copy("dve", tb.v[:, 0:2, :], hv(bankb[:, 0:256], 2), reads=[bankb.k], writes=[tb.k])
                            P.copy("dve", tb.v[0:32, 2, :], bankb[0:32, 256:384], reads=[bankb.k], writes=[tb.k])
                            kk = 128
                            k0, k1, k2 = tb.v[:, 0, :], tb.v[:, 1, :], tb.v[0:32, 2, :]
                            kdeps = [tb.k]
                            vsrc, vdeps = bb.ap, [bb.k]
                        else:
                            kk = 8
                            k0, k1, k2 = knT.v[:, 0, b * 8:(b + 1) * 8], knT.v[:, 1, b * 8:(b + 1) * 8], knT.v[0:32, 2, b * 8:(b + 1) * 8]
                            kdeps = [knT.k]
                            P.dma("sp", vb8.ap[0:8, :], vn.ap[b * 8:(b + 1) * 8, :], reads=[vn.k], writes=[vb8.k])
                            vsrc, vdeps = vb8.ap, [vb8.k]
                        P.mm(pscr[0:kk, 0:64], k0, qc.v[:, 0, :], start=True, stop=False, reads=kdeps + [qc.k], writes=[pscr.k])
                        P.mm(pscr[0:kk, 0:64], k1, qc.v[:, 1, :], start=False, stop=False, reads=kdeps + [qc.k], writes=[pscr.k])
                        P.mm(pscr[0:kk, 0:64], k2, qrc.ap[0:32, :], start=False, stop=True, reads=kdeps + [qrc.k], writes=[pscr.k])
                        P.act(ptsb.ap[0:kk, :], pscr[0:kk, 0:64], AF.Exp, scale=MLA_SCALE, reads=[pscr.k], writes=[ptsb.k])
                        if last:
                            P.tt("pool", hv(ptsb.ap[0:8, :], 8), hv(ptsb.ap[0:8, :], 8),
                                 m_causal8.unsqueeze(1).to_broadcast([8, 8, 8]), ALU.mult, reads=[cf.k, ptsb.k], writes=[ptsb.k])
                        for j in range(2):
                            P.mm(po[j][:, 0:64], vsrc[0:kk, j * 128:(j + 1) * 128], ptsb.ap[0:kk, :], start=first, stop=last,
                                 reads=vdeps + [ptsb.k], writes=[po[j].k])
                        P.mm(pd[:, 0:64], onesb[0:kk, :], ptsb.ap[0:kk, :], start=first, stop=last,
                             reads=[onesb.k, ptsb.k], writes=[pd.k])
                    P.act(rden.ap[:, 0:64], pd[:, 0:64], AF.Ln, reads=[pd.k], writes=[rden.k])
                    P.act(rden.ap[:, 0:64], rden.ap[:, 0:64], AF.Exp, scale=-1.0, reads=[rden.k], writes=[rden.k])
                    for j in range(2):
                        P.tt("dve", ols.v[:, j, :], po[j][:, 0:64], rden.ap[:, 0:64], ALU.mult,
                             reads=[po[j].k, rden.k], writes=[ols.k])
                    pm = bank()
                    for h in range(8):
                        for j in range(2):
                            P.mm(pm[0:64, h * 8:(h + 1) * 8], wuv[:, j, h * 64:(h + 1) * 64], ols.v[:, j, h * 8:(h + 1) * 8],
                                 start=(j == 0), stop=(j == 1), reads=[wuv.k, ols.k], writes=[pm.k])
                    P.tt("dve", yb.v[:, :, b * 8:(b + 1) * 8], hv(pm[0:64, 0:64], 8), yb.v[:, :, b * 8:(b + 1) * 8], ALU.mult,
                         reads=[pm.k, yb.k], writes=[yb.k])
        set_rot(range(7))
        dbg_store(f"yb{l}", yb.v, [yb.k])
        merge_branch(cfg, l, 1, yb.v, yb.k, w_br_mla, per_head=True)

    def stage_dn(cfg, l):
        new_stage()
        NT, B, Ls, prompt, ck, C = cfg["NT"], cfg["B"], cfg["Ls"], cfg["prompt"], cfg["ck"], cfg["C"]
        NSUB = NT // C
        LV = int(np.log2(C))
        W = 3 + Ls
        do_out = (not prompt) or ck == NCH - 1
        P.dma("sp", cw[:, :, :], conv_w[l], writes=[cw.k])
        P.dma("sp", gdn[:, :], dn_norm_g[l], writes=[gdn.k])
        P.dma("sp", a_bc[:, :], a_log[l].partition_broadcast(64), writes=[a_bc.k])
        P.dma("sp", dtb_bc[:, :], dt_bias[l].partition_broadcast(64), writes=[dtb_bc.k])
        nega = af([64, 8], "nega")
        P.act(nega.ap, a_bc[:, :], AF.Exp, reads=[a_bc.k], writes=[nega.k])
        P.ts("dve", nega.ap, nega.ap, -1.0, None, ALU.mult, reads=[nega.k], writes=[nega.k])
        yc = ab([64, 8, NT], "yc")
        extc = af([64, B, W], "extc")
        if not prompt:
            hs = af([64, 24, 4, 3], "hs")
            stgc = af([3, 1536], "stgc")
            for b in range(4):
                P.dma("sp", stgc.ap[0:3, :], sconv[l, b], writes=[stgc.k])
                pt = bank()
                for ht in range(24):
                    P.tr(pt[0:64, ht * 3:(ht + 1) * 3], stgc.ap[0:3, ht * 64:(ht + 1) * 64], identf[0:3, 0:3],
                         reads=[stgc.k, cf.k], writes=[pt.k])
                P.copy("act", hs.v[:, :, b, :], hv(pt[0:64, 0:72], 24), reads=[pt.k], writes=[hs.k])
        ost = [af([3, 64], f"ost{i}") for i in range(2)]
        qkvb = [ab([64, 4, NT], f"qkvb{i}") for i in range(3)]
        cacc = af([64, NT], "cacc")
        sq = af([64, NT], "sq")
        names = ["Gb", "dgb", "E", "E1", "kbg", "qg", "Q0", "qkT", "P0", "TT", "Qb", "Pb", "vb", "kd", "R", "vnw", "osq"]
        tmp = {n: af([64, 4, 64], n) for n in names}
        kbT = ab([64, 4, 64], "kbT")
        beta = af([64, 4], "beta")
        gg = af([64, 4], "gg")
        gc = af([64, 4], "gc")
        elast = af([64, 4], "elast")
        edl = af([64, 4], "edl")
        Ssm = af([64, 4, 64], "Ssm") if not prompt else None
        n_ost = 0
        for hg in range(2):
            hsl = slice(hg * 4, (hg + 1) * 4)
            for which in range(3):
                load_w_in(WA, l, C_QKV + which * 512 + hg * 256, 256)
                for hh in range(4):
                    h = hg * 4 + hh
                    ht = which * 8 + h
                    pp = bank()
                    fm_proj(pp[0:64, 0:NT], WA, hh * 64, 64, NT, WA.k, pp.k)
                    if prompt:
                        P.copy("pool", extc.v[:, 0, 0:3], hist_conv[l][:, ht, :], reads=[hist_conv[l].k], writes=[extc.k])
                    else:
                        P.copy("pool", extc.v[:, :, 0:3], hs.v[:, ht, :, :], reads=[hs.k], writes=[extc.k])
                    P.copy("act", extc.v[:, :, 3:W], pp[0:64, 0:NT].rearrange("p (b t) -> p b t", b=B), reads=[pp.k], writes=[extc.k])
                    if prompt:
                        P.copy("pool", hist_conv[l][:, ht, :], extc.v[:, 0, Ls:Ls + 3], reads=[extc.k], writes=[hist_conv[l].k])
                    if do_out:
                        for b in range(B):
                            pt = bank()
                            P.tr(pt[0:3, 0:64], extc.v[:, b, Ls:Ls + 3], identf[0:64, 0:64], reads=[extc.k, cf.k], writes=[pt.k])
                            o_ = ost[n_ost % 2]
                            n_ost += 1
                            P.copy("act", o_.ap[0:3, :], pt[0:3, 0:64], reads=[pt.k], writes=[o_.k])
                            dst = o_pconv[l] if prompt else o_sconv[l, b]
                            P.dma("sp", dst[:, ht * 64:(ht + 1) * 64], o_.ap[0:3, :], reads=[o_.k], is_output=True)
                    caccv = cacc.ap[:, 0:NT].rearrange("p (b t) -> p b t", b=B)
                    P.ts("dve", caccv, extc.v[:, :, 0:Ls], cw[:, ht, 0:1], None, ALU.mult, reads=[extc.k, cw.k], writes=[cacc.k])
                    for j in range(1, 4):
                        P.stt(caccv, extc.v[:, :, j:j + Ls], cw[:, ht, j:j + 1], caccv, ALU.mult, ALU.add,
                              reads=[extc.k, cw.k, cacc.k], writes=[cacc.k])
                    if which == 2:
                        P.act(qkvb[2].v[:, hh, :], cacc.ap[:, 0:NT], AF.Silu, reads=[cacc.k], writes=[qkvb[2].k])
                    else:
                        P.act(cacc.ap[:, 0:NT], cacc.ap[:, 0:NT], AF.Silu, reads=[cacc.k], writes=[cacc.k])
                        P.act(sq.ap[:, 0:NT], cacc.ap[:, 0:NT], AF.Square, reads=[cacc.k], writes=[sq.k])
                        pss = bank()
                        P.mm(pss[0:64, 0:NT], onesf[0:64, 0:64], sq.ap[:, 0:NT], reads=[cf.k, sq.k], writes=[pss.k])
                        P.ts("dve", sq.ap[:, 0:NT], pss[0:64, 0:NT], EPS, None, ALU.add, reads=[pss.k], writes=[sq.k])
                        P.act(sq.ap[:, 0:NT], sq.ap[:, 0:NT], AF.Ln, reads=[sq.k], writes=[sq.k])
                        P.act(sq.ap[:, 0:NT], sq.ap[:, 0:NT], AF.Exp, scale=-0.5, reads=[sq.k], writes=[sq.k])
                        if l == 0 and hg == 0 and which == 1 and hh == 0:
                            dbg_store("rs", sq.ap[:, 0:NT], [sq.k])
                            dbg_store("cs", cacc.ap[:, 0:NT], [cacc.k])
                        if which == 0:
                            P.stt(qkvb[0].v[:, hh, :], cacc.ap[:, 0:NT], 0.125, sq.ap[:, 0:NT], ALU.mult, ALU.mult,
                                  reads=[cacc.k, sq.k], writes=[qkvb[0].k])
                        else:
                            P.tt("dve", qkvb[1].v[:, hh, :], cacc.ap[:, 0:NT], sq.ap[:, 0:NT], ALU.mult,
                                 reads=[cacc.k, sq.k], writes=[qkvb[1].k])
            load_w_in(WB, l, C_ZDN + hg * 256, 256)
            load_w_in(WB, l, C_BETA, 16, dcol=256)
            for hh in range(4):
                pz = bank()
                fm_proj(pz[0:64, 0:NT], WB, hh * 64, 64, NT, WB.k, pz.k)
                P.act(yc.v[:, hg * 4 + hh, :], pz[0:64, 0:NT], AF.Silu, reads=[pz.k], writes=[yc.k])
            Gb, dgb, E, E1, kbg, qg = (tmp[n] for n in ("Gb", "dgb", "E", "E1", "kbg", "qg"))
            Q0, qkT, P0, TT_, Qb, Pb = (tmp[n] for n in ("Q0", "qkT", "P0", "TT", "Qb", "Pb"))
            vb, kd, R, vnw, osq = (tmp[n] for n in ("vb", "kd", "R", "vnw", "osq"))
            for s in range(NSUB if DN_NSUB is None else DN_NSUB):
                cs = slice(s * C, (s + 1) * C)
                bseq = s
                if not prompt:
                    P.dma("sp", Ssm.v, sdelta[l, bseq, hsl].rearrange("h k v -> k h v"), writes=[Ssm.k])
                    Sv, Sk = Ssm.v, Ssm.k
                else:
                    Sv, Sk = S_p[l][:, hsl, :], S_p[l].k
                pbg = bank()
                for kt in range(8):
                    P.mm(pbg[0:C, 0:16], xnT[:, kt, cs], WB[:, kt, 256:272], start=(kt == 0), stop=(kt == 7),
                         reads=[xnT.k, WB.k], writes=[pbg.k])
                P.act(beta.ap[0:C, :], pbg[0:C, hg * 4:(hg + 1) * 4], AF.Sigmoid, reads=[pbg.k], writes=[beta.k])
                P.tt("dve", gg.ap[0:C, :], pbg[0:C, 8 + hg * 4: 12 + hg * 4], dtb_bc[0:C, hsl], ALU.add, reads=[pbg.k, dtb_bc.k], writes=[gg.k])
                P.act(gg.ap[0:C, :], gg.ap[0:C, :], AF.Exp, reads=[gg.k], writes=[gg.k])
                P.ts("dve", gg.ap[0:C, :], gg.ap[0:C, :], 1.0, None, ALU.add, reads=[gg.k], writes=[gg.k])
                P.act(gg.ap[0:C, :], gg.ap[0:C, :], AF.Ln, reads=[gg.k], writes=[gg.k])
                P.tt("dve", gg.ap[0:C, :], gg.ap[0:C, :], nega.ap[0:C, hsl], ALU.mult, reads=[gg.k, nega.k], writes=[gg.k])
                pg1 = bank()
                P.mm(pg1[0:C, 0:4], m_incl[0:C, 0:C], gg.ap[0:C, :], reads=[cf.k, gg.k], writes=[pg1.k])
                P.mm(pg1[0:64, 4:8], onesf[0:C, 0:64], gg.ap[0:C, :], reads=[cf.k, gg.k], writes=[pg1.k])
                P.copy("act", gc.ap[0:C, :], pg1[0:C, 0:4], reads=[pg1.k], writes=[gc.k])
                P.act(elast.ap[:, :], pg1[0:64, 4:8], AF.Exp, reads=[pg1.k], writes=[elast.k])
                P.tt("dve", edl.ap[0:C, :], pg1[0:C, 4:8], gc.ap[0:C, :], ALU.subtract, reads=[pg1.k, gc.k], writes=[edl.k])
                P.act(edl.ap[0:C, :], edl.ap[0:C, :], AF.Exp, reads=[edl.k], writes=[edl.k])
                if DN_CUT <= 1:
                    continue
                P.copy("pool", Gb.v[0:C, :, :], gg.ap[0:C, :].unsqueeze(2).to_broadcast([C, 4, 64]), reads=[gg.k], writes=[Gb.k])
                if DN_CUT <= 1.2:
                    continue
                P.tt("pool", dgb.v[0:C, :, 0:C], identf[0:C, 0:C].unsqueeze(1).to_broadcast([C, 4, C]),
                     beta.ap[0:C, :].unsqueeze(2).to_broadcast([C, 4, C]), ALU.mult, reads=[cf.k, beta.k], writes=[dgb.k])
                if DN_CUT <= 1.4:
                    continue
                pgcb = bank()
                pbb = bank()
                for hh in range(4):
                    P.mm(pgcb[0:64, hh * 64: hh * 64 + C], Gb.v[0:C, hh, :], m_incl[0:C, 0:C], reads=[Gb.k, cf.k], writes=[pgcb.k])
                    P.mm(pbb[0:64, hh * 64: hh * 64 + C], onesf[0:C, 0:64], dgb.v[0:C, hh, 0:C], reads=[cf.k, dgb.k], writes=[pbb.k])
                if DN_CUT <= 1.6:
                    continue
                gcbv = hv(pgcb[0:64, 0:256], 4)
                pbbv = hv(pbb[0:64, 0:256], 4)
                P.act(E.v[:, :, 0:C], gcbv[:, :, 0:C], AF.Exp, reads=[pgcb.k], writes=[E.k])
                if DN_CUT <= 1.8:
                    continue
                P.ts("pool", dgb.v[0:C, :, :], Gb.v[0:C, :, :], -1.0, None, ALU.mult, reads=[Gb.k, dgb.k], writes=[dgb.k])
                pdf = bank()
                for hh in range(4):
                    P.mm(pdf[0:C, hh * 64: hh * 64 + C], Gb.v[0:C, hh, 0:C], m_incl[0:C, 0:C], start=True, stop=False,
                         reads=[Gb.k, cf.k], writes=[pdf.k])
                    P.mm(pdf[0:C, hh * 64: hh * 64 + C], m_incl[0:C, 0:C], dgb.v[0:C, hh, 0:C], start=False, stop=True,
                         reads=[dgb.k, cf.k], writes=[pdf.k])
                P.ts("dve", E1.v[0:C, :, 0:C], hv(pdf[0:64, 0:256], 4)[0:C, :, 0:C], 0.0, None, ALU.min, reads=[pdf.k], writes=[E1.k])
                if DN_CUT <= 1.9:
                    continue
                P.act(E1.v[0:C, :, 0:C], E1.v[0:C, :, 0:C], AF.Exp, reads=[E1.k], writes=[E1.k])
                if DN_CUT <= 2:
                    continue
                kTs = qkvb[1].v[:, :, cs]
                qTs = qkvb[0].v[:, :, cs]
                P.tt("dve", kbg.v[:, :, 0:C], kTs, pbbv[:, :, 0:C], ALU.mult, reads=[qkvb[1].k, pbb.k], writes=[kbg.k])
                P.copy("pool", kbT.v[:, :, 0:C], kbg.v[:, :, 0:C], reads=[kbg.k], writes=[kbT.k])
                P.tt("dve", kbg.v[:, :, 0:C], kbg.v[:, :, 0:C], E.v[:, :, 0:C], ALU.mult, reads=[kbg.k, E.k, kbT.k], writes=[kbg.k])
                P.tt("pool", qg.v[:, :, 0:C], qTs, E.v[:, :, 0:C], ALU.mult, reads=[qkvb[0].k, E.k], writes=[qg.k])
                pkk = bank()
                pqk = bank()
                for hh in range(4):
                    P.mm(pkk[0:C, hh * 64: hh * 64 + C], qkvb[1].v[:, hh, cs], kbT.v[:, hh, 0:C], reads=[qkvb[1].k, kbT.k], writes=[pkk.k])
                    P.mm(pqk[0:C, hh * 64: hh * 64 + C], qkvb[1].v[:, hh, cs], qkvb[0].v[:, hh, cs], reads=[qkvb[1].k, qkvb[0].k], writes=[pqk.k])
                P.tt("dve", Q0.v[0:C, :, 0:C], hv(pkk[0:64, 0:256], 4)[0:C, :, 0:C], E1.v[0:C, :, 0:C], ALU.mult, reads=[pkk.k, E1.k], writes=[Q0.k])
                P.tt("pool", Q0.v[0:C, :, 0:C], Q0.v[0:C, :, 0:C], m_nstrict[0:C, 0:C].unsqueeze(1).to_broadcast([C, 4, C]), ALU.mult,
                     reads=[Q0.k, cf.k], writes=[Q0.k])
                P.tt("dve", qkT.v[0:C, :, 0:C], hv(pqk[0:64, 0:256], 4)[0:C, :, 0:C], E1.v[0:C, :, 0:C], ALU.mult, reads=[pqk.k, E1.k], writes=[qkT.k])
                P.tt("pool", qkT.v[0:C, :, 0:C], qkT.v[0:C, :, 0:C], m_incl[0:C, 0:C].unsqueeze(1).to_broadcast([C, 4, C]), ALU.mult,
                     reads=[qkT.k, cf.k], writes=[qkT.k])
                if DN_CUT <= 3:
                    continue
                ptp = bank()
                for hh in range(4):
                    P.tr(ptp[0:C, hh * 64: hh * 64 + C], Q0.v[0:C, hh, 0:C], identf[0:C, 0:C], reads=[Q0.k, cf.k], writes=[ptp.k])
                P.copy("act", P0.v[0:C, :, 0:C], hv(ptp[0:64, 0:256], 4)[0:C, :, 0:C], reads=[ptp.k], writes=[P0.k])
                P.tt("dve", TT_.v[0:C, :, 0:C], Q0.v[0:C, :, 0:C], identf[0:C, 0:C].unsqueeze(1).to_broadcast([C, 4, C]), ALU.add,
                     reads=[Q0.k, cf.k], writes=[TT_.k])
                if DN_CUT <= 4:
                    continue
                Qa, Pa, Qn_, Pn_ = Q0, P0, Qb, Pb
                for lv in range(LV - 1):
                    pq2 = bank()
                    pp2 = bank()
                    lastlv = (lv == LV - 2)
                    for hh in range(4):
                        if not lastlv:
                            P.mm(pq2[0:C, hh * 64: hh * 64 + C], Pa.v[0:C, hh, 0:C], Qa.v[0:C, hh, 0:C], reads=[Pa.k, Qa.k], writes=[pq2.k])
                        P.mm(pp2[0:C, hh * 64: hh * 64 + C], Qa.v[0:C, hh, 0:C], Pa.v[0:C, hh, 0:C], reads=[Pa.k, Qa.k], writes=[pp2.k])
                    P.copy("act", Pn_.v[0:C, :, 0:C], hv(pp2[0:64, 0:256], 4)[0:C, :, 0:C], reads=[pp2.k], writes=[Pn_.k])
                    if not lastlv:
                        P.copy("dve", Qn_.v[0:C, :, 0:C], hv(pq2[0:64, 0:256], 4)[0:C, :, 0:C], reads=[pq2.k], writes=[Qn_.k])
                    pt2 = bank()
                    for hh in range(4):
                        P.mm(pt2[0:C, hh * 64: hh * 64 + C], Pn_.v[0:C, hh, 0:C], TT_.v[0:C, hh, 0:C], reads=[Pn_.k, TT_.k], writes=[pt2.k])
                    P.tt("dve", TT_.v[0:C, :, 0:C], TT_.v[0:C, :, 0:C], hv(pt2[0:64, 0:256], 4)[0:C, :, 0:C], ALU.add,
                         reads=[pt2.k, TT_.k], writes=[TT_.k])
                    Qa, Qn_ = Qn_, Qa
                    Pa, Pn_ = Pn_, Pa
                if DN_CUT <= 5:
                    continue
                for hh in range(4):
                    P.tr(bankb[0:C, hh * 64:(hh + 1) * 64], qkvb[2].v[:, hh, cs], identb[0:64, 0:64], reads=[qkvb[2].k, identb.k], writes=[bankb.k])
                    P.tr(bankb[0:C, 256 + hh * 64: 256 + (hh + 1) * 64], qkvb[1].v[:, hh, cs], identb[0:64, 0:64], reads=[qkvb[1].k, identb.k], writes=[bankb.k])
                P.tt("dve", vb.v[0:C, :, :], hv(bankb[0:C, 0:256], 4),
                     beta.ap[0:C, :].unsqueeze(2).to_broadcast([C, 4, 64]), ALU.mult, reads=[bankb.k, beta.k], writes=[vb.k])
                P.tt("dve", kd.v[0:C, :, :], hv(bankb[0:C, 256:512], 4),
                     edl.ap[0:C, :].unsqueeze(2).to_broadcast([C, 4, 64]), ALU.mult, reads=[bankb.k, edl.k], writes=[kd.k])
                if DN_CUT <= 6:
                    continue
                if l == 0 and hg == 0 and s == DBG_S:
                    dbg_store("gg", gg.ap[0:C, :], [gg.k])
                    dbg_store("beta", beta.ap[0:C, :], [beta.k])
                    dbg_store("gc", gc.ap[0:C, :], [gc.k])
                    dbg_store("E1", E1.v[0:C, :, 0:C], [E1.k])
                    dbg_store("Q0", Q0.v[0:C, :, 0:C], [Q0.k])
                    dbg_store("qkT", qkT.v[0:C, :, 0:C], [qkT.k])
                    dbg_store("TT", TT_.v[0:C, :, 0:C], [TT_.k])
                    dbg_store("kbg", kbg.v[:, :, 0:C], [kbg.k])
                    dbg_store("qg", qg.v[:, :, 0:C], [qg.k])
                    dbg_store("vb", vb.v[0:C, :, :], [vb.k])
                    dbg_store("kd", kd.v[0:C, :, :], [kd.k])
                pR = bank()
                for hh in range(4):
                    P.mm(pR[0:C, hh * 64:(hh + 1) * 64], kbg.v[:, hh, 0:C], Sv[:, hh, :], reads=[kbg.k, Sk], writes=[pR.k])
                P.tt("dve", R.v[0:C, :, :], vb.v[0:C, :, :], hv(pR[0:C, 0:256], 4), ALU.subtract,
                     reads=[vb.k, pR.k], writes=[R.k])
                pvn = bank()
                for hh in range(4):
                    P.mm(pvn[0:C, hh * 64:(hh + 1) * 64], TT_.v[0:C, hh, 0:C], R.v[0:C, hh, :], reads=[TT_.k, R.k], writes=[pvn.k])
                P.copy("act", vnw.v[0:C, :, :], hv(pvn[0:C, 0:256], 4), reads=[pvn.k], writes=[vnw.k])
                if DN_CUT <= 7:
                    continue
                po_ = bank()
                for hh in range(4):
                    P.mm(po_[0:64, hh * 64: hh * 64 + C], Sv[:, hh, :], qg.v[:, hh, 0:C], start=True, stop=False,
                         reads=[Sk, qg.k], writes=[po_.k])
                    P.mm(po_[0:64, hh * 64: hh * 64 + C], vnw.v[0:C, hh, :], qkT.v[0:C, hh, 0:C], start=False, stop=True,
                         reads=[vnw.k, qkT.k], writes=[po_.k])
                pS = bank()
                for hh in range(4):
                    P.mm(pS[0:64, hh * 64:(hh + 1) * 64], kd.v[0:C, hh, :], vnw.v[0:C, hh, :], reads=[kd.k, vnw.k], writes=[pS.k])
                for hh in range(4):
                    P.ts("dve", Sv[:, hh, :], Sv[:, hh, :], elast.ap[:, hh:hh + 1], None, ALU.mult, reads=[Sk, elast.k], writes=[Sk])
                P.tt("dve", Sv, Sv, hv(pS[0:64, 0:256], 4), ALU.add, reads=[pS.k, Sk], writes=[Sk])
                if DN_CUT <= 8:
                    continue
                if l == 0 and hg == 0 and s == DBG_S:
                    dbg_store("R", R.v[0:C, :, :], [R.k])
                    dbg_store("vnw", vnw.v[0:C, :, :], [vnw.k])
                    dbg_store("Snew", Sv, [Sk])
                ov = hv(po_[0:64, 0:256], 4)
                P.act(osq.v[:, :, 0:C], ov[:, :, 0:C], AF.Square, reads=[po_.k], writes=[osq.k])
                pn2 = bank()
                for hh in range(4):
                    P.mm(pn2[0:64, hh * 64: hh * 64 + C], onesf[0:64, 0:64], osq.v[:, hh, 0:C], reads=[cf.k, osq.k], writes=[pn2.k])
                P.ts("dve", osq.v[:, :, 0:C], hv(pn2[0:64, 0:256], 4)[:, :, 0:C], 1.0 / 64, EPS, ALU.mult, ALU.add, reads=[pn2.k], writes=[osq.k])
                P.act(osq.v[:, :, 0:C], osq.v[:, :, 0:C], AF.Ln, reads=[osq.k], writes=[osq.k])
                P.act(osq.v[:, :, 0:C], osq.v[:, :, 0:C], AF.Exp, scale=-0.5, reads=[osq.k], writes=[osq.k])
                P.stt(osq.v[:, :, 0:C], ov[:, :, 0:C], gdn[:, 0:1], osq.v[:, :, 0:C], ALU.mult, ALU.mult,
                      reads=[po_.k, gdn.k, osq.k], writes=[osq.k])
                P.tt("dve", yc.v[:, hsl, cs], osq.v[:, :, 0:C], yc.v[:, hsl, cs], ALU.mult, reads=[osq.k, yc.k], writes=[yc.k])
                if not prompt:
                    P.dma("sp", o_sdelta[l, bseq, hsl].rearrange("h k v -> k h v"), Sv, reads=[Sk], is_output=True)
            if prompt and ck == NCH - 1:
                P.dma("sp", o_pdelta[l, hsl].rearrange("h k v -> k h v"), S_p[l][:, hsl, :],
                      reads=[S_p[l].k], is_output=True)
        dbg_store(f"yc{l}", yc.v, [yc.k])
        new_stage(reset_b=False)
        merge_branch(cfg, l, 2, yc.v, yc.k, w_br_dn, per_head=True)

    def stage_final(cfg):
        new_stage()
        NT, TT, NTI, prompt, ck = cfg["NT"], cfg["TT"], cfg["NTI"], cfg["prompt"], cfg["ck"]
        P.dma("sp", gn_bc[:, :], final_norm_g.partition_broadcast(128), writes=[gn_bc.k])
        junk = af([128, D], "junkf")
        ssq = af([128, 4], "ssqf")
        yo = [af([128, D], f"yo{i}") for i in range(2)]
        for ti in range(NTI):
            xt = x_sb[0:TT, ti, :]
            P.act(junk.ap[0:TT, :], xt, AF.Square, accum_out=ssq.ap[0:TT, ti:ti + 1], reads=[x_sb.k], writes=[junk.k, ssq.k])
            rstd_inplace(ssq.ap[0:TT, ti:ti + 1], 1.0 / D, [ssq.k])
            y = yo[ti % 2]
            P.stt(y.ap[0:TT, :], xt, ssq.ap[0:TT, ti:ti + 1], gn_bc[0:TT, :], ALU.mult, ALU.mult,
                  reads=[x_sb.k, ssq.k, gn_bc.k], writes=[y.k])
            if prompt:
                r0 = ck * CH + ti * 128
                P.dma("sp", y_p[r0:r0 + 128, :], y.ap[0:128, :], reads=[y.k], is_output=True)
            else:
                P.dma("sp", y_s, y.ap[0:32, :], reads=[y.k], is_output=True)

    cfgs = [dict(NT=32, TT=32, NTI=1, B=4, Ls=8, prompt=False, ck=0, C=8)]
    if prompt_only:
        cfgs = []
    if not sample_only:
        for ck in range(prompt_chunks):
            cfgs.append(dict(NT=CH, TT=128, NTI=4, B=1, Ls=CH, prompt=True, ck=ck, C=64))
    for cfg in cfgs:
        new_stage()
        if cfg["prompt"]:
            r0 = cfg["ck"] * CH
            P.dma("sp", x_sb[:, :, :], xp[r0:r0 + CH, :].rearrange("(t p) d -> p t d", p=128), writes=[x_sb.k])
        else:
            P.dma("sp", x_sb[0:32, 0, :], xs, writes=[x_sb.k])
        for l in range(DEPTH):
            if "norm" not in skip:
                stage_norm(cfg, l)
            if "pool" in stages:
                stage_pool(cfg, l)
            if "mla" in stages:
                stage_mla(cfg, l)
            if "dn" in stages:
                stage_dn(cfg, l)
            if "out" not in skip:
                stage_out(cfg, l)
        if "final" not in skip:
            stage_final(cfg)
    P.fence()
    P.emit(sems, slot_sems)
    es.close()
    return nc, P


def _prep_inputs(inp):
    global _CONST
    if _CONST is None:
        _CONST = _consts()
    f32 = np.float32
    w_uq = np.asarray(inp["w_uq"], f32).reshape(DEPTH, 384, H, 96)
    rope = w_uq[..., 64:96]
    rope_sw = np.concatenate([rope[..., 16:32], rope[..., 0:16]], -1)
    w_uq_ext = np.ascontiguousarray(np.concatenate([w_uq, rope_sw], -1).reshape(DEPTH, 384, H * 128))
    shared = {
        "ckv": np.asarray(inp["cache_kv_latent"], f32),
        "ckr": np.asarray(inp["cache_k_rope"], f32),
        "norm_g": np.asarray(inp["norm_g"], f32),
        "w_in": np.asarray(inp["w_in"], f32),
        "pool_mix": np.ascontiguousarray(np.asarray(inp["pool_mix"], f32).transpose(0, 2, 1, 3)),
        "pool_scale": np.ascontiguousarray(np.asarray(inp["pool_scale"], f32).reshape(DEPTH, 4, 128).transpose(0, 2, 1)),
        "q_norm_g": np.asarray(inp["q_norm_g"], f32),
        "w_uq": w_uq_ext,
        "kv_norm_g": np.asarray(inp["kv_norm_g"], f32),
        "w_ukT": np.ascontiguousarray(np.asarray(inp["w_uk"], f32).transpose(0, 3, 2, 1)),
        "w_uv": np.ascontiguousarray(np.asarray(inp["w_uv"], f32).reshape(DEPTH, 256, H * 64)),
        "conv_w": np.ascontiguousarray(np.asarray(inp["conv_w"], f32).reshape(DEPTH, 4, 24, 64).transpose(0, 3, 2, 1)),
        "a_log": np.asarray(inp["a_log"], f32),
        "dt_bias": np.asarray(inp["dt_bias"], f32),
        "dn_norm_g": np.ascontiguousarray(np.asarray(inp["dn_norm_g"], f32).reshape(DEPTH, 64, 1)),
        "w_br_pool": np.asarray(inp["w_br_pool"], f32),
        "w_br_mla": np.asarray(inp["w_br_mla"], f32),
        "w_br_dn": np.asarray(inp["w_br_dn"], f32),
        "w_out": np.asarray(inp["w_out"], f32),
        "final_norm_g": np.asarray(inp["final_norm_g"], f32),
        "cf": _CONST["cf"], "ropeq": _CONST["ropeq"], "ropek": _CONST["ropek"],
    }
    xp = np.asarray(inp["x_prompt"], f32)
    xs = np.asarray(inp["x_sample"], f32)
    sp = np.asarray(inp["state_pool"], f32)
    sc = np.asarray(inp["state_conv"], f32)
    sd = np.asarray(inp["state_delta"], f32)
    pt = np.asarray(inp["page_table"], np.int32)
    in_maps = []
    for c in range(NCORE):
        m = dict(shared)
        m["xp"] = np.ascontiguousarray(xp[c])
        m["xs"] = np.ascontiguousarray(xs[4 * c:4 * c + 4].reshape(32, D))
        m["spool"] = np.ascontiguousarray(sp[:, 4 * c:4 * c + 4])
        m["sconv"] = np.ascontiguousarray(sc[:, 4 * c:4 * c + 4])
        m["sdelta"] = np.ascontiguousarray(sd[:, 4 * c:4 * c + 4])
        m["ptab"] = np.ascontiguousarray(pt[4 * c:4 * c + 4])
        in_maps.append(m)
    return in_maps


_NC = None


def kernel(**inputs):
    global _NC
    in_maps = _prep_inputs(inputs)
    if _NC is None:
        _NC = build()[0]
    res = run_bass_kernel_spmd(_NC, in_maps, core_ids=list(range(NCORE)))
    r = res.results
    cat = lambda k: np.stack([r[c][k] for c in range(NCORE)], 0)
    y_p = cat("y_p")
    y_s = np.concatenate([r[c]["y_s"].reshape(4, 8, D) for c in range(NCORE)], 0)
    p_kv = np.stack([r[c]["o_pkv"] for c in range(NCORE)], 1)
    p_kr = np.stack([r[c]["o_pkr"] for c in range(NCORE)], 1)
    p_pool = np.stack([r[c]["o_ppool"] for c in range(NCORE)], 1)
    p_conv = np.stack([r[c]["o_pconv"] for c in range(NCORE)], 1)
    p_delta = np.stack([r[c]["o_pdelta"] for c in range(NCORE)], 1)
    s_kv = np.concatenate([r[c]["o_skv"].reshape(DEPTH, 4, 8, 256) for c in range(NCORE)], 1)
    s_kr = np.concatenate([r[c]["o_skr"].reshape(DEPTH, 4, 8, 32) for c in range(NCORE)], 1)
    s_pool = np.concatenate([r[c]["o_spool"] for c in range(NCORE)], 1)
    s_conv = np.concatenate([r[c]["o_sconv"] for c in range(NCORE)], 1)
    s_delta = np.concatenate([r[c]["o_sdelta"] for c in range(NCORE)], 1)
    outs = (y_p, y_s, p_kv, p_kr, p_pool, p_conv, p_delta, s_kv, s_kr, s_pool, s_conv, s_delta)
    return tuple(np.ascontiguousarray(o, dtype=np.float32) for o in outs)
```

```python
import contextlib
import numpy as np
import concourse.bass as bass
import concourse.mybir as mybir
from concourse.bass_utils import run_bass_kernel_spmd

F32 = mybir.dt.float32
BF16 = mybir.dt.bfloat16
I32 = mybir.dt.int32
AF = mybir.ActivationFunctionType
ALU = mybir.AluOpType

D = 1024
SEQ = 2048
DEPTH = 2
EPS = 1e-6
NPAGE = 128
H = 8
MLA_SCALE = 96 ** -0.5
NCORE = 8
CH = 512
NCH = SEQ // CH
C_POOL, C_ZPOOL, C_Q, C_KV, C_KR, C_ZMLA, C_QKV, C_ZDN, C_BETA, C_ALPHA, C_GATE = (
    0, 512, 1024, 1408, 1664, 1696, 2208, 3744, 4256, 4264, 4272)
INW = 7344


class Tk:
    __slots__ = ("w", "r", "name")

    def __init__(self, name=""):
        self.w = {}
        self.r = {}
        self.name = name


class Op:
    __slots__ = ("fn", "waits", "signal", "dma", "tag")

    def __init__(self, fn, dma=None):
        self.fn = fn
        self.waits = []
        self.signal = False
        self.dma = dma
        self.tag = None


STREAMS = ("pe", "act", "dve", "pool", "sp")
NSLOT = {"sp": 28, "act": 8, "pool": 24}


class Prog:
    def __init__(self, nc):
        self.nc = nc
        self.ops = {s: [] for s in STREAMS}
        self.seen_c = {s: {} for s in STREAMS}
        self.seen_d = {s: {} for s in STREAMS}
        self.slot_next = {s: 0 for s in NSLOT}
        self.slot_val = {}
        self.out_dma_events = []
        self.pending_dma = {}
        self.last_c = {s: -1 for s in STREAMS}
        self.tag = None
        self.annotate = False

    def _need(self, stream, ev, waits, force_same=False):
        if ev[0] == "c":
            _, e2, idx = ev
            if idx < 0:
                return
            if e2 == stream and stream == "pe" and not force_same:
                return
            if self.seen_c[stream].get(e2, -1) >= idx:
                return
            self.seen_c[stream][e2] = idx
            self.ops[e2][idx].signal = True
            waits.append(ev)
        else:
            _, slot, val = ev
            if self.seen_d[stream].get(slot, 0) >= val:
                return
            self.seen_d[stream][slot] = val
            waits.append(ev)

    def _deps(self, stream, reads, writes, force_same=False):
        waits = []
        for t in reads:
            for ev in t.w.values():
                self._need(stream, ev, waits, force_same)
        for t in writes:
            for ev in t.w.values():
                self._need(stream, ev, waits, force_same)
            for ev in t.r.values():
                self._need(stream, ev, waits, force_same)
        return waits

    def _commit(self, ev, reads, writes):
        key = ev[:2]
        for t in reads:
            t.r[key] = ev
        for t in writes:
            t.w = {key: ev}
            t.r = {}

    def op(self, stream, fn, reads=(), writes=()):
        o = Op(fn)
        o.waits = self._deps(stream, reads, writes)
        idx = len(self.ops[stream])
        o.tag = self.tag
        self.ops[stream].append(o)
        self.last_c[stream] = idx
        self._commit(("c", stream, idx), reads, writes)
        return o

    def dma(self, stream, out, in_, reads=(), writes=(), is_output=False, **kw):
        n = NSLOT[stream]
        k = self.slot_next[stream]
        self.slot_next[stream] = k + 1
        slot = (stream, k % n)
        prev = self.slot_val.get(slot, 0)
        val = prev + 16
        self.slot_val[slot] = val
        o = Op(lambda e: e.dma_start(out=out, in_=in_, **kw), dma=(slot, val))
        o.waits = self._deps(stream, reads, writes, force_same=True)
        o.tag = self.tag
        if prev > 0:
            self._need(stream, ("d", slot, prev), o.waits)
        self.ops[stream].append(o)
        ev = ("d", slot, val)
        self._commit(ev, reads, writes)
        self.pending_dma[slot] = ev
        if is_output:
            self.out_dma_events.append(ev)
        return o

    def idma(self, out, in_, idx_ap, reads=(), writes=()):
        stream = "pool"
        n = NSLOT[stream]
        k = self.slot_next[stream]
        self.slot_next[stream] = k + 1
        slot = (stream, k % n)
        prev = self.slot_val.get(slot, 0)
        val = prev + 16
        self.slot_val[slot] = val
        o = Op(lambda e: e.indirect_dma_start(out=out, out_offset=None, in_=in_,
                                              in_offset=bass.IndirectOffsetOnAxis(ap=idx_ap, axis=0)), dma=(slot, val))
        o.waits = self._deps(stream, reads, writes, force_same=True)
        if prev > 0:
            self._need(stream, ("d", slot, prev), o.waits)
        self.ops[stream].append(o)
        ev = ("d", slot, val)
        self._commit(ev, reads, writes)
        self.pending_dma[slot] = ev
        return o

    def fence(self):
        last = dict(self.last_c)
        pend = list(self.pending_dma.values())
        self.pending_dma = {}
        self._fence_waits = {}
        for a in STREAMS:
            waits = []
            for b in STREAMS:
                if b != a:
                    self._need(a, ("c", b, last[b]), waits)
            for ev in pend:
                self._need(a, ev, waits)
            if waits:
                o = Op(None)
                o.waits = waits
                self.ops[a].append(o)

    def mm(self, out, lhsT, rhs, start=True, stop=True, reads=(), writes=(), **kw):
        return self.op("pe", lambda e: e.matmul(out, lhsT, rhs, start=start, stop=stop, **kw), reads, writes)

    def tr(self, out, in_, ident, reads=(), writes=()):
        return self.op("pe", lambda e: e.transpose(out, in_, ident), reads, writes)

    def act(self, out, in_, func, reads=(), writes=(), **kw):
        return self.op("act", lambda e: e.activation(out=out, in_=in_, func=func, **kw), reads, writes)

    def tt(self, stream, out, in0, in1, op, reads=(), writes=()):
        return self.op(stream, lambda e: e.tensor_tensor(out=out, in0=in0, in1=in1, op=op), reads, writes)

    def ts(self, stream, out, in0, s1, s2, op0, op1=None, reads=(), writes=(), **kw):
        if op1 is None:
            return self.op(stream, lambda e: e.tensor_scalar(out=out, in0=in0, scalar1=s1, scalar2=None, op0=op0, **kw), reads, writes)
        return self.op(stream, lambda e: e.tensor_scalar(out=out, in0=in0, scalar1=s1, scalar2=s2, op0=op0, op1=op1, **kw), reads, writes)

    def stt(self, out, in0, scalar, in1, op0, op1, reads=(), writes=(), **kw):
        return self.op("dve", lambda e: e.scalar_tensor_tensor(out=out, in0=in0, scalar=scalar, in1=in1, op0=op0, op1=op1, **kw), reads, writes)

    def copy(self, stream, out, in_, reads=(), writes=()):
        if stream == "act":
            return self.op("act", lambda e: e.copy(out=out, in_=in_), reads, writes)
        return self.op(stream, lambda e: e.tensor_copy(out=out, in_=in_), reads, writes)

    def memset(self, stream, ap, val, writes=()):
        return self.op(stream, lambda e: e.memset(ap, val), (), writes)

    def emit(self, sems, slot_sems):
        nc = self.nc
        cum = {}
        for s in STREAMS:
            c = 0
            arr = []
            for o in self.ops[s]:
                if o.signal and o.dma is None and o.fn is not None:
                    c += 1
                arr.append(c)
            cum[s] = arr
        final_waits = []
        for ev in self.out_dma_events:
            self._need("sp", ev, final_waits)
        self.n_instr = {s: len(self.ops[s]) for s in STREAMS}

        def run(stream, eng):
            for o in self.ops[stream]:
                for ev in o.waits:
                    if ev[0] == "c":
                        eng.wait_ge(sems[ev[1]], cum[ev[1]][ev[2]])
                    else:
                        eng.wait_ge(slot_sems[ev[1]], ev[2])
                if o.fn is None:
                    continue
                ins = o.fn(eng)
                if self.annotate and o.tag:
                    ins.annotate(o.tag)
                if o.dma is not None:
                    ins.then_inc(slot_sems[o.dma[0]], 16)
                elif o.signal:
                    ins.then_inc(sems[stream], 1)
            if stream == "sp":
                for ev in final_waits:
                    eng.wait_ge(slot_sems[ev[1]], ev[2])

        with nc.Block() as block:
            @block.tensor
            def _(e):
                run("pe", e)

            @block.scalar
            def _(e):
                run("act", e)

            @block.vector
            def _(e):
                run("dve", e)

            @block.gpsimd
            def _(e):
                run("pool", e)

            @block.sync
            def _(e):
                run("sp", e)


class Buf:
    def __init__(self, t, name):
        self.t = t
        self.k = Tk(name)

    def __getitem__(self, key):
        return self.t[key]


def _consts():
    c = {}
    half = 16
    inv = np.power(10000.0, -np.arange(half, dtype=np.float32) / half).astype(np.float32)

    def tabs(pos):
        ang = pos.astype(np.float32)[:, None] * inv[None, :]
        return np.cos(ang).astype(np.float32), np.sin(ang).astype(np.float32)

    posp = np.arange(SEQ)
    poss = 16384 + np.arange(8)
    cp, sp_ = tabs(posp)
    cs, ss = tabs(poss)
    ropeq = np.zeros((32, 2, SEQ + 32), np.float32)
    ropeq[:, 0, :SEQ] = np.concatenate([cp.T, cp.T], 0)
    ropeq[:, 1, :SEQ] = np.concatenate([-sp_.T, sp_.T], 0)
    cs4 = np.tile(cs, (4, 1))
    ss4 = np.tile(ss, (4, 1))
    ropeq[:, 0, SEQ:] = np.concatenate([cs4.T, cs4.T], 0)
    ropeq[:, 1, SEQ:] = np.concatenate([-ss4.T, ss4.T], 0)
    c["ropeq"] = ropeq
    ropek = np.zeros((128, 17, 32), np.float32)
    ropek[:, :16, :16] = cp.reshape(16, 128, 16).transpose(1, 0, 2)
    ropek[:, :16, 16:] = sp_.reshape(16, 128, 16).transpose(1, 0, 2)
    ropek[:32, 16, :16] = cs4
    ropek[:32, 16, 16:] = ss4
    c["ropek"] = ropek
    f = np.zeros((128, 1024), np.float32)
    f[:, 0:128] = np.eye(128)
    f[:, 128:256] = 1.0
    ii = np.arange(64)
    f[:64, 256:320] = (ii[None, :] >= ii[:, None])
    f[:64, 320:384] = -(ii[None, :] > ii[:, None]).astype(np.float32)
    jj = np.arange(128)
    f[:, 384:512] = (jj[:, None] <= jj[None, :])
    t15 = np.arange(15)
    for gi, w in enumerate((2, 4, 8, 16)):
        f[:, 512 + gi * 15: 512 + (gi + 1) * 15] = 1.0 / np.minimum(t15 + 1, w)
    f[:8, 576:584] = (np.arange(8)[:, None] <= np.arange(8)[None, :])
    f[:, 600] = np.arange(128)
    f[:, 601] = np.arange(128) + 5120 * 128
    c["cf"] = f
    return c


_CONST = None


DBG_S = 0
DN_NSUB = None
DN_CUT = 99


def build(sample_only=False, prompt_chunks=NCH, dbg=None, stages=("pool", "mla", "dn"), npool=5120, skip=(), prompt_only=False, annotate=False):
    nc = bass.Bass("TRN2", target_bir_lowering=False)
    es = contextlib.ExitStack()

    def din(name, shape, dt=F32):
        return nc.dram_tensor(name, list(shape), dt, kind="ExternalInput").ap()

    def dout(name, shape, dt=F32):
        return nc.dram_tensor(name, list(shape), dt, kind="ExternalOutput").ap()

    xp = din("xp", [SEQ, D])
    xs = din("xs", [32, D])
    ckv = din("ckv", [DEPTH, npool, 128, 256])
    ckr = din("ckr", [DEPTH, npool, 128, 32])
    spool = din("spool", [DEPTH, 4, 15, 512])
    sconv = din("sconv", [DEPTH, 4, 3, 1536])
    sdelta = din("sdelta", [DEPTH, 4, 8, 64, 64])
    ptab = din("ptab", [4, 128], I32)
    norm_g = din("norm_g", [DEPTH, D])
    w_in = din("w_in", [DEPTH, D, INW])
    pool_mix = din("pool_mix", [DEPTH, 128, 4, 128])
    pool_scale = din("pool_scale", [DEPTH, 128, 4])
    q_norm_g = din("q_norm_g", [DEPTH, 384])
    w_uq = din("w_uq", [DEPTH, 384, H * 128])
    kv_norm_g = din("kv_norm_g", [DEPTH, 256])
    w_ukT = din("w_ukT", [DEPTH, 64, H, 256])
    w_uv = din("w_uv", [DEPTH, 256, H * 64])
    conv_w = din("conv_w", [DEPTH, 64, 24, 4])
    a_log = din("a_log", [DEPTH, H])
    dt_bias = din("dt_bias", [DEPTH, H])
    dn_norm_g = din("dn_norm_g", [DEPTH, 64, 1])
    w_br_pool = din("w_br_pool", [DEPTH, 512, D])
    w_br_mla = din("w_br_mla", [DEPTH, 512, D])
    w_br_dn = din("w_br_dn", [DEPTH, 512, D])
    w_out = din("w_out", [DEPTH, D, D])
    final_norm_g = din("final_norm_g", [D])
    cf_d = din("cf", [128, 1024])
    ropeq_d = din("ropeq", [32, 2, SEQ + 32])
    ropek_d = din("ropek", [128, 17, 32])

    y_p = dout("y_p", [SEQ, D])
    y_s = dout("y_s", [32, D])
    o_pkv = dout("o_pkv", [DEPTH, SEQ, 256])
    o_pkr = dout("o_pkr", [DEPTH, SEQ, 32])
    o_ppool = dout("o_ppool", [DEPTH, 15, 512])
    o_pconv = dout("o_pconv", [DEPTH, 3, 1536])
    o_pdelta = dout("o_pdelta", [DEPTH, H, 64, 64])
    o_skv = dout("o_skv", [DEPTH, 32, 256])
    o_skr = dout("o_skr", [DEPTH, 32, 32])
    o_spool = dout("o_spool", [DEPTH, 4, 15, 512])
    o_sconv = dout("o_sconv", [DEPTH, 4, 3, 1536])
    o_sdelta = dout("o_sdelta", [DEPTH, 4, H, 64, 64])
    dbg_out = {}
    if dbg:
        for name, shape in dbg.items():
            dbg_out[name] = dout("dbg_" + name, shape)

    def sb(name, shape, dt=F32):
        return Buf(es.enter_context(nc.sbuf_tensor(name, list(shape), dt)), name)

    def pstile(name, shape, dt=F32):
        return Buf(es.enter_context(nc.psum_tensor(name, list(shape), dt)), name)

    P = Prog(nc)
    P.annotate = annotate

    x_sb = sb("x_sb", [128, 4, D])
    xnT = sb("xnT", [128, 8, CH], BF16)
    mrg = sb("mrg", [128, 8, CH])
    kTc = [sb(f"kTc{l}", [128, 3, SEQ], BF16) for l in range(DEPTH)]
    Vc = [sb(f"Vc{l}", [128, 16, 256], BF16) for l in range(DEPTH)]
    hist_pool = [sb(f"hpool{l}", [128, 4, 15]) for l in range(DEPTH)]
    hist_conv = [sb(f"hconv{l}", [64, 24, 3]) for l in range(DEPTH)]
    S_p = [sb(f"S_p{l}", [64, H, 64]) for l in range(DEPTH)]
    WA = sb("WA", [128, 8, 672], BF16)
    WB = sb("WB", [128, 8, 512], BF16)
    WBR = sb("WBR", [128, 8 * D], BF16)
    mixw = sb("mixw", [128, 4, 128], BF16)
    cw = sb("cw", [64, 24, 4])
    psc = sb("psc", [128, 4])
    gdn = sb("gdn", [64, 1])
    gn_bc = sb("gn_bc", [128, D])
    a_bc = sb("a_bc", [64, H])
    dtb_bc = sb("dtb_bc", [64, H])
    cf = sb("cf_sb", [128, 1024])
    identb = sb("identb", [128, 128], BF16)
    onesb = sb("onesb", [128, 128], BF16)
    ropek = sb("ropek_sb", [128, 17, 32])
    ptb = sb("ptb", [128, 128], I32)
    ridx = sb("ridx", [128, 128], I32)
    AF_N = 8832
    AB_N = 17408
    arena_f = sb("arena_f", [128, AF_N])
    arena_b = sb("arena_b", [128, AB_N], BF16)
    banks = [pstile(f"psf{i}", [128, 512]) for i in range(7)]
    bankb = pstile("psb", [128, 1024], BF16)

    globals()["_SBUF_LEFT"] = nc.sbuf_bytes_remaining
    sems = {s: es.enter_context(nc.semaphore("sem_" + s)) for s in ("pe", "act", "dve", "pool", "sp")}
    slot_sems = {}
    for s, n in NSLOT.items():
        for i in range(n):
            slot_sems[(s, i)] = es.enter_context(nc.semaphore(f"ds_{s}_{i}"))

    identf = cf[:, 0:128]
    onesf = cf[:, 128:256]
    m_incl = cf[0:64, 256:320]
    m_nstrict = cf[0:64, 320:384]
    m_causal = cf[:, 384:512]
    rc15 = cf[:, 512:572]
    m_causal8 = cf[0:8, 576:584]
    iota_p = cf[:, 600:601]

    st = {"af": 0, "ab": 0, "n": 0, "rot": list(range(7)), "ri": 0}

    class AB:
        pass

    def _arena(ar, key, cap, shape, name, even):
        n = int(np.prod(shape[1:]))
        na = (n + 1) // 2 * 2 if even else n
        off = st[key]
        st[key] = off + na
        assert st[key] <= cap, ("arena overflow", key, name, st[key], cap)
        st["n"] += 1
        b = AB()
        b.k = Tk(name or f"{key}{st['n']}")
        b.shape = list(shape)
        flat = ar.t[0:shape[0], off:off + n]
        b.ap = flat
        sh = shape
        if len(sh) == 2:
            b.v = flat
        elif len(sh) == 3:
            b.v = flat.rearrange("p (a b) -> p a b", b=sh[2])
        elif len(sh) == 4:
            b.v = flat.rearrange("p (a b c) -> p a b c", b=sh[2], c=sh[3])
        else:
            raise ValueError
        return b

    def af(shape, name=None):
        return _arena(arena_f, "af", AF_N, shape, name, False)

    def ab(shape, name=None):
        return _arena(arena_b, "ab", AB_N, shape, name, True)

    def new_stage(reset_b=True):
        P.fence()
        st["af"] = 0
        if reset_b:
            st["ab"] = 0

    def set_rot(lst):
        st["rot"] = list(lst)
        st["ri"] = 0

    def bank():
        b = banks[st["rot"][st["ri"] % len(st["rot"])]]
        st["ri"] += 1
        return b

    def hv(ap, n, t=None):
        return ap.rearrange("p (h t) -> p h t", h=n)

    P.dma("sp", cf[:, :], cf_d, writes=[cf.k])
    P.dma("sp", ropek[:, :, :], ropek_d, writes=[ropek.k])
    P.copy("dve", identb[:, :], cf[:, 0:128], reads=[cf.k], writes=[identb.k])
    P.copy("dve", onesb[:, :], cf[:, 128:256], reads=[cf.k], writes=[onesb.k])
    for l in range(DEPTH):
        P.memset("pool", hist_pool[l][:, :, :], 0.0, writes=[hist_pool[l].k])
        P.memset("pool", hist_conv[l][:, :, :], 0.0, writes=[hist_conv[l].k])
        P.memset("pool", S_p[l][:, :, :], 0.0, writes=[S_p[l].k])

    def dbg_store(name, ap, reads):
        if name in dbg_out:
            P.dma("pool", dbg_out[name], ap, reads=reads, is_output=True)

    w_in_v = [w_in[l].rearrange("(kt p) n -> p kt n", p=128) for l in range(DEPTH)]

    def load_w_in(dst, l, c0, ncol, dcol=0):
        P.dma("pool", dst[:, :, dcol:dcol + ncol], w_in_v[l][:, :, c0:c0 + ncol], writes=[dst.k])

    def fm_proj(ps_ap, Wb, wcol, M, NT, wk, psk):
        for kt in range(8):
            P.mm(ps_ap, Wb[:, kt, wcol:wcol + M], xnT[:, kt, 0:NT], start=(kt == 0), stop=(kt == 7),
                 reads=[wk, xnT.k], writes=[psk])

    def rstd_inplace(a, mult, keys):
        P.ts("dve", a, a, mult, EPS, ALU.mult, ALU.add, reads=keys, writes=keys)
        P.act(a, a, AF.Ln, reads=keys, writes=keys)
        P.act(a, a, AF.Exp, scale=-0.5, reads=keys, writes=keys)

    def stage_norm(cfg, l):
        new_stage()
        P.tag = 'norm'
        NT, TT, NTI = cfg["NT"], cfg["TT"], cfg["NTI"]
        P.dma("sp", gn_bc[:, :], norm_g[l].partition_broadcast(128), writes=[gn_bc.k])
        junk = af([128, D], "junk")
        ssq = af([128, 4], "ssq")
        xn = ab([128, D], "xn")
        for ti in range(NTI):
            xt = x_sb[0:TT, ti, :]
            P.act(junk.ap[0:TT, :], xt, AF.Square, accum_out=ssq.ap[0:TT, ti:ti + 1],
                  reads=[x_sb.k], writes=[junk.k, ssq.k])
            rstd_inplace(ssq.ap[0:TT, ti:ti + 1], 1.0 / D, [ssq.k])
            P.stt(xn.ap[0:TT, :], xt, ssq.ap[0:TT, ti:ti + 1], gn_bc[0:TT, :], ALU.mult, ALU.mult,
                  reads=[x_sb.k, ssq.k, gn_bc.k], writes=[xn.k])
            for kt in range(8):
                P.tr(bankb[:, kt * 128: kt * 128 + TT], xn.ap[0:TT, kt * 128:(kt + 1) * 128], identb[0:TT, 0:TT],
                     reads=[xn.k, identb.k], writes=[bankb.k])
            P.copy("act", xnT[:, :, ti * TT:(ti + 1) * TT], hv(bankb[:, :], 8)[:, :, 0:TT],
                   reads=[bankb.k], writes=[xnT.k])
        P.memset("pool", mrg[:, :, 0:NT], 0.0, writes=[mrg.k])
        dbg_store("xnT", xnT[:, :, 0:NT], [xnT.k])

    def merge_branch(cfg, l, bi, yv, yk, w_br, per_head):
        P.tag = 'merge'
        NT = cfg["NT"]
        if per_head:
            wv = WBR[0:64, :].rearrange("p (h n) -> p h n", h=8)
            P.dma("pool", wv, w_br[l].rearrange("(h p) n -> p h n", p=64), writes=[WBR.k])
        else:
            wv = WBR[:, 0:4 * D].rearrange("p (h n) -> p h n", h=4)
            P.dma("pool", wv, w_br[l].rearrange("(kt p) n -> p kt n", p=128), writes=[WBR.k])
        gs = af([128, CH], "gsig")
        for half in range(2):
            load_w_in(WB, l, C_GATE + bi * D + half * 512, 512)
            for jj in range(4):
                j = half * 4 + jj
                pg = bank()
                fm_proj(pg[:, 0:NT], WB, jj * 128, 128, NT, WB.k, pg.k)
                P.act(gs.ap[:, 0:NT], pg[:, 0:NT], AF.Sigmoid, reads=[pg.k], writes=[gs.k])
                pb = bank()
                nk = 8 if per_head else 4
                for kk in range(nk):
                    P.mm(pb[:, 0:NT], wv[:, kk, j * 128:(j + 1) * 128], yv[:, kk, 0:NT],
                         start=(kk == 0), stop=(kk == nk - 1), reads=[WBR.k, yk], writes=[pb.k])
                P.tt("dve", gs.ap[:, 0:NT], gs.ap[:, 0:NT], pb[:, 0:NT], ALU.mult, reads=[gs.k, pb.k], writes=[gs.k])
                P.tt("pool", mrg[:, j, 0:NT], mrg[:, j, 0:NT], gs.ap[:, 0:NT], ALU.add, reads=[gs.k, mrg.k], writes=[mrg.k])

    def stage_out(cfg, l):
        new_stage()
        P.tag = 'out'
        NT, TT, NTI = cfg["NT"], cfg["TT"], cfg["NTI"]
        dbg_store(f"mrg{l}", mrg[:, :, 0:NT], [mrg.k])
        mb = ab([128, 8, NT], "mrgb")
        P.copy("dve", mb.v, mrg[:, :, 0:NT], reads=[mrg.k], writes=[mb.k])
        wo = w_out[l].rearrange("(kt p) n -> p kt n", p=128)
        for half in range(2):
            P.dma("pool", WB[:, :, :], wo[:, :, half * 512:(half + 1) * 512], writes=[WB.k])
            for ti in range(NTI):
                pb = bank()
                for kt in range(8):
                    P.mm(pb[0:TT, :], mb.v[:, kt, ti * TT:(ti + 1) * TT], WB[:, kt, :], start=(kt == 0), stop=(kt == 7),
                         reads=[mb.k, WB.k], writes=[pb.k])
                xsl = x_sb[0:TT, ti, half * 512:(half + 1) * 512]
                P.tt("dve", xsl, xsl, pb[0:TT, :], ALU.add, reads=[pb.k, x_sb.k], writes=[x_sb.k])

    def stage_pool(cfg, l):
        new_stage()
        P.tag = 'pool'
        NT, B, Ls, prompt, ck = cfg["NT"], cfg["B"], cfg["Ls"], cfg["prompt"], cfg["ck"]
        W = 15 + Ls
        load_w_in(WA, l, C_POOL, 512)
        load_w_in(WB, l, C_ZPOOL, 512)
        P.dma("pool", mixw[:, :, :], pool_mix[l], writes=[mixw.k])
        P.dma("sp", psc[:, :], pool_scale[l], writes=[psc.k])
        ext = af([128, 4, B, W], "ext")
        if prompt:
            P.copy("pool", ext.v[:, :, 0, 0:15], hist_pool[l][:, :, :], reads=[hist_pool[l].k], writes=[ext.k])
        else:
            stg = af([15, 4 * 512], "stg")
            for b in range(4):
                P.dma("sp", stg.ap[:, b * 512:(b + 1) * 512], spool[l, b], writes=[stg.k])
            pt = bank()
            for b in range(4):
                for g in range(4):
                    P.tr(pt[:, (b * 4 + g) * 15:(b * 4 + g + 1) * 15], stg.ap[0:15, b * 512 + g * 128: b * 512 + (g + 1) * 128],
                         identf[0:15, 0:15], reads=[stg.k, cf.k], writes=[pt.k])
            P.copy("act", ext.v[:, :, :, 0:15], pt[:, 0:240].rearrange("p (b g t) -> p g b t", b=4, g=4),
                   reads=[pt.k], writes=[ext.k])
        for g in range(4):
            pu = bank()
            fm_proj(pu[:, 0:NT], WA, g * 128, 128, NT, WA.k, pu.k)
            P.copy("act", ext.v[:, g, :, 15:W], pu[:, 0:NT].rearrange("p (b t) -> p b t", b=B), reads=[pu.k], writes=[ext.k])
        if prompt:
            P.copy("pool", hist_pool[l][:, :, :], ext.v[:, :, 0, Ls:Ls + 15], reads=[ext.k], writes=[hist_pool[l].k])
        if (not prompt) or ck == NCH - 1:
            ostg = af([15, 512], "ostg")
            for b in range(B):
                pt = bank()
                for g in range(4):
                    P.tr(pt[0:15, g * 128:(g + 1) * 128], ext.v[:, g, b, Ls:Ls + 15], identf[:, :],
                         reads=[ext.k, cf.k], writes=[pt.k])
                P.copy("act", ostg.ap[0:15, :], pt[0:15, 0:512], reads=[pt.k], writes=[ostg.k])
                dst = o_ppool[l] if prompt else o_spool[l, b]
                P.dma("sp", dst, ostg.ap[0:15, :], reads=[ostg.k], is_output=True)
        wa = af([128, B, W], "wa")
        wb_ = af([128, B, W], "wb")
        dT = ab([128, 4, B, Ls], "dT")
        ya = ab([128, 4, NT], "ya")
        zs = af([128, CH], "zs")
        fx = af([128, 15], "fx")
        for g, wdw in enumerate((2, 4, 8, 16)):
            cur, curk, n = ext.v[:, g, :, :], ext.k, W
            sh = 1
            bufs = [wa, wb_]
            bi = 0
            while sh < wdw:
                o = bufs[bi]
                P.tt("pool", o.v[:, :, 0:n - sh], cur[:, :, sh:n], cur[:, :, 0:n - sh], ALU.add, reads=[curk], writes=[o.k])
                cur, curk, n = o.v, o.k, n - sh
                sh *= 2
                bi ^= 1
            o0 = n - Ls
            P.stt(dT.v[:, g, :, :], cur[:, :, o0:o0 + Ls], 1.0 / wdw, ext.v[:, g, :, 15:W], ALU.mult, ALU.subtract,
                  reads=[curk, ext.k], writes=[dT.k])
            if prompt and ck == 0:
                P.tt("pool", fx.ap, cur[:, 0, o0:o0 + 15], rc15[:, g * 15:(g + 1) * 15], ALU.mult,
                     reads=[curk, cf.k], writes=[fx.k])
                P.tt("dve", dT.v[:, g, 0, 0:15], fx.ap, ext.v[:, g, 0, 15:30], ALU.subtract,
                     reads=[fx.k, ext.k, dT.k], writes=[dT.k])
        for g in range(4):
            p1 = bank()
            P.mm(p1[:, 0:NT], mixw[:, g, :], dT.ap[:, g * NT:(g + 1) * NT], reads=[mixw.k, dT.k], writes=[p1.k])
            p2 = bank()
            fm_proj(p2[:, 0:NT], WB, g * 128, 128, NT, WB.k, p2.k)
            P.act(zs.ap[:, 0:NT], p2[:, 0:NT], AF.Silu, reads=[p2.k], writes=[zs.k])
            P.stt(ya.v[:, g, 0:NT], p1[:, 0:NT], psc[:, g:g + 1], zs.ap[:, 0:NT], ALU.mult, ALU.mult,
                  reads=[p1.k, psc.k, zs.k], writes=[ya.k])
        dbg_store(f"ya{l}", ya.v, [ya.k])
        merge_branch(cfg, l, 0, ya.v, ya.k, w_br_pool, per_head=False)

    def stage_mla(cfg, l):
        new_stage()
        P.tag = 'mla.pre'
        NT, TT, NTI, B, Ls, prompt, ck = cfg["NT"], cfg["TT"], cfg["NTI"], cfg["B"], cfg["Ls"], cfg["prompt"], cfg["ck"]
        tok0 = ck * CH if prompt else 0
        wuq = ab([128, 3, H * 128], "wuq")
        wuk = ab([64, H, 256], "wuk")
        wuv = ab([128, 2, H * 64], "wuv")
        ropeq = af([32, 2, NT], "ropeq")
        gq_bc = af([128, 384], "gq_bc")
        gkv_bc = af([128, 256], "gkv_bc")
        load_w_in(WA, l, C_Q, 672)
        load_w_in(WB, l, C_ZMLA, 512)
        P.dma("pool", wuq.v[:, :, :], w_uq[l].rearrange("(kt p) n -> p kt n", p=128), writes=[wuq.k])
        P.dma("pool", wuk.v[:, :, :], w_ukT[l], writes=[wuk.k])
        P.dma("pool", wuv.v[:, :, :], w_uv[l].rearrange("(kt p) n -> p kt n", p=128), writes=[wuv.k])
        P.dma("sp", gq_bc.v[:, :], q_norm_g[l].partition_broadcast(128), writes=[gq_bc.k])
        P.dma("sp", gkv_bc.v[:, :], kv_norm_g[l].partition_broadcast(128), writes=[gkv_bc.k])
        rq0 = tok0 if prompt else SEQ
        P.dma("sp", ropeq.v[:, :, 0:NT], ropeq_d[:, :, rq0:rq0 + NT], writes=[ropeq.k])
        yb = ab([64, 8, NT], "yb")
        for h in range(8):
            pz = bank()
            fm_proj(pz[0:64, 0:NT], WB, h * 64, 64, NT, WB.k, pz.k)
            P.act(yb.v[:, h, 0:NT], pz[0:64, 0:NT], AF.Silu, reads=[pz.k], writes=[yb.k])
        ckvf = af([128, 288], "ckvf")
        ssq = af([128, 2], "ssq2")
        junk = af([128, 384], "junk2")
        t1 = af([128, 64], "ropetmp")
        qr = af([32, 2, 8, TT], "qr")
        rden = af([128, 4 * TT], "rden")
        ckvb = ab([128, 288], "ckvb")
        cqb = ab([128, 384], "cqb")
        cqT = ab([128, 3, TT], "cqT")
        qn = ab([64, 8, TT], "qn")
        qrT = ab([32, 8, TT], "qrT")
        qlT = ab([128, 2, 8, TT], "qlT")
        if prompt:
            pT = [ab([128, 4, TT], f"pT{i}") for i in range(2)]
        else:
            knT = ab([128, 3, 32], "knT")
            vn = ab([32, 256], "vn")
            pg_b = [ab([128, 288], f"pgb{i}") for i in range(3)]
            pg_T = [ab([128, 3, 128], f"pgT{i}") for i in range(2)]
            pts = [ab([128, 64], f"pts{i}") for i in range(2)]
            vb8 = ab([8, 256], "vb8")
            qc = ab([128, 2, 64], "qc")
            qrc = ab([32, 64], "qrc")
            ols = ab([128, 2, 64], "ols")
        for ti in range(NTI):
            set_rot(range(7))
            ktile = (tok0 // 128 + ti) if prompt else 16
            tsl = slice(ti * TT, (ti + 1) * TT)
            P.tag = 'mla.proj'
            pk = bank()
            for kt in range(8):
                P.mm(pk[0:TT, 0:288], xnT[:, kt, tsl], WA[:, kt, 384:672], start=(kt == 0), stop=(kt == 7),
                     reads=[xnT.k, WA.k], writes=[pk.k])
            P.act(junk.ap[0:TT, 0:256], pk[0:TT, 0:256], AF.Square, accum_out=ssq.ap[0:TT, 0:1],
                  reads=[pk.k], writes=[junk.k, ssq.k])
            rstd_inplace(ssq.ap[0:TT, 0:1], 1.0 / 256, [ssq.k])
            P.stt(ckvf.ap[0:TT, 0:256], pk[0:TT, 0:256], ssq.ap[0:TT, 0:1], gkv_bc.v[0:TT, :], ALU.mult, ALU.mult,
                  reads=[pk.k, ssq.k, gkv_bc.k], writes=[ckvf.k])
            cosk = ropek[0:TT, ktile, 0:16]
            sink = ropek[0:TT, ktile, 16:32]
            P.tt("dve", t1.ap[0:TT, 0:16], pk[0:TT, 256:272], cosk, ALU.mult, reads=[pk.k, ropek.k], writes=[t1.k])
            P.tt("dve", t1.ap[0:TT, 16:32], pk[0:TT, 272:288], sink, ALU.mult, reads=[pk.k, ropek.k], writes=[t1.k])
            P.tt("dve", t1.ap[0:TT, 32:48], pk[0:TT, 272:288], cosk, ALU.mult, reads=[pk.k, ropek.k], writes=[t1.k])
            P.tt("dve", t1.ap[0:TT, 48:64], pk[0:TT, 256:272], sink, ALU.mult, reads=[pk.k, ropek.k], writes=[t1.k])
            P.tt("dve", ckvf.ap[0:TT, 256:272], t1.ap[0:TT, 0:16], t1.ap[0:TT, 16:32], ALU.subtract, reads=[t1.k], writes=[ckvf.k])
            P.tt("dve", ckvf.ap[0:TT, 272:288], t1.ap[0:TT, 32:48], t1.ap[0:TT, 48:64], ALU.add, reads=[t1.k], writes=[ckvf.k])
            if prompt:
                r0 = tok0 + ti * 128
                P.dma("sp", o_pkv[l, r0:r0 + 128, :], ckvf.ap[0:128, 0:256], reads=[ckvf.k], is_output=True)
                P.dma("sp", o_pkr[l, r0:r0 + 128, :], ckvf.ap[0:128, 256:288], reads=[ckvf.k], is_output=True)
            else:
                P.dma("sp", o_skv[l], ckvf.ap[0:32, 0:256], reads=[ckvf.k], is_output=True)
                P.dma("sp", o_skr[l], ckvf.ap[0:32, 256:288], reads=[ckvf.k], is_output=True)
            P.copy("pool", ckvb.ap[0:TT, :], ckvf.ap[0:TT, :], reads=[ckvf.k], writes=[ckvb.k])
            for j, (c0, cn) in enumerate(((0, 128), (128, 128), (256, 32))):
                P.tr(bankb[0:cn, j * 128: j * 128 + TT], ckvb.ap[0:TT, c0:c0 + cn], identb[0:TT, 0:TT],
                     reads=[ckvb.k, identb.k], writes=[bankb.k])
            if prompt:
                P.copy("pool", Vc[l][:, ktile, :], ckvb.ap[:, 0:256], reads=[ckvb.k], writes=[Vc[l].k])
                P.copy("act", kTc[l][:, 0:2, ktile * 128:(ktile + 1) * 128], hv(bankb[:, 0:256], 2),
                       reads=[bankb.k], writes=[kTc[l].k])
                P.copy("act", kTc[l][0:32, 2, ktile * 128:(ktile + 1) * 128], bankb[0:32, 256:384],
                       reads=[bankb.k], writes=[kTc[l].k])
            else:
                P.copy("pool", vn.ap[0:32, :], ckvb.ap[0:32, 0:256], reads=[ckvb.k], writes=[vn.k])
                P.copy("act", knT.v[:, 0:2, :], hv(bankb[:, 0:256], 2)[:, :, 0:32], reads=[bankb.k], writes=[knT.k])
                P.copy("act", knT.v[0:32, 2, :], bankb[0:32, 256:288], reads=[bankb.k], writes=[knT.k])
            pq = bank()
            for kt in range(8):
                P.mm(pq[0:TT, 0:384], xnT[:, kt, tsl], WA[:, kt, 0:384], start=(kt == 0), stop=(kt == 7),
                     reads=[xnT.k, WA.k], writes=[pq.k])
            P.act(junk.ap[0:TT, 0:384], pq[0:TT, 0:384], AF.Square, accum_out=ssq.ap[0:TT, 1:2],
                  reads=[pq.k], writes=[junk.k, ssq.k])
            rstd_inplace(ssq.ap[0:TT, 1:2], 1.0 / 384, [ssq.k])
            P.stt(cqb.ap[0:TT, :], pq[0:TT, 0:384], ssq.ap[0:TT, 1:2], gq_bc.v[0:TT, :], ALU.mult, ALU.mult,
                  reads=[pq.k, ssq.k, gq_bc.k], writes=[cqb.k])
            for j in range(3):
                P.tr(bankb[:, 384 + j * 128: 384 + j * 128 + TT], cqb.ap[0:TT, j * 128:(j + 1) * 128], identb[0:TT, 0:TT],
                     reads=[cqb.k, identb.k], writes=[bankb.k])
            P.copy("act", cqT.v[:, :, 0:TT], hv(bankb[:, 384:768], 3)[:, :, 0:TT], reads=[bankb.k], writes=[cqT.k])
            for hg in range(2):
                pn = bank()
                for hh in range(4):
                    h = hg * 4 + hh
                    for j in range(3):
                        P.mm(pn[0:64, hh * 128: hh * 128 + TT], wuq.v[:, j, h * 128: h * 128 + 64], cqT.v[:, j, 0:TT],
                             start=(j == 0), stop=(j == 2), reads=[wuq.k, cqT.k], writes=[pn.k])
                P.copy("act", qn.v[:, hg * 4:(hg + 1) * 4, :], hv(pn[0:64, :], 4)[:, :, 0:TT], reads=[pn.k], writes=[qn.k])
            for v in range(2):
                for hg in range(2):
                    pr = bank()
                    for hh in range(4):
                        h = hg * 4 + hh
                        c0 = h * 128 + 64 + v * 32
                        for j in range(3):
                            P.mm(pr[0:32, hh * 128: hh * 128 + TT], wuq.v[:, j, c0:c0 + 32],
                                 cqT.v[:, j, 0:TT], start=(j == 0), stop=(j == 2), reads=[wuq.k, cqT.k], writes=[pr.k])
                    tab = ropeq.v[:, v, tsl]
                    P.tt("dve", qr.v[:, v, hg * 4:(hg + 1) * 4, :], hv(pr[0:32, :], 4)[:, :, 0:TT],
                         tab.unsqueeze(1).to_broadcast([32, 4, TT]), ALU.mult, reads=[pr.k, ropeq.k], writes=[qr.k])
            P.tt("pool", qrT.v, qr.v[:, 0, :, :], qr.v[:, 1, :, :], ALU.add, reads=[qr.k], writes=[qrT.k])
            for j in range(2):
                for hg in range(2):
                    pl = bank()
                    for hh in range(4):
                        h = hg * 4 + hh
                        P.mm(pl[:, hh * 128: hh * 128 + TT], wuk.v[:, h, j * 128:(j + 1) * 128], qn.v[:, h, :],
                             reads=[wuk.k, qn.k], writes=[pl.k])
                    P.copy("act" if hg == 0 else "dve", qlT.v[:, j, hg * 4:(hg + 1) * 4, :], hv(pl[:, :], 4)[:, :, 0:TT],
                           reads=[pl.k], writes=[qlT.k])
            dbg_store(f"qlT{l}", qlT.v, [qlT.k])
            dbg_store(f"qrT{l}", qrT.v, [qrT.k])
            P.tag = 'mla.attn'
            po = [banks[0], banks[1]]
            pd = banks[2]
            set_rot([3, 4, 5, 6])
            if prompt:
                nkt = ktile + 1
                for hg in range(2):
                    qsl = slice(hg * 4, (hg + 1) * 4)
                    for kt in range(nkt):
                        pscr = bank()
                        ksl = slice(kt * 128, (kt + 1) * 128)
                        P.mm(pscr[:, :], kTc[l][:, 0, ksl], qlT.v[:, 0, qsl, :], start=True, stop=False,
                             reads=[kTc[l].k, qlT.k], writes=[pscr.k])
                        P.mm(pscr[:, :], kTc[l][:, 1, ksl], qlT.v[:, 1, qsl, :], start=False, stop=False,
                             reads=[kTc[l].k, qlT.k], writes=[pscr.k])
                        P.mm(pscr[:, :], kTc[l][0:32, 2, ksl], qrT.v[0:32, qsl, :], start=False, stop=True,
                             reads=[kTc[l].k, qrT.k], writes=[pscr.k])
                        pt_ = pT[kt % 2]
                        P.act(pt_.ap[:, :], pscr[:, :], AF.Exp, scale=MLA_SCALE, reads=[pscr.k], writes=[pt_.k])
                        if kt == ktile:
                            P.tt("pool", pt_.v, pt_.v, m_causal.unsqueeze(1).to_broadcast([128, 4, 128]), ALU.mult,
                                 reads=[cf.k, pt_.k], writes=[pt_.k])
                        for j in range(2):
                            P.mm(po[j][:, :], Vc[l][:, kt, j * 128:(j + 1) * 128], pt_.ap[:, :], start=(kt == 0), stop=(kt == nkt - 1),
                                 reads=[Vc[l].k, pt_.k], writes=[po[j].k])
                        P.mm(pd[:, :], onesb[:, :], pt_.ap[:, :], start=(kt == 0), stop=(kt == nkt - 1),
                             reads=[onesb.k, pt_.k], writes=[pd.k])
                    P.act(rden.ap[:, :], pd[:, :], AF.Ln, reads=[pd.k], writes=[rden.k])
                    P.act(rden.ap[:, :], rden.ap[:, :], AF.Exp, scale=-1.0, reads=[rden.k], writes=[rden.k])
                    for j in range(2):
                        P.tt("dve", qlT.v[:, j, qsl, :], hv(po[j][:, :], 4), hv(rden.ap[:, :], 4), ALU.mult,
                             reads=[po[j].k, rden.k, qlT.k], writes=[qlT.k])
                for hg in range(2):
                    pm = bank()
                    for hh in range(4):
                        h = hg * 4 + hh
                        for j in range(2):
                            P.mm(pm[0:64, hh * 128:(hh + 1) * 128], wuv.v[:, j, h * 64:(h + 1) * 64], qlT.v[:, j, h, :],
                                 start=(j == 0), stop=(j == 1), reads=[wuv.k, qlT.k], writes=[pm.k])
                    P.tt("dve", yb.v[:, hg * 4:(hg + 1) * 4, tsl], hv(pm[0:64, :], 4), yb.v[:, hg * 4:(hg + 1) * 4, tsl], ALU.mult,
                         reads=[pm.k, yb.k], writes=[yb.k])
            else:
                ckv_rows = ckv.rearrange("l n t c -> (l n t) c")
                ckr_rows = ckr.rearrange("l n t c -> (l n t) c")
                for b in range(4):
                    P.dma("sp", ptb[:, :], ptab[b].partition_broadcast(128), writes=[ptb.k])
                    P.ts("dve", ridx[:, :], ptb[:, :], 128.0, cf[:, 600 + l:601 + l], ALU.mult, ALU.add, reads=[ptb.k, cf.k], writes=[ridx.k])
                    P.copy("dve", qc.v.rearrange("p j (h t) -> p j h t", t=8), qlT.v[:, :, :, b * 8:(b + 1) * 8],
                           reads=[qlT.k], writes=[qc.k])
                    P.copy("dve", qrc.ap.rearrange("p (h t) -> p h t", t=8), qrT.v[:, :, b * 8:(b + 1) * 8],
                           reads=[qrT.k], writes=[qrc.k])
                    for page in range(NPAGE + 1):
                        pscr = bank()
                        ptsb = pts[page % 2]
                        first = (page == 0)
                        last = (page == NPAGE)
                        if page < NPAGE:
                            bb = pg_b[page % 3]
                            tb = pg_T[page % 2]
                            P.idma(bb.ap[:, 0:256], ckv_rows, ridx[:, page:page + 1], reads=[ridx.k], writes=[bb.k])
                            P.idma(bb.ap[:, 256:288], ckr_rows, ridx[:, page:page + 1], reads=[ridx.k], writes=[bb.k])
                            for j, (c0, cn) in enumerate(((0, 128), (128, 128), (256, 32))):
                                P.tr(bankb[0:cn, j * 128:(j + 1) * 128], bb.ap[:, c0:c0 + cn], identb[:, :],
                                     reads=[bb.k, identb.k], writes=[bankb.k])
                            P.copy("dve", tb.v[:, 0:2, :], hv(bankb[:, 0:256], 2), reads=[bankb.k], writes=[tb.k])
                            P.copy("dve", tb.v[0:32, 2, :], bankb[0:32, 256:384], reads=[bankb.k], writes=[tb.k])
                            kk = 128
                            k0, k1, k2 = tb.v[:, 0, :], tb.v[:, 1, :], tb.v[0:32, 2, :]
                            kdeps = [tb.k]
                            vsrc, vdeps = bb.ap, [bb.k]
                        else:
                            kk = 8
                            k0, k1, k2 = knT.v[:, 0, b * 8:(b + 1) * 8], knT.v[:, 1, b * 8:(b + 1) * 8], knT.v[0:32, 2, b * 8:(b + 1) * 8]
                            kdeps = [knT.k]
                            P.dma("sp", vb8.ap[0:8, :], vn.ap[b * 8:(b + 1) * 8, :], reads=[vn.k], writes=[vb8.k])
                            vsrc, vdeps = vb8.ap, [vb8.k]
                        P.mm(pscr[0:kk, 0:64], k0, qc.v[:, 0, :], start=True, stop=False, reads=kdeps + [qc.k], writes=[pscr.k])
                        P.mm(pscr[0:kk, 0:64], k1, qc.v[:, 1, :], start=False, stop=False, reads=kdeps + [qc.k], writes=[pscr.k])
                        P.mm(pscr[0:kk, 0:64], k2, qrc.ap[0:32, :], start=False, stop=True, reads=kdeps + [qrc.k], writes=[pscr.k])
                        P.act(ptsb.ap[0:kk, :], pscr[0:kk, 0:64], AF.Exp, scale=MLA_SCALE, reads=[pscr.k], writes=[ptsb.k])
                        if last:
                            P.tt("pool", hv(ptsb.ap[0:8, :], 8), hv(ptsb.ap[0:8, :], 8),
                                 m_causal8.unsqueeze(1).to_broadcast([8, 8, 8]), ALU.mult, reads=[cf.k, ptsb.k], writes=[ptsb.k])
                        for j in range(2):
                            P.mm(po[j][:, 0:64], vsrc[0:kk, j * 128:(j + 1) * 128], ptsb.ap[0:kk, :], start=first, stop=last,
                                 reads=vdeps + [ptsb.k], writes=[po[j].k])
                        P.mm(pd[:, 0:64], onesb[0:kk, :], ptsb.ap[0:kk, :], start=first, stop=last,
                             reads=[onesb.k, ptsb.k], writes=[pd.k])
                    P.act(rden.ap[:, 0:64], pd[:, 0:64], AF.Ln, reads=[pd.k], writes=[rden.k])
                    P.act(rden.ap[:, 0:64], rden.ap[:, 0:64], AF.Exp, scale=-1.0, reads=[rden.k], writes=[rden.k])
                    for j in range(2):
                        P.tt("dve", ols.v[:, j, :], po[j][:, 0:64], rden.ap[:, 0:64], ALU.mult,
                             reads=[po[j].k, rden.k], writes=[ols.k])
                    pm = bank()
                    for h in range(8):
                        for j in range(2):
                            P.mm(pm[0:64, h * 8:(h + 1) * 8], wuv.v[:, j, h * 64:(h + 1) * 64], ols.v[:, j, h * 8:(h + 1) * 8],
                                 start=(j == 0), stop=(j == 1), reads=[wuv.k, ols.k], writes=[pm.k])
                    P.tt("dve", yb.v[:, :, b * 8:(b + 1) * 8], hv(pm[0:64, 0:64], 8), yb.v[:, :, b * 8:(b + 1) * 8], ALU.mult,
                         reads=[pm.k, yb.k], writes=[yb.k])
        set_rot(range(7))
        dbg_store(f"yb{l}", yb.v, [yb.k])
        merge_branch(cfg, l, 1, yb.v, yb.k, w_br_mla, per_head=True)

    def stage_dn(cfg, l):
        new_stage()
        P.tag = 'dn.d1'
        NT, B, Ls, prompt, ck, C = cfg["NT"], cfg["B"], cfg["Ls"], cfg["prompt"], cfg["ck"], cfg["C"]
        NSUB = NT // C
        LV = int(np.log2(C))
        W = 3 + Ls
        HG = 8
        NHG = 8 // HG
        HW_ = HG * 64
        do_out = (not prompt) or ck == NCH - 1
        P.dma("sp", cw[:, :, :], conv_w[l], writes=[cw.k])
        P.dma("sp", gdn[:, :], dn_norm_g[l], writes=[gdn.k])
        P.dma("sp", a_bc[:, :], a_log[l].partition_broadcast(64), writes=[a_bc.k])
        P.dma("sp", dtb_bc[:, :], dt_bias[l].partition_broadcast(64), writes=[dtb_bc.k])
        nega = af([64, 8], "nega")
        P.act(nega.ap, a_bc[:, :], AF.Exp, reads=[a_bc.k], writes=[nega.k])
        P.ts("dve", nega.ap, nega.ap, -1.0, None, ALU.mult, reads=[nega.k], writes=[nega.k])
        yc = ab([64, 8, NT], "yc")
        extc = af([64, B, W], "extc")
        if not prompt:
            hs = af([64, 24, 4, 3], "hs")
            stgc = af([3, 1536], "stgc")
            for b in range(4):
                P.dma("sp", stgc.ap[0:3, :], sconv[l, b], writes=[stgc.k])
                pt = bank()
                for ht in range(24):
                    P.tr(pt[0:64, ht * 3:(ht + 1) * 3], stgc.ap[0:3, ht * 64:(ht + 1) * 64], identf[0:3, 0:3],
                         reads=[stgc.k, cf.k], writes=[pt.k])
                P.copy("act", hs.v[:, :, b, :], hv(pt[0:64, 0:72], 24), reads=[pt.k], writes=[hs.k])
        ost = [af([3, 64], f"ost{i}") for i in range(2)]
        qkvb = [ab([64, HG, NT], f"qkvb{i}") for i in range(3)]
        cacc = af([64, NT], "cacc")
        sq = af([64, NT], "sq")
        names = ["Gb", "dgb", "E", "E1", "kbg", "qg", "Q0", "qkT", "P0", "TT", "Qb", "Pb"]
        tmp = {n: af([64, HG, 64], n) for n in names}
        tmp["vb"] = tmp["Gb"]
        tmp["kd"] = tmp["dgb"]
        tmp["R"] = tmp["E1"]
        tmp["vnw"] = tmp["Qb"]
        tmp["osq"] = tmp["Q0"]
        kbT = ab([64, HG, 64], "kbT")
        beta = af([64, HG], "beta")
        gg = af([64, HG], "gg")
        gc = af([64, HG], "gc")
        elast = af([64, HG], "elast")
        edl = af([64, HG], "edl")
        Ssm = af([64, HG, 64], "Ssm") if not prompt else None
        n_ost = 0
        for hg in range(NHG):
            hsl = slice(hg * HG, (hg + 1) * HG)
            P.tag = 'dn.d1'
            for which in range(3):
                load_w_in(WA, l, C_QKV + which * 512 + hg * HW_, HW_)
                for hh in range(HG):
                    h = hg * HG + hh
                    ht = which * 8 + h
                    pp = bank()
                    fm_proj(pp[0:64, 0:NT], WA, hh * 64, 64, NT, WA.k, pp.k)
                    if prompt:
                        P.copy("pool", extc.v[:, 0, 0:3], hist_conv[l][:, ht, :], reads=[hist_conv[l].k], writes=[extc.k])
                    else:
                        P.copy("pool", extc.v[:, :, 0:3], hs.v[:, ht, :, :], reads=[hs.k], writes=[extc.k])
                    P.copy("act", extc.v[:, :, 3:W], pp[0:64, 0:NT].rearrange("p (b t) -> p b t", b=B), reads=[pp.k], writes=[extc.k])
                    if prompt:
                        P.copy("pool", hist_conv[l][:, ht, :], extc.v[:, 0, Ls:Ls + 3], reads=[extc.k], writes=[hist_conv[l].k])
                    if do_out:
                        for b in range(B):
                            pt = bank()
                            P.tr(pt[0:3, 0:64], extc.v[:, b, Ls:Ls + 3], identf[0:64, 0:64], reads=[extc.k, cf.k], writes=[pt.k])
                            o_ = ost[n_ost % 2]
                            n_ost += 1
                            P.copy("act", o_.ap[0:3, :], pt[0:3, 0:64], reads=[pt.k], writes=[o_.k])
                            dst = o_pconv[l] if prompt else o_sconv[l, b]
                            P.dma("sp", dst[:, ht * 64:(ht + 1) * 64], o_.ap[0:3, :], reads=[o_.k], is_output=True)
                    caccv = cacc.ap[:, 0:NT].rearrange("p (b t) -> p b t", b=B)
                    P.ts("dve", caccv, extc.v[:, :, 0:Ls], cw[:, ht, 0:1], None, ALU.mult, reads=[extc.k, cw.k], writes=[cacc.k])
                    for j in range(1, 4):
                        P.stt(caccv, extc.v[:, :, j:j + Ls], cw[:, ht, j:j + 1], caccv, ALU.mult, ALU.add,
                              reads=[extc.k, cw.k, cacc.k], writes=[cacc.k])
                    if which == 2:
                        P.act(qkvb[2].v[:, hh, :], cacc.ap[:, 0:NT], AF.Silu, reads=[cacc.k], writes=[qkvb[2].k])
                    else:
                        P.act(cacc.ap[:, 0:NT], cacc.ap[:, 0:NT], AF.Silu, reads=[cacc.k], writes=[cacc.k])
                        P.act(sq.ap[:, 0:NT], cacc.ap[:, 0:NT], AF.Square, reads=[cacc.k], writes=[sq.k])
                        pss = bank()
                        P.mm(pss[0:64, 0:NT], onesf[0:64, 0:64], sq.ap[:, 0:NT], reads=[cf.k, sq.k], writes=[pss.k])
                        P.ts("dve", sq.ap[:, 0:NT], pss[0:64, 0:NT], EPS, None, ALU.add, reads=[pss.k], writes=[sq.k])
                        P.act(sq.ap[:, 0:NT], sq.ap[:, 0:NT], AF.Ln, reads=[sq.k], writes=[sq.k])
                        P.act(sq.ap[:, 0:NT], sq.ap[:, 0:NT], AF.Exp, scale=-0.5, reads=[sq.k], writes=[sq.k])
                        if l == 0 and hg == 0 and which == 1 and hh == 0:
                            dbg_store("rs", sq.ap[:, 0:NT], [sq.k])
                            dbg_store("cs", cacc.ap[:, 0:NT], [cacc.k])
                        if which == 0:
                            P.stt(qkvb[0].v[:, hh, :], cacc.ap[:, 0:NT], 0.125, sq.ap[:, 0:NT], ALU.mult, ALU.mult,
                                  reads=[cacc.k, sq.k], writes=[qkvb[0].k])
                        else:
                            P.tt("dve", qkvb[1].v[:, hh, :], cacc.ap[:, 0:NT], sq.ap[:, 0:NT], ALU.mult,
                                 reads=[cacc.k, sq.k], writes=[qkvb[1].k])
            load_w_in(WB, l, C_ZDN + hg * HW_, HW_)
            load_w_in(WA, l, C_BETA, 16, dcol=0)
            for hh in range(HG):
                pz = bank()
                fm_proj(pz[0:64, 0:NT], WB, hh * 64, 64, NT, WB.k, pz.k)
                P.act(yc.v[:, hg * HG + hh, :], pz[0:64, 0:NT], AF.Silu, reads=[pz.k], writes=[yc.k])
            Gb, dgb, E, E1, kbg, qg = (tmp[n] for n in ("Gb", "dgb", "E", "E1", "kbg", "qg"))
            Q0, qkT, P0, TT_, Qb, Pb = (tmp[n] for n in ("Q0", "qkT", "P0", "TT", "Qb", "Pb"))
            vb, kd, R, vnw, osq = (tmp[n] for n in ("vb", "kd", "R", "vnw", "osq"))
            for s in range(NSUB if DN_NSUB is None else DN_NSUB):
                cs = slice(s * C, (s + 1) * C)
                bseq = s
                if not prompt:
                    P.dma("sp", Ssm.v, sdelta[l, bseq, hsl].rearrange("h k v -> k h v"), writes=[Ssm.k])
                    Sv, Sk = Ssm.v, Ssm.k
                else:
                    Sv, Sk = S_p[l][:, hsl, :], S_p[l].k
                P.tag = 'dn.pre'
                pbg = bank()
                for kt in range(8):
                    P.mm(pbg[0:C, 0:16], xnT[:, kt, cs], WA[:, kt, 0:16], start=(kt == 0), stop=(kt == 7),
                         reads=[xnT.k, WA.k], writes=[pbg.k])
                P.act(beta.ap[0:C, :], pbg[0:C, hg * HG:(hg + 1) * HG], AF.Sigmoid, reads=[pbg.k], writes=[beta.k])
                P.tt("dve", gg.ap[0:C, :], pbg[0:C, 8 + hg * HG: 8 + (hg + 1) * HG], dtb_bc[0:C, hsl], ALU.add, reads=[pbg.k, dtb_bc.k], writes=[gg.k])
                P.act(gg.ap[0:C, :], gg.ap[0:C, :], AF.Exp, reads=[gg.k], writes=[gg.k])
                P.ts("dve", gg.ap[0:C, :], gg.ap[0:C, :], 1.0, None, ALU.add, reads=[gg.k], writes=[gg.k])
                P.act(gg.ap[0:C, :], gg.ap[0:C, :], AF.Ln, reads=[gg.k], writes=[gg.k])
                P.tt("dve", gg.ap[0:C, :], gg.ap[0:C, :], nega.ap[0:C, hsl], ALU.mult, reads=[gg.k, nega.k], writes=[gg.k])
                pg1 = bank()
                P.mm(pg1[0:C, 0:HG], m_incl[0:C, 0:C], gg.ap[0:C, :], reads=[cf.k, gg.k], writes=[pg1.k])
                P.mm(pg1[0:64, 8:8 + HG], onesf[0:C, 0:64], gg.ap[0:C, :], reads=[cf.k, gg.k], writes=[pg1.k])
                P.copy("act", gc.ap[0:C, :], pg1[0:C, 0:HG], reads=[pg1.k], writes=[gc.k])
                P.act(elast.ap[:, :], pg1[0:64, 8:8 + HG], AF.Exp, reads=[pg1.k], writes=[elast.k])
                P.tt("dve", edl.ap[0:C, :], pg1[0:C, 8:8 + HG], gc.ap[0:C, :], ALU.subtract, reads=[pg1.k, gc.k], writes=[edl.k])
                P.act(edl.ap[0:C, :], edl.ap[0:C, :], AF.Exp, reads=[edl.k], writes=[edl.k])
                if DN_CUT <= 1:
                    continue
                P.copy("pool", Gb.v[0:C, :, :], gg.ap[0:C, :].unsqueeze(2).to_broadcast([C, HG, 64]), reads=[gg.k], writes=[Gb.k])
                if DN_CUT <= 1.2:
                    continue
                P.tt("pool", dgb.v[0:C, :, 0:C], identf[0:C, 0:C].unsqueeze(1).to_broadcast([C, HG, C]),
                     beta.ap[0:C, :].unsqueeze(2).to_broadcast([C, HG, C]), ALU.mult, reads=[cf.k, beta.k], writes=[dgb.k])
                if DN_CUT <= 1.4:
                    continue
                pgcb = bank()
                pbb = bank()
                for hh in range(HG):
                    P.mm(pgcb[0:64, hh * 64: hh * 64 + C], Gb.v[0:C, hh, :], m_incl[0:C, 0:C], reads=[Gb.k, cf.k], writes=[pgcb.k])
                    P.mm(pbb[0:64, hh * 64: hh * 64 + C], onesf[0:C, 0:64], dgb.v[0:C, hh, 0:C], reads=[cf.k, dgb.k], writes=[pbb.k])
                if DN_CUT <= 1.6:
                    continue
                gcbv = hv(pgcb[0:64, 0:HW_], HG)
                pbbv = hv(pbb[0:64, 0:HW_], HG)
                P.act(E.v[:, :, 0:C], gcbv[:, :, 0:C], AF.Exp, reads=[pgcb.k], writes=[E.k])
                if DN_CUT <= 1.8:
                    continue
                P.ts("pool", dgb.v[0:C, :, :], Gb.v[0:C, :, :], -1.0, None, ALU.mult, reads=[Gb.k, dgb.k], writes=[dgb.k])
                pdf = bank()
                for hh in range(HG):
                    P.mm(pdf[0:C, hh * 64: hh * 64 + C], Gb.v[0:C, hh, 0:C], m_incl[0:C, 0:C], start=True, stop=False,
                         reads=[Gb.k, cf.k], writes=[pdf.k])
                    P.mm(pdf[0:C, hh * 64: hh * 64 + C], m_incl[0:C, 0:C], dgb.v[0:C, hh, 0:C], start=False, stop=True,
                         reads=[dgb.k, cf.k], writes=[pdf.k])
                P.ts("dve", E1.v[0:C, :, 0:C], hv(pdf[0:64, 0:HW_], HG)[0:C, :, 0:C], 0.0, None, ALU.min, reads=[pdf.k], writes=[E1.k])
                if DN_CUT <= 1.9:
                    continue
                P.act(E1.v[0:C, :, 0:C], E1.v[0:C, :, 0:C], AF.Exp, reads=[E1.k], writes=[E1.k])
                if DN_CUT <= 2:
                    continue
                kTs = qkvb[1].v[:, :, cs]
                qTs = qkvb[0].v[:, :, cs]
                P.tt("dve", kbg.v[:, :, 0:C], kTs, pbbv[:, :, 0:C], ALU.mult, reads=[qkvb[1].k, pbb.k], writes=[kbg.k])
                P.copy("pool", kbT.v[:, :, 0:C], kbg.v[:, :, 0:C], reads=[kbg.k], writes=[kbT.k])
                P.tt("dve", kbg.v[:, :, 0:C], kbg.v[:, :, 0:C], E.v[:, :, 0:C], ALU.mult, reads=[kbg.k, E.k, kbT.k], writes=[kbg.k])
                P.tt("pool", qg.v[:, :, 0:C], qTs, E.v[:, :, 0:C], ALU.mult, reads=[qkvb[0].k, E.k], writes=[qg.k])
                pkk = bank()
                pqk = bank()
                for hh in range(HG):
                    P.mm(pkk[0:C, hh * 64: hh * 64 + C], qkvb[1].v[:, hh, cs], kbT.v[:, hh, 0:C], reads=[qkvb[1].k, kbT.k], writes=[pkk.k])
                    P.mm(pqk[0:C, hh * 64: hh * 64 + C], qkvb[1].v[:, hh, cs], qkvb[0].v[:, hh, cs], reads=[qkvb[1].k, qkvb[0].k], writes=[pqk.k])
                P.tt("dve", Q0.v[0:C, :, 0:C], hv(pkk[0:64, 0:HW_], HG)[0:C, :, 0:C], E1.v[0:C, :, 0:C], ALU.mult, reads=[pkk.k, E1.k], writes=[Q0.k])
                P.tt("pool", Q0.v[0:C, :, 0:C], Q0.v[0:C, :, 0:C], m_nstrict[0:C, 0:C].unsqueeze(1).to_broadcast([C, HG, C]), ALU.mult,
                     reads=[Q0.k, cf.k], writes=[Q0.k])
                P.tt("dve", qkT.v[0:C, :, 0:C], hv(pqk[0:64, 0:HW_], HG)[0:C, :, 0:C], E1.v[0:C, :, 0:C], ALU.mult, reads=[pqk.k, E1.k], writes=[qkT.k])
                P.tt("pool", qkT.v[0:C, :, 0:C], qkT.v[0:C, :, 0:C], m_incl[0:C, 0:C].unsqueeze(1).to_broadcast([C, HG, C]), ALU.mult,
                     reads=[qkT.k, cf.k], writes=[qkT.k])
                if DN_CUT <= 3:
                    continue
                ptp = bank()
                for hh in range(HG):
                    P.tr(ptp[0:C, hh * 64: hh * 64 + C], Q0.v[0:C, hh, 0:C], identf[0:C, 0:C], reads=[Q0.k, cf.k], writes=[ptp.k])
                P.copy("act", P0.v[0:C, :, 0:C], hv(ptp[0:64, 0:HW_], HG)[0:C, :, 0:C], reads=[ptp.k], writes=[P0.k])
                P.tt("dve", TT_.v[0:C, :, 0:C], Q0.v[0:C, :, 0:C], identf[0:C, 0:C].unsqueeze(1).to_broadcast([C, HG, C]), ALU.add,
                     reads=[Q0.k, cf.k], writes=[TT_.k])
                if DN_CUT <= 4:
                    continue
                P.tag = 'dn.neu'
                Qa, Pa, Qn_, Pn_ = Q0, P0, Qb, Pb
                for lv in range(LV - 1):
                    pq2 = bank()
                    pp2 = bank()
                    lastlv = (lv == LV - 2)
                    for hh in range(HG):
                        if not lastlv:
                            P.mm(pq2[0:C, hh * 64: hh * 64 + C], Pa.v[0:C, hh, 0:C], Qa.v[0:C, hh, 0:C], reads=[Pa.k, Qa.k], writes=[pq2.k])
                        P.mm(pp2[0:C, hh * 64: hh * 64 + C], Qa.v[0:C, hh, 0:C], Pa.v[0:C, hh, 0:C], reads=[Pa.k, Qa.k], writes=[pp2.k])
                    P.copy("act", Pn_.v[0:C, :, 0:C], hv(pp2[0:64, 0:HW_], HG)[0:C, :, 0:C], reads=[pp2.k], writes=[Pn_.k])
                    if not lastlv:
                        P.copy("dve", Qn_.v[0:C, :, 0:C], hv(pq2[0:64, 0:HW_], HG)[0:C, :, 0:C], reads=[pq2.k], writes=[Qn_.k])
                    pt2 = bank()
                    for hh in range(HG):
                        P.mm(pt2[0:C, hh * 64: hh * 64 + C], Pn_.v[0:C, hh, 0:C], TT_.v[0:C, hh, 0:C], reads=[Pn_.k, TT_.k], writes=[pt2.k])
                    P.tt("dve", TT_.v[0:C, :, 0:C], TT_.v[0:C, :, 0:C], hv(pt2[0:64, 0:HW_], HG)[0:C, :, 0:C], ALU.add,
                         reads=[pt2.k, TT_.k], writes=[TT_.k])
                    Qa, Qn_ = Qn_, Qa
                    Pa, Pn_ = Pn_, Pa
                if DN_CUT <= 5:
                    continue
                P.tag = 'dn.scan'
                for hh in range(HG):
                    P.tr(bankb[0:C, hh * 64:(hh + 1) * 64], qkvb[2].v[:, hh, cs], identb[0:64, 0:64], reads=[qkvb[2].k, identb.k], writes=[bankb.k])
                    P.tr(bankb[0:C, HW_ + hh * 64: HW_ + (hh + 1) * 64], qkvb[1].v[:, hh, cs], identb[0:64, 0:64], reads=[qkvb[1].k, identb.k], writes=[bankb.k])
                P.tt("dve", vb.v[0:C, :, :], hv(bankb[0:C, 0:HW_], HG),
                     beta.ap[0:C, :].unsqueeze(2).to_broadcast([C, HG, 64]), ALU.mult, reads=[bankb.k, beta.k], writes=[vb.k])
                P.tt("dve", kd.v[0:C, :, :], hv(bankb[0:C, HW_:2 * HW_], HG),
                     edl.ap[0:C, :].unsqueeze(2).to_broadcast([C, HG, 64]), ALU.mult, reads=[bankb.k, edl.k], writes=[kd.k])
                if DN_CUT <= 6:
                    continue
                if l == 0 and hg == 0 and s == DBG_S:
                    dbg_store("gg", gg.ap[0:C, :], [gg.k])
                    dbg_store("beta", beta.ap[0:C, :], [beta.k])
                    dbg_store("gc", gc.ap[0:C, :], [gc.k])
                    dbg_store("E1", E1.v[0:C, :, 0:C], [E1.k])
                    dbg_store("Q0", Q0.v[0:C, :, 0:C], [Q0.k])
                    dbg_store("qkT", qkT.v[0:C, :, 0:C], [qkT.k])
                    dbg_store("TT", TT_.v[0:C, :, 0:C], [TT_.k])
                    dbg_store("kbg", kbg.v[:, :, 0:C], [kbg.k])
                    dbg_store("qg", qg.v[:, :, 0:C], [qg.k])
                    dbg_store("vb", vb.v[0:C, :, :], [vb.k])
                    dbg_store("kd", kd.v[0:C, :, :], [kd.k])
                pR = bank()
                for hh in range(HG):
                    P.mm(pR[0:C, hh * 64:(hh + 1) * 64], kbg.v[:, hh, 0:C], Sv[:, hh, :], reads=[kbg.k, Sk], writes=[pR.k])
                P.tt("dve", R.v[0:C, :, :], vb.v[0:C, :, :], hv(pR[0:C, 0:HW_], HG), ALU.subtract,
                     reads=[vb.k, pR.k], writes=[R.k])
                pvn = bank()
                for hh in range(HG):
                    P.mm(pvn[0:C, hh * 64:(hh + 1) * 64], TT_.v[0:C, hh, 0:C], R.v[0:C, hh, :], reads=[TT_.k, R.k], writes=[pvn.k])
                P.copy("act", vnw.v[0:C, :, :], hv(pvn[0:C, 0:HW_], HG), reads=[pvn.k], writes=[vnw.k])
                if DN_CUT <= 7:
                    continue
                po_ = bank()
                for hh in range(HG):
                    P.mm(po_[0:64, hh * 64: hh * 64 + C], Sv[:, hh, :], qg.v[:, hh, 0:C], start=True, stop=False,
                         reads=[Sk, qg.k], writes=[po_.k])
                    P.mm(po_[0:64, hh * 64: hh * 64 + C], vnw.v[0:C, hh, :], qkT.v[0:C, hh, 0:C], start=False, stop=True,
                         reads=[vnw.k, qkT.k], writes=[po_.k])
                pS = bank()
                for hh in range(HG):
                    P.mm(pS[0:64, hh * 64:(hh + 1) * 64], kd.v[0:C, hh, :], vnw.v[0:C, hh, :], reads=[kd.k, vnw.k], writes=[pS.k])
                for hh in range(HG):
                    P.ts("dve", Sv[:, hh, :], Sv[:, hh, :], elast.ap[:, hh:hh + 1], None, ALU.mult, reads=[Sk, elast.k], writes=[Sk])
                P.tt("dve", Sv, Sv, hv(pS[0:64, 0:HW_], HG), ALU.add, reads=[pS.k, Sk], writes=[Sk])
                if DN_CUT <= 8:
                    continue
                if l == 0 and hg == 0 and s == DBG_S:
                    dbg_store("R", R.v[0:C, :, :], [R.k])
                    dbg_store("vnw", vnw.v[0:C, :, :], [vnw.k])
                    dbg_store("Snew", Sv, [Sk])
                ov = hv(po_[0:64, 0:HW_], HG)
                P.act(osq.v[:, :, 0:C], ov[:, :, 0:C], AF.Square, reads=[po_.k], writes=[osq.k])
                pn2 = bank()
                for hh in range(HG):
                    P.mm(pn2[0:64, hh * 64: hh * 64 + C], onesf[0:64, 0:64], osq.v[:, hh, 0:C], reads=[cf.k, osq.k], writes=[pn2.k])
                P.ts("dve", osq.v[:, :, 0:C], hv(pn2[0:64, 0:HW_], HG)[:, :, 0:C], 1.0 / 64, EPS, ALU.mult, ALU.add, reads=[pn2.k], writes=[osq.k])
                P.act(osq.v[:, :, 0:C], osq.v[:, :, 0:C], AF.Ln, reads=[osq.k], writes=[osq.k])
                P.act(osq.v[:, :, 0:C], osq.v[:, :, 0:C], AF.Exp, scale=-0.5, reads=[osq.k], writes=[osq.k])
                P.stt(osq.v[:, :, 0:C], ov[:, :, 0:C], gdn[:, 0:1], osq.v[:, :, 0:C], ALU.mult, ALU.mult,
                      reads=[po_.k, gdn.k, osq.k], writes=[osq.k])
                P.tt("dve", yc.v[:, hsl, cs], osq.v[:, :, 0:C], yc.v[:, hsl, cs], ALU.mult, reads=[osq.k, yc.k], writes=[yc.k])
                if not prompt:
                    P.dma("sp", o_sdelta[l, bseq, hsl].rearrange("h k v -> k h v"), Sv, reads=[Sk], is_output=True)
            if prompt and ck == NCH - 1:
                P.dma("sp", o_pdelta[l, hsl].rearrange("h k v -> k h v"), S_p[l][:, hsl, :],
                      reads=[S_p[l].k], is_output=True)
        dbg_store(f"yc{l}", yc.v, [yc.k])
        new_stage(reset_b=False)
        merge_branch(cfg, l, 2, yc.v, yc.k, w_br_dn, per_head=True)

    def stage_final(cfg):
        new_stage()
        P.tag = 'final'
        NT, TT, NTI, prompt, ck = cfg["NT"], cfg["TT"], cfg["NTI"], cfg["prompt"], cfg["ck"]
        P.dma("sp", gn_bc[:, :], final_norm_g.partition_broadcast(128), writes=[gn_bc.k])
        junk = af([128, D], "junkf")
        ssq = af([128, 4], "ssqf")
        yo = [af([128, D], f"yo{i}") for i in range(2)]
        for ti in range(NTI):
            xt = x_sb[0:TT, ti, :]
            P.act(junk.ap[0:TT, :], xt, AF.Square, accum_out=ssq.ap[0:TT, ti:ti + 1], reads=[x_sb.k], writes=[junk.k, ssq.k])
            rstd_inplace(ssq.ap[0:TT, ti:ti + 1], 1.0 / D, [ssq.k])
            y = yo[ti % 2]
            P.stt(y.ap[0:TT, :], xt, ssq.ap[0:TT, ti:ti + 1], gn_bc[0:TT, :], ALU.mult, ALU.mult,
                  reads=[x_sb.k, ssq.k, gn_bc.k], writes=[y.k])
            if prompt:
                r0 = ck * CH + ti * 128
                P.dma("sp", y_p[r0:r0 + 128, :], y.ap[0:128, :], reads=[y.k], is_output=True)
            else:
                P.dma("sp", y_s, y.ap[0:32, :], reads=[y.k], is_output=True)

    cfgs = [dict(NT=32, TT=32, NTI=1, B=4, Ls=8, prompt=False, ck=0, C=8)]
    if prompt_only:
        cfgs = []
    if not sample_only:
        for ck in range(prompt_chunks):
            cfgs.append(dict(NT=CH, TT=128, NTI=4, B=1, Ls=CH, prompt=True, ck=ck, C=64))
    for cfg in cfgs:
        new_stage()
        if cfg["prompt"]:
            r0 = cfg["ck"] * CH
            P.dma("sp", x_sb[:, :, :], xp[r0:r0 + CH, :].rearrange("(t p) d -> p t d", p=128), writes=[x_sb.k])
        else:
            P.dma("sp", x_sb[0:32, 0, :], xs, writes=[x_sb.k])
        for l in range(DEPTH):
            if "norm" not in skip:
                stage_norm(cfg, l)
            if "pool" in stages:
                stage_pool(cfg, l)
            if "mla" in stages:
                stage_mla(cfg, l)
            if "dn" in stages:
                stage_dn(cfg, l)
            if "out" not in skip:
                stage_out(cfg, l)
        if "final" not in skip:
            stage_final(cfg)
    P.fence()
    P.emit(sems, slot_sems)
    es.close()
    return nc, P


def _prep_inputs(inp):
    global _CONST
    if _CONST is None:
        _CONST = _consts()
    f32 = np.float32
    w_uq = np.asarray(inp["w_uq"], f32).reshape(DEPTH, 384, H, 96)
    rope = w_uq[..., 64:96]
    rope_sw = np.concatenate([rope[..., 16:32], rope[..., 0:16]], -1)
    w_uq_ext = np.ascontiguousarray(np.concatenate([w_uq, rope_sw], -1).reshape(DEPTH, 384, H * 128))
    shared = {
        "ckv": np.asarray(inp["cache_kv_latent"], f32),
        "ckr": np.asarray(inp["cache_k_rope"], f32),
        "norm_g": np.asarray(inp["norm_g"], f32),
        "w_in": np.asarray(inp["w_in"], f32),
        "pool_mix": np.ascontiguousarray(np.asarray(inp["pool_mix"], f32).transpose(0, 2, 1, 3)),
        "pool_scale": np.ascontiguousarray(np.asarray(inp["pool_scale"], f32).reshape(DEPTH, 4, 128).transpose(0, 2, 1)),
        "q_norm_g": np.asarray(inp["q_norm_g"], f32),
        "w_uq": w_uq_ext,
        "kv_norm_g": np.asarray(inp["kv_norm_g"], f32),
        "w_ukT": np.ascontiguousarray(np.asarray(inp["w_uk"], f32).transpose(0, 3, 2, 1)),
        "w_uv": np.ascontiguousarray(np.asarray(inp["w_uv"], f32).reshape(DEPTH, 256, H * 64)),
        "conv_w": np.ascontiguousarray(np.asarray(inp["conv_w"], f32).reshape(DEPTH, 4, 24, 64).transpose(0, 3, 2, 1)),
        "a_log": np.asarray(inp["a_log"], f32),
        "dt_bias": np.asarray(inp["dt_bias"], f32),
        "dn_norm_g": np.ascontiguousarray(np.asarray(inp["dn_norm_g"], f32).reshape(DEPTH, 64, 1)),
        "w_br_pool": np.asarray(inp["w_br_pool"], f32),
        "w_br_mla": np.asarray(inp["w_br_mla"], f32),
        "w_br_dn": np.asarray(inp["w_br_dn"], f32),
        "w_out": np.asarray(inp["w_out"], f32),
        "final_norm_g": np.asarray(inp["final_norm_g"], f32),
        "cf": _CONST["cf"], "ropeq": _CONST["ropeq"], "ropek": _CONST["ropek"],
    }
    xp = np.asarray(inp["x_prompt"], f32)
    xs = np.asarray(inp["x_sample"], f32)
    sp = np.asarray(inp["state_pool"], f32)
    sc = np.asarray(inp["state_conv"], f32)
    sd = np.asarray(inp["state_delta"], f32)
    pt = np.asarray(inp["page_table"], np.int32)
    in_maps = []
    for c in range(NCORE):
        m = dict(shared)
        m["xp"] = np.ascontiguousarray(xp[c])
        m["xs"] = np.ascontiguousarray(xs[4 * c:4 * c + 4].reshape(32, D))
        m["spool"] = np.ascontiguousarray(sp[:, 4 * c:4 * c + 4])
        m["sconv"] = np.ascontiguousarray(sc[:, 4 * c:4 * c + 4])
        m["sdelta"] = np.ascontiguousarray(sd[:, 4 * c:4 * c + 4])
        m["ptab"] = np.ascontiguousarray(pt[4 * c:4 * c + 4])
        in_maps.append(m)
    return in_maps


_NC = None


def kernel(**inputs):
    global _NC
    in_maps = _prep_inputs(inputs)
    if _NC is None:
        _NC = build()[0]
    res = run_bass_kernel_spmd(_NC, in_maps, core_ids=list(range(NCORE)))
    r = res.results
    cat = lambda k: np.stack([r[c][k] for c in range(NCORE)], 0)
    y_p = cat("y_p")
    y_s = np.concatenate([r[c]["y_s"].reshape(4, 8, D) for c in range(NCORE)], 0)
    p_kv = np.stack([r[c]["o_pkv"] for c in range(NCORE)], 1)
    p_kr = np.stack([r[c]["o_pkr"] for c in range(NCORE)], 1)
    p_pool = np.stack([r[c]["o_ppool"] for c in range(NCORE)], 1)
    p_conv = np.stack([r[c]["o_pconv"] for c in range(NCORE)], 1)
    p_delta = np.stack([r[c]["o_pdelta"] for c in range(NCORE)], 1)
    s_kv = np.concatenate([r[c]["o_skv"].reshape(DEPTH, 4, 8, 256) for c in range(NCORE)], 1)
    s_kr = np.concatenate([r[c]["o_skr"].reshape(DEPTH, 4, 8, 32) for c in range(NCORE)], 1)
    s_pool = np.concatenate([r[c]["o_spool"] for c in range(NCORE)], 1)
    s_conv = np.concatenate([r[c]["o_sconv"] for c in range(NCORE)], 1)
    s_delta = np.concatenate([r[c]["o_sdelta"] for c in range(NCORE)], 1)
    outs = (y_p, y_s, p_kv, p_kr, p_pool, p_conv, p_delta, s_kv, s_kr, s_pool, s_conv, s_delta)
    return tuple(np.ascontiguousarray(o, dtype=np.float32) for o in outs)
```

```python
import contextlib
import numpy as np
import concourse.bass as bass
import concourse.mybir as mybir
from concourse.bass_utils import run_bass_kernel_spmd

F32 = mybir.dt.float32
BF16 = mybir.dt.bfloat16
I32 = mybir.dt.int32
AF = mybir.ActivationFunctionType
ALU = mybir.AluOpType

D = 1024
SEQ = 2048
DEPTH = 2
EPS = 1e-6
NPAGE = 128
H = 8
MLA_SCALE = 96 ** -0.5
NCORE = 8
CH = 512
NCH = SEQ // CH
C_POOL, C_ZPOOL, C_Q, C_KV, C_KR, C_ZMLA, C_QKV, C_ZDN, C_BETA, C_ALPHA, C_GATE = (
    0, 512, 1024, 1408, 1664, 1696, 2208, 3744, 4256, 4264, 4272)
INW = 7344


class Tk:
    __slots__ = ("w", "r", "name")

    def __init__(self, name=""):
        self.w = {}
        self.r = {}
        self.name = name


class Op:
    __slots__ = ("fn", "waits", "signal", "dma", "tag")

    def __init__(self, fn, dma=None):
        self.fn = fn
        self.waits = []
        self.signal = False
        self.dma = dma
        self.tag = None


STREAMS = ("pe", "act", "dve", "pool", "sp")
NSLOT = {"sp": 28, "act": 8, "pool": 24}


class Prog:
    def __init__(self, nc):
        self.nc = nc
        self.ops = {s: [] for s in STREAMS}
        self.seen_c = {s: {} for s in STREAMS}
        self.seen_d = {s: {} for s in STREAMS}
        self.slot_next = {s: 0 for s in NSLOT}
        self.slot_val = {}
        self.out_dma_events = []
        self.pending_dma = {}
        self.last_c = {s: -1 for s in STREAMS}
        self.tag = None
        self.annotate = False

    def _need(self, stream, ev, waits, force_same=False):
        if ev[0] == "c":
            _, e2, idx = ev
            if idx < 0:
                return
            if e2 == stream and stream == "pe" and not force_same:
                return
            if self.seen_c[stream].get(e2, -1) >= idx:
                return
            self.seen_c[stream][e2] = idx
            self.ops[e2][idx].signal = True
            waits.append(ev)
        else:
            _, slot, val = ev
            if self.seen_d[stream].get(slot, 0) >= val:
                return
            self.seen_d[stream][slot] = val
            waits.append(ev)

    def _deps(self, stream, reads, writes, force_same=False):
        waits = []
        for t in reads:
            for ev in t.w.values():
                self._need(stream, ev, waits, force_same)
        for t in writes:
            for ev in t.w.values():
                self._need(stream, ev, waits, force_same)
            for ev in t.r.values():
                self._need(stream, ev, waits, force_same)
        return waits

    def _commit(self, ev, reads, writes):
        key = ev[:2]
        for t in reads:
            t.r[key] = ev
        for t in writes:
            t.w = {key: ev}
            t.r = {}

    def op(self, stream, fn, reads=(), writes=()):
        o = Op(fn)
        o.waits = self._deps(stream, reads, writes)
        idx = len(self.ops[stream])
        o.tag = self.tag
        self.ops[stream].append(o)
        self.last_c[stream] = idx
        self._commit(("c", stream, idx), reads, writes)
        return o

    def dma(self, stream, out, in_, reads=(), writes=(), is_output=False, **kw):
        n = NSLOT[stream]
        k = self.slot_next[stream]
        self.slot_next[stream] = k + 1
        slot = (stream, k % n)
        prev = self.slot_val.get(slot, 0)
        val = prev + 16
        self.slot_val[slot] = val
        o = Op(lambda e: e.dma_start(out=out, in_=in_, **kw), dma=(slot, val))
        o.waits = self._deps(stream, reads, writes, force_same=True)
        o.tag = self.tag
        if prev > 0:
            self._need(stream, ("d", slot, prev), o.waits)
        self.ops[stream].append(o)
        ev = ("d", slot, val)
        self._commit(ev, reads, writes)
        self.pending_dma[slot] = ev
        if is_output:
            self.out_dma_events.append(ev)
        return o

    def idma(self, out, in_, idx_ap, reads=(), writes=()):
        stream = "pool"
        n = NSLOT[stream]
        k = self.slot_next[stream]
        self.slot_next[stream] = k + 1
        slot = (stream, k % n)
        prev = self.slot_val.get(slot, 0)
        val = prev + 16
        self.slot_val[slot] = val
        o = Op(lambda e: e.indirect_dma_start(out=out, out_offset=None, in_=in_,
                                              in_offset=bass.IndirectOffsetOnAxis(ap=idx_ap, axis=0)), dma=(slot, val))
        o.waits = self._deps(stream, reads, writes, force_same=True)
        if prev > 0:
            self._need(stream, ("d", slot, prev), o.waits)
        self.ops[stream].append(o)
        ev = ("d", slot, val)
        self._commit(ev, reads, writes)
        self.pending_dma[slot] = ev
        return o

    def fence(self):
        last = dict(self.last_c)
        pend = list(self.pending_dma.values())
        self.pending_dma = {}
        self._fence_waits = {}
        for a in STREAMS:
            waits = []
            for b in STREAMS:
                if b != a:
                    self._need(a, ("c", b, last[b]), waits)
            for ev in pend:
                self._need(a, ev, waits)
            if waits:
                o = Op(None)
                o.waits = waits
                self.ops[a].append(o)

    def mm(self, out, lhsT, rhs, start=True, stop=True, reads=(), writes=(), **kw):
        return self.op("pe", lambda e: e.matmul(out, lhsT, rhs, start=start, stop=stop, **kw), reads, writes)

    def tr(self, out, in_, ident, reads=(), writes=()):
        return self.op("pe", lambda e: e.transpose(out, in_, ident), reads, writes)

    def act(self, out, in_, func, reads=(), writes=(), **kw):
        return self.op("act", lambda e: e.activation(out=out, in_=in_, func=func, **kw), reads, writes)

    def tt(self, stream, out, in0, in1, op, reads=(), writes=()):
        return self.op(stream, lambda e: e.tensor_tensor(out=out, in0=in0, in1=in1, op=op), reads, writes)

    def ts(self, stream, out, in0, s1, s2, op0, op1=None, reads=(), writes=(), **kw):
        if op1 is None:
            return self.op(stream, lambda e: e.tensor_scalar(out=out, in0=in0, scalar1=s1, scalar2=None, op0=op0, **kw), reads, writes)
        return self.op(stream, lambda e: e.tensor_scalar(out=out, in0=in0, scalar1=s1, scalar2=s2, op0=op0, op1=op1, **kw), reads, writes)

    def stt(self, out, in0, scalar, in1, op0, op1, reads=(), writes=(), **kw):
        return self.op("dve", lambda e: e.scalar_tensor_tensor(out=out, in0=in0, scalar=scalar, in1=in1, op0=op0, op1=op1, **kw), reads, writes)

    def copy(self, stream, out, in_, reads=(), writes=()):
        if stream == "act":
            return self.op("act", lambda e: e.copy(out=out, in_=in_), reads, writes)
        return self.op(stream, lambda e: e.tensor_copy(out=out, in_=in_), reads, writes)

    def memset(self, stream, ap, val, writes=()):
        return self.op(stream, lambda e: e.memset(ap, val), (), writes)

    def emit(self, sems, slot_sems):
        nc = self.nc
        cum = {}
        for s in STREAMS:
            c = 0
            arr = []
            for o in self.ops[s]:
                if o.signal and o.dma is None and o.fn is not None:
                    c += 1
                arr.append(c)
            cum[s] = arr
        final_waits = []
        for ev in self.out_dma_events:
            self._need("sp", ev, final_waits)
        self.n_instr = {s: len(self.ops[s]) for s in STREAMS}

        def run(stream, eng):
            for o in self.ops[stream]:
                for ev in o.waits:
                    if ev[0] == "c":
                        eng.wait_ge(sems[ev[1]], cum[ev[1]][ev[2]])
                    else:
                        eng.wait_ge(slot_sems[ev[1]], ev[2])
                if o.fn is None:
                    continue
                ins = o.fn(eng)
                if self.annotate and o.tag:
                    ins.annotate(o.tag)
                if o.dma is not None:
                    ins.then_inc(slot_sems[o.dma[0]], 16)
                elif o.signal:
                    ins.then_inc(sems[stream], 1)
            if stream == "sp":
                for ev in final_waits:
                    eng.wait_ge(slot_sems[ev[1]], ev[2])

        with nc.Block() as block:
            @block.tensor
            def _(e):
                run("pe", e)

            @block.scalar
            def _(e):
                run("act", e)

            @block.vector
            def _(e):
                run("dve", e)

            @block.gpsimd
            def _(e):
                run("pool", e)

            @block.sync
            def _(e):
                run("sp", e)


class Buf:
    def __init__(self, t, name):
        self.t = t
        self.k = Tk(name)

    def __getitem__(self, key):
        return self.t[key]


def _consts():
    c = {}
    half = 16
    inv = np.power(10000.0, -np.arange(half, dtype=np.float32) / half).astype(np.float32)

    def tabs(pos):
        ang = pos.astype(np.float32)[:, None] * inv[None, :]
        return np.cos(ang).astype(np.float32), np.sin(ang).astype(np.float32)

    posp = np.arange(SEQ)
    poss = 16384 + np.arange(8)
    cp, sp_ = tabs(posp)
    cs, ss = tabs(poss)
    ropeq = np.zeros((32, 2, SEQ + 32), np.float32)
    ropeq[:, 0, :SEQ] = np.concatenate([cp.T, cp.T], 0)
    ropeq[:, 1, :SEQ] = np.concatenate([-sp_.T, sp_.T], 0)
    cs4 = np.tile(cs, (4, 1))
    ss4 = np.tile(ss, (4, 1))
    ropeq[:, 0, SEQ:] = np.concatenate([cs4.T, cs4.T], 0)
    ropeq[:, 1, SEQ:] = np.concatenate([-ss4.T, ss4.T], 0)
    c["ropeq"] = ropeq
    ropek = np.zeros((128, 17, 32), np.float32)
    ropek[:, :16, :16] = cp.reshape(16, 128, 16).transpose(1, 0, 2)
    ropek[:, :16, 16:] = sp_.reshape(16, 128, 16).transpose(1, 0, 2)
    ropek[:32, 16, :16] = cs4
    ropek[:32, 16, 16:] = ss4
    c["ropek"] = ropek
    f = np.zeros((128, 1024), np.float32)
    f[:, 0:128] = np.eye(128)
    f[:, 128:256] = 1.0
    ii = np.arange(64)
    f[:64, 256:320] = (ii[None, :] >= ii[:, None])
    f[:64, 320:384] = -(ii[None, :] > ii[:, None]).astype(np.float32)
    jj = np.arange(128)
    f[:, 384:512] = (jj[:, None] <= jj[None, :])
    t15 = np.arange(15)
    for gi, w in enumerate((2, 4, 8, 16)):
        f[:, 512 + gi * 15: 512 + (gi + 1) * 15] = 1.0 / np.minimum(t15 + 1, w)
    f[:8, 576:584] = (np.arange(8)[:, None] <= np.arange(8)[None, :])
    f[:, 600] = np.arange(128)
    f[:, 601] = np.arange(128) + 5120 * 128
    c["cf"] = f
    return c


_CONST = None


DBG_S = 0
DN_NSUB = None
DN_CUT = 99


def build(sample_only=False, prompt_chunks=NCH, dbg=None, stages=("pool", "mla", "dn"), npool=5120, skip=(), prompt_only=False, annotate=False):
    nc = bass.Bass("TRN2", target_bir_lowering=False)
    es = contextlib.ExitStack()

    def din(name, shape, dt=F32):
        return nc.dram_tensor(name, list(shape), dt, kind="ExternalInput").ap()

    def dout(name, shape, dt=F32):
        return nc.dram_tensor(name, list(shape), dt, kind="ExternalOutput").ap()

    xp = din("xp", [SEQ, D])
    xs = din("xs", [32, D])
    ckv = din("ckv", [DEPTH, npool, 128, 256])
    ckr = din("ckr", [DEPTH, npool, 128, 32])
    spool = din("spool", [DEPTH, 4, 15, 512])
    sconv = din("sconv", [DEPTH, 4, 3, 1536])
    sdelta = din("sdelta", [DEPTH, 4, 8, 64, 64])
    ptab = din("ptab", [4, 128], I32)
    norm_g = din("norm_g", [DEPTH, D])
    w_in = din("w_in", [DEPTH, D, INW])
    pool_mix = din("pool_mix", [DEPTH, 128, 4, 128])
    pool_scale = din("pool_scale", [DEPTH, 128, 4])
    q_norm_g = din("q_norm_g", [DEPTH, 384])
    w_uq = din("w_uq", [DEPTH, 384, H * 128])
    kv_norm_g = din("kv_norm_g", [DEPTH, 256])
    w_ukT = din("w_ukT", [DEPTH, 64, H, 256])
    w_uv = din("w_uv", [DEPTH, 256, H * 64])
    conv_w = din("conv_w", [DEPTH, 64, 24, 4])
    a_log = din("a_log", [DEPTH, H])
    dt_bias = din("dt_bias", [DEPTH, H])
    dn_norm_g = din("dn_norm_g", [DEPTH, 64, 1])
    w_br_pool = din("w_br_pool", [DEPTH, 512, D])
    w_br_mla = din("w_br_mla", [DEPTH, 512, D])
    w_br_dn = din("w_br_dn", [DEPTH, 512, D])
    w_out = din("w_out", [DEPTH, D, D])
    final_norm_g = din("final_norm_g", [D])
    cf_d = din("cf", [128, 1024])
    ropeq_d = din("ropeq", [32, 2, SEQ + 32])
    ropek_d = din("ropek", [128, 17, 32])

    y_p = dout("y_p", [SEQ, D])
    y_s = dout("y_s", [32, D])
    o_pkv = dout("o_pkv", [DEPTH, SEQ, 256])
    o_pkr = dout("o_pkr", [DEPTH, SEQ, 32])
    o_ppool = dout("o_ppool", [DEPTH, 15, 512])
    o_pconv = dout("o_pconv", [DEPTH, 3, 1536])
    o_pdelta = dout("o_pdelta", [DEPTH, H, 64, 64])
    o_skv = dout("o_skv", [DEPTH, 32, 256])
    o_skr = dout("o_skr", [DEPTH, 32, 32])
    o_spool = dout("o_spool", [DEPTH, 4, 15, 512])
    o_sconv = dout("o_sconv", [DEPTH, 4, 3, 1536])
    o_sdelta = dout("o_sdelta", [DEPTH, 4, H, 64, 64])
    dbg_out = {}
    if dbg:
        for name, shape in dbg.items():
            dbg_out[name] = dout("dbg_" + name, shape)

    def sb(name, shape, dt=F32):
        return Buf(es.enter_context(nc.sbuf_tensor(name, list(shape), dt)), name)

    def pstile(name, shape, dt=F32):
        return Buf(es.enter_context(nc.psum_tensor(name, list(shape), dt)), name)

    P = Prog(nc)
    P.annotate = annotate

    x_sb = sb("x_sb", [128, 4, D])
    xnT = sb("xnT", [128, 8, CH], BF16)
    mrg = sb("mrg", [128, 8, CH])
    kTc = [sb(f"kTc{l}", [128, 3, SEQ], BF16) for l in range(DEPTH)]
    Vc = [sb(f"Vc{l}", [128, 16, 256], BF16) for l in range(DEPTH)]
    hist_pool = [sb(f"hpool{l}", [128, 4, 15]) for l in range(DEPTH)]
    hist_conv = [sb(f"hconv{l}", [64, 24, 3]) for l in range(DEPTH)]
    S_p = [sb(f"S_p{l}", [64, H, 64]) for l in range(DEPTH)]
    WA = sb("WA", [128, 8, 672], BF16)
    WB = sb("WB", [128, 8, 512], BF16)
    WBR = sb("WBR", [128, 8 * D], BF16)
    mixw = sb("mixw", [128, 4, 128], BF16)
    cw = sb("cw", [64, 24, 4])
    psc = sb("psc", [128, 4])
    gdn = sb("gdn", [64, 1])
    gn_bc = sb("gn_bc", [128, D])
    a_bc = sb("a_bc", [64, H])
    dtb_bc = sb("dtb_bc", [64, H])
    cf = sb("cf_sb", [128, 1024])
    identb = sb("identb", [128, 128], BF16)
    onesb = sb("onesb", [128, 128], BF16)
    ropek = sb("ropek_sb", [128, 17, 32])
    ptb = sb("ptb", [128, 128], I32)
    ridx = sb("ridx", [128, 128], I32)
    AF_N = 10368
    AB_N = 17408
    arena_f = sb("arena_f", [128, AF_N])
    arena_b = sb("arena_b", [128, AB_N], BF16)
    banks = [pstile(f"psf{i}", [128, 512]) for i in range(7)]
    bankb = pstile("psb", [128, 1024], BF16)

    globals()["_SBUF_LEFT"] = nc.sbuf_bytes_remaining
    sems = {s: es.enter_context(nc.semaphore("sem_" + s)) for s in ("pe", "act", "dve", "pool", "sp")}
    slot_sems = {}
    for s, n in NSLOT.items():
        for i in range(n):
            slot_sems[(s, i)] = es.enter_context(nc.semaphore(f"ds_{s}_{i}"))

    identf = cf[:, 0:128]
    onesf = cf[:, 128:256]
    m_incl = cf[0:64, 256:320]
    m_nstrict = cf[0:64, 320:384]
    m_causal = cf[:, 384:512]
    rc15 = cf[:, 512:572]
    m_causal8 = cf[0:8, 576:584]
    iota_p = cf[:, 600:601]

    st = {"af": 0, "ab": 0, "n": 0, "rot": list(range(7)), "ri": 0}

    class AB:
        pass

    def _arena(ar, key, cap, shape, name, even):
        n = int(np.prod(shape[1:]))
        na = (n + 1) // 2 * 2 if even else n
        off = st[key]
        st[key] = off + na
        assert st[key] <= cap, ("arena overflow", key, name, st[key], cap)
        st["n"] += 1
        b = AB()
        b.k = Tk(name or f"{key}{st['n']}")
        b.shape = list(shape)
        flat = ar.t[0:shape[0], off:off + n]
        b.ap = flat
        sh = shape
        if len(sh) == 2:
            b.v = flat
        elif len(sh) == 3:
            b.v = flat.rearrange("p (a b) -> p a b", b=sh[2])
        elif len(sh) == 4:
            b.v = flat.rearrange("p (a b c) -> p a b c", b=sh[2], c=sh[3])
        else:
            raise ValueError
        return b

    def af(shape, name=None):
        return _arena(arena_f, "af", AF_N, shape, name, False)

    def ab(shape, name=None):
        return _arena(arena_b, "ab", AB_N, shape, name, True)

    def new_stage(reset_b=True):
        P.fence()
        st["af"] = 0
        if reset_b:
            st["ab"] = 0

    def set_rot(lst):
        st["rot"] = list(lst)
        st["ri"] = 0

    def bank():
        b = banks[st["rot"][st["ri"] % len(st["rot"])]]
        st["ri"] += 1
        return b

    def hv(ap, n, t=None):
        return ap.rearrange("p (h t) -> p h t", h=n)

    P.dma("sp", cf[:, :], cf_d, writes=[cf.k])
    P.dma("sp", ropek[:, :, :], ropek_d, writes=[ropek.k])
    P.copy("dve", identb[:, :], cf[:, 0:128], reads=[cf.k], writes=[identb.k])
    P.copy("dve", onesb[:, :], cf[:, 128:256], reads=[cf.k], writes=[onesb.k])
    for l in range(DEPTH):
        P.memset("pool", hist_pool[l][:, :, :], 0.0, writes=[hist_pool[l].k])
        P.memset("pool", hist_conv[l][:, :, :], 0.0, writes=[hist_conv[l].k])
        P.memset("pool", S_p[l][:, :, :], 0.0, writes=[S_p[l].k])

    def dbg_store(name, ap, reads):
        if name in dbg_out:
            P.dma("pool", dbg_out[name], ap, reads=reads, is_output=True)

    w_in_v = [w_in[l].rearrange("(kt p) n -> p kt n", p=128) for l in range(DEPTH)]

    def load_w_in(dst, l, c0, ncol, dcol=0):
        P.dma("pool", dst[:, :, dcol:dcol + ncol], w_in_v[l][:, :, c0:c0 + ncol], writes=[dst.k])

    def fm_proj(ps_ap, Wb, wcol, M, NT, wk, psk):
        for kt in range(8):
            P.mm(ps_ap, Wb[:, kt, wcol:wcol + M], xnT[:, kt, 0:NT], start=(kt == 0), stop=(kt == 7),
                 reads=[wk, xnT.k], writes=[psk])

    def rstd_inplace(a, mult, keys):
        P.ts("dve", a, a, mult, EPS, ALU.mult, ALU.add, reads=keys, writes=keys)
        P.act(a, a, AF.Ln, reads=keys, writes=keys)
        P.act(a, a, AF.Exp, scale=-0.5, reads=keys, writes=keys)

    def stage_norm(cfg, l):
        new_stage()
        P.tag = 'norm'
        NT, TT, NTI = cfg["NT"], cfg["TT"], cfg["NTI"]
        P.dma("sp", gn_bc[:, :], norm_g[l].partition_broadcast(128), writes=[gn_bc.k])
        junk = af([128, D], "junk")
        ssq = af([128, 4], "ssq")
        xn = ab([128, D], "xn")
        for ti in range(NTI):
            xt = x_sb[0:TT, ti, :]
            P.act(junk.ap[0:TT, :], xt, AF.Square, accum_out=ssq.ap[0:TT, ti:ti + 1],
                  reads=[x_sb.k], writes=[junk.k, ssq.k])
            rstd_inplace(ssq.ap[0:TT, ti:ti + 1], 1.0 / D, [ssq.k])
            P.stt(xn.ap[0:TT, :], xt, ssq.ap[0:TT, ti:ti + 1], gn_bc[0:TT, :], ALU.mult, ALU.mult,
                  reads=[x_sb.k, ssq.k, gn_bc.k], writes=[xn.k])
            for kt in range(8):
                P.tr(bankb[:, kt * 128: kt * 128 + TT], xn.ap[0:TT, kt * 128:(kt + 1) * 128], identb[0:TT, 0:TT],
                     reads=[xn.k, identb.k], writes=[bankb.k])
            P.copy("act", xnT[:, :, ti * TT:(ti + 1) * TT], hv(bankb[:, :], 8)[:, :, 0:TT],
                   reads=[bankb.k], writes=[xnT.k])
        P.memset("pool", mrg[:, :, 0:NT], 0.0, writes=[mrg.k])
        dbg_store("xnT", xnT[:, :, 0:NT], [xnT.k])

    def merge_branch(cfg, l, bi, yv, yk, w_br, per_head):
        P.tag = 'merge'
        NT = cfg["NT"]
        if per_head:
            wv = WBR[0:64, :].rearrange("p (h n) -> p h n", h=8)
            P.dma("pool", wv, w_br[l].rearrange("(h p) n -> p h n", p=64), writes=[WBR.k])
        else:
            wv = WBR[:, 0:4 * D].rearrange("p (h n) -> p h n", h=4)
            P.dma("pool", wv, w_br[l].rearrange("(kt p) n -> p kt n", p=128), writes=[WBR.k])
        gs = af([128, CH], "gsig")
        for half in range(2):
            load_w_in(WB, l, C_GATE + bi * D + half * 512, 512)
            for jj in range(4):
                j = half * 4 + jj
                pg = bank()
                fm_proj(pg[:, 0:NT], WB, jj * 128, 128, NT, WB.k, pg.k)
                P.act(gs.ap[:, 0:NT], pg[:, 0:NT], AF.Sigmoid, reads=[pg.k], writes=[gs.k])
                pb = bank()
                nk = 8 if per_head else 4
                for kk in range(nk):
                    P.mm(pb[:, 0:NT], wv[:, kk, j * 128:(j + 1) * 128], yv[:, kk, 0:NT],
                         start=(kk == 0), stop=(kk == nk - 1), reads=[WBR.k, yk], writes=[pb.k])
                P.tt("dve", gs.ap[:, 0:NT], gs.ap[:, 0:NT], pb[:, 0:NT], ALU.mult, reads=[gs.k, pb.k], writes=[gs.k])
                P.tt("pool", mrg[:, j, 0:NT], mrg[:, j, 0:NT], gs.ap[:, 0:NT], ALU.add, reads=[gs.k, mrg.k], writes=[mrg.k])

    def stage_out(cfg, l):
        new_stage()
        P.tag = 'out'
        NT, TT, NTI = cfg["NT"], cfg["TT"], cfg["NTI"]
        dbg_store(f"mrg{l}", mrg[:, :, 0:NT], [mrg.k])
        mb = ab([128, 8, NT], "mrgb")
        P.copy("dve", mb.v, mrg[:, :, 0:NT], reads=[mrg.k], writes=[mb.k])
        wo = w_out[l].rearrange("(kt p) n -> p kt n", p=128)
        for half in range(2):
            P.dma("pool", WB[:, :, :], wo[:, :, half * 512:(half + 1) * 512], writes=[WB.k])
            for ti in range(NTI):
                pb = bank()
                for kt in range(8):
                    P.mm(pb[0:TT, :], mb.v[:, kt, ti * TT:(ti + 1) * TT], WB[:, kt, :], start=(kt == 0), stop=(kt == 7),
                         reads=[mb.k, WB.k], writes=[pb.k])
                xsl = x_sb[0:TT, ti, half * 512:(half + 1) * 512]
                P.tt("dve", xsl, xsl, pb[0:TT, :], ALU.add, reads=[pb.k, x_sb.k], writes=[x_sb.k])

    def stage_pool(cfg, l):
        new_stage()
        P.tag = 'pool'
        NT, B, Ls, prompt, ck = cfg["NT"], cfg["B"], cfg["Ls"], cfg["prompt"], cfg["ck"]
        W = 15 + Ls
        load_w_in(WA, l, C_POOL, 512)
        load_w_in(WB, l, C_ZPOOL, 512)
        P.dma("pool", mixw[:, :, :], pool_mix[l], writes=[mixw.k])
        P.dma("sp", psc[:, :], pool_scale[l], writes=[psc.k])
        ext = af([128, 4, B, W], "ext")
        if prompt:
            P.copy("pool", ext.v[:, :, 0, 0:15], hist_pool[l][:, :, :], reads=[hist_pool[l].k], writes=[ext.k])
        else:
            stg = af([15, 4 * 512], "stg")
            for b in range(4):
                P.dma("sp", stg.ap[:, b * 512:(b + 1) * 512], spool[l, b], writes=[stg.k])
            pt = bank()
            for b in range(4):
                for g in range(4):
                    P.tr(pt[:, (b * 4 + g) * 15:(b * 4 + g + 1) * 15], stg.ap[0:15, b * 512 + g * 128: b * 512 + (g + 1) * 128],
                         identf[0:15, 0:15], reads=[stg.k, cf.k], writes=[pt.k])
            P.copy("act", ext.v[:, :, :, 0:15], pt[:, 0:240].rearrange("p (b g t) -> p g b t", b=4, g=4),
                   reads=[pt.k], writes=[ext.k])
        for g in range(4):
            pu = bank()
            fm_proj(pu[:, 0:NT], WA, g * 128, 128, NT, WA.k, pu.k)
            P.copy("act", ext.v[:, g, :, 15:W], pu[:, 0:NT].rearrange("p (b t) -> p b t", b=B), reads=[pu.k], writes=[ext.k])
        if prompt:
            P.copy("pool", hist_pool[l][:, :, :], ext.v[:, :, 0, Ls:Ls + 15], reads=[ext.k], writes=[hist_pool[l].k])
        if (not prompt) or ck == NCH - 1:
            ostg = af([15, 512], "ostg")
            for b in range(B):
                pt = bank()
                for g in range(4):
                    P.tr(pt[0:15, g * 128:(g + 1) * 128], ext.v[:, g, b, Ls:Ls + 15], identf[:, :],
                         reads=[ext.k, cf.k], writes=[pt.k])
                P.copy("act", ostg.ap[0:15, :], pt[0:15, 0:512], reads=[pt.k], writes=[ostg.k])
                dst = o_ppool[l] if prompt else o_spool[l, b]
                P.dma("sp", dst, ostg.ap[0:15, :], reads=[ostg.k], is_output=True)
        wa = af([128, B, W], "wa")
        wb_ = af([128, B, W], "wb")
        dT = ab([128, 4, B, Ls], "dT")
        ya = ab([128, 4, NT], "ya")
        zs = af([128, CH], "zs")
        fx = af([128, 15], "fx")
        for g, wdw in enumerate((2, 4, 8, 16)):
            cur, curk, n = ext.v[:, g, :, :], ext.k, W
            sh = 1
            bufs = [wa, wb_]
            bi = 0
            while sh < wdw:
                o = bufs[bi]
                P.tt("pool", o.v[:, :, 0:n - sh], cur[:, :, sh:n], cur[:, :, 0:n - sh], ALU.add, reads=[curk], writes=[o.k])
                cur, curk, n = o.v, o.k, n - sh
                sh *= 2
                bi ^= 1
            o0 = n - Ls
            P.stt(dT.v[:, g, :, :], cur[:, :, o0:o0 + Ls], 1.0 / wdw, ext.v[:, g, :, 15:W], ALU.mult, ALU.subtract,
                  reads=[curk, ext.k], writes=[dT.k])
            if prompt and ck == 0:
                P.tt("pool", fx.ap, cur[:, 0, o0:o0 + 15], rc15[:, g * 15:(g + 1) * 15], ALU.mult,
                     reads=[curk, cf.k], writes=[fx.k])
                P.tt("dve", dT.v[:, g, 0, 0:15], fx.ap, ext.v[:, g, 0, 15:30], ALU.subtract,
                     reads=[fx.k, ext.k, dT.k], writes=[dT.k])
        for g in range(4):
            p1 = bank()
            P.mm(p1[:, 0:NT], mixw[:, g, :], dT.ap[:, g * NT:(g + 1) * NT], reads=[mixw.k, dT.k], writes=[p1.k])
            p2 = bank()
            fm_proj(p2[:, 0:NT], WB, g * 128, 128, NT, WB.k, p2.k)
            P.act(zs.ap[:, 0:NT], p2[:, 0:NT], AF.Silu, reads=[p2.k], writes=[zs.k])
            P.stt(ya.v[:, g, 0:NT], p1[:, 0:NT], psc[:, g:g + 1], zs.ap[:, 0:NT], ALU.mult, ALU.mult,
                  reads=[p1.k, psc.k, zs.k], writes=[ya.k])
        dbg_store(f"ya{l}", ya.v, [ya.k])
        merge_branch(cfg, l, 0, ya.v, ya.k, w_br_pool, per_head=False)

    def stage_mla(cfg, l):
        new_stage()
        P.tag = 'mla.pre'
        NT, TT, NTI, B, Ls, prompt, ck = cfg["NT"], cfg["TT"], cfg["NTI"], cfg["B"], cfg["Ls"], cfg["prompt"], cfg["ck"]
        tok0 = ck * CH if prompt else 0
        wuq = ab([128, 3, H * 128], "wuq")
        wuk = ab([64, H, 256], "wuk")
        wuv = ab([128, 2, H * 64], "wuv")
        ropeq = af([32, 2, NT], "ropeq")
        gq_bc = af([128, 384], "gq_bc")
        gkv_bc = af([128, 256], "gkv_bc")
        load_w_in(WA, l, C_Q, 672)
        load_w_in(WB, l, C_ZMLA, 512)
        P.dma("pool", wuq.v[:, :, :], w_uq[l].rearrange("(kt p) n -> p kt n", p=128), writes=[wuq.k])
        P.dma("pool", wuk.v[:, :, :], w_ukT[l], writes=[wuk.k])
        P.dma("pool", wuv.v[:, :, :], w_uv[l].rearrange("(kt p) n -> p kt n", p=128), writes=[wuv.k])
        P.dma("sp", gq_bc.v[:, :], q_norm_g[l].partition_broadcast(128), writes=[gq_bc.k])
        P.dma("sp", gkv_bc.v[:, :], kv_norm_g[l].partition_broadcast(128), writes=[gkv_bc.k])
        rq0 = tok0 if prompt else SEQ
        P.dma("sp", ropeq.v[:, :, 0:NT], ropeq_d[:, :, rq0:rq0 + NT], writes=[ropeq.k])
        yb = ab([64, 8, NT], "yb")
        for h in range(8):
            pz = bank()
            fm_proj(pz[0:64, 0:NT], WB, h * 64, 64, NT, WB.k, pz.k)
            P.act(yb.v[:, h, 0:NT], pz[0:64, 0:NT], AF.Silu, reads=[pz.k], writes=[yb.k])
        ckvf = af([128, 288], "ckvf")
        ssq = af([128, 2], "ssq2")
        junk = af([128, 384], "junk2")
        t1 = af([128, 64], "ropetmp")
        qr = af([32, 2, 8, TT], "qr")
        rden = af([128, 4 * TT], "rden")
        ckvb = ab([128, 288], "ckvb")
        cqb = ab([128, 384], "cqb")
        cqT = ab([128, 3, TT], "cqT")
        qn = ab([64, 8, TT], "qn")
        qrT = ab([32, 8, TT], "qrT")
        qlT = ab([128, 2, 8, TT], "qlT")
        if prompt:
            pT = [ab([128, 4, TT], f"pT{i}") for i in range(2)]
        else:
            knT = ab([128, 3, 32], "knT")
            vn = ab([32, 256], "vn")
            pg_b = [ab([128, 288], f"pgb{i}") for i in range(3)]
            pg_T = [ab([128, 3, 128], f"pgT{i}") for i in range(2)]
            pts = [ab([128, 64], f"pts{i}") for i in range(2)]
            vb8 = ab([8, 256], "vb8")
            qc = ab([128, 2, 64], "qc")
            qrc = ab([32, 64], "qrc")
            ols = ab([128, 2, 64], "ols")
        for ti in range(NTI):
            set_rot(range(7))
            ktile = (tok0 // 128 + ti) if prompt else 16
            tsl = slice(ti * TT, (ti + 1) * TT)
            P.tag = 'mla.proj'
            pk = bank()
            for kt in range(8):
                P.mm(pk[0:TT, 0:288], xnT[:, kt, tsl], WA[:, kt, 384:672], start=(kt == 0), stop=(kt == 7),
                     reads=[xnT.k, WA.k], writes=[pk.k])
            P.act(junk.ap[0:TT, 0:256], pk[0:TT, 0:256], AF.Square, accum_out=ssq.ap[0:TT, 0:1],
                  reads=[pk.k], writes=[junk.k, ssq.k])
            rstd_inplace(ssq.ap[0:TT, 0:1], 1.0 / 256, [ssq.k])
            P.stt(ckvf.ap[0:TT, 0:256], pk[0:TT, 0:256], ssq.ap[0:TT, 0:1], gkv_bc.v[0:TT, :], ALU.mult, ALU.mult,
                  reads=[pk.k, ssq.k, gkv_bc.k], writes=[ckvf.k])
            cosk = ropek[0:TT, ktile, 0:16]
            sink = ropek[0:TT, ktile, 16:32]
            P.tt("dve", t1.ap[0:TT, 0:16], pk[0:TT, 256:272], cosk, ALU.mult, reads=[pk.k, ropek.k], writes=[t1.k])
            P.tt("dve", t1.ap[0:TT, 16:32], pk[0:TT, 272:288], sink, ALU.mult, reads=[pk.k, ropek.k], writes=[t1.k])
            P.tt("dve", t1.ap[0:TT, 32:48], pk[0:TT, 272:288], cosk, ALU.mult, reads=[pk.k, ropek.k], writes=[t1.k])
            P.tt("dve", t1.ap[0:TT, 48:64], pk[0:TT, 256:272], sink, ALU.mult, reads=[pk.k, ropek.k], writes=[t1.k])
            P.tt("dve", ckvf.ap[0:TT, 256:272], t1.ap[0:TT, 0:16], t1.ap[0:TT, 16:32], ALU.subtract, reads=[t1.k], writes=[ckvf.k])
            P.tt("dve", ckvf.ap[0:TT, 272:288], t1.ap[0:TT, 32:48], t1.ap[0:TT, 48:64], ALU.add, reads=[t1.k], writes=[ckvf.k])
            if prompt:
                r0 = tok0 + ti * 128
                P.dma("sp", o_pkv[l, r0:r0 + 128, :], ckvf.ap[0:128, 0:256], reads=[ckvf.k], is_output=True)
                P.dma("sp", o_pkr[l, r0:r0 + 128, :], ckvf.ap[0:128, 256:288], reads=[ckvf.k], is_output=True)
            else:
                P.dma("sp", o_skv[l], ckvf.ap[0:32, 0:256], reads=[ckvf.k], is_output=True)
                P.dma("sp", o_skr[l], ckvf.ap[0:32, 256:288], reads=[ckvf.k], is_output=True)
            P.copy("pool", ckvb.ap[0:TT, :], ckvf.ap[0:TT, :], reads=[ckvf.k], writes=[ckvb.k])
            for j, (c0, cn) in enumerate(((0, 128), (128, 128), (256, 32))):
                P.tr(bankb[0:cn, j * 128: j * 128 + TT], ckvb.ap[0:TT, c0:c0 + cn], identb[0:TT, 0:TT],
                     reads=[ckvb.k, identb.k], writes=[bankb.k])
            if prompt:
                P.copy("pool", Vc[l][:, ktile, :], ckvb.ap[:, 0:256], reads=[ckvb.k], writes=[Vc[l].k])
                P.copy("act", kTc[l][:, 0:2, ktile * 128:(ktile + 1) * 128], hv(bankb[:, 0:256], 2),
                       reads=[bankb.k], writes=[kTc[l].k])
                P.copy("act", kTc[l][0:32, 2, ktile * 128:(ktile + 1) * 128], bankb[0:32, 256:384],
                       reads=[bankb.k], writes=[kTc[l].k])
            else:
                P.copy("pool", vn.ap[0:32, :], ckvb.ap[0:32, 0:256], reads=[ckvb.k], writes=[vn.k])
                P.copy("act", knT.v[:, 0:2, :], hv(bankb[:, 0:256], 2)[:, :, 0:32], reads=[bankb.k], writes=[knT.k])
                P.copy("act", knT.v[0:32, 2, :], bankb[0:32, 256:288], reads=[bankb.k], writes=[knT.k])
            pq = bank()
            for kt in range(8):
                P.mm(pq[0:TT, 0:384], xnT[:, kt, tsl], WA[:, kt, 0:384], start=(kt == 0), stop=(kt == 7),
                     reads=[xnT.k, WA.k], writes=[pq.k])
            P.act(junk.ap[0:TT, 0:384], pq[0:TT, 0:384], AF.Square, accum_out=ssq.ap[0:TT, 1:2],
                  reads=[pq.k], writes=[junk.k, ssq.k])
            rstd_inplace(ssq.ap[0:TT, 1:2], 1.0 / 384, [ssq.k])
            P.stt(cqb.ap[0:TT, :], pq[0:TT, 0:384], ssq.ap[0:TT, 1:2], gq_bc.v[0:TT, :], ALU.mult, ALU.mult,
                  reads=[pq.k, ssq.k, gq_bc.k], writes=[cqb.k])
            for j in range(3):
                P.tr(bankb[:, 384 + j * 128: 384 + j * 128 + TT], cqb.ap[0:TT, j * 128:(j + 1) * 128], identb[0:TT, 0:TT],
                     reads=[cqb.k, identb.k], writes=[bankb.k])
            P.copy("act", cqT.v[:, :, 0:TT], hv(bankb[:, 384:768], 3)[:, :, 0:TT], reads=[bankb.k], writes=[cqT.k])
            for hg in range(2):
                pn = bank()
                for hh in range(4):
                    h = hg * 4 + hh
                    for j in range(3):
                        P.mm(pn[0:64, hh * 128: hh * 128 + TT], wuq.v[:, j, h * 128: h * 128 + 64], cqT.v[:, j, 0:TT],
                             start=(j == 0), stop=(j == 2), reads=[wuq.k, cqT.k], writes=[pn.k])
                P.copy("act", qn.v[:, hg * 4:(hg + 1) * 4, :], hv(pn[0:64, :], 4)[:, :, 0:TT], reads=[pn.k], writes=[qn.k])
            for v in range(2):
                for hg in range(2):
                    pr = bank()
                    for hh in range(4):
                        h = hg * 4 + hh
                        c0 = h * 128 + 64 + v * 32
                        for j in range(3):
                            P.mm(pr[0:32, hh * 128: hh * 128 + TT], wuq.v[:, j, c0:c0 + 32],
                                 cqT.v[:, j, 0:TT], start=(j == 0), stop=(j == 2), reads=[wuq.k, cqT.k], writes=[pr.k])
                    tab = ropeq.v[:, v, tsl]
                    P.tt("dve", qr.v[:, v, hg * 4:(hg + 1) * 4, :], hv(pr[0:32, :], 4)[:, :, 0:TT],
                         tab.unsqueeze(1).to_broadcast([32, 4, TT]), ALU.mult, reads=[pr.k, ropeq.k], writes=[qr.k])
            P.tt("pool", qrT.v, qr.v[:, 0, :, :], qr.v[:, 1, :, :], ALU.add, reads=[qr.k], writes=[qrT.k])
            for j in range(2):
                for hg in range(2):
                    pl = bank()
                    for hh in range(4):
                        h = hg * 4 + hh
                        P.mm(pl[:, hh * 128: hh * 128 + TT], wuk.v[:, h, j * 128:(j + 1) * 128], qn.v[:, h, :],
                             reads=[wuk.k, qn.k], writes=[pl.k])
                    P.copy("act" if hg == 0 else "dve", qlT.v[:, j, hg * 4:(hg + 1) * 4, :], hv(pl[:, :], 4)[:, :, 0:TT],
                           reads=[pl.k], writes=[qlT.k])
            dbg_store(f"qlT{l}", qlT.v, [qlT.k])
            dbg_store(f"qrT{l}", qrT.v, [qrT.k])
            P.tag = 'mla.attn'
            po = [banks[0], banks[1]]
            pd = banks[2]
            set_rot([3, 4, 5, 6])
            if prompt:
                nkt = ktile + 1
                for hg in range(2):
                    qsl = slice(hg * 4, (hg + 1) * 4)
                    def att_S(kt):
                        pscr = bank()
                        ksl = slice(kt * 128, (kt + 1) * 128)
                        P.mm(pscr[:, :], kTc[l][:, 0, ksl], qlT.v[:, 0, qsl, :], start=True, stop=False,
                             reads=[kTc[l].k, qlT.k], writes=[pscr.k])
                        P.mm(pscr[:, :], kTc[l][:, 1, ksl], qlT.v[:, 1, qsl, :], start=False, stop=False,
                             reads=[kTc[l].k, qlT.k], writes=[pscr.k])
                        P.mm(pscr[:, :], kTc[l][0:32, 2, ksl], qrT.v[0:32, qsl, :], start=False, stop=True,
                             reads=[kTc[l].k, qrT.k], writes=[pscr.k])
                        pt_ = pT[kt % 2]
                        P.act(pt_.ap[:, :], pscr[:, :], AF.Exp, scale=MLA_SCALE, reads=[pscr.k], writes=[pt_.k])
                        if kt == ktile:
                            P.tt("pool", pt_.v, pt_.v, m_causal.unsqueeze(1).to_broadcast([128, 4, 128]), ALU.mult,
                                 reads=[cf.k, pt_.k], writes=[pt_.k])

                    def att_PV(kt):
                        pt_ = pT[kt % 2]
                        for j in range(2):
                            P.mm(po[j][:, :], Vc[l][:, kt, j * 128:(j + 1) * 128], pt_.ap[:, :], start=(kt == 0), stop=(kt == nkt - 1),
                                 reads=[Vc[l].k, pt_.k], writes=[po[j].k])
                        P.mm(pd[:, :], onesb[:, :], pt_.ap[:, :], start=(kt == 0), stop=(kt == nkt - 1),
                             reads=[onesb.k, pt_.k], writes=[pd.k])

                    att_S(0)
                    for kt in range(nkt):
                        if kt + 1 < nkt:
                            att_S(kt + 1)
                        att_PV(kt)
                    P.act(rden.ap[:, :], pd[:, :], AF.Ln, reads=[pd.k], writes=[rden.k])
                    P.act(rden.ap[:, :], rden.ap[:, :], AF.Exp, scale=-1.0, reads=[rden.k], writes=[rden.k])
                    for j in range(2):
                        P.tt("dve", qlT.v[:, j, qsl, :], hv(po[j][:, :], 4), hv(rden.ap[:, :], 4), ALU.mult,
                             reads=[po[j].k, rden.k, qlT.k], writes=[qlT.k])
                for hg in range(2):
                    pm = bank()
                    for hh in range(4):
                        h = hg * 4 + hh
                        for j in range(2):
                            P.mm(pm[0:64, hh * 128:(hh + 1) * 128], wuv.v[:, j, h * 64:(h + 1) * 64], qlT.v[:, j, h, :],
                                 start=(j == 0), stop=(j == 1), reads=[wuv.k, qlT.k], writes=[pm.k])
                    P.tt("dve", yb.v[:, hg * 4:(hg + 1) * 4, tsl], hv(pm[0:64, :], 4), yb.v[:, hg * 4:(hg + 1) * 4, tsl], ALU.mult,
                         reads=[pm.k, yb.k], writes=[yb.k])
            else:
                ckv_rows = ckv.rearrange("l n t c -> (l n t) c")
                ckr_rows = ckr.rearrange("l n t c -> (l n t) c")
                for b in range(4):
                    P.dma("sp", ptb[:, :], ptab[b].partition_broadcast(128), writes=[ptb.k])
                    P.ts("dve", ridx[:, :], ptb[:, :], 128.0, cf[:, 600 + l:601 + l], ALU.mult, ALU.add, reads=[ptb.k, cf.k], writes=[ridx.k])
                    P.copy("dve", qc.v.rearrange("p j (h t) -> p j h t", t=8), qlT.v[:, :, :, b * 8:(b + 1) * 8],
                           reads=[qlT.k], writes=[qc.k])
                    P.copy("dve", qrc.ap.rearrange("p (h t) -> p h t", t=8), qrT.v[:, :, b * 8:(b + 1) * 8],
                           reads=[qrT.k], writes=[qrc.k])
                    for page in range(NPAGE + 1):
                        pscr = bank()
                        ptsb = pts[page % 2]
                        first = (page == 0)
                        last = (page == NPAGE)
                        if page < NPAGE:
                            bb = pg_b[page % 3]
                            tb = pg_T[page % 2]
                            P.idma(bb.ap[:, 0:256], ckv_rows, ridx[:, page:page + 1], reads=[ridx.k], writes=[bb.k])
                            P.idma(bb.ap[:, 256:288], ckr_rows, ridx[:, page:page + 1], reads=[ridx.k], writes=[bb.k])
                            for j, (c0, cn) in enumerate(((0, 128), (128, 128), (256, 32))):
                                P.tr(bankb[0:cn, j * 128:(j + 1) * 128], bb.ap[:, c0:c0 + cn], identb[:, :],
                                     reads=[bb.k, identb.k], writes=[bankb.k])
                            P.copy("dve", tb.v[:, 0:2, :], hv(bankb[:, 0:256], 2), reads=[bankb.k], writes=[tb.k])
                            P.copy("dve", tb.v[0:32, 2, :], bankb[0:32, 256:384], reads=[bankb.k], writes=[tb.k])
                            kk = 128
                            k0, k1, k2 = tb.v[:, 0, :], tb.v[:, 1, :], tb.v[0:32, 2, :]
                            kdeps = [tb.k]
                            vsrc, vdeps = bb.ap, [bb.k]
                        else:
                            kk = 8
                            k0, k1, k2 = knT.v[:, 0, b * 8:(b + 1) * 8], knT.v[:, 1, b * 8:(b + 1) * 8], knT.v[0:32, 2, b * 8:(b + 1) * 8]
                            kdeps = [knT.k]
                            P.dma("sp", vb8.ap[0:8, :], vn.ap[b * 8:(b + 1) * 8, :], reads=[vn.k], writes=[vb8.k])
                            vsrc, vdeps = vb8.ap, [vb8.k]
                        P.mm(pscr[0:kk, 0:64], k0, qc.v[:, 0, :], start=True, stop=False, reads=kdeps + [qc.k], writes=[pscr.k])
                        P.mm(pscr[0:kk, 0:64], k1, qc.v[:, 1, :], start=False, stop=False, reads=kdeps + [qc.k], writes=[pscr.k])
                        P.mm(pscr[0:kk, 0:64], k2, qrc.ap[0:32, :], start=False, stop=True, reads=kdeps + [qrc.k], writes=[pscr.k])
                        P.act(ptsb.ap[0:kk, :], pscr[0:kk, 0:64], AF.Exp, scale=MLA_SCALE, reads=[pscr.k], writes=[ptsb.k])
                        if last:
                            P.tt("pool", hv(ptsb.ap[0:8, :], 8), hv(ptsb.ap[0:8, :], 8),
                                 m_causal8.unsqueeze(1).to_broadcast([8, 8, 8]), ALU.mult, reads=[cf.k, ptsb.k], writes=[ptsb.k])
                        for j in range(2):
                            P.mm(po[j][:, 0:64], vsrc[0:kk, j * 128:(j + 1) * 128], ptsb.ap[0:kk, :], start=first, stop=last,
                                 reads=vdeps + [ptsb.k], writes=[po[j].k])
                        P.mm(pd[:, 0:64], onesb[0:kk, :], ptsb.ap[0:kk, :], start=first, stop=last,
                             reads=[onesb.k, ptsb.k], writes=[pd.k])
                    P.act(rden.ap[:, 0:64], pd[:, 0:64], AF.Ln, reads=[pd.k], writes=[rden.k])
                    P.act(rden.ap[:, 0:64], rden.ap[:, 0:64], AF.Exp, scale=-1.0, reads=[rden.k], writes=[rden.k])
                    for j in range(2):
                        P.tt("dve", ols.v[:, j, :], po[j][:, 0:64], rden.ap[:, 0:64], ALU.mult,
                             reads=[po[j].k, rden.k], writes=[ols.k])
                    pm = bank()
                    for h in range(8):
                        for j in range(2):
                            P.mm(pm[0:64, h * 8:(h + 1) * 8], wuv.v[:, j, h * 64:(h + 1) * 64], ols.v[:, j, h * 8:(h + 1) * 8],
                                 start=(j == 0), stop=(j == 1), reads=[wuv.k, ols.k], writes=[pm.k])
                    P.tt("dve", yb.v[:, :, b * 8:(b + 1) * 8], hv(pm[0:64, 0:64], 8), yb.v[:, :, b * 8:(b + 1) * 8], ALU.mult,
                         reads=[pm.k, yb.k], writes=[yb.k])
        set_rot(range(7))
        dbg_store(f"yb{l}", yb.v, [yb.k])
        merge_branch(cfg, l, 1, yb.v, yb.k, w_br_mla, per_head=True)

    def stage_dn(cfg, l):
        new_stage()
        P.tag = 'dn.d1'
        NT, B, Ls, prompt, ck, C = cfg["NT"], cfg["B"], cfg["Ls"], cfg["prompt"], cfg["ck"], cfg["C"]
        NSUB = NT // C
        LV = int(np.log2(C))
        W = 3 + Ls
        HG = 8
        NHG = 8 // HG
        HW_ = HG * 64
        do_out = (not prompt) or ck == NCH - 1
        P.dma("sp", cw[:, :, :], conv_w[l], writes=[cw.k])
        P.dma("sp", gdn[:, :], dn_norm_g[l], writes=[gdn.k])
        P.dma("sp", a_bc[:, :], a_log[l].partition_broadcast(64), writes=[a_bc.k])
        P.dma("sp", dtb_bc[:, :], dt_bias[l].partition_broadcast(64), writes=[dtb_bc.k])
        nega = af([64, 8], "nega")
        P.act(nega.ap, a_bc[:, :], AF.Exp, reads=[a_bc.k], writes=[nega.k])
        P.ts("dve", nega.ap, nega.ap, -1.0, None, ALU.mult, reads=[nega.k], writes=[nega.k])
        yc = ab([64, 8, NT], "yc")
        extc2 = [af([64, B, W], f"extc{i}") for i in range(2)]
        if not prompt:
            hs = af([64, 24, 4, 3], "hs")
            stgc = af([3, 1536], "stgc")
            for b in range(4):
                P.dma("sp", stgc.ap[0:3, :], sconv[l, b], writes=[stgc.k])
                pt = bank()
                for ht in range(24):
                    P.tr(pt[0:64, ht * 3:(ht + 1) * 3], stgc.ap[0:3, ht * 64:(ht + 1) * 64], identf[0:3, 0:3],
                         reads=[stgc.k, cf.k], writes=[pt.k])
                P.copy("act", hs.v[:, :, b, :], hv(pt[0:64, 0:72], 24), reads=[pt.k], writes=[hs.k])
        ost = [af([3, 64], f"ost{i}") for i in range(2)]
        qkvb = [ab([64, HG, NT], f"qkvb{i}") for i in range(3)]
        cacc2 = [af([64, NT], f"cacc{i}") for i in range(2)]
        sq2 = [af([64, NT], f"sq{i}") for i in range(2)]
        names = ["Gb", "dgb", "E", "E1", "kbg", "qg", "Q0", "qkT", "P0", "TT", "Qb", "Pb"]
        tmp = {n: af([64, HG, 64], n) for n in names}
        tmp["vb"] = tmp["Gb"]
        tmp["kd"] = tmp["dgb"]
        tmp["R"] = tmp["E1"]
        tmp["vnw"] = tmp["Qb"]
        tmp["osq"] = tmp["Q0"]
        kbT = ab([64, HG, 64], "kbT")
        beta = af([64, HG], "beta")
        gg = af([64, HG], "gg")
        gc = af([64, HG], "gc")
        elast = af([64, HG], "elast")
        edl = af([64, HG], "edl")
        Ssm = af([64, HG, 64], "Ssm") if not prompt else None
        n_ost = 0
        for hg in range(NHG):
            hsl = slice(hg * HG, (hg + 1) * HG)
            P.tag = 'dn.d1'
            for which in range(3):
                load_w_in(WA, l, C_QKV + which * 512 + hg * HW_, HW_)
                for hh in range(HG):
                    h = hg * HG + hh
                    ht = which * 8 + h
                    extc, cacc, sq = extc2[hh % 2], cacc2[hh % 2], sq2[hh % 2]
                    pp = bank()
                    fm_proj(pp[0:64, 0:NT], WA, hh * 64, 64, NT, WA.k, pp.k)
                    if prompt:
                        P.copy("pool", extc.v[:, 0, 0:3], hist_conv[l][:, ht, :], reads=[hist_conv[l].k], writes=[extc.k])
                    else:
                        P.copy("pool", extc.v[:, :, 0:3], hs.v[:, ht, :, :], reads=[hs.k], writes=[extc.k])
                    P.copy("act", extc.v[:, :, 3:W], pp[0:64, 0:NT].rearrange("p (b t) -> p b t", b=B), reads=[pp.k], writes=[extc.k])
                    if prompt:
                        P.copy("pool", hist_conv[l][:, ht, :], extc.v[:, 0, Ls:Ls + 3], reads=[extc.k], writes=[hist_conv[l].k])
                    if do_out:
                        for b in range(B):
                            pt = bank()
                            P.tr(pt[0:3, 0:64], extc.v[:, b, Ls:Ls + 3], identf[0:64, 0:64], reads=[extc.k, cf.k], writes=[pt.k])
                            o_ = ost[n_ost % 2]
                            n_ost += 1
                            P.copy("act", o_.ap[0:3, :], pt[0:3, 0:64], reads=[pt.k], writes=[o_.k])
                            dst = o_pconv[l] if prompt else o_sconv[l, b]
                            P.dma("sp", dst[:, ht * 64:(ht + 1) * 64], o_.ap[0:3, :], reads=[o_.k], is_output=True)
                    caccv = cacc.ap[:, 0:NT].rearrange("p (b t) -> p b t", b=B)
                    P.ts("dve", caccv, extc.v[:, :, 0:Ls], cw[:, ht, 0:1], None, ALU.mult, reads=[extc.k, cw.k], writes=[cacc.k])
                    for j in range(1, 4):
                        P.stt(caccv, extc.v[:, :, j:j + Ls], cw[:, ht, j:j + 1], caccv, ALU.mult, ALU.add,
                              reads=[extc.k, cw.k, cacc.k], writes=[cacc.k])
                    if which == 2:
                        P.act(qkvb[2].v[:, hh, :], cacc.ap[:, 0:NT], AF.Silu, reads=[cacc.k], writes=[qkvb[2].k])
                    else:
                        P.act(cacc.ap[:, 0:NT], cacc.ap[:, 0:NT], AF.Silu, reads=[cacc.k], writes=[cacc.k])
                        P.act(sq.ap[:, 0:NT], cacc.ap[:, 0:NT], AF.Square, reads=[cacc.k], writes=[sq.k])
                        pss = bank()
                        P.mm(pss[0:64, 0:NT], onesf[0:64, 0:64], sq.ap[:, 0:NT], reads=[cf.k, sq.k], writes=[pss.k])
                        P.ts("dve", sq.ap[:, 0:NT], pss[0:64, 0:NT], EPS, None, ALU.add, reads=[pss.k], writes=[sq.k])
                        P.act(sq.ap[:, 0:NT], sq.ap[:, 0:NT], AF.Ln, reads=[sq.k], writes=[sq.k])
                        P.act(sq.ap[:, 0:NT], sq.ap[:, 0:NT], AF.Exp, scale=-0.5, reads=[sq.k], writes=[sq.k])
                        if l == 0 and hg == 0 and which == 1 and hh == 0:
                            dbg_store("rs", sq.ap[:, 0:NT], [sq.k])
                            dbg_store("cs", cacc.ap[:, 0:NT], [cacc.k])
                        if which == 0:
                            P.stt(qkvb[0].v[:, hh, :], cacc.ap[:, 0:NT], 0.125, sq.ap[:, 0:NT], ALU.mult, ALU.mult,
                                  reads=[cacc.k, sq.k], writes=[qkvb[0].k])
                        else:
                            P.tt("dve", qkvb[1].v[:, hh, :], cacc.ap[:, 0:NT], sq.ap[:, 0:NT], ALU.mult,
                                 reads=[cacc.k, sq.k], writes=[qkvb[1].k])
            load_w_in(WB, l, C_ZDN + hg * HW_, HW_)
            load_w_in(WA, l, C_BETA, 16, dcol=0)
            for hh in range(HG):
                pz = bank()
                fm_proj(pz[0:64, 0:NT], WB, hh * 64, 64, NT, WB.k, pz.k)
                P.act(yc.v[:, hg * HG + hh, :], pz[0:64, 0:NT], AF.Silu, reads=[pz.k], writes=[yc.k])
            Gb, dgb, E, E1, kbg, qg = (tmp[n] for n in ("Gb", "dgb", "E", "E1", "kbg", "qg"))
            Q0, qkT, P0, TT_, Qb, Pb = (tmp[n] for n in ("Q0", "qkT", "P0", "TT", "Qb", "Pb"))
            vb, kd, R, vnw, osq = (tmp[n] for n in ("vb", "kd", "R", "vnw", "osq"))
            for s in range(NSUB if DN_NSUB is None else DN_NSUB):
                cs = slice(s * C, (s + 1) * C)
                bseq = s
                if not prompt:
                    P.dma("sp", Ssm.v, sdelta[l, bseq, hsl].rearrange("h k v -> k h v"), writes=[Ssm.k])
                    Sv, Sk = Ssm.v, Ssm.k
                else:
                    Sv, Sk = S_p[l][:, hsl, :], S_p[l].k
                P.tag = 'dn.pre'
                pbg = bank()
                for kt in range(8):
                    P.mm(pbg[0:C, 0:16], xnT[:, kt, cs], WA[:, kt, 0:16], start=(kt == 0), stop=(kt == 7),
                         reads=[xnT.k, WA.k], writes=[pbg.k])
                P.act(beta.ap[0:C, :], pbg[0:C, hg * HG:(hg + 1) * HG], AF.Sigmoid, reads=[pbg.k], writes=[beta.k])
                P.tt("dve", gg.ap[0:C, :], pbg[0:C, 8 + hg * HG: 8 + (hg + 1) * HG], dtb_bc[0:C, hsl], ALU.add, reads=[pbg.k, dtb_bc.k], writes=[gg.k])
                P.act(gg.ap[0:C, :], gg.ap[0:C, :], AF.Exp, reads=[gg.k], writes=[gg.k])
                P.ts("dve", gg.ap[0:C, :], gg.ap[0:C, :], 1.0, None, ALU.add, reads=[gg.k], writes=[gg.k])
                P.act(gg.ap[0:C, :], gg.ap[0:C, :], AF.Ln, reads=[gg.k], writes=[gg.k])
                P.tt("dve", gg.ap[0:C, :], gg.ap[0:C, :], nega.ap[0:C, hsl], ALU.mult, reads=[gg.k, nega.k], writes=[gg.k])
                pg1 = bank()
                P.mm(pg1[0:C, 0:HG], m_incl[0:C, 0:C], gg.ap[0:C, :], reads=[cf.k, gg.k], writes=[pg1.k])
                P.mm(pg1[0:64, 8:8 + HG], onesf[0:C, 0:64], gg.ap[0:C, :], reads=[cf.k, gg.k], writes=[pg1.k])
                P.copy("act", gc.ap[0:C, :], pg1[0:C, 0:HG], reads=[pg1.k], writes=[gc.k])
                P.act(elast.ap[:, :], pg1[0:64, 8:8 + HG], AF.Exp, reads=[pg1.k], writes=[elast.k])
                P.tt("dve", edl.ap[0:C, :], pg1[0:C, 8:8 + HG], gc.ap[0:C, :], ALU.subtract, reads=[pg1.k, gc.k], writes=[edl.k])
                P.act(edl.ap[0:C, :], edl.ap[0:C, :], AF.Exp, reads=[edl.k], writes=[edl.k])
                if DN_CUT <= 1:
                    continue
                P.copy("act", Gb.v[0:C, :, :], gg.ap[0:C, :].unsqueeze(2).to_broadcast([C, HG, 64]), reads=[gg.k], writes=[Gb.k])
                if DN_CUT <= 1.2:
                    continue
                P.tt("pool", dgb.v[0:C, :, 0:C], identf[0:C, 0:C].unsqueeze(1).to_broadcast([C, HG, C]),
                     beta.ap[0:C, :].unsqueeze(2).to_broadcast([C, HG, C]), ALU.mult, reads=[cf.k, beta.k], writes=[dgb.k])
                if DN_CUT <= 1.4:
                    continue
                pgcb = bank()
                pbb = bank()
                for hh in range(HG):
                    P.mm(pgcb[0:64, hh * 64: hh * 64 + C], Gb.v[0:C, hh, :], m_incl[0:C, 0:C], reads=[Gb.k, cf.k], writes=[pgcb.k])
                    P.mm(pbb[0:64, hh * 64: hh * 64 + C], onesf[0:C, 0:64], dgb.v[0:C, hh, 0:C], reads=[cf.k, dgb.k], writes=[pbb.k])
                if DN_CUT <= 1.6:
                    continue
                gcbv = hv(pgcb[0:64, 0:HW_], HG)
                pbbv = hv(pbb[0:64, 0:HW_], HG)
                P.act(E.v[:, :, 0:C], gcbv[:, :, 0:C], AF.Exp, reads=[pgcb.k], writes=[E.k])
                if DN_CUT <= 1.8:
                    continue
                P.op("act", lambda e, o=dgb.v, i=Gb.v, c=C: e.mul(out=o[0:c, :, :], in_=i[0:c, :, :], mul=-1.0), [Gb.k, dgb.k], [dgb.k])
                pdf = bank()
                for hh in range(HG):
                    P.mm(pdf[0:C, hh * 64: hh * 64 + C], Gb.v[0:C, hh, 0:C], m_incl[0:C, 0:C], start=True, stop=False,
                         reads=[Gb.k, cf.k], writes=[pdf.k])
                    P.mm(pdf[0:C, hh * 64: hh * 64 + C], m_incl[0:C, 0:C], dgb.v[0:C, hh, 0:C], start=False, stop=True,
                         reads=[dgb.k, cf.k], writes=[pdf.k])
                P.ts("dve", E1.v[0:C, :, 0:C], hv(pdf[0:64, 0:HW_], HG)[0:C, :, 0:C], 0.0, None, ALU.min, reads=[pdf.k], writes=[E1.k])
                if DN_CUT <= 1.9:
                    continue
                P.act(E1.v[0:C, :, 0:C], E1.v[0:C, :, 0:C], AF.Exp, reads=[E1.k], writes=[E1.k])
                if DN_CUT <= 2:
                    continue
                kTs = qkvb[1].v[:, :, cs]
                qTs = qkvb[0].v[:, :, cs]
                P.tt("dve", kbg.v[:, :, 0:C], kTs, pbbv[:, :, 0:C], ALU.mult, reads=[qkvb[1].k, pbb.k], writes=[kbg.k])
                P.copy("act", kbT.v[:, :, 0:C], kbg.v[:, :, 0:C], reads=[kbg.k], writes=[kbT.k])
                P.tt("dve", kbg.v[:, :, 0:C], kbg.v[:, :, 0:C], E.v[:, :, 0:C], ALU.mult, reads=[kbg.k, E.k, kbT.k], writes=[kbg.k])
                P.tt("pool", qg.v[:, :, 0:C], qTs, E.v[:, :, 0:C], ALU.mult, reads=[qkvb[0].k, E.k], writes=[qg.k])
                pkk = bank()
                pqk = bank()
                for hh in range(HG):
                    P.mm(pkk[0:C, hh * 64: hh * 64 + C], qkvb[1].v[:, hh, cs], kbT.v[:, hh, 0:C], reads=[qkvb[1].k, kbT.k], writes=[pkk.k])
                    P.mm(pqk[0:C, hh * 64: hh * 64 + C], qkvb[1].v[:, hh, cs], qkvb[0].v[:, hh, cs], reads=[qkvb[1].k, qkvb[0].k], writes=[pqk.k])
                P.tt("dve", Q0.v[0:C, :, 0:C], hv(pkk[0:64, 0:HW_], HG)[0:C, :, 0:C], E1.v[0:C, :, 0:C], ALU.mult, reads=[pkk.k, E1.k], writes=[Q0.k])
                P.tt("pool", Q0.v[0:C, :, 0:C], Q0.v[0:C, :, 0:C], m_nstrict[0:C, 0:C].unsqueeze(1).to_broadcast([C, HG, C]), ALU.mult,
                     reads=[Q0.k, cf.k], writes=[Q0.k])
                P.tt("dve", qkT.v[0:C, :, 0:C], hv(pqk[0:64, 0:HW_], HG)[0:C, :, 0:C], E1.v[0:C, :, 0:C], ALU.mult, reads=[pqk.k, E1.k], writes=[qkT.k])
                P.tt("pool", qkT.v[0:C, :, 0:C], qkT.v[0:C, :, 0:C], m_incl[0:C, 0:C].unsqueeze(1).to_broadcast([C, HG, C]), ALU.mult,
                     reads=[qkT.k, cf.k], writes=[qkT.k])
                if DN_CUT <= 3:
                    continue
                ptp = bank()
                for hh in range(HG):
                    P.tr(ptp[0:C, hh * 64: hh * 64 + C], Q0.v[0:C, hh, 0:C], identf[0:C, 0:C], reads=[Q0.k, cf.k], writes=[ptp.k])
                P.copy("act", P0.v[0:C, :, 0:C], hv(ptp[0:64, 0:HW_], HG)[0:C, :, 0:C], reads=[ptp.k], writes=[P0.k])
                P.tt("dve", TT_.v[0:C, :, 0:C], Q0.v[0:C, :, 0:C], identf[0:C, 0:C].unsqueeze(1).to_broadcast([C, HG, C]), ALU.add,
                     reads=[Q0.k, cf.k], writes=[TT_.k])
                if DN_CUT <= 4:
                    continue
                P.tag = 'dn.neu'
                Qa, Pa, Qn_, Pn_ = Q0, P0, Qb, Pb
                for lv in range(LV - 1):
                    pq2 = bank()
                    pp2 = bank()
                    lastlv = (lv == LV - 2)
                    for hh in range(HG):
                        if not lastlv:
                            P.mm(pq2[0:C, hh * 64: hh * 64 + C], Pa.v[0:C, hh, 0:C], Qa.v[0:C, hh, 0:C], reads=[Pa.k, Qa.k], writes=[pq2.k])
                        P.mm(pp2[0:C, hh * 64: hh * 64 + C], Qa.v[0:C, hh, 0:C], Pa.v[0:C, hh, 0:C], reads=[Pa.k, Qa.k], writes=[pp2.k])
                    P.copy("act", Pn_.v[0:C, :, 0:C], hv(pp2[0:64, 0:HW_], HG)[0:C, :, 0:C], reads=[pp2.k], writes=[Pn_.k])
                    if not lastlv:
                        P.copy("dve", Qn_.v[0:C, :, 0:C], hv(pq2[0:64, 0:HW_], HG)[0:C, :, 0:C], reads=[pq2.k], writes=[Qn_.k])
                    pt2 = bank()
                    for hh in range(HG):
                        P.mm(pt2[0:C, hh * 64: hh * 64 + C], Pn_.v[0:C, hh, 0:C], TT_.v[0:C, hh, 0:C], reads=[Pn_.k, TT_.k], writes=[pt2.k])
                    P.tt("dve", TT_.v[0:C, :, 0:C], TT_.v[0:C, :, 0:C], hv(pt2[0:64, 0:HW_], HG)[0:C, :, 0:C], ALU.add,
                         reads=[pt2.k, TT_.k], writes=[TT_.k])
                    Qa, Qn_ = Qn_, Qa
                    Pa, Pn_ = Pn_, Pa
                if DN_CUT <= 5:
                    continue
                P.tag = 'dn.scan'
                for hh in range(HG):
                    P.tr(bankb[0:C, hh * 64:(hh + 1) * 64], qkvb[2].v[:, hh, cs], identb[0:64, 0:64], reads=[qkvb[2].k, identb.k], writes=[bankb.k])
                    P.tr(bankb[0:C, HW_ + hh * 64: HW_ + (hh + 1) * 64], qkvb[1].v[:, hh, cs], identb[0:64, 0:64], reads=[qkvb[1].k, identb.k], writes=[bankb.k])
                P.tt("dve", vb.v[0:C, :, :], hv(bankb[0:C, 0:HW_], HG),
                     beta.ap[0:C, :].unsqueeze(2).to_broadcast([C, HG, 64]), ALU.mult, reads=[bankb.k, beta.k], writes=[vb.k])
                P.tt("dve", kd.v[0:C, :, :], hv(bankb[0:C, HW_:2 * HW_], HG),
                     edl.ap[0:C, :].unsqueeze(2).to_broadcast([C, HG, 64]), ALU.mult, reads=[bankb.k, edl.k], writes=[kd.k])
                if DN_CUT <= 6:
                    continue
                if l == 0 and hg == 0 and s == DBG_S:
                    dbg_store("gg", gg.ap[0:C, :], [gg.k])
                    dbg_store("beta", beta.ap[0:C, :], [beta.k])
                    dbg_store("gc", gc.ap[0:C, :], [gc.k])
                    dbg_store("E1", E1.v[0:C, :, 0:C], [E1.k])
                    dbg_store("Q0", Q0.v[0:C, :, 0:C], [Q0.k])
                    dbg_store("qkT", qkT.v[0:C, :, 0:C], [qkT.k])
                    dbg_store("TT", TT_.v[0:C, :, 0:C], [TT_.k])
                    dbg_store("kbg", kbg.v[:, :, 0:C], [kbg.k])
                    dbg_store("qg", qg.v[:, :, 0:C], [qg.k])
                    dbg_store("vb", vb.v[0:C, :, :], [vb.k])
                    dbg_store("kd", kd.v[0:C, :, :], [kd.k])
                pR = bank()
                for hh in range(HG):
                    P.mm(pR[0:C, hh * 64:(hh + 1) * 64], kbg.v[:, hh, 0:C], Sv[:, hh, :], reads=[kbg.k, Sk], writes=[pR.k])
                P.tt("dve", R.v[0:C, :, :], vb.v[0:C, :, :], hv(pR[0:C, 0:HW_], HG), ALU.subtract,
                     reads=[vb.k, pR.k], writes=[R.k])
                pvn = bank()
                for hh in range(HG):
                    P.mm(pvn[0:C, hh * 64:(hh + 1) * 64], TT_.v[0:C, hh, 0:C], R.v[0:C, hh, :], reads=[TT_.k, R.k], writes=[pvn.k])
                P.copy("act", vnw.v[0:C, :, :], hv(pvn[0:C, 0:HW_], HG), reads=[pvn.k], writes=[vnw.k])
                if DN_CUT <= 7:
                    continue
                po_ = bank()
                for hh in range(HG):
                    P.mm(po_[0:64, hh * 64: hh * 64 + C], Sv[:, hh, :], qg.v[:, hh, 0:C], start=True, stop=False,
                         reads=[Sk, qg.k], writes=[po_.k])
                    P.mm(po_[0:64, hh * 64: hh * 64 + C], vnw.v[0:C, hh, :], qkT.v[0:C, hh, 0:C], start=False, stop=True,
                         reads=[vnw.k, qkT.k], writes=[po_.k])
                pS = bank()
                for hh in range(HG):
                    P.mm(pS[0:64, hh * 64:(hh + 1) * 64], kd.v[0:C, hh, :], vnw.v[0:C, hh, :], reads=[kd.k, vnw.k], writes=[pS.k])
                for hh in range(HG):
                    P.ts("dve", Sv[:, hh, :], Sv[:, hh, :], elast.ap[:, hh:hh + 1], None, ALU.mult, reads=[Sk, elast.k], writes=[Sk])
                P.tt("dve", Sv, Sv, hv(pS[0:64, 0:HW_], HG), ALU.add, reads=[pS.k, Sk], writes=[Sk])
                if DN_CUT <= 8:
                    continue
                if l == 0 and hg == 0 and s == DBG_S:
                    dbg_store("R", R.v[0:C, :, :], [R.k])
                    dbg_store("vnw", vnw.v[0:C, :, :], [vnw.k])
                    dbg_store("Snew", Sv, [Sk])
                ov = hv(po_[0:64, 0:HW_], HG)
                P.act(osq.v[:, :, 0:C], ov[:, :, 0:C], AF.Square, reads=[po_.k], writes=[osq.k])
                pn2 = bank()
                for hh in range(HG):
                    P.mm(pn2[0:64, hh * 64: hh * 64 + C], onesf[0:64, 0:64], osq.v[:, hh, 0:C], reads=[cf.k, osq.k], writes=[pn2.k])
                P.ts("dve", osq.v[:, :, 0:C], hv(pn2[0:64, 0:HW_], HG)[:, :, 0:C], 1.0 / 64, EPS, ALU.mult, ALU.add, reads=[pn2.k], writes=[osq.k])
                P.act(osq.v[:, :, 0:C], osq.v[:, :, 0:C], AF.Ln, reads=[osq.k], writes=[osq.k])
                P.act(osq.v[:, :, 0:C], osq.v[:, :, 0:C], AF.Exp, scale=-0.5, reads=[osq.k], writes=[osq.k])
                P.stt(osq.v[:, :, 0:C], ov[:, :, 0:C], gdn[:, 0:1], osq.v[:, :, 0:C], ALU.mult, ALU.mult,
                      reads=[po_.k, gdn.k, osq.k], writes=[osq.k])
                P.tt("dve", yc.v[:, hsl, cs], osq.v[:, :, 0:C], yc.v[:, hsl, cs], ALU.mult, reads=[osq.k, yc.k], writes=[yc.k])
                if not prompt:
                    P.dma("sp", o_sdelta[l, bseq, hsl].rearrange("h k v -> k h v"), Sv, reads=[Sk], is_output=True)
            if prompt and ck == NCH - 1:
                P.dma("sp", o_pdelta[l, hsl].rearrange("h k v -> k h v"), S_p[l][:, hsl, :],
                      reads=[S_p[l].k], is_output=True)
        dbg_store(f"yc{l}", yc.v, [yc.k])
        new_stage(reset_b=False)
        merge_branch(cfg, l, 2, yc.v, yc.k, w_br_dn, per_head=True)

    def stage_final(cfg):
        new_stage()
        P.tag = 'final'
        NT, TT, NTI, prompt, ck = cfg["NT"], cfg["TT"], cfg["NTI"], cfg["prompt"], cfg["ck"]
        P.dma("sp", gn_bc[:, :], final_norm_g.partition_broadcast(128), writes=[gn_bc.k])
        junk = af([128, D], "junkf")
        ssq = af([128, 4], "ssqf")
        yo = [af([128, D], f"yo{i}") for i in range(2)]
        for ti in range(NTI):
            xt = x_sb[0:TT, ti, :]
            P.act(junk.ap[0:TT, :], xt, AF.Square, accum_out=ssq.ap[0:TT, ti:ti + 1], reads=[x_sb.k], writes=[junk.k, ssq.k])
            rstd_inplace(ssq.ap[0:TT, ti:ti + 1], 1.0 / D, [ssq.k])
            y = yo[ti % 2]
            P.stt(y.ap[0:TT, :], xt, ssq.ap[0:TT, ti:ti + 1], gn_bc[0:TT, :], ALU.mult, ALU.mult,
                  reads=[x_sb.k, ssq.k, gn_bc.k], writes=[y.k])
            if prompt:
                r0 = ck * CH + ti * 128
                P.dma("sp", y_p[r0:r0 + 128, :], y.ap[0:128, :], reads=[y.k], is_output=True)
            else:
                P.dma("sp", y_s, y.ap[0:32, :], reads=[y.k], is_output=True)

    cfgs = [dict(NT=32, TT=32, NTI=1, B=4, Ls=8, prompt=False, ck=0, C=8)]
    if prompt_only:
        cfgs = []
    if not sample_only:
        for ck in range(prompt_chunks):
            cfgs.append(dict(NT=CH, TT=128, NTI=4, B=1, Ls=CH, prompt=True, ck=ck, C=64))
    for cfg in cfgs:
        new_stage()
        if cfg["prompt"]:
            r0 = cfg["ck"] * CH
            P.dma("sp", x_sb[:, :, :], xp[r0:r0 + CH, :].rearrange("(t p) d -> p t d", p=128), writes=[x_sb.k])
        else:
            P.dma("sp", x_sb[0:32, 0, :], xs, writes=[x_sb.k])
        for l in range(DEPTH):
            if "norm" not in skip:
                stage_norm(cfg, l)
            if "pool" in stages:
                stage_pool(cfg, l)
            if "mla" in stages:
                stage_mla(cfg, l)
            if "dn" in stages:
                stage_dn(cfg, l)
            if "out" not in skip:
                stage_out(cfg, l)
        if "final" not in skip:
            stage_final(cfg)
    P.fence()
    P.emit(sems, slot_sems)
    es.close()
    return nc, P


def _prep_inputs(inp):
    global _CONST
    if _CONST is None:
        _CONST = _consts()
    f32 = np.float32
    w_uq = np.asarray(inp["w_uq"], f32).reshape(DEPTH, 384, H, 96)
    rope = w_uq[..., 64:96]
    rope_sw = np.concatenate([rope[..., 16:32], rope[..., 0:16]], -1)
    w_uq_ext = np.ascontiguousarray(np.concatenate([w_uq, rope_sw], -1).reshape(DEPTH, 384, H * 128))
    shared = {
        "ckv": np.asarray(inp["cache_kv_latent"], f32),
        "ckr": np.asarray(inp["cache_k_rope"], f32),
        "norm_g": np.asarray(inp["norm_g"], f32),
        "w_in": np.asarray(inp["w_in"], f32),
        "pool_mix": np.ascontiguousarray(np.asarray(inp["pool_mix"], f32).transpose(0, 2, 1, 3)),
        "pool_scale": np.ascontiguousarray(np.asarray(inp["pool_scale"], f32).reshape(DEPTH, 4, 128).transpose(0, 2, 1)),
        "q_norm_g": np.asarray(inp["q_norm_g"], f32),
        "w_uq": w_uq_ext,
        "kv_norm_g": np.asarray(inp["kv_norm_g"], f32),
        "w_ukT": np.ascontiguousarray(np.asarray(inp["w_uk"], f32).transpose(0, 3, 2, 1)),
        "w_uv": np.ascontiguousarray(np.asarray(inp["w_uv"], f32).reshape(DEPTH, 256, H * 64)),
        "conv_w": np.ascontiguousarray(np.asarray(inp["conv_w"], f32).reshape(DEPTH, 4, 24, 64).transpose(0, 3, 2, 1)),
        "a_log": np.asarray(inp["a_log"], f32),
        "dt_bias": np.asarray(inp["dt_bias"], f32),
        "dn_norm_g": np.ascontiguousarray(np.asarray(inp["dn_norm_g"], f32).reshape(DEPTH, 64, 1)),
        "w_br_pool": np.asarray(inp["w_br_pool"], f32),
        "w_br_mla": np.asarray(inp["w_br_mla"], f32),
        "w_br_dn": np.asarray(inp["w_br_dn"], f32),
        "w_out": np.asarray(inp["w_out"], f32),
        "final_norm_g": np.asarray(inp["final_norm_g"], f32),
        "cf": _CONST["cf"], "ropeq": _CONST["ropeq"], "ropek": _CONST["ropek"],
    }
    xp = np.asarray(inp["x_prompt"], f32)
    xs = np.asarray(inp["x_sample"], f32)
    sp = np.asarray(inp["state_pool"], f32)
    sc = np.asarray(inp["state_conv"], f32)
    sd = np.asarray(inp["state_delta"], f32)
    pt = np.asarray(inp["page_table"], np.int32)
    in_maps = []
    for c in range(NCORE):
        m = dict(shared)
        m["xp"] = np.ascontiguousarray(xp[c])
        m["xs"] = np.ascontiguousarray(xs[4 * c:4 * c + 4].reshape(32, D))
        m["spool"] = np.ascontiguousarray(sp[:, 4 * c:4 * c + 4])
        m["sconv"] = np.ascontiguousarray(sc[:, 4 * c:4 * c + 4])
        m["sdelta"] = np.ascontiguousarray(sd[:, 4 * c:4 * c + 4])
        m["ptab"] = np.ascontiguousarray(pt[4 * c:4 * c + 4])
        in_maps.append(m)
    return in_maps


_NC = None


def kernel(**inputs):
    global _NC
    in_maps = _prep_inputs(inputs)
    if _NC is None:
        _NC = build()[0]
    res = run_bass_kernel_spmd(_NC, in_maps, core_ids=list(range(NCORE)))
    r = res.results
    cat = lambda k: np.stack([r[c][k] for c in range(NCORE)], 0)
    y_p = cat("y_p")
    y_s = np.concatenate([r[c]["y_s"].reshape(4, 8, D) for c in range(NCORE)], 0)
    p_kv = np.stack([r[c]["o_pkv"] for c in range(NCORE)], 1)
    p_kr = np.stack([r[c]["o_pkr"] for c in range(NCORE)], 1)
    p_pool = np.stack([r[c]["o_ppool"] for c in range(NCORE)], 1)
    p_conv = np.stack([r[c]["o_pconv"] for c in range(NCORE)], 1)
    p_delta = np.stack([r[c]["o_pdelta"] for c in range(NCORE)], 1)
    s_kv = np.concatenate([r[c]["o_skv"].reshape(DEPTH, 4, 8, 256) for c in range(NCORE)], 1)
    s_kr = np.concatenate([r[c]["o_skr"].reshape(DEPTH, 4, 8, 32) for c in range(NCORE)], 1)
    s_pool = np.concatenate([r[c]["o_spool"] for c in range(NCORE)], 1)
    s_conv = np.concatenate([r[c]["o_sconv"] for c in range(NCORE)], 1)
    s_delta = np.concatenate([r[c]["o_sdelta"] for c in range(NCORE)], 1)
    outs = (y_p, y_s, p_kv, p_kr, p_pool, p_conv, p_delta, s_kv, s_kr, s_pool, s_conv, s_delta)
    return tuple(np.ascontiguousarray(o, dtype=np.float32) for o in outs)
```

```python
import contextlib
import numpy as np
import concourse.bass as bass
import concourse.mybir as mybir
from concourse.bass_utils import run_bass_kernel_spmd

F32 = mybir.dt.float32
BF16 = mybir.dt.bfloat16
I32 = mybir.dt.int32
AF = mybir.ActivationFunctionType
ALU = mybir.AluOpType

D = 1024
SEQ = 2048
DEPTH = 2
EPS = 1e-6
NPAGE = 128
H = 8
MLA_SCALE = 96 ** -0.5
NCORE = 8
CH = 512
NCH = SEQ // CH
C_POOL, C_ZPOOL, C_Q, C_KV, C_KR, C_ZMLA, C_QKV, C_ZDN, C_BETA, C_ALPHA, C_GATE = (
    0, 512, 1024, 1408, 1664, 1696, 2208, 3744, 4256, 4264, 4272)
INW = 7344


class Tk:
    __slots__ = ("w", "r", "name")

    def __init__(self, name=""):
        self.w = {}
        self.r = {}
        self.name = name


class Op:
    __slots__ = ("fn", "waits", "signal", "dma", "tag")

    def __init__(self, fn, dma=None):
        self.fn = fn
        self.waits = []
        self.signal = False
        self.dma = dma
        self.tag = None


STREAMS = ("pe", "act", "dve", "pool", "sp")
NSLOT = {"sp": 28, "act": 8, "pool": 24}


class Prog:
    def __init__(self, nc):
        self.nc = nc
        self.ops = {s: [] for s in STREAMS}
        self.seen_c = {s: {} for s in STREAMS}
        self.seen_d = {s: {} for s in STREAMS}
        self.slot_next = {s: 0 for s in NSLOT}
        self.slot_val = {}
        self.out_dma_events = []
        self.pending_dma = {}
        self.last_c = {s: -1 for s in STREAMS}
        self.tag = None
        self.annotate = False

    def _need(self, stream, ev, waits, force_same=False):
        if ev[0] == "c":
            _, e2, idx = ev
            if idx < 0:
                return
            if e2 == stream and stream == "pe" and not force_same:
                return
            if self.seen_c[stream].get(e2, -1) >= idx:
                return
            self.seen_c[stream][e2] = idx
            self.ops[e2][idx].signal = True
            waits.append(ev)
        else:
            _, slot, val = ev
            if self.seen_d[stream].get(slot, 0) >= val:
                return
            self.seen_d[stream][slot] = val
            waits.append(ev)

    def _deps(self, stream, reads, writes, force_same=False):
        waits = []
        for t in reads:
            for ev in t.w.values():
                self._need(stream, ev, waits, force_same)
        for t in writes:
            for ev in t.w.values():
                self._need(stream, ev, waits, force_same)
            for ev in t.r.values():
                self._need(stream, ev, waits, force_same)
        return waits

    def _commit(self, ev, reads, writes):
        key = ev[:2]
        for t in reads:
            t.r[key] = ev
        for t in writes:
            t.w = {key: ev}
            t.r = {}

    def op(self, stream, fn, reads=(), writes=()):
        o = Op(fn)
        o.waits = self._deps(stream, reads, writes)
        idx = len(self.ops[stream])
        o.tag = self.tag
        self.ops[stream].append(o)
        self.last_c[stream] = idx
        self._commit(("c", stream, idx), reads, writes)
        return o

    def dma(self, stream, out, in_, reads=(), writes=(), is_output=False, **kw):
        n = NSLOT[stream]
        k = self.slot_next[stream]
        self.slot_next[stream] = k + 1
        slot = (stream, k % n)
        prev = self.slot_val.get(slot, 0)
        val = prev + 16
        self.slot_val[slot] = val
        o = Op(lambda e: e.dma_start(out=out, in_=in_, **kw), dma=(slot, val))
        o.waits = self._deps(stream, reads, writes, force_same=True)
        o.tag = self.tag
        if prev > 0:
            self._need(stream, ("d", slot, prev), o.waits)
        self.ops[stream].append(o)
        ev = ("d", slot, val)
        self._commit(ev, reads, writes)
        self.pending_dma[slot] = ev
        if is_output:
            self.out_dma_events.append(ev)
        return o

    def idma(self, out, in_, idx_ap, reads=(), writes=()):
        stream = "pool"
        n = NSLOT[stream]
        k = self.slot_next[stream]
        self.slot_next[stream] = k + 1
        slot = (stream, k % n)
        prev = self.slot_val.get(slot, 0)
        val = prev + 16
        self.slot_val[slot] = val
        o = Op(lambda e: e.indirect_dma_start(out=out, out_offset=None, in_=in_,
                                              in_offset=bass.IndirectOffsetOnAxis(ap=idx_ap, axis=0)), dma=(slot, val))
        o.waits = self._deps(stream, reads, writes, force_same=True)
        if prev > 0:
            self._need(stream, ("d", slot, prev), o.waits)
        self.ops[stream].append(o)
        ev = ("d", slot, val)
        self._commit(ev, reads, writes)
        self.pending_dma[slot] = ev
        return o

    def fence(self):
        last = dict(self.last_c)
        pend = list(self.pending_dma.values())
        self.pending_dma = {}
        self._fence_waits = {}
        for a in STREAMS:
            waits = []
            for b in STREAMS:
                if b != a:
                    self._need(a, ("c", b, last[b]), waits)
            for ev in pend:
                self._need(a, ev, waits)
            if waits:
                o = Op(None)
                o.waits = waits
                self.ops[a].append(o)

    def mm(self, out, lhsT, rhs, start=True, stop=True, reads=(), writes=(), **kw):
        return self.op("pe", lambda e: e.matmul(out, lhsT, rhs, start=start, stop=stop, **kw), reads, writes)

    def tr(self, out, in_, ident, reads=(), writes=()):
        return self.op("pe", lambda e: e.transpose(out, in_, ident), reads, writes)

    def act(self, out, in_, func, reads=(), writes=(), **kw):
        return self.op("act", lambda e: e.activation(out=out, in_=in_, func=func, **kw), reads, writes)

    def tt(self, stream, out, in0, in1, op, reads=(), writes=()):
        return self.op(stream, lambda e: e.tensor_tensor(out=out, in0=in0, in1=in1, op=op), reads, writes)

    def ts(self, stream, out, in0, s1, s2, op0, op1=None, reads=(), writes=(), **kw):
        if op1 is None:
            return self.op(stream, lambda e: e.tensor_scalar(out=out, in0=in0, scalar1=s1, scalar2=None, op0=op0, **kw), reads, writes)
        return self.op(stream, lambda e: e.tensor_scalar(out=out, in0=in0, scalar1=s1, scalar2=s2, op0=op0, op1=op1, **kw), reads, writes)

    def stt(self, out, in0, scalar, in1, op0, op1, reads=(), writes=(), **kw):
        return self.op("dve", lambda e: e.scalar_tensor_tensor(out=out, in0=in0, scalar=scalar, in1=in1, op0=op0, op1=op1, **kw), reads, writes)

    def copy(self, stream, out, in_, reads=(), writes=()):
        if stream == "act":
            return self.op("act", lambda e: e.copy(out=out, in_=in_), reads, writes)
        return self.op(stream, lambda e: e.tensor_copy(out=out, in_=in_), reads, writes)

    def memset(self, stream, ap, val, writes=()):
        return self.op(stream, lambda e: e.memset(ap, val), (), writes)

    def emit(self, sems, slot_sems):
        nc = self.nc
        cum = {}
        for s in STREAMS:
            c = 0
            arr = []
            for o in self.ops[s]:
                if o.signal and o.dma is None and o.fn is not None:
                    c += 1
                arr.append(c)
            cum[s] = arr
        final_waits = []
        for ev in self.out_dma_events:
            self._need("sp", ev, final_waits)
        self.n_instr = {s: len(self.ops[s]) for s in STREAMS}

        def run(stream, eng):
            for o in self.ops[stream]:
                for ev in o.waits:
                    if ev[0] == "c":
                        eng.wait_ge(sems[ev[1]], cum[ev[1]][ev[2]])
                    else:
                        eng.wait_ge(slot_sems[ev[1]], ev[2])
                if o.fn is None:
                    continue
                ins = o.fn(eng)
                if self.annotate and o.tag:
                    ins.annotate(o.tag)
                if o.dma is not None:
                    ins.then_inc(slot_sems[o.dma[0]], 16)
                elif o.signal:
                    ins.then_inc(sems[stream], 1)
            if stream == "sp":
                for ev in final_waits:
                    eng.wait_ge(slot_sems[ev[1]], ev[2])

        with nc.Block() as block:
            @block.tensor
            def _(e):
                run("pe", e)

            @block.scalar
            def _(e):
                run("act", e)

            @block.vector
            def _(e):
                run("dve", e)

            @block.gpsimd
            def _(e):
                run("pool", e)

            @block.sync
            def _(e):
                run("sp", e)


class Buf:
    def __init__(self, t, name):
        self.t = t
        self.k = Tk(name)

    def __getitem__(self, key):
        return self.t[key]


def _consts():
    c = {}
    half = 16
    inv = np.power(10000.0, -np.arange(half, dtype=np.float32) / half).astype(np.float32)

    def tabs(pos):
        ang = pos.astype(np.float32)[:, None] * inv[None, :]
        return np.cos(ang).astype(np.float32), np.sin(ang).astype(np.float32)

    posp = np.arange(SEQ)
    poss = 16384 + np.arange(8)
    cp, sp_ = tabs(posp)
    cs, ss = tabs(poss)
    ropeq = np.zeros((32, 2, SEQ + 32), np.float32)
    ropeq[:, 0, :SEQ] = np.concatenate([cp.T, cp.T], 0)
    ropeq[:, 1, :SEQ] = np.concatenate([-sp_.T, sp_.T], 0)
    cs4 = np.tile(cs, (4, 1))
    ss4 = np.tile(ss, (4, 1))
    ropeq[:, 0, SEQ:] = np.concatenate([cs4.T, cs4.T], 0)
    ropeq[:, 1, SEQ:] = np.concatenate([-ss4.T, ss4.T], 0)
    c["ropeq"] = ropeq
    ropek = np.zeros((128, 17, 32), np.float32)
    ropek[:, :16, :16] = cp.reshape(16, 128, 16).transpose(1, 0, 2)
    ropek[:, :16, 16:] = sp_.reshape(16, 128, 16).transpose(1, 0, 2)
    ropek[:32, 16, :16] = cs4
    ropek[:32, 16, 16:] = ss4
    c["ropek"] = ropek
    f = np.zeros((128, 1024), np.float32)
    f[:, 0:128] = np.eye(128)
    f[:, 128:256] = 1.0
    ii = np.arange(64)
    f[:64, 256:320] = (ii[None, :] >= ii[:, None])
    f[:64, 320:384] = -(ii[None, :] > ii[:, None]).astype(np.float32)
    jj = np.arange(128)
    f[:, 384:512] = (jj[:, None] <= jj[None, :])
    t15 = np.arange(15)
    for gi, w in enumerate((2, 4, 8, 16)):
        f[:, 512 + gi * 15: 512 + (gi + 1) * 15] = 1.0 / np.minimum(t15 + 1, w)
    f[:8, 576:584] = (np.arange(8)[:, None] <= np.arange(8)[None, :])
    f[:, 600] = np.arange(128)
    f[:, 601] = np.arange(128) + 5120 * 128
    f[:, 610:626] = np.arange(16)[None, :]
    f[:, 626:642] = np.arange(16)[None, :] + 5120 * 16
    c["cf"] = f
    return c


_CONST = None


DBG_S = 0
DN_NSUB = None
DN_CUT = 99


def build(sample_only=False, prompt_chunks=NCH, dbg=None, stages=("pool", "mla", "dn"), npool=5120, skip=(), prompt_only=False, annotate=False):
    nc = bass.Bass("TRN2", target_bir_lowering=False)
    es = contextlib.ExitStack()

    def din(name, shape, dt=F32):
        return nc.dram_tensor(name, list(shape), dt, kind="ExternalInput").ap()

    def dout(name, shape, dt=F32):
        return nc.dram_tensor(name, list(shape), dt, kind="ExternalOutput").ap()

    xp = din("xp", [SEQ, D])
    xs = din("xs", [32, D])
    ckv = din("ckv", [DEPTH, npool, 128, 256])
    ckr = din("ckr", [DEPTH, npool, 128, 32])
    spool = din("spool", [DEPTH, 4, 15, 512])
    sconv = din("sconv", [DEPTH, 4, 3, 1536])
    sdelta = din("sdelta", [DEPTH, 4, 8, 64, 64])
    ptab = din("ptab", [4, 128], I32)
    norm_g = din("norm_g", [DEPTH, D])
    w_in = din("w_in", [DEPTH, D, INW])
    pool_mix = din("pool_mix", [DEPTH, 128, 4, 128])
    pool_scale = din("pool_scale", [DEPTH, 128, 4])
    q_norm_g = din("q_norm_g", [DEPTH, 384])
    w_uq = din("w_uq", [DEPTH, 384, H * 128])
    kv_norm_g = din("kv_norm_g", [DEPTH, 256])
    w_ukT = din("w_ukT", [DEPTH, 64, H, 256])
    w_uv = din("w_uv", [DEPTH, 256, H * 64])
    conv_w = din("conv_w", [DEPTH, 64, 24, 4])
    a_log = din("a_log", [DEPTH, H])
    dt_bias = din("dt_bias", [DEPTH, H])
    dn_norm_g = din("dn_norm_g", [DEPTH, 64, 1])
    w_br_pool = din("w_br_pool", [DEPTH, 512, D])
    w_br_mla = din("w_br_mla", [DEPTH, 512, D])
    w_br_dn = din("w_br_dn", [DEPTH, 512, D])
    w_out = din("w_out", [DEPTH, D, D])
    final_norm_g = din("final_norm_g", [D])
    cf_d = din("cf", [128, 1024])
    ropeq_d = din("ropeq", [32, 2, SEQ + 32])
    ropek_d = din("ropek", [128, 17, 32])

    y_p = dout("y_p", [SEQ, D])
    y_s = dout("y_s", [32, D])
    o_pkv = dout("o_pkv", [DEPTH, SEQ, 256])
    o_pkr = dout("o_pkr", [DEPTH, SEQ, 32])
    o_ppool = dout("o_ppool", [DEPTH, 15, 512])
    o_pconv = dout("o_pconv", [DEPTH, 3, 1536])
    o_pdelta = dout("o_pdelta", [DEPTH, H, 64, 64])
    o_skv = dout("o_skv", [DEPTH, 32, 256])
    o_skr = dout("o_skr", [DEPTH, 32, 32])
    o_spool = dout("o_spool", [DEPTH, 4, 15, 512])
    o_sconv = dout("o_sconv", [DEPTH, 4, 3, 1536])
    o_sdelta = dout("o_sdelta", [DEPTH, 4, H, 64, 64])
    dbg_out = {}
    if dbg:
        for name, shape in dbg.items():
            dbg_out[name] = dout("dbg_" + name, shape)

    def sb(name, shape, dt=F32):
        return Buf(es.enter_context(nc.sbuf_tensor(name, list(shape), dt)), name)

    def pstile(name, shape, dt=F32):
        return Buf(es.enter_context(nc.psum_tensor(name, list(shape), dt)), name)

    P = Prog(nc)
    P.annotate = annotate

    x_sb = sb("x_sb", [128, 4, D])
    xnT = sb("xnT", [128, 8, CH], BF16)
    mrg = sb("mrg", [128, 8, CH])
    kTc = [sb(f"kTc{l}", [128, 3, SEQ], BF16) for l in range(DEPTH)]
    Vc = [sb(f"Vc{l}", [128, 16, 256], BF16) for l in range(DEPTH)]
    hist_pool = [sb(f"hpool{l}", [128, 4, 15]) for l in range(DEPTH)]
    hist_conv = [sb(f"hconv{l}", [64, 24, 3]) for l in range(DEPTH)]
    S_p = [sb(f"S_p{l}", [64, H, 64]) for l in range(DEPTH)]
    WA = sb("WA", [128, 8, 672], BF16)
    WB = sb("WB", [128, 8, 512], BF16)
    WBR = sb("WBR", [128, 8 * D], BF16)
    mixw = sb("mixw", [128, 4, 128], BF16)
    cw = sb("cw", [64, 24, 4])
    psc = sb("psc", [128, 4])
    gdn = sb("gdn", [64, 1])
    gn_bc = sb("gn_bc", [128, D])
    a_bc = sb("a_bc", [64, H])
    dtb_bc = sb("dtb_bc", [64, H])
    cf = sb("cf_sb", [128, 1024])
    identb = sb("identb", [128, 128], BF16)
    onesb = sb("onesb", [128, 128], BF16)
    ropek = sb("ropek_sb", [128, 17, 32])
    ptb = sb("ptb", [128, 128], I32)
    ridx = sb("ridx", [128, 128], I32)
    AF_N = 10368
    AB_N = 17408
    arena_f = sb("arena_f", [128, AF_N])
    arena_b = sb("arena_b", [128, AB_N], BF16)
    banks = [pstile(f"psf{i}", [128, 512]) for i in range(6)]
    bankb = pstile("psb", [128, 1024], BF16)
    bankb2 = pstile("psb2", [128, 1024], BF16)

    globals()["_SBUF_LEFT"] = nc.sbuf_bytes_remaining
    sems = {s: es.enter_context(nc.semaphore("sem_" + s)) for s in ("pe", "act", "dve", "pool", "sp")}
    slot_sems = {}
    for s, n in NSLOT.items():
        for i in range(n):
            slot_sems[(s, i)] = es.enter_context(nc.semaphore(f"ds_{s}_{i}"))

    identf = cf[:, 0:128]
    onesf = cf[:, 128:256]
    m_incl = cf[0:64, 256:320]
    m_nstrict = cf[0:64, 320:384]
    m_causal = cf[:, 384:512]
    rc15 = cf[:, 512:572]
    m_causal8 = cf[0:8, 576:584]
    iota_p = cf[:, 600:601]

    st = {"af": 0, "ab": 0, "n": 0, "rot": list(range(6)), "ri": 0}

    class AB:
        pass

    def _arena(ar, key, cap, shape, name, even):
        n = int(np.prod(shape[1:]))
        na = (n + 1) // 2 * 2 if even else n
        off = st[key]
        st[key] = off + na
        assert st[key] <= cap, ("arena overflow", key, name, st[key], cap)
        st["n"] += 1
        b = AB()
        b.k = Tk(name or f"{key}{st['n']}")
        b.shape = list(shape)
        flat = ar.t[0:shape[0], off:off + n]
        b.ap = flat
        sh = shape
        if len(sh) == 2:
            b.v = flat
        elif len(sh) == 3:
            b.v = flat.rearrange("p (a b) -> p a b", b=sh[2])
        elif len(sh) == 4:
            b.v = flat.rearrange("p (a b c) -> p a b c", b=sh[2], c=sh[3])
        else:
            raise ValueError
        return b

    def af(shape, name=None):
        return _arena(arena_f, "af", AF_N, shape, name, False)

    def ab(shape, name=None):
        return _arena(arena_b, "ab", AB_N, shape, name, True)

    def afb(shape, name=None):
        n = int(np.prod(shape[1:]))
        nf = (n + 1) // 2
        off = st["af"]
        st["af"] = off + nf
        assert st["af"] <= AF_N, ("arena overflow", "afb", name, st["af"], AF_N)
        b = AB()
        b.k = Tk(name or "afb")
        b.shape = list(shape)
        flat = arena_f.t[0:shape[0], off:off + nf].bitcast(BF16)[:, 0:n]
        b.ap = flat
        sh = shape
        if len(sh) == 2:
            b.v = flat
        elif len(sh) == 3:
            b.v = flat.rearrange("p (a b) -> p a b", b=sh[2])
        else:
            b.v = flat.rearrange("p (a b c) -> p a b c", b=sh[2], c=sh[3])
        return b

    def new_stage(reset_b=True):
        P.fence()
        st["af"] = 0
        if reset_b:
            st["ab"] = 0

    def set_rot(lst):
        st["rot"] = list(lst)
        st["ri"] = 0

    def bank():
        b = banks[st["rot"][st["ri"] % len(st["rot"])]]
        st["ri"] += 1
        return b

    def hv(ap, n, t=None):
        return ap.rearrange("p (h t) -> p h t", h=n)

    P.dma("sp", cf[:, :], cf_d, writes=[cf.k])
    P.dma("sp", ropek[:, :, :], ropek_d, writes=[ropek.k])
    P.copy("dve", identb[:, :], cf[:, 0:128], reads=[cf.k], writes=[identb.k])
    P.copy("dve", onesb[:, :], cf[:, 128:256], reads=[cf.k], writes=[onesb.k])
    for l in range(DEPTH):
        P.memset("pool", hist_pool[l][:, :, :], 0.0, writes=[hist_pool[l].k])
        P.memset("pool", hist_conv[l][:, :, :], 0.0, writes=[hist_conv[l].k])
        P.memset("pool", S_p[l][:, :, :], 0.0, writes=[S_p[l].k])

    def dbg_store(name, ap, reads):
        if name in dbg_out:
            P.dma("pool", dbg_out[name], ap, reads=reads, is_output=True)

    w_in_v = [w_in[l].rearrange("(kt p) n -> p kt n", p=128) for l in range(DEPTH)]

    def load_w_in(dst, l, c0, ncol, dcol=0):
        P.dma("pool", dst[:, :, dcol:dcol + ncol], w_in_v[l][:, :, c0:c0 + ncol], writes=[dst.k])

    def fm_proj(ps_ap, Wb, wcol, M, NT, wk, psk):
        for kt in range(8):
            P.mm(ps_ap, Wb[:, kt, wcol:wcol + M], xnT[:, kt, 0:NT], start=(kt == 0), stop=(kt == 7),
                 reads=[wk, xnT.k], writes=[psk])

    def rstd_inplace(a, mult, keys):
        P.ts("dve", a, a, mult, EPS, ALU.mult, ALU.add, reads=keys, writes=keys)
        P.act(a, a, AF.Ln, reads=keys, writes=keys)
        P.act(a, a, AF.Exp, scale=-0.5, reads=keys, writes=keys)

    def stage_norm(cfg, l):
        new_stage()
        P.tag = 'norm'
        NT, TT, NTI = cfg["NT"], cfg["TT"], cfg["NTI"]
        P.dma("sp", gn_bc[:, :], norm_g[l].partition_broadcast(128), writes=[gn_bc.k])
        junk = af([128, D], "junk")
        ssq = af([128, 4], "ssq")
        xn = ab([128, D], "xn")
        for ti in range(NTI):
            xt = x_sb[0:TT, ti, :]
            P.act(junk.ap[0:TT, :], xt, AF.Square, accum_out=ssq.ap[0:TT, ti:ti + 1],
                  reads=[x_sb.k], writes=[junk.k, ssq.k])
            rstd_inplace(ssq.ap[0:TT, ti:ti + 1], 1.0 / D, [ssq.k])
            P.stt(xn.ap[0:TT, :], xt, ssq.ap[0:TT, ti:ti + 1], gn_bc[0:TT, :], ALU.mult, ALU.mult,
                  reads=[x_sb.k, ssq.k, gn_bc.k], writes=[xn.k])
            for kt in range(8):
                P.tr(bankb[:, kt * 128: kt * 128 + TT], xn.ap[0:TT, kt * 128:(kt + 1) * 128], identb[0:TT, 0:TT],
                     reads=[xn.k, identb.k], writes=[bankb.k])
            P.copy("act", xnT[:, :, ti * TT:(ti + 1) * TT], hv(bankb[:, :], 8)[:, :, 0:TT],
                   reads=[bankb.k], writes=[xnT.k])
        P.memset("pool", mrg[:, :, 0:NT], 0.0, writes=[mrg.k])
        dbg_store("xnT", xnT[:, :, 0:NT], [xnT.k])

    def merge_branch(cfg, l, bi, yv, yk, w_br, per_head):
        P.tag = 'merge'
        NT = cfg["NT"]
        if per_head:
            wv = WBR[0:64, :].rearrange("p (h n) -> p h n", h=8)
            P.dma("pool", wv, w_br[l].rearrange("(h p) n -> p h n", p=64), writes=[WBR.k])
        else:
            wv = WBR[:, 0:4 * D].rearrange("p (h n) -> p h n", h=4)
            P.dma("pool", wv, w_br[l].rearrange("(kt p) n -> p kt n", p=128), writes=[WBR.k])
        gs = af([128, CH], "gsig")
        for half in range(2):
            load_w_in(WB, l, C_GATE + bi * D + half * 512, 512)
            for jj in range(4):
                j = half * 4 + jj
                pg = bank()
                fm_proj(pg[:, 0:NT], WB, jj * 128, 128, NT, WB.k, pg.k)
                P.act(gs.ap[:, 0:NT], pg[:, 0:NT], AF.Sigmoid, reads=[pg.k], writes=[gs.k])
                pb = bank()
                nk = 8 if per_head else 4
                for kk in range(nk):
                    P.mm(pb[:, 0:NT], wv[:, kk, j * 128:(j + 1) * 128], yv[:, kk, 0:NT],
                         start=(kk == 0), stop=(kk == nk - 1), reads=[WBR.k, yk], writes=[pb.k])
                P.tt("dve", gs.ap[:, 0:NT], gs.ap[:, 0:NT], pb[:, 0:NT], ALU.mult, reads=[gs.k, pb.k], writes=[gs.k])
                P.tt("pool", mrg[:, j, 0:NT], mrg[:, j, 0:NT], gs.ap[:, 0:NT], ALU.add, reads=[gs.k, mrg.k], writes=[mrg.k])

    def stage_out(cfg, l):
        new_stage()
        P.tag = 'out'
        NT, TT, NTI = cfg["NT"], cfg["TT"], cfg["NTI"]
        dbg_store(f"mrg{l}", mrg[:, :, 0:NT], [mrg.k])
        mb = ab([128, 8, NT], "mrgb")
        P.copy("dve", mb.v, mrg[:, :, 0:NT], reads=[mrg.k], writes=[mb.k])
        wo = w_out[l].rearrange("(kt p) n -> p kt n", p=128)
        for half in range(2):
            P.dma("pool", WB[:, :, :], wo[:, :, half * 512:(half + 1) * 512], writes=[WB.k])
            for ti in range(NTI):
                pb = bank()
                for kt in range(8):
                    P.mm(pb[0:TT, :], mb.v[:, kt, ti * TT:(ti + 1) * TT], WB[:, kt, :], start=(kt == 0), stop=(kt == 7),
                         reads=[mb.k, WB.k], writes=[pb.k])
                xsl = x_sb[0:TT, ti, half * 512:(half + 1) * 512]
                P.tt("dve", xsl, xsl, pb[0:TT, :], ALU.add, reads=[pb.k, x_sb.k], writes=[x_sb.k])

    def stage_pool(cfg, l):
        new_stage()
        P.tag = 'pool'
        NT, B, Ls, prompt, ck = cfg["NT"], cfg["B"], cfg["Ls"], cfg["prompt"], cfg["ck"]
        W = 15 + Ls
        load_w_in(WA, l, C_POOL, 512)
        load_w_in(WB, l, C_ZPOOL, 512)
        P.dma("pool", mixw[:, :, :], pool_mix[l], writes=[mixw.k])
        P.dma("sp", psc[:, :], pool_scale[l], writes=[psc.k])
        ext = af([128, 4, B, W], "ext")
        if prompt:
            P.copy("pool", ext.v[:, :, 0, 0:15], hist_pool[l][:, :, :], reads=[hist_pool[l].k], writes=[ext.k])
        else:
            stg = af([15, 4 * 512], "stg")
            for b in range(4):
                P.dma("sp", stg.ap[:, b * 512:(b + 1) * 512], spool[l, b], writes=[stg.k])
            pt = bank()
            for b in range(4):
                for g in range(4):
                    P.tr(pt[:, (b * 4 + g) * 15:(b * 4 + g + 1) * 15], stg.ap[0:15, b * 512 + g * 128: b * 512 + (g + 1) * 128],
                         identf[0:15, 0:15], reads=[stg.k, cf.k], writes=[pt.k])
            P.copy("act", ext.v[:, :, :, 0:15], pt[:, 0:240].rearrange("p (b g t) -> p g b t", b=4, g=4),
                   reads=[pt.k], writes=[ext.k])
        for g in range(4):
            pu = bank()
            fm_proj(pu[:, 0:NT], WA, g * 128, 128, NT, WA.k, pu.k)
            P.copy("act", ext.v[:, g, :, 15:W], pu[:, 0:NT].rearrange("p (b t) -> p b t", b=B), reads=[pu.k], writes=[ext.k])
        if prompt:
            P.copy("pool", hist_pool[l][:, :, :], ext.v[:, :, 0, Ls:Ls + 15], reads=[ext.k], writes=[hist_pool[l].k])
        if (not prompt) or ck == NCH - 1:
            ostg = af([15, 512], "ostg")
            for b in range(B):
                pt = bank()
                for g in range(4):
                    P.tr(pt[0:15, g * 128:(g + 1) * 128], ext.v[:, g, b, Ls:Ls + 15], identf[:, :],
                         reads=[ext.k, cf.k], writes=[pt.k])
                P.copy("act", ostg.ap[0:15, :], pt[0:15, 0:512], reads=[pt.k], writes=[ostg.k])
                dst = o_ppool[l] if prompt else o_spool[l, b]
                P.dma("sp", dst, ostg.ap[0:15, :], reads=[ostg.k], is_output=True)
        wa = af([128, B, W], "wa")
        wb_ = af([128, B, W], "wb")
        dT = ab([128, 4, B, Ls], "dT")
        ya = ab([128, 4, NT], "ya")
        zs = af([128, CH], "zs")
        fx = af([128, 15], "fx")
        for g, wdw in enumerate((2, 4, 8, 16)):
            cur, curk, n = ext.v[:, g, :, :], ext.k, W
            sh = 1
            bufs = [wa, wb_]
            bi = 0
            while sh < wdw:
                o = bufs[bi]
                P.tt("pool", o.v[:, :, 0:n - sh], cur[:, :, sh:n], cur[:, :, 0:n - sh], ALU.add, reads=[curk], writes=[o.k])
                cur, curk, n = o.v, o.k, n - sh
                sh *= 2
                bi ^= 1
            o0 = n - Ls
            P.stt(dT.v[:, g, :, :], cur[:, :, o0:o0 + Ls], 1.0 / wdw, ext.v[:, g, :, 15:W], ALU.mult, ALU.subtract,
                  reads=[curk, ext.k], writes=[dT.k])
            if prompt and ck == 0:
                P.tt("pool", fx.ap, cur[:, 0, o0:o0 + 15], rc15[:, g * 15:(g + 1) * 15], ALU.mult,
                     reads=[curk, cf.k], writes=[fx.k])
                P.tt("dve", dT.v[:, g, 0, 0:15], fx.ap, ext.v[:, g, 0, 15:30], ALU.subtract,
                     reads=[fx.k, ext.k, dT.k], writes=[dT.k])
        for g in range(4):
            p1 = bank()
            P.mm(p1[:, 0:NT], mixw[:, g, :], dT.ap[:, g * NT:(g + 1) * NT], reads=[mixw.k, dT.k], writes=[p1.k])
            p2 = bank()
            fm_proj(p2[:, 0:NT], WB, g * 128, 128, NT, WB.k, p2.k)
            P.act(zs.ap[:, 0:NT], p2[:, 0:NT], AF.Silu, reads=[p2.k], writes=[zs.k])
            P.stt(ya.v[:, g, 0:NT], p1[:, 0:NT], psc[:, g:g + 1], zs.ap[:, 0:NT], ALU.mult, ALU.mult,
                  reads=[p1.k, psc.k, zs.k], writes=[ya.k])
        dbg_store(f"ya{l}", ya.v, [ya.k])
        merge_branch(cfg, l, 0, ya.v, ya.k, w_br_pool, per_head=False)

    def stage_mla(cfg, l):
        new_stage()
        P.tag = 'mla.pre'
        NT, TT, NTI, B, Ls, prompt, ck = cfg["NT"], cfg["TT"], cfg["NTI"], cfg["B"], cfg["Ls"], cfg["prompt"], cfg["ck"]
        tok0 = ck * CH if prompt else 0
        wuq = ab([128, 3, H * 128], "wuq")
        wuk = ab([64, H, 256], "wuk")
        wuv = ab([128, 2, H * 64], "wuv")
        ropeq = af([32, 2, NT], "ropeq")
        gq_bc = af([128, 384], "gq_bc")
        gkv_bc = af([128, 256], "gkv_bc")
        load_w_in(WA, l, C_Q, 672)
        load_w_in(WB, l, C_ZMLA, 512)
        P.dma("pool", wuq.v[:, :, :], w_uq[l].rearrange("(kt p) n -> p kt n", p=128), writes=[wuq.k])
        P.dma("pool", wuk.v[:, :, :], w_ukT[l], writes=[wuk.k])
        P.dma("pool", wuv.v[:, :, :], w_uv[l].rearrange("(kt p) n -> p kt n", p=128), writes=[wuv.k])
        P.dma("sp", gq_bc.v[:, :], q_norm_g[l].partition_broadcast(128), writes=[gq_bc.k])
        P.dma("sp", gkv_bc.v[:, :], kv_norm_g[l].partition_broadcast(128), writes=[gkv_bc.k])
        rq0 = tok0 if prompt else SEQ
        P.dma("sp", ropeq.v[:, :, 0:NT], ropeq_d[:, :, rq0:rq0 + NT], writes=[ropeq.k])
        yb = ab([64, 8, NT], "yb")
        for h in range(8):
            pz = bank()
            fm_proj(pz[0:64, 0:NT], WB, h * 64, 64, NT, WB.k, pz.k)
            P.act(yb.v[:, h, 0:NT], pz[0:64, 0:NT], AF.Silu, reads=[pz.k], writes=[yb.k])
        ckvf = af([128, 288], "ckvf")
        ssq = af([128, 2], "ssq2")
        junk = af([128, 384], "junk2")
        t1 = af([128, 64], "ropetmp")
        qr = af([32, 2, 8, TT], "qr")
        rden = af([128, 4 * TT], "rden")
        ckvb = ab([128, 288], "ckvb")
        cqb = ab([128, 384], "cqb")
        cqT = ab([128, 3, TT], "cqT")
        qn = ab([64, 8, TT], "qn")
        qrT = ab([32, 8, TT], "qrT")
        qlT = ab([128, 2, 8, TT], "qlT")
        if prompt:
            pT = [ab([128, 4, TT], f"pT{i}") for i in range(2)]
        else:
            knT = ab([128, 3, 32], "knT")
            vn = ab([32, 256], "vn")
            kvb = [afb([128, 8, 256], f"kvb{i}") for i in range(4)]
            krb = [afb([128, 8, 32], f"krb{i}") for i in range(4)]
            kTp = [ab([128, 2, 3, 128], f"kTp{i}") for i in range(4)]
            pts = [ab([128, 512], f"pts{i}") for i in range(2)]
            ptn = ab([8, 64], "ptn")
            vb8 = ab([8, 256], "vb8")
            qc = ab([128, 2, 64], "qc")
            qrc = ab([32, 64], "qrc")
            ols = ab([128, 2, 64], "ols")
        for ti in range(NTI):
            set_rot(range(6))
            ktile = (tok0 // 128 + ti) if prompt else 16
            tsl = slice(ti * TT, (ti + 1) * TT)
            P.tag = 'mla.proj'
            pk = bank()
            for kt in range(8):
                P.mm(pk[0:TT, 0:288], xnT[:, kt, tsl], WA[:, kt, 384:672], start=(kt == 0), stop=(kt == 7),
                     reads=[xnT.k, WA.k], writes=[pk.k])
            P.act(junk.ap[0:TT, 0:256], pk[0:TT, 0:256], AF.Square, accum_out=ssq.ap[0:TT, 0:1],
                  reads=[pk.k], writes=[junk.k, ssq.k])
            rstd_inplace(ssq.ap[0:TT, 0:1], 1.0 / 256, [ssq.k])
            P.stt(ckvf.ap[0:TT, 0:256], pk[0:TT, 0:256], ssq.ap[0:TT, 0:1], gkv_bc.v[0:TT, :], ALU.mult, ALU.mult,
                  reads=[pk.k, ssq.k, gkv_bc.k], writes=[ckvf.k])
            cosk = ropek[0:TT, ktile, 0:16]
            sink = ropek[0:TT, ktile, 16:32]
            P.tt("dve", t1.ap[0:TT, 0:16], pk[0:TT, 256:272], cosk, ALU.mult, reads=[pk.k, ropek.k], writes=[t1.k])
            P.tt("dve", t1.ap[0:TT, 16:32], pk[0:TT, 272:288], sink, ALU.mult, reads=[pk.k, ropek.k], writes=[t1.k])
            P.tt("dve", t1.ap[0:TT, 32:48], pk[0:TT, 272:288], cosk, ALU.mult, reads=[pk.k, ropek.k], writes=[t1.k])
            P.tt("dve", t1.ap[0:TT, 48:64], pk[0:TT, 256:272], sink, ALU.mult, reads=[pk.k, ropek.k], writes=[t1.k])
            P.tt("dve", ckvf.ap[0:TT, 256:272], t1.ap[0:TT, 0:16], t1.ap[0:TT, 16:32], ALU.subtract, reads=[t1.k], writes=[ckvf.k])
            P.tt("dve", ckvf.ap[0:TT, 272:288], t1.ap[0:TT, 32:48], t1.ap[0:TT, 48:64], ALU.add, reads=[t1.k], writes=[ckvf.k])
            if prompt:
                r0 = tok0 + ti * 128
                P.dma("sp", o_pkv[l, r0:r0 + 128, :], ckvf.ap[0:128, 0:256], reads=[ckvf.k], is_output=True)
                P.dma("sp", o_pkr[l, r0:r0 + 128, :], ckvf.ap[0:128, 256:288], reads=[ckvf.k], is_output=True)
            else:
                P.dma("sp", o_skv[l], ckvf.ap[0:32, 0:256], reads=[ckvf.k], is_output=True)
                P.dma("sp", o_skr[l], ckvf.ap[0:32, 256:288], reads=[ckvf.k], is_output=True)
            P.copy("pool", ckvb.ap[0:TT, :], ckvf.ap[0:TT, :], reads=[ckvf.k], writes=[ckvb.k])
            for j, (c0, cn) in enumerate(((0, 128), (128, 128), (256, 32))):
                P.tr(bankb[0:cn, j * 128: j * 128 + TT], ckvb.ap[0:TT, c0:c0 + cn], identb[0:TT, 0:TT],
                     reads=[ckvb.k, identb.k], writes=[bankb.k])
            if prompt:
                P.copy("pool", Vc[l][:, ktile, :], ckvb.ap[:, 0:256], reads=[ckvb.k], writes=[Vc[l].k])
                P.copy("act", kTc[l][:, 0:2, ktile * 128:(ktile + 1) * 128], hv(bankb[:, 0:256], 2),
                       reads=[bankb.k], writes=[kTc[l].k])
                P.copy("act", kTc[l][0:32, 2, ktile * 128:(ktile + 1) * 128], bankb[0:32, 256:384],
                       reads=[bankb.k], writes=[kTc[l].k])
            else:
                P.copy("pool", vn.ap[0:32, :], ckvb.ap[0:32, 0:256], reads=[ckvb.k], writes=[vn.k])
                P.copy("act", knT.v[:, 0:2, :], hv(bankb[:, 0:256], 2)[:, :, 0:32], reads=[bankb.k], writes=[knT.k])
                P.copy("act", knT.v[0:32, 2, :], bankb[0:32, 256:288], reads=[bankb.k], writes=[knT.k])
            pq = bank()
            for kt in range(8):
                P.mm(pq[0:TT, 0:384], xnT[:, kt, tsl], WA[:, kt, 0:384], start=(kt == 0), stop=(kt == 7),
                     reads=[xnT.k, WA.k], writes=[pq.k])
            P.act(junk.ap[0:TT, 0:384], pq[0:TT, 0:384], AF.Square, accum_out=ssq.ap[0:TT, 1:2],
                  reads=[pq.k], writes=[junk.k, ssq.k])
            rstd_inplace(ssq.ap[0:TT, 1:2], 1.0 / 384, [ssq.k])
            P.stt(cqb.ap[0:TT, :], pq[0:TT, 0:384], ssq.ap[0:TT, 1:2], gq_bc.v[0:TT, :], ALU.mult, ALU.mult,
                  reads=[pq.k, ssq.k, gq_bc.k], writes=[cqb.k])
            for j in range(3):
                P.tr(bankb[:, 384 + j * 128: 384 + j * 128 + TT], cqb.ap[0:TT, j * 128:(j + 1) * 128], identb[0:TT, 0:TT],
                     reads=[cqb.k, identb.k], writes=[bankb.k])
            P.copy("act", cqT.v[:, :, 0:TT], hv(bankb[:, 384:768], 3)[:, :, 0:TT], reads=[bankb.k], writes=[cqT.k])
            for hg in range(2):
                pn = bank()
                for hh in range(4):
                    h = hg * 4 + hh
                    for j in range(3):
                        P.mm(pn[0:64, hh * 128: hh * 128 + TT], wuq.v[:, j, h * 128: h * 128 + 64], cqT.v[:, j, 0:TT],
                             start=(j == 0), stop=(j == 2), reads=[wuq.k, cqT.k], writes=[pn.k])
                P.copy("act", qn.v[:, hg * 4:(hg + 1) * 4, :], hv(pn[0:64, :], 4)[:, :, 0:TT], reads=[pn.k], writes=[qn.k])
            for v in range(2):
                for hg in range(2):
                    pr = bank()
                    for hh in range(4):
                        h = hg * 4 + hh
                        c0 = h * 128 + 64 + v * 32
                        for j in range(3):
                            P.mm(pr[0:32, hh * 128: hh * 128 + TT], wuq.v[:, j, c0:c0 + 32],
                                 cqT.v[:, j, 0:TT], start=(j == 0), stop=(j == 2), reads=[wuq.k, cqT.k], writes=[pr.k])
                    tab = ropeq.v[:, v, tsl]
                    P.tt("dve", qr.v[:, v, hg * 4:(hg + 1) * 4, :], hv(pr[0:32, :], 4)[:, :, 0:TT],
                         tab.unsqueeze(1).to_broadcast([32, 4, TT]), ALU.mult, reads=[pr.k, ropeq.k], writes=[qr.k])
            P.tt("pool", qrT.v, qr.v[:, 0, :, :], qr.v[:, 1, :, :], ALU.add, reads=[qr.k], writes=[qrT.k])
            for j in range(2):
                for hg in range(2):
                    pl = bank()
                    for hh in range(4):
                        h = hg * 4 + hh
                        P.mm(pl[:, hh * 128: hh * 128 + TT], wuk.v[:, h, j * 128:(j + 1) * 128], qn.v[:, h, :],
                             reads=[wuk.k, qn.k], writes=[pl.k])
                    P.copy("act" if hg == 0 else "dve", qlT.v[:, j, hg * 4:(hg + 1) * 4, :], hv(pl[:, :], 4)[:, :, 0:TT],
                           reads=[pl.k], writes=[qlT.k])
            dbg_store(f"qlT{l}", qlT.v, [qlT.k])
            dbg_store(f"qrT{l}", qrT.v, [qrT.k])
            P.tag = 'mla.attn'
            po = [banks[0], banks[1]]
            pd = banks[2]
            set_rot([3, 4, 5])
            if prompt:
                nkt = ktile + 1
                for hg in range(2):
                    qsl = slice(hg * 4, (hg + 1) * 4)
                    def att_S(kt):
                        pscr = bank()
                        ksl = slice(kt * 128, (kt + 1) * 128)
                        P.mm(pscr[:, :], kTc[l][:, 0, ksl], qlT.v[:, 0, qsl, :], start=True, stop=False,
                             reads=[kTc[l].k, qlT.k], writes=[pscr.k])
                        P.mm(pscr[:, :], kTc[l][:, 1, ksl], qlT.v[:, 1, qsl, :], start=False, stop=False,
                             reads=[kTc[l].k, qlT.k], writes=[pscr.k])
                        P.mm(pscr[:, :], kTc[l][0:32, 2, ksl], qrT.v[0:32, qsl, :], start=False, stop=True,
                             reads=[kTc[l].k, qrT.k], writes=[pscr.k])
                        pt_ = pT[kt % 2]
                        P.act(pt_.ap[:, :], pscr[:, :], AF.Exp, scale=MLA_SCALE, reads=[pscr.k], writes=[pt_.k])
                        if kt == ktile:
                            P.tt("pool", pt_.v, pt_.v, m_causal.unsqueeze(1).to_broadcast([128, 4, 128]), ALU.mult,
                                 reads=[cf.k, pt_.k], writes=[pt_.k])

                    def att_PV(kt):
                        pt_ = pT[kt % 2]
                        for j in range(2):
                            P.mm(po[j][:, :], Vc[l][:, kt, j * 128:(j + 1) * 128], pt_.ap[:, :], start=(kt == 0), stop=(kt == nkt - 1),
                                 reads=[Vc[l].k, pt_.k], writes=[po[j].k])
                        P.mm(pd[:, :], onesb[:, :], pt_.ap[:, :], start=(kt == 0), stop=(kt == nkt - 1),
                             reads=[onesb.k, pt_.k], writes=[pd.k])

                    att_S(0)
                    for kt in range(nkt):
                        if kt + 1 < nkt:
                            att_S(kt + 1)
                        att_PV(kt)
                    P.act(rden.ap[:, :], pd[:, :], AF.Ln, reads=[pd.k], writes=[rden.k])
                    P.act(rden.ap[:, :], rden.ap[:, :], AF.Exp, scale=-1.0, reads=[rden.k], writes=[rden.k])
                    for j in range(2):
                        P.tt("dve", qlT.v[:, j, qsl, :], hv(po[j][:, :], 4), hv(rden.ap[:, :], 4), ALU.mult,
                             reads=[po[j].k, rden.k, qlT.k], writes=[qlT.k])
                for hg in range(2):
                    pm = bank()
                    for hh in range(4):
                        h = hg * 4 + hh
                        for j in range(2):
                            P.mm(pm[0:64, hh * 128:(hh + 1) * 128], wuv.v[:, j, h * 64:(h + 1) * 64], qlT.v[:, j, h, :],
                                 start=(j == 0), stop=(j == 1), reads=[wuv.k, qlT.k], writes=[pm.k])
                    P.tt("dve", yb.v[:, hg * 4:(hg + 1) * 4, tsl], hv(pm[0:64, :], 4), yb.v[:, hg * 4:(hg + 1) * 4, tsl], ALU.mult,
                         reads=[pm.k, yb.k], writes=[yb.k])
            else:
                ckv8 = ckv.rearrange("l n (a t) c -> (l n a) (t c)", t=8)
                ckr8 = ckr.rearrange("l n (a t) c -> (l n a) (t c)", t=8)
                NG = 16
                set_rot([3, 4, 5])
                trbufs = [(bankb, bankb[:, 0:768]), (bankb2, bankb2[:, 0:768])]
                for b in range(4):
                    P.dma("sp", ptb[:, 0:1], ptab[b].rearrange("(p o) -> p o", o=1), writes=[ptb.k])
                    P.stt(ridx[:, 0:16], ptb[:, 0:1].to_broadcast([128, 16]), 16.0, cf[:, 610 + 16 * l:626 + 16 * l], ALU.mult, ALU.add,
                          reads=[ptb.k, cf.k], writes=[ridx.k])
                    P.copy("dve", qc.v.rearrange("p j (h t) -> p j h t", t=8), qlT.v[:, :, :, b * 8:(b + 1) * 8],
                           reads=[qlT.k], writes=[qc.k])
                    P.copy("dve", qrc.ap.rearrange("p (h t) -> p h t", t=8), qrT.v[:, :, b * 8:(b + 1) * 8],
                           reads=[qrT.k], writes=[qrc.k])

                    def issue_dma(g):
                        kb, kr_ = kvb[g % 4], krb[g % 4]
                        P.idma(kb.ap, ckv8, ridx[:, g:g + 1], reads=[ridx.k], writes=[kb.k])
                        P.idma(kr_.ap, ckr8, ridx[:, g:g + 1], reads=[ridx.k], writes=[kr_.k])

                    def T_S(g):
                        kb, kr_ = kvb[g % 4], krb[g % 4]
                        pscr = bank()
                        for pr in range(4):
                            kt_ = kTp[pr]
                            trb, trv = trbufs[pr % 2]
                            for t2 in range(2):
                                t = pr * 2 + t2
                                for j in range(3):
                                    cc = (t2 * 3 + j) * 128
                                    src = kb.v[:, t, j * 128:(j + 1) * 128] if j < 2 else kr_.v[:, t, :]
                                    cn = 128 if j < 2 else 32
                                    P.tr(trv[0:cn, cc:cc + 128], src, identb[:, :],
                                         reads=[kb.k, kr_.k, identb.k], writes=[trb.k])
                            view = trv.rearrange("p (t j c) -> p t j c", t=2, j=3)
                            P.copy("dve", kt_.v[:, :, 0:2, :], view[:, :, 0:2, :], reads=[trb.k], writes=[kt_.k])
                            P.copy("act", kt_.v[0:32, :, 2, :], view[0:32, :, 2, :], reads=[trb.k], writes=[kt_.k])
                            for t2 in range(2):
                                t = pr * 2 + t2
                                osl = slice(t * 64, (t + 1) * 64)
                                P.mm(pscr[:, osl], kt_.v[:, t2, 0, :], qc.v[:, 0, :], start=True, stop=False, reads=[kt_.k, qc.k], writes=[pscr.k])
                                P.mm(pscr[:, osl], kt_.v[:, t2, 1, :], qc.v[:, 1, :], start=False, stop=False, reads=[kt_.k, qc.k], writes=[pscr.k])
                                P.mm(pscr[:, osl], kt_.v[0:32, t2, 2, :], qrc.ap[0:32, :], start=False, stop=True, reads=[kt_.k, qrc.k], writes=[pscr.k])
                        pt_ = pts[g % 2]
                        P.act(pt_.ap[:, :], pscr[:, :], AF.Exp, scale=MLA_SCALE, reads=[pscr.k], writes=[pt_.k])

                    def PVg(g):
                        kb = kvb[g % 4]
                        pt_ = pts[g % 2]
                        for t in range(8):
                            first = (g == 0 and t == 0)
                            osl = slice(t * 64, (t + 1) * 64)
                            for j in range(2):
                                P.mm(po[j][:, 0:64], kb.v[:, t, j * 128:(j + 1) * 128], pt_.ap[:, osl], start=first, stop=False,
                                     reads=[kb.k, pt_.k], writes=[po[j].k])
                            P.mm(pd[:, 0:64], onesb[:, :], pt_.ap[:, osl], start=first, stop=False,
                                 reads=[onesb.k, pt_.k], writes=[pd.k])

                    for g in range(3):
                        issue_dma(g)
                    if l == 0 and b == 0:
                        dbg_store("ridx", ridx[:, 0:16], [ridx.k])
                        dbg_store("kvb0", kvb[0].v, [kvb[0].k])
                    for g in range(NG):
                        T_S(g)
                        if l == 0 and b == 0 and g == 0:
                            dbg_store("pts0", pts[0].ap, [pts[0].k])
                            dbg_store("kTp0", kTp[0].v, [kTp[0].k])
                        if g > 0:
                            PVg(g - 1)
                        if g + 3 < NG:
                            issue_dma(g + 3)
                    PVg(NG - 1)
                    pscr = bank()
                    k0, k1, k2 = knT.v[:, 0, b * 8:(b + 1) * 8], knT.v[:, 1, b * 8:(b + 1) * 8], knT.v[0:32, 2, b * 8:(b + 1) * 8]
                    P.dma("sp", vb8.ap[0:8, :], vn.ap[b * 8:(b + 1) * 8, :], reads=[vn.k], writes=[vb8.k])
                    P.mm(pscr[0:8, 0:64], k0, qc.v[:, 0, :], start=True, stop=False, reads=[knT.k, qc.k], writes=[pscr.k])
                    P.mm(pscr[0:8, 0:64], k1, qc.v[:, 1, :], start=False, stop=False, reads=[knT.k, qc.k], writes=[pscr.k])
                    P.mm(pscr[0:8, 0:64], k2, qrc.ap[0:32, :], start=False, stop=True, reads=[knT.k, qrc.k], writes=[pscr.k])
                    P.act(ptn.ap[0:8, :], pscr[0:8, 0:64], AF.Exp, scale=MLA_SCALE, reads=[pscr.k], writes=[ptn.k])
                    P.tt("pool", hv(ptn.ap[0:8, :], 8), hv(ptn.ap[0:8, :], 8),
                         m_causal8.unsqueeze(1).to_broadcast([8, 8, 8]), ALU.mult, reads=[cf.k, ptn.k], writes=[ptn.k])
                    for j in range(2):
                        P.mm(po[j][:, 0:64], vb8.ap[0:8, j * 128:(j + 1) * 128], ptn.ap[0:8, :], start=False, stop=True,
                             reads=[vb8.k, ptn.k], writes=[po[j].k])
                    P.mm(pd[:, 0:64], onesb[0:8, :], ptn.ap[0:8, :], start=False, stop=True,
                         reads=[onesb.k, ptn.k], writes=[pd.k])
                    P.act(rden.ap[:, 0:64], pd[:, 0:64], AF.Ln, reads=[pd.k], writes=[rden.k])
                    P.act(rden.ap[:, 0:64], rden.ap[:, 0:64], AF.Exp, scale=-1.0, reads=[rden.k], writes=[rden.k])
                    for j in range(2):
                        P.tt("dve", ols.v[:, j, :], po[j][:, 0:64], rden.ap[:, 0:64], ALU.mult,
                             reads=[po[j].k, rden.k], writes=[ols.k])
                    pm = bank()
                    for h in range(8):
                        for j in range(2):
                            P.mm(pm[0:64, h * 8:(h + 1) * 8], wuv.v[:, j, h * 64:(h + 1) * 64], ols.v[:, j, h * 8:(h + 1) * 8],
                                 start=(j == 0), stop=(j == 1), reads=[wuv.k, ols.k], writes=[pm.k])
                    P.tt("dve", yb.v[:, :, b * 8:(b + 1) * 8], hv(pm[0:64, 0:64], 8), yb.v[:, :, b * 8:(b + 1) * 8], ALU.mult,
                         reads=[pm.k, yb.k], writes=[yb.k])
        set_rot(range(6))
        dbg_store(f"yb{l}", yb.v, [yb.k])
        merge_branch(cfg, l, 1, yb.v, yb.k, w_br_mla, per_head=True)

    def stage_dn(cfg, l):
        new_stage()
        P.tag = 'dn.d1'
        NT, B, Ls, prompt, ck, C = cfg["NT"], cfg["B"], cfg["Ls"], cfg["prompt"], cfg["ck"], cfg["C"]
        NSUB = NT // C
        LV = int(np.log2(C))
        W = 3 + Ls
        HG = 8
        NHG = 8 // HG
        HW_ = HG * 64
        do_out = (not prompt) or ck == NCH - 1
        P.dma("sp", cw[:, :, :], conv_w[l], writes=[cw.k])
        P.dma("sp", gdn[:, :], dn_norm_g[l], writes=[gdn.k])
        P.dma("sp", a_bc[:, :], a_log[l].partition_broadcast(64), writes=[a_bc.k])
        P.dma("sp", dtb_bc[:, :], dt_bias[l].partition_broadcast(64), writes=[dtb_bc.k])
        nega = af([64, 8], "nega")
        P.act(nega.ap, a_bc[:, :], AF.Exp, reads=[a_bc.k], writes=[nega.k])
        P.ts("dve", nega.ap, nega.ap, -1.0, None, ALU.mult, reads=[nega.k], writes=[nega.k])
        yc = ab([64, 8, NT], "yc")
        extc2 = [af([64, B, W], f"extc{i}") for i in range(2)]
        if not prompt:
            hs = af([64, 24, 4, 3], "hs")
            stgc = af([3, 1536], "stgc")
            for b in range(4):
                P.dma("sp", stgc.ap[0:3, :], sconv[l, b], writes=[stgc.k])
                pt = bank()
                for ht in range(24):
                    P.tr(pt[0:64, ht * 3:(ht + 1) * 3], stgc.ap[0:3, ht * 64:(ht + 1) * 64], identf[0:3, 0:3],
                         reads=[stgc.k, cf.k], writes=[pt.k])
                P.copy("act", hs.v[:, :, b, :], hv(pt[0:64, 0:72], 24), reads=[pt.k], writes=[hs.k])
        ost = [af([3, 64], f"ost{i}") for i in range(2)]
        qkvb = [ab([64, HG, NT], f"qkvb{i}") for i in range(3)]
        cacc2 = [af([64, NT], f"cacc{i}") for i in range(2)]
        sq2 = [af([64, NT], f"sq{i}") for i in range(2)]
        names = ["Gb", "dgb", "E", "E1", "kbg", "qg", "Q0", "qkT", "P0", "TT", "Qb", "Pb"]
        tmp = {n: af([64, HG, 64], n) for n in names}
        tmp["vb"] = tmp["Gb"]
        tmp["kd"] = tmp["dgb"]
        tmp["R"] = tmp["E1"]
        tmp["vnw"] = tmp["Qb"]
        tmp["osq"] = tmp["Q0"]
        kbT = ab([64, HG, 64], "kbT")
        beta = af([64, HG], "beta")
        gg = af([64, HG], "gg")
        gc = af([64, HG], "gc")
        elast = af([64, HG], "elast")
        edl = af([64, HG], "edl")
        Ssm = af([64, HG, 64], "Ssm") if not prompt else None
        n_ost = 0
        for hg in range(NHG):
            hsl = slice(hg * HG, (hg + 1) * HG)
            P.tag = 'dn.d1'
            for which in range(3):
                load_w_in(WA, l, C_QKV + which * 512 + hg * HW_, HW_)
                for hh in range(HG):
                    h = hg * HG + hh
                    ht = which * 8 + h
                    extc, cacc, sq = extc2[hh % 2], cacc2[hh % 2], sq2[hh % 2]
                    pp = bank()
                    fm_proj(pp[0:64, 0:NT], WA, hh * 64, 64, NT, WA.k, pp.k)
                    if prompt:
                        P.copy("pool", extc.v[:, 0, 0:3], hist_conv[l][:, ht, :], reads=[hist_conv[l].k], writes=[extc.k])
                    else:
                        P.copy("pool", extc.v[:, :, 0:3], hs.v[:, ht, :, :], reads=[hs.k], writes=[extc.k])
                    P.copy("act", extc.v[:, :, 3:W], pp[0:64, 0:NT].rearrange("p (b t) -> p b t", b=B), reads=[pp.k], writes=[extc.k])
                    if prompt:
                        P.copy("pool", hist_conv[l][:, ht, :], extc.v[:, 0, Ls:Ls + 3], reads=[extc.k], writes=[hist_conv[l].k])
                    if do_out:
                        for b in range(B):
                            pt = bank()
                            P.tr(pt[0:3, 0:64], extc.v[:, b, Ls:Ls + 3], identf[0:64, 0:64], reads=[extc.k, cf.k], writes=[pt.k])
                            o_ = ost[n_ost % 2]
                            n_ost += 1
                            P.copy("act", o_.ap[0:3, :], pt[0:3, 0:64], reads=[pt.k], writes=[o_.k])
                            dst = o_pconv[l] if prompt else o_sconv[l, b]
                            P.dma("sp", dst[:, ht * 64:(ht + 1) * 64], o_.ap[0:3, :], reads=[o_.k], is_output=True)
                    caccv = cacc.ap[:, 0:NT].rearrange("p (b t) -> p b t", b=B)
                    P.ts("dve", caccv, extc.v[:, :, 0:Ls], cw[:, ht, 0:1], None, ALU.mult, reads=[extc.k, cw.k], writes=[cacc.k])
                    for j in range(1, 4):
                        P.stt(caccv, extc.v[:, :, j:j + Ls], cw[:, ht, j:j + 1], caccv, ALU.mult, ALU.add,
                              reads=[extc.k, cw.k, cacc.k], writes=[cacc.k])
                    if which == 2:
                        P.act(qkvb[2].v[:, hh, :], cacc.ap[:, 0:NT], AF.Silu, reads=[cacc.k], writes=[qkvb[2].k])
                    else:
                        P.act(cacc.ap[:, 0:NT], cacc.ap[:, 0:NT], AF.Silu, reads=[cacc.k], writes=[cacc.k])
                        P.act(sq.ap[:, 0:NT], cacc.ap[:, 0:NT], AF.Square, reads=[cacc.k], writes=[sq.k])
                        pss = bank()
                        P.mm(pss[0:64, 0:NT], onesf[0:64, 0:64], sq.ap[:, 0:NT], reads=[cf.k, sq.k], writes=[pss.k])
                        P.ts("dve", sq.ap[:, 0:NT], pss[0:64, 0:NT], EPS, None, ALU.add, reads=[pss.k], writes=[sq.k])
                        P.act(sq.ap[:, 0:NT], sq.ap[:, 0:NT], AF.Ln, reads=[sq.k], writes=[sq.k])
                        P.act(sq.ap[:, 0:NT], sq.ap[:, 0:NT], AF.Exp, scale=-0.5, reads=[sq.k], writes=[sq.k])
                        if l == 0 and hg == 0 and which == 1 and hh == 0:
                            dbg_store("rs", sq.ap[:, 0:NT], [sq.k])
                            dbg_store("cs", cacc.ap[:, 0:NT], [cacc.k])
                        if which == 0:
                            P.stt(qkvb[0].v[:, hh, :], cacc.ap[:, 0:NT], 0.125, sq.ap[:, 0:NT], ALU.mult, ALU.mult,
                                  reads=[cacc.k, sq.k], writes=[qkvb[0].k])
                        else:
                            P.tt("dve", qkvb[1].v[:, hh, :], cacc.ap[:, 0:NT], sq.ap[:, 0:NT], ALU.mult,
                                 reads=[cacc.k, sq.k], writes=[qkvb[1].k])
            load_w_in(WB, l, C_ZDN + hg * HW_, HW_)
            load_w_in(WA, l, C_BETA, 16, dcol=0)
            for hh in range(HG):
                pz = bank()
                fm_proj(pz[0:64, 0:NT], WB, hh * 64, 64, NT, WB.k, pz.k)
                P.act(yc.v[:, hg * HG + hh, :], pz[0:64, 0:NT], AF.Silu, reads=[pz.k], writes=[yc.k])
            Gb, dgb, E, E1, kbg, qg = (tmp[n] for n in ("Gb", "dgb", "E", "E1", "kbg", "qg"))
            Q0, qkT, P0, TT_, Qb, Pb = (tmp[n] for n in ("Q0", "qkT", "P0", "TT", "Qb", "Pb"))
            vb, kd, R, vnw, osq = (tmp[n] for n in ("vb", "kd", "R", "vnw", "osq"))
            for s in range(NSUB if DN_NSUB is None else DN_NSUB):
                cs = slice(s * C, (s + 1) * C)
                bseq = s
                if not prompt:
                    P.dma("sp", Ssm.v, sdelta[l, bseq, hsl].rearrange("h k v -> k h v"), writes=[Ssm.k])
                    Sv, Sk = Ssm.v, Ssm.k
                else:
                    Sv, Sk = S_p[l][:, hsl, :], S_p[l].k
                P.tag = 'dn.pre'
                pbg = bank()
                for kt in range(8):
                    P.mm(pbg[0:C, 0:16], xnT[:, kt, cs], WA[:, kt, 0:16], start=(kt == 0), stop=(kt == 7),
                         reads=[xnT.k, WA.k], writes=[pbg.k])
                P.act(beta.ap[0:C, :], pbg[0:C, hg * HG:(hg + 1) * HG], AF.Sigmoid, reads=[pbg.k], writes=[beta.k])
                P.tt("dve", gg.ap[0:C, :], pbg[0:C, 8 + hg * HG: 8 + (hg + 1) * HG], dtb_bc[0:C, hsl], ALU.add, reads=[pbg.k, dtb_bc.k], writes=[gg.k])
                P.act(gg.ap[0:C, :], gg.ap[0:C, :], AF.Exp, reads=[gg.k], writes=[gg.k])
                P.ts("dve", gg.ap[0:C, :], gg.ap[0:C, :], 1.0, None, ALU.add, reads=[gg.k], writes=[gg.k])
                P.act(gg.ap[0:C, :], gg.ap[0:C, :], AF.Ln, reads=[gg.k], writes=[gg.k])
                P.tt("dve", gg.ap[0:C, :], gg.ap[0:C, :], nega.ap[0:C, hsl], ALU.mult, reads=[gg.k, nega.k], writes=[gg.k])
                pg1 = bank()
                P.mm(pg1[0:C, 0:HG], m_incl[0:C, 0:C], gg.ap[0:C, :], reads=[cf.k, gg.k], writes=[pg1.k])
                P.mm(pg1[0:64, 8:8 + HG], onesf[0:C, 0:64], gg.ap[0:C, :], reads=[cf.k, gg.k], writes=[pg1.k])
                P.copy("act", gc.ap[0:C, :], pg1[0:C, 0:HG], reads=[pg1.k], writes=[gc.k])
                P.act(elast.ap[:, :], pg1[0:64, 8:8 + HG], AF.Exp, reads=[pg1.k], writes=[elast.k])
                P.tt("dve", edl.ap[0:C, :], pg1[0:C, 8:8 + HG], gc.ap[0:C, :], ALU.subtract, reads=[pg1.k, gc.k], writes=[edl.k])
                P.act(edl.ap[0:C, :], edl.ap[0:C, :], AF.Exp, reads=[edl.k], writes=[edl.k])
                if DN_CUT <= 1:
                    continue
                P.copy("act", Gb.v[0:C, :, :], gg.ap[0:C, :].unsqueeze(2).to_broadcast([C, HG, 64]), reads=[gg.k], writes=[Gb.k])
                if DN_CUT <= 1.2:
                    continue
                P.tt("pool", dgb.v[0:C, :, 0:C], identf[0:C, 0:C].unsqueeze(1).to_broadcast([C, HG, C]),
                     beta.ap[0:C, :].unsqueeze(2).to_broadcast([C, HG, C]), ALU.mult, reads=[cf.k, beta.k], writes=[dgb.k])
                if DN_CUT <= 1.4:
                    continue
                pgcb = bank()
                pbb = bank()
                for hh in range(HG):
                    P.mm(pgcb[0:64, hh * 64: hh * 64 + C], Gb.v[0:C, hh, :], m_incl[0:C, 0:C], reads=[Gb.k, cf.k], writes=[pgcb.k])
                    P.mm(pbb[0:64, hh * 64: hh * 64 + C], onesf[0:C, 0:64], dgb.v[0:C, hh, 0:C], reads=[cf.k, dgb.k], writes=[pbb.k])
                if DN_CUT <= 1.6:
                    continue
                gcbv = hv(pgcb[0:64, 0:HW_], HG)
                pbbv = hv(pbb[0:64, 0:HW_], HG)
                P.act(E.v[:, :, 0:C], gcbv[:, :, 0:C], AF.Exp, reads=[pgcb.k], writes=[E.k])
                if DN_CUT <= 1.8:
                    continue
                P.op("act", lambda e, o=dgb.v, i=Gb.v, c=C: e.mul(out=o[0:c, :, :], in_=i[0:c, :, :], mul=-1.0), [Gb.k, dgb.k], [dgb.k])
                pdf = bank()
                for hh in range(HG):
                    P.mm(pdf[0:C, hh * 64: hh * 64 + C], Gb.v[0:C, hh, 0:C], m_incl[0:C, 0:C], start=True, stop=False,
                         reads=[Gb.k, cf.k], writes=[pdf.k])
                    P.mm(pdf[0:C, hh * 64: hh * 64 + C], m_incl[0:C, 0:C], dgb.v[0:C, hh, 0:C], start=False, stop=True,
                         reads=[dgb.k, cf.k], writes=[pdf.k])
                P.ts("dve", E1.v[0:C, :, 0:C], hv(pdf[0:64, 0:HW_], HG)[0:C, :, 0:C], 0.0, None, ALU.min, reads=[pdf.k], writes=[E1.k])
                if DN_CUT <= 1.9:
                    continue
                P.act(E1.v[0:C, :, 0:C], E1.v[0:C, :, 0:C], AF.Exp, reads=[E1.k], writes=[E1.k])
                if DN_CUT <= 2:
                    continue
                kTs = qkvb[1].v[:, :, cs]
                qTs = qkvb[0].v[:, :, cs]
                P.tt("dve", kbg.v[:, :, 0:C], kTs, pbbv[:, :, 0:C], ALU.mult, reads=[qkvb[1].k, pbb.k], writes=[kbg.k])
                P.copy("act", kbT.v[:, :, 0:C], kbg.v[:, :, 0:C], reads=[kbg.k], writes=[kbT.k])
                P.tt("dve", kbg.v[:, :, 0:C], kbg.v[:, :, 0:C], E.v[:, :, 0:C], ALU.mult, reads=[kbg.k, E.k, kbT.k], writes=[kbg.k])
                P.tt("pool", qg.v[:, :, 0:C], qTs, E.v[:, :, 0:C], ALU.mult, reads=[qkvb[0].k, E.k], writes=[qg.k])
                pkk = bank()
                pqk = bank()
                for hh in range(HG):
                    P.mm(pkk[0:C, hh * 64: hh * 64 + C], qkvb[1].v[:, hh, cs], kbT.v[:, hh, 0:C], reads=[qkvb[1].k, kbT.k], writes=[pkk.k])
                    P.mm(pqk[0:C, hh * 64: hh * 64 + C], qkvb[1].v[:, hh, cs], qkvb[0].v[:, hh, cs], reads=[qkvb[1].k, qkvb[0].k], writes=[pqk.k])
                P.tt("dve", Q0.v[0:C, :, 0:C], hv(pkk[0:64, 0:HW_], HG)[0:C, :, 0:C], E1.v[0:C, :, 0:C], ALU.mult, reads=[pkk.k, E1.k], writes=[Q0.k])
                P.tt("pool", Q0.v[0:C, :, 0:C], Q0.v[0:C, :, 0:C], m_nstrict[0:C, 0:C].unsqueeze(1).to_broadcast([C, HG, C]), ALU.mult,
                     reads=[Q0.k, cf.k], writes=[Q0.k])
                P.tt("dve", qkT.v[0:C, :, 0:C], hv(pqk[0:64, 0:HW_], HG)[0:C, :, 0:C], E1.v[0:C, :, 0:C], ALU.mult, reads=[pqk.k, E1.k], writes=[qkT.k])
                P.tt("pool", qkT.v[0:C, :, 0:C], qkT.v[0:C, :, 0:C], m_incl[0:C, 0:C].unsqueeze(1).to_broadcast([C, HG, C]), ALU.mult,
                     reads=[qkT.k, cf.k], writes=[qkT.k])
                if DN_CUT <= 3:
                    continue
                ptp = bank()
                for hh in range(HG):
                    P.tr(ptp[0:C, hh * 64: hh * 64 + C], Q0.v[0:C, hh, 0:C], identf[0:C, 0:C], reads=[Q0.k, cf.k], writes=[ptp.k])
                P.copy("act", P0.v[0:C, :, 0:C], hv(ptp[0:64, 0:HW_], HG)[0:C, :, 0:C], reads=[ptp.k], writes=[P0.k])
                P.tt("dve", TT_.v[0:C, :, 0:C], Q0.v[0:C, :, 0:C], identf[0:C, 0:C].unsqueeze(1).to_broadcast([C, HG, C]), ALU.add,
                     reads=[Q0.k, cf.k], writes=[TT_.k])
                if DN_CUT <= 4:
                    continue
                P.tag = 'dn.neu'
                Qa, Pa, Qn_, Pn_ = Q0, P0, Qb, Pb
                for lv in range(LV - 1):
                    pq2 = bank()
                    pp2 = bank()
                    lastlv = (lv == LV - 2)
                    for hh in range(HG):
                        if not lastlv:
                            P.mm(pq2[0:C, hh * 64: hh * 64 + C], Pa.v[0:C, hh, 0:C], Qa.v[0:C, hh, 0:C], reads=[Pa.k, Qa.k], writes=[pq2.k])
                        P.mm(pp2[0:C, hh * 64: hh * 64 + C], Qa.v[0:C, hh, 0:C], Pa.v[0:C, hh, 0:C], reads=[Pa.k, Qa.k], writes=[pp2.k])
                    P.copy("act", Pn_.v[0:C, :, 0:C], hv(pp2[0:64, 0:HW_], HG)[0:C, :, 0:C], reads=[pp2.k], writes=[Pn_.k])
                    if not lastlv:
                        P.copy("dve", Qn_.v[0:C, :, 0:C], hv(pq2[0:64, 0:HW_], HG)[0:C, :, 0:C], reads=[pq2.k], writes=[Qn_.k])
                    pt2 = bank()
                    for hh in range(HG):
                        P.mm(pt2[0:C, hh * 64: hh * 64 + C], Pn_.v[0:C, hh, 0:C], TT_.v[0:C, hh, 0:C], reads=[Pn_.k, TT_.k], writes=[pt2.k])
                    P.tt("dve", TT_.v[0:C, :, 0:C], TT_.v[0:C, :, 0:C], hv(pt2[0:64, 0:HW_], HG)[0:C, :, 0:C], ALU.add,
                         reads=[pt2.k, TT_.k], writes=[TT_.k])
                    Qa, Qn_ = Qn_, Qa
                    Pa, Pn_ = Pn_, Pa
                if DN_CUT <= 5:
                    continue
                P.tag = 'dn.scan'
                for hh in range(HG):
                    P.tr(bankb[0:C, hh * 64:(hh + 1) * 64], qkvb[2].v[:, hh, cs], identb[0:64, 0:64], reads=[qkvb[2].k, identb.k], writes=[bankb.k])
                    P.tr(bankb[0:C, HW_ + hh * 64: HW_ + (hh + 1) * 64], qkvb[1].v[:, hh, cs], identb[0:64, 0:64], reads=[qkvb[1].k, identb.k], writes=[bankb.k])
                P.tt("dve", vb.v[0:C, :, :], hv(bankb[0:C, 0:HW_], HG),
                     beta.ap[0:C, :].unsqueeze(2).to_broadcast([C, HG, 64]), ALU.mult, reads=[bankb.k, beta.k], writes=[vb.k])
                P.tt("dve", kd.v[0:C, :, :], hv(bankb[0:C, HW_:2 * HW_], HG),
                     edl.ap[0:C, :].unsqueeze(2).to_broadcast([C, HG, 64]), ALU.mult, reads=[bankb.k, edl.k], writes=[kd.k])
                if DN_CUT <= 6:
                    continue
                if l == 0 and hg == 0 and s == DBG_S:
                    dbg_store("gg", gg.ap[0:C, :], [gg.k])
                    dbg_store("beta", beta.ap[0:C, :], [beta.k])
                    dbg_store("gc", gc.ap[0:C, :], [gc.k])
                    dbg_store("E1", E1.v[0:C, :, 0:C], [E1.k])
                    dbg_store("Q0", Q0.v[0:C, :, 0:C], [Q0.k])
                    dbg_store("qkT", qkT.v[0:C, :, 0:C], [qkT.k])
                    dbg_store("TT", TT_.v[0:C, :, 0:C], [TT_.k])
                    dbg_store("kbg", kbg.v[:, :, 0:C], [kbg.k])
                    dbg_store("qg", qg.v[:, :, 0:C], [qg.k])
                    dbg_store("vb", vb.v[0:C, :, :], [vb.k])
                    dbg_store("kd", kd.v[0:C, :, :], [kd.k])
                pR = bank()
                for hh in range(HG):
                    P.mm(pR[0:C, hh * 64:(hh + 1) * 64], kbg.v[:, hh, 0:C], Sv[:, hh, :], reads=[kbg.k, Sk], writes=[pR.k])
                P.tt("dve", R.v[0:C, :, :], vb.v[0:C, :, :], hv(pR[0:C, 0:HW_], HG), ALU.subtract,
                     reads=[vb.k, pR.k], writes=[R.k])
                pvn = bank()
                for hh in range(HG):
                    P.mm(pvn[0:C, hh * 64:(hh + 1) * 64], TT_.v[0:C, hh, 0:C], R.v[0:C, hh, :], reads=[TT_.k, R.k], writes=[pvn.k])
                P.copy("act", vnw.v[0:C, :, :], hv(pvn[0:C, 0:HW_], HG), reads=[pvn.k], writes=[vnw.k])
                if DN_CUT <= 7:
                    continue
                po_ = bank()
                for hh in range(HG):
                    P.mm(po_[0:64, hh * 64: hh * 64 + C], Sv[:, hh, :], qg.v[:, hh, 0:C], start=True, stop=False,
                         reads=[Sk, qg.k], writes=[po_.k])
                    P.mm(po_[0:64, hh * 64: hh * 64 + C], vnw.v[0:C, hh, :], qkT.v[0:C, hh, 0:C], start=False, stop=True,
                         reads=[vnw.k, qkT.k], writes=[po_.k])
                pS = bank()
                for hh in range(HG):
                    P.mm(pS[0:64, hh * 64:(hh + 1) * 64], kd.v[0:C, hh, :], vnw.v[0:C, hh, :], reads=[kd.k, vnw.k], writes=[pS.k])
                for hh in range(HG):
                    P.ts("dve", Sv[:, hh, :], Sv[:, hh, :], elast.ap[:, hh:hh + 1], None, ALU.mult, reads=[Sk, elast.k], writes=[Sk])
                P.tt("dve", Sv, Sv, hv(pS[0:64, 0:HW_], HG), ALU.add, reads=[pS.k, Sk], writes=[Sk])
                if DN_CUT <= 8:
                    continue
                if l == 0 and hg == 0 and s == DBG_S:
                    dbg_store("R", R.v[0:C, :, :], [R.k])
                    dbg_store("vnw", vnw.v[0:C, :, :], [vnw.k])
                    dbg_store("Snew", Sv, [Sk])
                ov = hv(po_[0:64, 0:HW_], HG)
                P.act(osq.v[:, :, 0:C], ov[:, :, 0:C], AF.Square, reads=[po_.k], writes=[osq.k])
                pn2 = bank()
                for hh in range(HG):
                    P.mm(pn2[0:64, hh * 64: hh * 64 + C], onesf[0:64, 0:64], osq.v[:, hh, 0:C], reads=[cf.k, osq.k], writes=[pn2.k])
                P.ts("dve", osq.v[:, :, 0:C], hv(pn2[0:64, 0:HW_], HG)[:, :, 0:C], 1.0 / 64, EPS, ALU.mult, ALU.add, reads=[pn2.k], writes=[osq.k])
                P.act(osq.v[:, :, 0:C], osq.v[:, :, 0:C], AF.Ln, reads=[osq.k], writes=[osq.k])
                P.act(osq.v[:, :, 0:C], osq.v[:, :, 0:C], AF.Exp, scale=-0.5, reads=[osq.k], writes=[osq.k])
                P.stt(osq.v[:, :, 0:C], ov[:, :, 0:C], gdn[:, 0:1], osq.v[:, :, 0:C], ALU.mult, ALU.mult,
                      reads=[po_.k, gdn.k, osq.k], writes=[osq.k])
                P.tt("dve", yc.v[:, hsl, cs], osq.v[:, :, 0:C], yc.v[:, hsl, cs], ALU.mult, reads=[osq.k, yc.k], writes=[yc.k])
                if not prompt:
                    P.dma("sp", o_sdelta[l, bseq, hsl].rearrange("h k v -> k h v"), Sv, reads=[Sk], is_output=True)
            if prompt and ck == NCH - 1:
                P.dma("sp", o_pdelta[l, hsl].rearrange("h k v -> k h v"), S_p[l][:, hsl, :],
                      reads=[S_p[l].k], is_output=True)
        dbg_store(f"yc{l}", yc.v, [yc.k])
        new_stage(reset_b=False)
        merge_branch(cfg, l, 2, yc.v, yc.k, w_br_dn, per_head=True)

    def stage_final(cfg):
        new_stage()
        P.tag = 'final'
        NT, TT, NTI, prompt, ck = cfg["NT"], cfg["TT"], cfg["NTI"], cfg["prompt"], cfg["ck"]
        P.dma("sp", gn_bc[:, :], final_norm_g.partition_broadcast(128), writes=[gn_bc.k])
        junk = af([128, D], "junkf")
        ssq = af([128, 4], "ssqf")
        yo = [af([128, D], f"yo{i}") for i in range(2)]
        for ti in range(NTI):
            xt = x_sb[0:TT, ti, :]
            P.act(junk.ap[0:TT, :], xt, AF.Square, accum_out=ssq.ap[0:TT, ti:ti + 1], reads=[x_sb.k], writes=[junk.k, ssq.k])
            rstd_inplace(ssq.ap[0:TT, ti:ti + 1], 1.0 / D, [ssq.k])
            y = yo[ti % 2]
            P.stt(y.ap[0:TT, :], xt, ssq.ap[0:TT, ti:ti + 1], gn_bc[0:TT, :], ALU.mult, ALU.mult,
                  reads=[x_sb.k, ssq.k, gn_bc.k], writes=[y.k])
            if prompt:
                r0 = ck * CH + ti * 128
                P.dma("sp", y_p[r0:r0 + 128, :], y.ap[0:128, :], reads=[y.k], is_output=True)
            else:
                P.dma("sp", y_s, y.ap[0:32, :], reads=[y.k], is_output=True)

    cfgs = []
    if not sample_only:
        for ck in range(prompt_chunks):
            cfgs.append(dict(NT=CH, TT=128, NTI=4, B=1, Ls=CH, prompt=True, ck=ck, C=64))
    if not prompt_only:
        cfgs.append(dict(NT=32, TT=32, NTI=1, B=4, Ls=8, prompt=False, ck=0, C=8))
    for cfg in cfgs:
        new_stage()
        if cfg["prompt"]:
            r0 = cfg["ck"] * CH
            P.dma("sp", x_sb[:, :, :], xp[r0:r0 + CH, :].rearrange("(t p) d -> p t d", p=128), writes=[x_sb.k])
        else:
            P.dma("sp", x_sb[0:32, 0, :], xs, writes=[x_sb.k])
        for l in range(DEPTH):
            if "norm" not in skip:
                stage_norm(cfg, l)
            if "pool" in stages:
                stage_pool(cfg, l)
            if "mla" in stages:
                stage_mla(cfg, l)
            if "dn" in stages:
                stage_dn(cfg, l)
            if "out" not in skip:
                stage_out(cfg, l)
        if "final" not in skip:
            stage_final(cfg)
    P.fence()
    P.emit(sems, slot_sems)
    es.close()
    return nc, P


def _prep_inputs(inp):
    global _CONST
    if _CONST is None:
        _CONST = _consts()
    f32 = np.float32
    w_uq = np.asarray(inp["w_uq"], f32).reshape(DEPTH, 384, H, 96)
    rope = w_uq[..., 64:96]
    rope_sw = np.concatenate([rope[..., 16:32], rope[..., 0:16]], -1)
    w_uq_ext = np.ascontiguousarray(np.concatenate([w_uq, rope_sw], -1).reshape(DEPTH, 384, H * 128))
    shared = {
        "ckv": np.asarray(inp["cache_kv_latent"], f32),
        "ckr": np.asarray(inp["cache_k_rope"], f32),
        "norm_g": np.asarray(inp["norm_g"], f32),
        "w_in": np.asarray(inp["w_in"], f32),
        "pool_mix": np.ascontiguousarray(np.asarray(inp["pool_mix"], f32).transpose(0, 2, 1, 3)),
        "pool_scale": np.ascontiguousarray(np.asarray(inp["pool_scale"], f32).reshape(DEPTH, 4, 128).transpose(0, 2, 1)),
        "q_norm_g": np.asarray(inp["q_norm_g"], f32),
        "w_uq": w_uq_ext,
        "kv_norm_g": np.asarray(inp["kv_norm_g"], f32),
        "w_ukT": np.ascontiguousarray(np.asarray(inp["w_uk"], f32).transpose(0, 3, 2, 1)),
        "w_uv": np.ascontiguousarray(np.asarray(inp["w_uv"], f32).reshape(DEPTH, 256, H * 64)),
        "conv_w": np.ascontiguousarray(np.asarray(inp["conv_w"], f32).reshape(DEPTH, 4, 24, 64).transpose(0, 3, 2, 1)),
        "a_log": np.asarray(inp["a_log"], f32),
        "dt_bias": np.asarray(inp["dt_bias"], f32),
        "dn_norm_g": np.ascontiguousarray(np.asarray(inp["dn_norm_g"], f32).reshape(DEPTH, 64, 1)),
        "w_br_pool": np.asarray(inp["w_br_pool"], f32),
        "w_br_mla": np.asarray(inp["w_br_mla"], f32),
        "w_br_dn": np.asarray(inp["w_br_dn"], f32),
        "w_out": np.asarray(inp["w_out"], f32),
        "final_norm_g": np.asarray(inp["final_norm_g"], f32),
        "cf": _CONST["cf"], "ropeq": _CONST["ropeq"], "ropek": _CONST["ropek"],
    }
    xp = np.asarray(inp["x_prompt"], f32)
    xs = np.asarray(inp["x_sample"], f32)
    sp = np.asarray(inp["state_pool"], f32)
    sc = np.asarray(inp["state_conv"], f32)
    sd = np.asarray(inp["state_delta"], f32)
    pt = np.asarray(inp["page_table"], np.int32)
    in_maps = []
    for c in range(NCORE):
        m = dict(shared)
        m["xp"] = np.ascontiguousarray(xp[c])
        m["xs"] = np.ascontiguousarray(xs[4 * c:4 * c + 4].reshape(32, D))
        m["spool"] = np.ascontiguousarray(sp[:, 4 * c:4 * c + 4])
        m["sconv"] = np.ascontiguousarray(sc[:, 4 * c:4 * c + 4])
        m["sdelta"] = np.ascontiguousarray(sd[:, 4 * c:4 * c + 4])
        m["ptab"] = np.ascontiguousarray(pt[4 * c:4 * c + 4])
        in_maps.append(m)
    return in_maps


_NC = None


def kernel(**inputs):
    global _NC
    in_maps = _prep_inputs(inputs)
    if _NC is None:
        _NC = build()[0]
    res = run_bass_kernel_spmd(_NC, in_maps, core_ids=list(range(NCORE)))
    r = res.results
    cat = lambda k: np.stack([r[c][k] for c in range(NCORE)], 0)
    y_p = cat("y_p")
    y_s = np.concatenate([r[c]["y_s"].reshape(4, 8, D) for c in range(NCORE)], 0)
    p_kv = np.stack([r[c]["o_pkv"] for c in range(NCORE)], 1)
    p_kr = np.stack([r[c]["o_pkr"] for c in range(NCORE)], 1)
    p_pool = np.stack([r[c]["o_ppool"] for c in range(NCORE)], 1)
    p_conv = np.stack([r[c]["o_pconv"] for c in range(NCORE)], 1)
    p_delta = np.stack([r[c]["o_pdelta"] for c in range(NCORE)], 1)
    s_kv = np.concatenate([r[c]["o_skv"].reshape(DEPTH, 4, 8, 256) for c in range(NCORE)], 1)
    s_kr = np.concatenate([r[c]["o_skr"].reshape(DEPTH, 4, 8, 32) for c in range(NCORE)], 1)
    s_pool = np.concatenate([r[c]["o_spool"] for c in range(NCORE)], 1)
    s_conv = np.concatenate([r[c]["o_sconv"] for c in range(NCORE)], 1)
    s_delta = np.concatenate([r[c]["o_sdelta"] for c in range(NCORE)], 1)
    outs = (y_p, y_s, p_kv, p_kr, p_pool, p_conv, p_delta, s_kv, s_kr, s_pool, s_conv, s_delta)
    return tuple(np.ascontiguousarray(o, dtype=np.float32) for o in outs)
```

```python
import contextlib
import numpy as np
import concourse.bass as bass
import concourse.mybir as mybir
from concourse.bass_utils import run_bass_kernel_spmd

F32 = mybir.dt.float32
BF16 = mybir.dt.bfloat16
I32 = mybir.dt.int32
AF = mybir.ActivationFunctionType
ALU = mybir.AluOpType

D = 1024
SEQ = 2048
DEPTH = 2
EPS = 1e-6
NPAGE = 128
H = 8
MLA_SCALE = 96 ** -0.5
NCORE = 8
CH = 512
NCH = SEQ // CH
C_POOL, C_ZPOOL, C_Q, C_KV, C_KR, C_ZMLA, C_QKV, C_ZDN, C_BETA, C_ALPHA, C_GATE = (
    0, 512, 1024, 1408, 1664, 1696, 2208, 3744, 4256, 4264, 4272)
INW = 7344


class Tk:
    __slots__ = ("w", "r", "name")

    def __init__(self, name=""):
        self.w = {}
        self.r = {}
        self.name = name


class Op:
    __slots__ = ("fn", "waits", "signal", "dma", "tag")

    def __init__(self, fn, dma=None):
        self.fn = fn
        self.waits = []
        self.signal = False
        self.dma = dma
        self.tag = None


STREAMS = ("pe", "act", "dve", "pool", "sp")
NSLOT = {"sp": 28, "act": 8, "pool": 24}


class Prog:
    def __init__(self, nc):
        self.nc = nc
        self.ops = {s: [] for s in STREAMS}
        self.seen_c = {s: {} for s in STREAMS}
        self.seen_d = {s: {} for s in STREAMS}
        self.slot_next = {s: 0 for s in NSLOT}
        self.slot_val = {}
        self.out_dma_events = []
        self.pending_dma = {}
        self.last_c = {s: -1 for s in STREAMS}
        self.tag = None
        self.annotate = False

    def _need(self, stream, ev, waits, force_same=False):
        if ev[0] == "c":
            _, e2, idx = ev
            if idx < 0:
                return
            if e2 == stream and stream == "pe" and not force_same:
                return
            if self.seen_c[stream].get(e2, -1) >= idx:
                return
            self.seen_c[stream][e2] = idx
            self.ops[e2][idx].signal = True
            waits.append(ev)
        else:
            _, slot, val = ev
            if self.seen_d[stream].get(slot, 0) >= val:
                return
            self.seen_d[stream][slot] = val
            waits.append(ev)

    def _deps(self, stream, reads, writes, force_same=False):
        waits = []
        for t in reads:
            for ev in t.w.values():
                self._need(stream, ev, waits, force_same)
        for t in writes:
            for ev in t.w.values():
                self._need(stream, ev, waits, force_same)
            for ev in t.r.values():
                self._need(stream, ev, waits, force_same)
        return waits

    def _commit(self, ev, reads, writes):
        key = ev[:2]
        for t in reads:
            t.r[key] = ev
        for t in writes:
            t.w = {key: ev}
            t.r = {}

    def op(self, stream, fn, reads=(), writes=()):
        o = Op(fn)
        o.waits = self._deps(stream, reads, writes)
        idx = len(self.ops[stream])
        o.tag = self.tag
        self.ops[stream].append(o)
        self.last_c[stream] = idx
        self._commit(("c", stream, idx), reads, writes)
        return o

    def dma(self, stream, out, in_, reads=(), writes=(), is_output=False, **kw):
        n = NSLOT[stream]
        k = self.slot_next[stream]
        self.slot_next[stream] = k + 1
        slot = (stream, k % n)
        prev = self.slot_val.get(slot, 0)
        val = prev + 16
        self.slot_val[slot] = val
        o = Op(lambda e: e.dma_start(out=out, in_=in_, **kw), dma=(slot, val))
        o.waits = self._deps(stream, reads, writes, force_same=True)
        o.tag = self.tag
        if prev > 0:
            self._need(stream, ("d", slot, prev), o.waits)
        self.ops[stream].append(o)
        ev = ("d", slot, val)
        self._commit(ev, reads, writes)
        self.pending_dma[slot] = ev
        if is_output:
            self.out_dma_events.append(ev)
        return o

    def idma(self, out, in_, idx_ap, reads=(), writes=()):
        stream = "pool"
        n = NSLOT[stream]
        k = self.slot_next[stream]
        self.slot_next[stream] = k + 1
        slot = (stream, k % n)
        prev = self.slot_val.get(slot, 0)
        val = prev + 16
        self.slot_val[slot] = val
        o = Op(lambda e: e.indirect_dma_start(out=out, out_offset=None, in_=in_,
                                              in_offset=bass.IndirectOffsetOnAxis(ap=idx_ap, axis=0)), dma=(slot, val))
        o.waits = self._deps(stream, reads, writes, force_same=True)
        if prev > 0:
            self._need(stream, ("d", slot, prev), o.waits)
        self.ops[stream].append(o)
        ev = ("d", slot, val)
        self._commit(ev, reads, writes)
        self.pending_dma[slot] = ev
        return o

    def fence(self):
        last = dict(self.last_c)
        pend = list(self.pending_dma.values())
        self.pending_dma = {}
        self._fence_waits = {}
        for a in STREAMS:
            waits = []
            for b in STREAMS:
                if b != a:
                    self._need(a, ("c", b, last[b]), waits)
            for ev in pend:
                self._need(a, ev, waits)
            if waits:
                o = Op(None)
                o.waits = waits
                self.ops[a].append(o)

    def mm(self, out, lhsT, rhs, start=True, stop=True, reads=(), writes=(), **kw):
        return self.op("pe", lambda e: e.matmul(out, lhsT, rhs, start=start, stop=stop, **kw), reads, writes)

    def tr(self, out, in_, ident, reads=(), writes=()):
        return self.op("pe", lambda e: e.transpose(out, in_, ident), reads, writes)

    def act(self, out, in_, func, reads=(), writes=(), **kw):
        return self.op("act", lambda e: e.activation(out=out, in_=in_, func=func, **kw), reads, writes)

    def tt(self, stream, out, in0, in1, op, reads=(), writes=()):
        return self.op(stream, lambda e: e.tensor_tensor(out=out, in0=in0, in1=in1, op=op), reads, writes)

    def ts(self, stream, out, in0, s1, s2, op0, op1=None, reads=(), writes=(), **kw):
        if op1 is None:
            return self.op(stream, lambda e: e.tensor_scalar(out=out, in0=in0, scalar1=s1, scalar2=None, op0=op0, **kw), reads, writes)
        return self.op(stream, lambda e: e.tensor_scalar(out=out, in0=in0, scalar1=s1, scalar2=s2, op0=op0, op1=op1, **kw), reads, writes)

    def stt(self, out, in0, scalar, in1, op0, op1, reads=(), writes=(), **kw):
        return self.op("dve", lambda e: e.scalar_tensor_tensor(out=out, in0=in0, scalar=scalar, in1=in1, op0=op0, op1=op1, **kw), reads, writes)

    def copy(self, stream, out, in_, reads=(), writes=()):
        if stream == "act":
            return self.op("act", lambda e: e.copy(out=out, in_=in_), reads, writes)
        return self.op(stream, lambda e: e.tensor_copy(out=out, in_=in_), reads, writes)

    def memset(self, stream, ap, val, writes=()):
        return self.op(stream, lambda e: e.memset(ap, val), (), writes)

    def emit(self, sems, slot_sems):
        nc = self.nc
        cum = {}
        for s in STREAMS:
            c = 0
            arr = []
            for o in self.ops[s]:
                if o.signal and o.dma is None and o.fn is not None:
                    c += 1
                arr.append(c)
            cum[s] = arr
        final_waits = []
        for ev in self.out_dma_events:
            self._need("sp", ev, final_waits)
        self.n_instr = {s: len(self.ops[s]) for s in STREAMS}

        def run(stream, eng):
            for o in self.ops[stream]:
                for ev in o.waits:
                    if ev[0] == "c":
                        eng.wait_ge(sems[ev[1]], cum[ev[1]][ev[2]])
                    else:
                        eng.wait_ge(slot_sems[ev[1]], ev[2])
                if o.fn is None:
                    continue
                ins = o.fn(eng)
                if self.annotate and o.tag:
                    ins.annotate(o.tag)
                if o.dma is not None:
                    ins.then_inc(slot_sems[o.dma[0]], 16)
                elif o.signal:
                    ins.then_inc(sems[stream], 1)
            if stream == "sp":
                for ev in final_waits:
                    eng.wait_ge(slot_sems[ev[1]], ev[2])

        with nc.Block() as block:
            @block.tensor
            def _(e):
                run("pe", e)

            @block.scalar
            def _(e):
                run("act", e)

            @block.vector
            def _(e):
                run("dve", e)

            @block.gpsimd
            def _(e):
                run("pool", e)

            @block.sync
            def _(e):
                run("sp", e)


class Buf:
    def __init__(self, t, name):
        self.t = t
        self.k = Tk(name)

    def __getitem__(self, key):
        return self.t[key]


def _consts():
    c = {}
    half = 16
    inv = np.power(10000.0, -np.arange(half, dtype=np.float32) / half).astype(np.float32)

    def tabs(pos):
        ang = pos.astype(np.float32)[:, None] * inv[None, :]
        return np.cos(ang).astype(np.float32), np.sin(ang).astype(np.float32)

    posp = np.arange(SEQ)
    poss = 16384 + np.arange(8)
    cp, sp_ = tabs(posp)
    cs, ss = tabs(poss)
    ropeq = np.zeros((32, 2, SEQ + 32), np.float32)
    ropeq[:, 0, :SEQ] = np.concatenate([cp.T, cp.T], 0)
    ropeq[:, 1, :SEQ] = np.concatenate([-sp_.T, sp_.T], 0)
    cs4 = np.tile(cs, (4, 1))
    ss4 = np.tile(ss, (4, 1))
    ropeq[:, 0, SEQ:] = np.concatenate([cs4.T, cs4.T], 0)
    ropeq[:, 1, SEQ:] = np.concatenate([-ss4.T, ss4.T], 0)
    c["ropeq"] = ropeq
    ropek = np.zeros((128, 17, 32), np.float32)
    ropek[:, :16, :16] = cp.reshape(16, 128, 16).transpose(1, 0, 2)
    ropek[:, :16, 16:] = sp_.reshape(16, 128, 16).transpose(1, 0, 2)
    ropek[:32, 16, :16] = cs4
    ropek[:32, 16, 16:] = ss4
    c["ropek"] = ropek
    f = np.zeros((128, 1024), np.float32)
    f[:, 0:128] = np.eye(128)
    f[:, 128:256] = 1.0
    ii = np.arange(64)
    f[:64, 256:320] = (ii[None, :] >= ii[:, None])
    f[:64, 320:384] = -(ii[None, :] > ii[:, None]).astype(np.float32)
    jj = np.arange(128)
    f[:, 384:512] = (jj[:, None] <= jj[None, :])
    t15 = np.arange(15)
    for gi, w in enumerate((2, 4, 8, 16)):
        f[:, 512 + gi * 15: 512 + (gi + 1) * 15] = 1.0 / np.minimum(t15 + 1, w)
    f[:8, 576:584] = (np.arange(8)[:, None] <= np.arange(8)[None, :])
    f[:, 600] = np.arange(128)
    f[:, 601] = np.arange(128) + 5120 * 128
    f[:, 610:626] = np.arange(16)[None, :]
    f[:, 626:642] = np.arange(16)[None, :] + 5120 * 16
    c["cf"] = f
    return c


_CONST = None


DBG_S = 0
DN_NSUB = None
DN_CUT = 99


def build(sample_only=False, prompt_chunks=NCH, dbg=None, stages=("pool", "mla", "dn"), npool=5120, skip=(), prompt_only=False, annotate=False):
    nc = bass.Bass("TRN2", target_bir_lowering=False)
    es = contextlib.ExitStack()

    def din(name, shape, dt=F32):
        return nc.dram_tensor(name, list(shape), dt, kind="ExternalInput").ap()

    def dout(name, shape, dt=F32):
        return nc.dram_tensor(name, list(shape), dt, kind="ExternalOutput").ap()

    xp = din("xp", [SEQ, D])
    xs = din("xs", [32, D])
    ckv = din("ckv", [DEPTH, npool, 128, 256])
    ckr = din("ckr", [DEPTH, npool, 128, 32])
    spool = din("spool", [DEPTH, 4, 15, 512])
    sconv = din("sconv", [DEPTH, 4, 3, 1536])
    sdelta = din("sdelta", [DEPTH, 4, 8, 64, 64])
    ptab = din("ptab", [4, 128], I32)
    norm_g = din("norm_g", [DEPTH, D])
    w_in = din("w_in", [DEPTH, D, INW])
    pool_mix = din("pool_mix", [DEPTH, 128, 4, 128])
    pool_scale = din("pool_scale", [DEPTH, 128, 4])
    q_norm_g = din("q_norm_g", [DEPTH, 384])
    w_uq = din("w_uq", [DEPTH, 384, H * 128])
    kv_norm_g = din("kv_norm_g", [DEPTH, 256])
    w_ukT = din("w_ukT", [DEPTH, 64, H, 256])
    w_uv = din("w_uv", [DEPTH, 256, H * 64])
    conv_w = din("conv_w", [DEPTH, 64, 24, 4])
    a_log = din("a_log", [DEPTH, H])
    dt_bias = din("dt_bias", [DEPTH, H])
    dn_norm_g = din("dn_norm_g", [DEPTH, 64, 1])
    w_br_pool = din("w_br_pool", [DEPTH, 512, D])
    w_br_mla = din("w_br_mla", [DEPTH, 512, D])
    w_br_dn = din("w_br_dn", [DEPTH, 512, D])
    w_out = din("w_out", [DEPTH, D, D])
    final_norm_g = din("final_norm_g", [D])
    cf_d = din("cf", [128, 1024])
    ropeq_d = din("ropeq", [32, 2, SEQ + 32])
    ropek_d = din("ropek", [128, 17, 32])

    y_p = dout("y_p", [SEQ, D])
    y_s = dout("y_s", [32, D])
    o_pkv = dout("o_pkv", [DEPTH, SEQ, 256])
    o_pkr = dout("o_pkr", [DEPTH, SEQ, 32])
    o_ppool = dout("o_ppool", [DEPTH, 15, 512])
    o_pconv = dout("o_pconv", [DEPTH, 3, 1536])
    o_pdelta = dout("o_pdelta", [DEPTH, H, 64, 64])
    o_skv = dout("o_skv", [DEPTH, 32, 256])
    o_skr = dout("o_skr", [DEPTH, 32, 32])
    o_spool = dout("o_spool", [DEPTH, 4, 15, 512])
    o_sconv = dout("o_sconv", [DEPTH, 4, 3, 1536])
    o_sdelta = dout("o_sdelta", [DEPTH, 4, H, 64, 64])
    dbg_out = {}
    if dbg:
        for name, shape in dbg.items():
            dbg_out[name] = dout("dbg_" + name, shape)

    def sb(name, shape, dt=F32):
        return Buf(es.enter_context(nc.sbuf_tensor(name, list(shape), dt)), name)

    def pstile(name, shape, dt=F32):
        return Buf(es.enter_context(nc.psum_tensor(name, list(shape), dt)), name)

    P = Prog(nc)
    P.annotate = annotate

    x_sb = sb("x_sb", [128, 4, D])
    xnT = sb("xnT", [128, 8, CH], BF16)
    mrg = sb("mrg", [128, 8, CH])
    kTc = [sb(f"kTc{l}", [128, 3, SEQ], BF16) for l in range(DEPTH)]
    Vc = [sb(f"Vc{l}", [128, 16, 256], BF16) for l in range(DEPTH)]
    hist_pool = [sb(f"hpool{l}", [128, 4, 15]) for l in range(DEPTH)]
    hist_conv = [sb(f"hconv{l}", [64, 24, 3]) for l in range(DEPTH)]
    S_p = [sb(f"S_p{l}", [64, H, 64]) for l in range(DEPTH)]
    WA = sb("WA", [128, 8, 672], BF16)
    WB = sb("WB", [128, 8, 512], BF16)
    WBR = sb("WBR", [128, 8 * D], BF16)
    mixw = sb("mixw", [128, 4, 128], BF16)
    cw = sb("cw", [64, 24, 4])
    psc = sb("psc", [128, 4])
    gdn = sb("gdn", [64, 1])
    gn_bc = sb("gn_bc", [128, D])
    a_bc = sb("a_bc", [64, H])
    dtb_bc = sb("dtb_bc", [64, H])
    cf = sb("cf_sb", [128, 1024])
    identb = sb("identb", [128, 128], BF16)
    onesb = sb("onesb", [128, 128], BF16)
    ropek = sb("ropek_sb", [128, 17, 32])
    ptb = sb("ptb", [128, 128], I32)
    ridx = sb("ridx", [128, 128], I32)
    AF_N = 10368
    AB_N = 17408
    arena_f = sb("arena_f", [128, AF_N])
    arena_b = sb("arena_b", [128, AB_N], BF16)
    banks = [pstile(f"psf{i}", [128, 512]) for i in range(6)]
    bankb = pstile("psb", [128, 1024], BF16)
    bankb2 = pstile("psb2", [128, 1024], BF16)

    globals()["_SBUF_LEFT"] = nc.sbuf_bytes_remaining
    sems = {s: es.enter_context(nc.semaphore("sem_" + s)) for s in ("pe", "act", "dve", "pool", "sp")}
    slot_sems = {}
    for s, n in NSLOT.items():
        for i in range(n):
            slot_sems[(s, i)] = es.enter_context(nc.semaphore(f"ds_{s}_{i}"))

    identf = cf[:, 0:128]
    onesf = cf[:, 128:256]
    m_incl = cf[0:64, 256:320]
    m_nstrict = cf[0:64, 320:384]
    m_causal = cf[:, 384:512]
    rc15 = cf[:, 512:572]
    m_causal8 = cf[0:8, 576:584]
    iota_p = cf[:, 600:601]

    st = {"af": 0, "ab": 0, "n": 0, "rot": list(range(6)), "ri": 0}

    class AB:
        pass

    def _arena(ar, key, cap, shape, name, even):
        n = int(np.prod(shape[1:]))
        na = (n + 1) // 2 * 2 if even else n
        off = st[key]
        st[key] = off + na
        assert st[key] <= cap, ("arena overflow", key, name, st[key], cap)
        st["n"] += 1
        b = AB()
        b.k = Tk(name or f"{key}{st['n']}")
        b.shape = list(shape)
        flat = ar.t[0:shape[0], off:off + n]
        b.ap = flat
        sh = shape
        if len(sh) == 2:
            b.v = flat
        elif len(sh) == 3:
            b.v = flat.rearrange("p (a b) -> p a b", b=sh[2])
        elif len(sh) == 4:
            b.v = flat.rearrange("p (a b c) -> p a b c", b=sh[2], c=sh[3])
        else:
            raise ValueError
        return b

    def af(shape, name=None):
        return _arena(arena_f, "af", AF_N, shape, name, False)

    def ab(shape, name=None):
        return _arena(arena_b, "ab", AB_N, shape, name, True)

    def afb(shape, name=None):
        n = int(np.prod(shape[1:]))
        nf = (n + 1) // 2
        off = st["af"]
        st["af"] = off + nf
        assert st["af"] <= AF_N, ("arena overflow", "afb", name, st["af"], AF_N)
        b = AB()
        b.k = Tk(name or "afb")
        b.shape = list(shape)
        flat = arena_f.t[0:shape[0], off:off + nf].bitcast(BF16)[:, 0:n]
        b.ap = flat
        sh = shape
        if len(sh) == 2:
            b.v = flat
        elif len(sh) == 3:
            b.v = flat.rearrange("p (a b) -> p a b", b=sh[2])
        else:
            b.v = flat.rearrange("p (a b c) -> p a b c", b=sh[2], c=sh[3])
        return b

    def new_stage(reset_b=True):
        P.fence()
        st["af"] = 0
        if reset_b:
            st["ab"] = 0

    def set_rot(lst):
        st["rot"] = list(lst)
        st["ri"] = 0

    def bank():
        b = banks[st["rot"][st["ri"] % len(st["rot"])]]
        st["ri"] += 1
        return b

    def hv(ap, n, t=None):
        return ap.rearrange("p (h t) -> p h t", h=n)

    P.dma("sp", cf[:, :], cf_d, writes=[cf.k])
    P.dma("sp", ropek[:, :, :], ropek_d, writes=[ropek.k])
    P.copy("dve", identb[:, :], cf[:, 0:128], reads=[cf.k], writes=[identb.k])
    P.copy("dve", onesb[:, :], cf[:, 128:256], reads=[cf.k], writes=[onesb.k])
    for l in range(DEPTH):
        P.memset("pool", hist_pool[l][:, :, :], 0.0, writes=[hist_pool[l].k])
        P.memset("pool", hist_conv[l][:, :, :], 0.0, writes=[hist_conv[l].k])
        P.memset("pool", S_p[l][:, :, :], 0.0, writes=[S_p[l].k])

    def dbg_store(name, ap, reads):
        if name in dbg_out:
            P.dma("pool", dbg_out[name], ap, reads=reads, is_output=True)

    w_in_v = [w_in[l].rearrange("(kt p) n -> p kt n", p=128) for l in range(DEPTH)]

    def load_w_in(dst, l, c0, ncol, dcol=0):
        P.dma("pool", dst[:, :, dcol:dcol + ncol], w_in_v[l][:, :, c0:c0 + ncol], writes=[dst.k])

    def fm_proj(ps_ap, Wb, wcol, M, NT, wk, psk):
        for kt in range(8):
            P.mm(ps_ap, Wb[:, kt, wcol:wcol + M], xnT[:, kt, 0:NT], start=(kt == 0), stop=(kt == 7),
                 reads=[wk, xnT.k], writes=[psk])

    def rstd_inplace(a, mult, keys):
        P.ts("dve", a, a, mult, EPS, ALU.mult, ALU.add, reads=keys, writes=keys)
        P.act(a, a, AF.Ln, reads=keys, writes=keys)
        P.act(a, a, AF.Exp, scale=-0.5, reads=keys, writes=keys)

    def stage_norm(cfg, l):
        new_stage()
        P.tag = 'norm'
        NT, TT, NTI = cfg["NT"], cfg["TT"], cfg["NTI"]
        P.dma("sp", gn_bc[:, :], norm_g[l].partition_broadcast(128), writes=[gn_bc.k])
        junk = af([128, D], "junk")
        ssq = af([128, 4], "ssq")
        xn = ab([128, D], "xn")
        for ti in range(NTI):
            xt = x_sb[0:TT, ti, :]
            P.act(junk.ap[0:TT, :], xt, AF.Square, accum_out=ssq.ap[0:TT, ti:ti + 1],
                  reads=[x_sb.k], writes=[junk.k, ssq.k])
            rstd_inplace(ssq.ap[0:TT, ti:ti + 1], 1.0 / D, [ssq.k])
            P.stt(xn.ap[0:TT, :], xt, ssq.ap[0:TT, ti:ti + 1], gn_bc[0:TT, :], ALU.mult, ALU.mult,
                  reads=[x_sb.k, ssq.k, gn_bc.k], writes=[xn.k])
            for kt in range(8):
                P.tr(bankb[:, kt * 128: kt * 128 + TT], xn.ap[0:TT, kt * 128:(kt + 1) * 128], identb[0:TT, 0:TT],
                     reads=[xn.k, identb.k], writes=[bankb.k])
            P.copy("act", xnT[:, :, ti * TT:(ti + 1) * TT], hv(bankb[:, :], 8)[:, :, 0:TT],
                   reads=[bankb.k], writes=[xnT.k])
        P.memset("pool", mrg[:, :, 0:NT], 0.0, writes=[mrg.k])
        dbg_store("xnT", xnT[:, :, 0:NT], [xnT.k])

    def merge_branch(cfg, l, bi, yv, yk, w_br, per_head):
        P.tag = 'merge'
        NT = cfg["NT"]
        if per_head:
            wv = WBR[0:64, :].rearrange("p (h n) -> p h n", h=8)
            P.dma("pool", wv, w_br[l].rearrange("(h p) n -> p h n", p=64), writes=[WBR.k])
        else:
            wv = WBR[:, 0:4 * D].rearrange("p (h n) -> p h n", h=4)
            P.dma("pool", wv, w_br[l].rearrange("(kt p) n -> p kt n", p=128), writes=[WBR.k])
        gs = af([128, CH], "gsig")
        Ws = [WB, WA]
        for half in range(2):
            load_w_in(Ws[half], l, C_GATE + bi * D + half * 512, 512)
        for half in range(2):
            Wc = Ws[half]
            for jj in range(4):
                j = half * 4 + jj
                pg = bank()
                fm_proj(pg[:, 0:NT], Wc, jj * 128, 128, NT, Wc.k, pg.k)
                P.act(gs.ap[:, 0:NT], pg[:, 0:NT], AF.Sigmoid, reads=[pg.k], writes=[gs.k])
                pb = bank()
                nk = 8 if per_head else 4
                for kk in range(nk):
                    P.mm(pb[:, 0:NT], wv[:, kk, j * 128:(j + 1) * 128], yv[:, kk, 0:NT],
                         start=(kk == 0), stop=(kk == nk - 1), reads=[WBR.k, yk], writes=[pb.k])
                P.tt("dve", gs.ap[:, 0:NT], gs.ap[:, 0:NT], pb[:, 0:NT], ALU.mult, reads=[gs.k, pb.k], writes=[gs.k])
                P.tt("pool", mrg[:, j, 0:NT], mrg[:, j, 0:NT], gs.ap[:, 0:NT], ALU.add, reads=[gs.k, mrg.k], writes=[mrg.k])

    def stage_out(cfg, l):
        new_stage()
        P.tag = 'out'
        NT, TT, NTI = cfg["NT"], cfg["TT"], cfg["NTI"]
        dbg_store(f"mrg{l}", mrg[:, :, 0:NT], [mrg.k])
        mb = ab([128, 8, NT], "mrgb")
        P.copy("dve", mb.v, mrg[:, :, 0:NT], reads=[mrg.k], writes=[mb.k])
        wo = w_out[l].rearrange("(kt p) n -> p kt n", p=128)
        Ws = [WB, WA]
        for half in range(2):
            P.dma("pool", Ws[half][:, :, 0:512], wo[:, :, half * 512:(half + 1) * 512], writes=[Ws[half].k])
        for half in range(2):
            Wc = Ws[half]
            for ti in range(NTI):
                pb = bank()
                for kt in range(8):
                    P.mm(pb[0:TT, :], mb.v[:, kt, ti * TT:(ti + 1) * TT], Wc[:, kt, 0:512], start=(kt == 0), stop=(kt == 7),
                         reads=[mb.k, Wc.k], writes=[pb.k])
                xsl = x_sb[0:TT, ti, half * 512:(half + 1) * 512]
                P.tt("dve", xsl, xsl, pb[0:TT, :], ALU.add, reads=[pb.k, x_sb.k], writes=[x_sb.k])

    def stage_pool(cfg, l):
        new_stage()
        P.tag = 'pool'
        NT, B, Ls, prompt, ck = cfg["NT"], cfg["B"], cfg["Ls"], cfg["prompt"], cfg["ck"]
        W = 15 + Ls
        load_w_in(WA, l, C_POOL, 512)
        load_w_in(WB, l, C_ZPOOL, 512)
        P.dma("pool", mixw[:, :, :], pool_mix[l], writes=[mixw.k])
        P.dma("sp", psc[:, :], pool_scale[l], writes=[psc.k])
        ext = af([128, 4, B, W], "ext")
        if prompt:
            P.copy("pool", ext.v[:, :, 0, 0:15], hist_pool[l][:, :, :], reads=[hist_pool[l].k], writes=[ext.k])
        else:
            stg = af([15, 4 * 512], "stg")
            for b in range(4):
                P.dma("sp", stg.ap[:, b * 512:(b + 1) * 512], spool[l, b], writes=[stg.k])
            pt = bank()
            for b in range(4):
                for g in range(4):
                    P.tr(pt[:, (b * 4 + g) * 15:(b * 4 + g + 1) * 15], stg.ap[0:15, b * 512 + g * 128: b * 512 + (g + 1) * 128],
                         identf[0:15, 0:15], reads=[stg.k, cf.k], writes=[pt.k])
            P.copy("act", ext.v[:, :, :, 0:15], pt[:, 0:240].rearrange("p (b g t) -> p g b t", b=4, g=4),
                   reads=[pt.k], writes=[ext.k])
        for g in range(4):
            pu = bank()
            fm_proj(pu[:, 0:NT], WA, g * 128, 128, NT, WA.k, pu.k)
            P.copy("act", ext.v[:, g, :, 15:W], pu[:, 0:NT].rearrange("p (b t) -> p b t", b=B), reads=[pu.k], writes=[ext.k])
        if prompt:
            P.copy("pool", hist_pool[l][:, :, :], ext.v[:, :, 0, Ls:Ls + 15], reads=[ext.k], writes=[hist_pool[l].k])
        if (not prompt) or ck == NCH - 1:
            ostg = af([15, 512], "ostg")
            for b in range(B):
                pt = bank()
                for g in range(4):
                    P.tr(pt[0:15, g * 128:(g + 1) * 128], ext.v[:, g, b, Ls:Ls + 15], identf[:, :],
                         reads=[ext.k, cf.k], writes=[pt.k])
                P.copy("act", ostg.ap[0:15, :], pt[0:15, 0:512], reads=[pt.k], writes=[ostg.k])
                dst = o_ppool[l] if prompt else o_spool[l, b]
                P.dma("sp", dst, ostg.ap[0:15, :], reads=[ostg.k], is_output=True)
        wa = af([128, B, W], "wa")
        wb_ = af([128, B, W], "wb")
        dT = ab([128, 4, B, Ls], "dT")
        ya = ab([128, 4, NT], "ya")
        zs = af([128, CH], "zs")
        fx = af([128, 15], "fx")
        for g, wdw in enumerate((2, 4, 8, 16)):
            cur, curk, n = ext.v[:, g, :, :], ext.k, W
            sh = 1
            bufs = [wa, wb_]
            bi = 0
            while sh < wdw:
                o = bufs[bi]
                P.tt("pool", o.v[:, :, 0:n - sh], cur[:, :, sh:n], cur[:, :, 0:n - sh], ALU.add, reads=[curk], writes=[o.k])
                cur, curk, n = o.v, o.k, n - sh
                sh *= 2
                bi ^= 1
            o0 = n - Ls
            P.stt(dT.v[:, g, :, :], cur[:, :, o0:o0 + Ls], 1.0 / wdw, ext.v[:, g, :, 15:W], ALU.mult, ALU.subtract,
                  reads=[curk, ext.k], writes=[dT.k])
            if prompt and ck == 0:
                P.tt("pool", fx.ap, cur[:, 0, o0:o0 + 15], rc15[:, g * 15:(g + 1) * 15], ALU.mult,
                     reads=[curk, cf.k], writes=[fx.k])
                P.tt("dve", dT.v[:, g, 0, 0:15], fx.ap, ext.v[:, g, 0, 15:30], ALU.subtract,
                     reads=[fx.k, ext.k, dT.k], writes=[dT.k])
        for g in range(4):
            p1 = bank()
            P.mm(p1[:, 0:NT], mixw[:, g, :], dT.ap[:, g * NT:(g + 1) * NT], reads=[mixw.k, dT.k], writes=[p1.k])
            p2 = bank()
            fm_proj(p2[:, 0:NT], WB, g * 128, 128, NT, WB.k, p2.k)
            P.act(zs.ap[:, 0:NT], p2[:, 0:NT], AF.Silu, reads=[p2.k], writes=[zs.k])
            P.stt(ya.v[:, g, 0:NT], p1[:, 0:NT], psc[:, g:g + 1], zs.ap[:, 0:NT], ALU.mult, ALU.mult,
                  reads=[p1.k, psc.k, zs.k], writes=[ya.k])
        dbg_store(f"ya{l}", ya.v, [ya.k])
        merge_branch(cfg, l, 0, ya.v, ya.k, w_br_pool, per_head=False)

    def stage_mla(cfg, l):
        new_stage()
        P.tag = 'mla.pre'
        NT, TT, NTI, B, Ls, prompt, ck = cfg["NT"], cfg["TT"], cfg["NTI"], cfg["B"], cfg["Ls"], cfg["prompt"], cfg["ck"]
        tok0 = ck * CH if prompt else 0
        wuq = ab([128, 3, H * 128], "wuq")
        wuk = ab([64, H, 256], "wuk")
        wuv = ab([128, 2, H * 64], "wuv")
        ropeq = af([32, 2, NT], "ropeq")
        gq_bc = af([128, 384], "gq_bc")
        gkv_bc = af([128, 256], "gkv_bc")
        load_w_in(WA, l, C_Q, 672)
        load_w_in(WB, l, C_ZMLA, 512)
        P.dma("pool", wuq.v[:, :, :], w_uq[l].rearrange("(kt p) n -> p kt n", p=128), writes=[wuq.k])
        P.dma("pool", wuk.v[:, :, :], w_ukT[l], writes=[wuk.k])
        P.dma("pool", wuv.v[:, :, :], w_uv[l].rearrange("(kt p) n -> p kt n", p=128), writes=[wuv.k])
        P.dma("sp", gq_bc.v[:, :], q_norm_g[l].partition_broadcast(128), writes=[gq_bc.k])
        P.dma("sp", gkv_bc.v[:, :], kv_norm_g[l].partition_broadcast(128), writes=[gkv_bc.k])
        rq0 = tok0 if prompt else SEQ
        P.dma("sp", ropeq.v[:, :, 0:NT], ropeq_d[:, :, rq0:rq0 + NT], writes=[ropeq.k])
        yb = ab([64, 8, NT], "yb")
        for h in range(8):
            pz = bank()
            fm_proj(pz[0:64, 0:NT], WB, h * 64, 64, NT, WB.k, pz.k)
            P.act(yb.v[:, h, 0:NT], pz[0:64, 0:NT], AF.Silu, reads=[pz.k], writes=[yb.k])
        ckvf = af([128, 288], "ckvf")
        ssq = af([128, 2], "ssq2")
        junk = af([128, 384], "junk2")
        t1 = af([128, 64], "ropetmp")
        qr = af([32, 2, 8, TT], "qr")
        rden = af([128, 4 * TT], "rden")
        ckvb = ab([128, 288], "ckvb")
        cqb = ab([128, 384], "cqb")
        cqT = ab([128, 3, TT], "cqT")
        qn = ab([64, 8, TT], "qn")
        qrT = ab([32, 8, TT], "qrT")
        qlT = ab([128, 2, 8, TT], "qlT")
        if prompt:
            pT = [ab([128, 4, TT], f"pT{i}") for i in range(2)]
        else:
            knT = ab([128, 3, 32], "knT")
            vn = ab([32, 256], "vn")
            kvb = [afb([128, 8, 256], f"kvb{i}") for i in range(4)]
            krb = [afb([128, 8, 32], f"krb{i}") for i in range(4)]
            kTp = [ab([128, 2, 3, 128], f"kTp{i}") for i in range(4)]
            pts = [ab([128, 512], f"pts{i}") for i in range(2)]
            ptn = ab([8, 64], "ptn")
            vb8 = ab([8, 256], "vb8")
            qc = ab([128, 2, 64], "qc")
            qrc = ab([32, 64], "qrc")
            ols = ab([128, 2, 64], "ols")
        for ti in range(NTI):
            set_rot(range(6))
            ktile = (tok0 // 128 + ti) if prompt else 16
            tsl = slice(ti * TT, (ti + 1) * TT)
            P.tag = 'mla.proj'
            pk = bank()
            for kt in range(8):
                P.mm(pk[0:TT, 0:288], xnT[:, kt, tsl], WA[:, kt, 384:672], start=(kt == 0), stop=(kt == 7),
                     reads=[xnT.k, WA.k], writes=[pk.k])
            P.act(junk.ap[0:TT, 0:256], pk[0:TT, 0:256], AF.Square, accum_out=ssq.ap[0:TT, 0:1],
                  reads=[pk.k], writes=[junk.k, ssq.k])
            rstd_inplace(ssq.ap[0:TT, 0:1], 1.0 / 256, [ssq.k])
            P.stt(ckvf.ap[0:TT, 0:256], pk[0:TT, 0:256], ssq.ap[0:TT, 0:1], gkv_bc.v[0:TT, :], ALU.mult, ALU.mult,
                  reads=[pk.k, ssq.k, gkv_bc.k], writes=[ckvf.k])
            cosk = ropek[0:TT, ktile, 0:16]
            sink = ropek[0:TT, ktile, 16:32]
            P.tt("dve", t1.ap[0:TT, 0:16], pk[0:TT, 256:272], cosk, ALU.mult, reads=[pk.k, ropek.k], writes=[t1.k])
            P.tt("dve", t1.ap[0:TT, 16:32], pk[0:TT, 272:288], sink, ALU.mult, reads=[pk.k, ropek.k], writes=[t1.k])
            P.tt("dve", t1.ap[0:TT, 32:48], pk[0:TT, 272:288], cosk, ALU.mult, reads=[pk.k, ropek.k], writes=[t1.k])
            P.tt("dve", t1.ap[0:TT, 48:64], pk[0:TT, 256:272], sink, ALU.mult, reads=[pk.k, ropek.k], writes=[t1.k])
            P.tt("dve", ckvf.ap[0:TT, 256:272], t1.ap[0:TT, 0:16], t1.ap[0:TT, 16:32], ALU.subtract, reads=[t1.k], writes=[ckvf.k])
            P.tt("dve", ckvf.ap[0:TT, 272:288], t1.ap[0:TT, 32:48], t1.ap[0:TT, 48:64], ALU.add, reads=[t1.k], writes=[ckvf.k])
            if prompt:
                r0 = tok0 + ti * 128
                P.dma("sp", o_pkv[l, r0:r0 + 128, :], ckvf.ap[0:128, 0:256], reads=[ckvf.k], is_output=True)
                P.dma("sp", o_pkr[l, r0:r0 + 128, :], ckvf.ap[0:128, 256:288], reads=[ckvf.k], is_output=True)
            else:
                P.dma("sp", o_skv[l], ckvf.ap[0:32, 0:256], reads=[ckvf.k], is_output=True)
                P.dma("sp", o_skr[l], ckvf.ap[0:32, 256:288], reads=[ckvf.k], is_output=True)
            P.copy("pool", ckvb.ap[0:TT, :], ckvf.ap[0:TT, :], reads=[ckvf.k], writes=[ckvb.k])
            for j, (c0, cn) in enumerate(((0, 128), (128, 128), (256, 32))):
                P.tr(bankb[0:cn, j * 128: j * 128 + TT], ckvb.ap[0:TT, c0:c0 + cn], identb[0:TT, 0:TT],
                     reads=[ckvb.k, identb.k], writes=[bankb.k])
            if prompt:
                P.copy("pool", Vc[l][:, ktile, :], ckvb.ap[:, 0:256], reads=[ckvb.k], writes=[Vc[l].k])
                P.copy("act", kTc[l][:, 0:2, ktile * 128:(ktile + 1) * 128], hv(bankb[:, 0:256], 2),
                       reads=[bankb.k], writes=[kTc[l].k])
                P.copy("act", kTc[l][0:32, 2, ktile * 128:(ktile + 1) * 128], bankb[0:32, 256:384],
                       reads=[bankb.k], writes=[kTc[l].k])
            else:
                P.copy("pool", vn.ap[0:32, :], ckvb.ap[0:32, 0:256], reads=[ckvb.k], writes=[vn.k])
                P.copy("act", knT.v[:, 0:2, :], hv(bankb[:, 0:256], 2)[:, :, 0:32], reads=[bankb.k], writes=[knT.k])
                P.copy("act", knT.v[0:32, 2, :], bankb[0:32, 256:288], reads=[bankb.k], writes=[knT.k])
            pq = bank()
            for kt in range(8):
                P.mm(pq[0:TT, 0:384], xnT[:, kt, tsl], WA[:, kt, 0:384], start=(kt == 0), stop=(kt == 7),
                     reads=[xnT.k, WA.k], writes=[pq.k])
            P.act(junk.ap[0:TT, 0:384], pq[0:TT, 0:384], AF.Square, accum_out=ssq.ap[0:TT, 1:2],
                  reads=[pq.k], writes=[junk.k, ssq.k])
            rstd_inplace(ssq.ap[0:TT, 1:2], 1.0 / 384, [ssq.k])
            P.stt(cqb.ap[0:TT, :], pq[0:TT, 0:384], ssq.ap[0:TT, 1:2], gq_bc.v[0:TT, :], ALU.mult, ALU.mult,
                  reads=[pq.k, ssq.k, gq_bc.k], writes=[cqb.k])
            for j in range(3):
                P.tr(bankb[:, 384 + j * 128: 384 + j * 128 + TT], cqb.ap[0:TT, j * 128:(j + 1) * 128], identb[0:TT, 0:TT],
                     reads=[cqb.k, identb.k], writes=[bankb.k])
            P.copy("act", cqT.v[:, :, 0:TT], hv(bankb[:, 384:768], 3)[:, :, 0:TT], reads=[bankb.k], writes=[cqT.k])
            for hg in range(2):
                pn = bank()
                for hh in range(4):
                    h = hg * 4 + hh
                    for j in range(3):
                        P.mm(pn[0:64, hh * 128: hh * 128 + TT], wuq.v[:, j, h * 128: h * 128 + 64], cqT.v[:, j, 0:TT],
                             start=(j == 0), stop=(j == 2), reads=[wuq.k, cqT.k], writes=[pn.k])
                P.copy("act", qn.v[:, hg * 4:(hg + 1) * 4, :], hv(pn[0:64, :], 4)[:, :, 0:TT], reads=[pn.k], writes=[qn.k])
            for v in range(2):
                for hg in range(2):
                    pr = bank()
                    for hh in range(4):
                        h = hg * 4 + hh
                        c0 = h * 128 + 64 + v * 32
                        for j in range(3):
                            P.mm(pr[0:32, hh * 128: hh * 128 + TT], wuq.v[:, j, c0:c0 + 32],
                                 cqT.v[:, j, 0:TT], start=(j == 0), stop=(j == 2), reads=[wuq.k, cqT.k], writes=[pr.k])
                    tab = ropeq.v[:, v, tsl]
                    P.tt("dve", qr.v[:, v, hg * 4:(hg + 1) * 4, :], hv(pr[0:32, :], 4)[:, :, 0:TT],
                         tab.unsqueeze(1).to_broadcast([32, 4, TT]), ALU.mult, reads=[pr.k, ropeq.k], writes=[qr.k])
            P.tt("pool", qrT.v, qr.v[:, 0, :, :], qr.v[:, 1, :, :], ALU.add, reads=[qr.k], writes=[qrT.k])
            for j in range(2):
                for hg in range(2):
                    pl = bank()
                    for hh in range(4):
                        h = hg * 4 + hh
                        P.mm(pl[:, hh * 128: hh * 128 + TT], wuk.v[:, h, j * 128:(j + 1) * 128], qn.v[:, h, :],
                             reads=[wuk.k, qn.k], writes=[pl.k])
                    P.copy("act" if hg == 0 else "dve", qlT.v[:, j, hg * 4:(hg + 1) * 4, :], hv(pl[:, :], 4)[:, :, 0:TT],
                           reads=[pl.k], writes=[qlT.k])
            dbg_store(f"qlT{l}", qlT.v, [qlT.k])
            dbg_store(f"qrT{l}", qrT.v, [qrT.k])
            P.tag = 'mla.attn'
            po = [banks[0], banks[1]]
            pd = banks[2]
            set_rot([3, 4, 5])
            if prompt:
                nkt = ktile + 1
                for hg in range(2):
                    qsl = slice(hg * 4, (hg + 1) * 4)
                    def att_S(kt):
                        pscr = bank()
                        ksl = slice(kt * 128, (kt + 1) * 128)
                        P.mm(pscr[:, :], kTc[l][:, 0, ksl], qlT.v[:, 0, qsl, :], start=True, stop=False,
                             reads=[kTc[l].k, qlT.k], writes=[pscr.k])
                        P.mm(pscr[:, :], kTc[l][:, 1, ksl], qlT.v[:, 1, qsl, :], start=False, stop=False,
                             reads=[kTc[l].k, qlT.k], writes=[pscr.k])
                        P.mm(pscr[:, :], kTc[l][0:32, 2, ksl], qrT.v[0:32, qsl, :], start=False, stop=True,
                             reads=[kTc[l].k, qrT.k], writes=[pscr.k])
                        pt_ = pT[kt % 2]
                        P.act(pt_.ap[:, :], pscr[:, :], AF.Exp, scale=MLA_SCALE, reads=[pscr.k], writes=[pt_.k])
                        if kt == ktile:
                            P.tt("pool", pt_.v, pt_.v, m_causal.unsqueeze(1).to_broadcast([128, 4, 128]), ALU.mult,
                                 reads=[cf.k, pt_.k], writes=[pt_.k])

                    def att_PV(kt):
                        pt_ = pT[kt % 2]
                        for j in range(2):
                            P.mm(po[j][:, :], Vc[l][:, kt, j * 128:(j + 1) * 128], pt_.ap[:, :], start=(kt == 0), stop=(kt == nkt - 1),
                                 reads=[Vc[l].k, pt_.k], writes=[po[j].k])
                        P.mm(pd[:, :], onesb[:, :], pt_.ap[:, :], start=(kt == 0), stop=(kt == nkt - 1),
                             reads=[onesb.k, pt_.k], writes=[pd.k])

                    att_S(0)
                    for kt in range(nkt):
                        if kt + 1 < nkt:
                            att_S(kt + 1)
                        att_PV(kt)
                    P.act(rden.ap[:, :], pd[:, :], AF.Ln, reads=[pd.k], writes=[rden.k])
                    P.act(rden.ap[:, :], rden.ap[:, :], AF.Exp, scale=-1.0, reads=[rden.k], writes=[rden.k])
                    for j in range(2):
                        P.tt("dve", qlT.v[:, j, qsl, :], hv(po[j][:, :], 4), hv(rden.ap[:, :], 4), ALU.mult,
                             reads=[po[j].k, rden.k, qlT.k], writes=[qlT.k])
                for hg in range(2):
                    pm = bank()
                    for hh in range(4):
                        h = hg * 4 + hh
                        for j in range(2):
                            P.mm(pm[0:64, hh * 128:(hh + 1) * 128], wuv.v[:, j, h * 64:(h + 1) * 64], qlT.v[:, j, h, :],
                                 start=(j == 0), stop=(j == 1), reads=[wuv.k, qlT.k], writes=[pm.k])
                    P.tt("dve", yb.v[:, hg * 4:(hg + 1) * 4, tsl], hv(pm[0:64, :], 4), yb.v[:, hg * 4:(hg + 1) * 4, tsl], ALU.mult,
                         reads=[pm.k, yb.k], writes=[yb.k])
            else:
                ckv8 = ckv.rearrange("l n (a t) c -> (l n a) (t c)", t=8)
                ckr8 = ckr.rearrange("l n (a t) c -> (l n a) (t c)", t=8)
                NG = 16
                set_rot([3, 4, 5])
                trbufs = [(bankb, bankb[:, 0:768]), (bankb2, bankb2[:, 0:768])]
                for b in range(4):
                    P.dma("sp", ptb[:, 0:1], ptab[b].rearrange("(p o) -> p o", o=1), writes=[ptb.k])
                    P.stt(ridx[:, 0:16], ptb[:, 0:1].to_broadcast([128, 16]), 16.0, cf[:, 610 + 16 * l:626 + 16 * l], ALU.mult, ALU.add,
                          reads=[ptb.k, cf.k], writes=[ridx.k])
                    P.copy("dve", qc.v.rearrange("p j (h t) -> p j h t", t=8), qlT.v[:, :, :, b * 8:(b + 1) * 8],
                           reads=[qlT.k], writes=[qc.k])
                    P.copy("dve", qrc.ap.rearrange("p (h t) -> p h t", t=8), qrT.v[:, :, b * 8:(b + 1) * 8],
                           reads=[qrT.k], writes=[qrc.k])

                    def issue_dma(g):
                        kb, kr_ = kvb[g % 4], krb[g % 4]
                        P.idma(kb.ap, ckv8, ridx[:, g:g + 1], reads=[ridx.k], writes=[kb.k])
                        P.idma(kr_.ap, ckr8, ridx[:, g:g + 1], reads=[ridx.k], writes=[kr_.k])

                    def T_S(g):
                        kb, kr_ = kvb[g % 4], krb[g % 4]
                        pscr = bank()
                        for pr in range(4):
                            kt_ = kTp[pr]
                            trb, trv = trbufs[pr % 2]
                            for t2 in range(2):
                                t = pr * 2 + t2
                                for j in range(3):
                                    cc = (t2 * 3 + j) * 128
                                    src = kb.v[:, t, j * 128:(j + 1) * 128] if j < 2 else kr_.v[:, t, :]
                                    cn = 128 if j < 2 else 32
                                    P.tr(trv[0:cn, cc:cc + 128], src, identb[:, :],
                                         reads=[kb.k, kr_.k, identb.k], writes=[trb.k])
                            view = trv.rearrange("p (t j c) -> p t j c", t=2, j=3)
                            P.copy("dve", kt_.v[:, :, 0:2, :], view[:, :, 0:2, :], reads=[trb.k], writes=[kt_.k])
                            P.copy("act", kt_.v[0:32, :, 2, :], view[0:32, :, 2, :], reads=[trb.k], writes=[kt_.k])
                            for t2 in range(2):
                                t = pr * 2 + t2
                                osl = slice(t * 64, (t + 1) * 64)
                                P.mm(pscr[:, osl], kt_.v[:, t2, 0, :], qc.v[:, 0, :], start=True, stop=False, reads=[kt_.k, qc.k], writes=[pscr.k])
                                P.mm(pscr[:, osl], kt_.v[:, t2, 1, :], qc.v[:, 1, :], start=False, stop=False, reads=[kt_.k, qc.k], writes=[pscr.k])
                                P.mm(pscr[:, osl], kt_.v[0:32, t2, 2, :], qrc.ap[0:32, :], start=False, stop=True, reads=[kt_.k, qrc.k], writes=[pscr.k])
                        pt_ = pts[g % 2]
                        P.act(pt_.ap[:, :], pscr[:, :], AF.Exp, scale=MLA_SCALE, reads=[pscr.k], writes=[pt_.k])

                    def PVg(g):
                        kb = kvb[g % 4]
                        pt_ = pts[g % 2]
                        for t in range(8):
                            first = (g == 0 and t == 0)
                            osl = slice(t * 64, (t + 1) * 64)
                            for j in range(2):
                                P.mm(po[j][:, 0:64], kb.v[:, t, j * 128:(j + 1) * 128], pt_.ap[:, osl], start=first, stop=False,
                                     reads=[kb.k, pt_.k], writes=[po[j].k])
                            P.mm(pd[:, 0:64], onesb[:, :], pt_.ap[:, osl], start=first, stop=False,
                                 reads=[onesb.k, pt_.k], writes=[pd.k])

                    for g in range(3):
                        issue_dma(g)
                    if l == 0 and b == 0:
                        dbg_store("ridx", ridx[:, 0:16], [ridx.k])
                        dbg_store("kvb0", kvb[0].v, [kvb[0].k])
                    for g in range(NG):
                        T_S(g)
                        if l == 0 and b == 0 and g == 0:
                            dbg_store("pts0", pts[0].ap, [pts[0].k])
                            dbg_store("kTp0", kTp[0].v, [kTp[0].k])
                        if g > 0:
                            PVg(g - 1)
                        if g + 3 < NG:
                            issue_dma(g + 3)
                    PVg(NG - 1)
                    pscr = bank()
                    k0, k1, k2 = knT.v[:, 0, b * 8:(b + 1) * 8], knT.v[:, 1, b * 8:(b + 1) * 8], knT.v[0:32, 2, b * 8:(b + 1) * 8]
                    P.dma("sp", vb8.ap[0:8, :], vn.ap[b * 8:(b + 1) * 8, :], reads=[vn.k], writes=[vb8.k])
                    P.mm(pscr[0:8, 0:64], k0, qc.v[:, 0, :], start=True, stop=False, reads=[knT.k, qc.k], writes=[pscr.k])
                    P.mm(pscr[0:8, 0:64], k1, qc.v[:, 1, :], start=False, stop=False, reads=[knT.k, qc.k], writes=[pscr.k])
                    P.mm(pscr[0:8, 0:64], k2, qrc.ap[0:32, :], start=False, stop=True, reads=[knT.k, qrc.k], writes=[pscr.k])
                    P.act(ptn.ap[0:8, :], pscr[0:8, 0:64], AF.Exp, scale=MLA_SCALE, reads=[pscr.k], writes=[ptn.k])
                    P.tt("pool", hv(ptn.ap[0:8, :], 8), hv(ptn.ap[0:8, :], 8),
                         m_causal8.unsqueeze(1).to_broadcast([8, 8, 8]), ALU.mult, reads=[cf.k, ptn.k], writes=[ptn.k])
                    for j in range(2):
                        P.mm(po[j][:, 0:64], vb8.ap[0:8, j * 128:(j + 1) * 128], ptn.ap[0:8, :], start=False, stop=True,
                             reads=[vb8.k, ptn.k], writes=[po[j].k])
                    P.mm(pd[:, 0:64], onesb[0:8, :], ptn.ap[0:8, :], start=False, stop=True,
                         reads=[onesb.k, ptn.k], writes=[pd.k])
                    P.act(rden.ap[:, 0:64], pd[:, 0:64], AF.Ln, reads=[pd.k], writes=[rden.k])
                    P.act(rden.ap[:, 0:64], rden.ap[:, 0:64], AF.Exp, scale=-1.0, reads=[rden.k], writes=[rden.k])
                    for j in range(2):
                        P.tt("dve", ols.v[:, j, :], po[j][:, 0:64], rden.ap[:, 0:64], ALU.mult,
                             reads=[po[j].k, rden.k], writes=[ols.k])
                    pm = bank()
                    for h in range(8):
                        for j in range(2):
                            P.mm(pm[0:64, h * 8:(h + 1) * 8], wuv.v[:, j, h * 64:(h + 1) * 64], ols.v[:, j, h * 8:(h + 1) * 8],
                                 start=(j == 0), stop=(j == 1), reads=[wuv.k, ols.k], writes=[pm.k])
                    P.tt("dve", yb.v[:, :, b * 8:(b + 1) * 8], hv(pm[0:64, 0:64], 8), yb.v[:, :, b * 8:(b + 1) * 8], ALU.mult,
                         reads=[pm.k, yb.k], writes=[yb.k])
        set_rot(range(6))
        dbg_store(f"yb{l}", yb.v, [yb.k])
        merge_branch(cfg, l, 1, yb.v, yb.k, w_br_mla, per_head=True)

    def stage_dn(cfg, l):
        new_stage()
        P.tag = 'dn.d1'
        NT, B, Ls, prompt, ck, C = cfg["NT"], cfg["B"], cfg["Ls"], cfg["prompt"], cfg["ck"], cfg["C"]
        NSUB = NT // C
        LV = int(np.log2(C))
        W = 3 + Ls
        HG = 8
        NHG = 8 // HG
        HW_ = HG * 64
        do_out = (not prompt) or ck == NCH - 1
        P.dma("sp", cw[:, :, :], conv_w[l], writes=[cw.k])
        P.dma("sp", gdn[:, :], dn_norm_g[l], writes=[gdn.k])
        P.dma("sp", a_bc[:, :], a_log[l].partition_broadcast(64), writes=[a_bc.k])
        P.dma("sp", dtb_bc[:, :], dt_bias[l].partition_broadcast(64), writes=[dtb_bc.k])
        nega = af([64, 8], "nega")
        P.act(nega.ap, a_bc[:, :], AF.Exp, reads=[a_bc.k], writes=[nega.k])
        P.ts("dve", nega.ap, nega.ap, -1.0, None, ALU.mult, reads=[nega.k], writes=[nega.k])
        yc = ab([64, 8, NT], "yc")
        extc2 = [af([64, B, W], f"extc{i}") for i in range(2)]
        if not prompt:
            hs = af([64, 24, 4, 3], "hs")
            stgc = af([3, 1536], "stgc")
            for b in range(4):
                P.dma("sp", stgc.ap[0:3, :], sconv[l, b], writes=[stgc.k])
                pt = bank()
                for ht in range(24):
                    P.tr(pt[0:64, ht * 3:(ht + 1) * 3], stgc.ap[0:3, ht * 64:(ht + 1) * 64], identf[0:3, 0:3],
                         reads=[stgc.k, cf.k], writes=[pt.k])
                P.copy("act", hs.v[:, :, b, :], hv(pt[0:64, 0:72], 24), reads=[pt.k], writes=[hs.k])
        ost = [af([3, 64], f"ost{i}") for i in range(2)]
        qkvb = [ab([64, HG, NT], f"qkvb{i}") for i in range(3)]
        cacc2 = [af([64, NT], f"cacc{i}") for i in range(2)]
        sq2 = [af([64, NT], f"sq{i}") for i in range(2)]
        names = ["Gb", "dgb", "E", "E1", "kbg", "qg", "Q0", "qkT", "P0", "TT", "Qb", "Pb"]
        tmp = {n: af([64, HG, 64], n) for n in names}
        tmp["vb"] = tmp["Gb"]
        tmp["kd"] = tmp["dgb"]
        tmp["R"] = tmp["E1"]
        tmp["vnw"] = tmp["Qb"]
        tmp["osq"] = tmp["Q0"]
        kbT = ab([64, HG, 64], "kbT")
        beta = af([64, HG], "beta")
        gg = af([64, HG], "gg")
        gc = af([64, HG], "gc")
        elast = af([64, HG], "elast")
        edl = af([64, HG], "edl")
        Ssm = af([64, HG, 64], "Ssm") if not prompt else None
        n_ost = 0
        for hg in range(NHG):
            hsl = slice(hg * HG, (hg + 1) * HG)
            P.tag = 'dn.d1'
            for which in range(3):
                load_w_in(WA, l, C_QKV + which * 512 + hg * HW_, HW_)
                for hh in range(HG):
                    h = hg * HG + hh
                    ht = which * 8 + h
                    extc, cacc, sq = extc2[hh % 2], cacc2[hh % 2], sq2[hh % 2]
                    pp = bank()
                    fm_proj(pp[0:64, 0:NT], WA, hh * 64, 64, NT, WA.k, pp.k)
                    if prompt:
                        P.copy("pool", extc.v[:, 0, 0:3], hist_conv[l][:, ht, :], reads=[hist_conv[l].k], writes=[extc.k])
                    else:
                        P.copy("pool", extc.v[:, :, 0:3], hs.v[:, ht, :, :], reads=[hs.k], writes=[extc.k])
                    P.copy("act", extc.v[:, :, 3:W], pp[0:64, 0:NT].rearrange("p (b t) -> p b t", b=B), reads=[pp.k], writes=[extc.k])
                    if prompt:
                        P.copy("pool", hist_conv[l][:, ht, :], extc.v[:, 0, Ls:Ls + 3], reads=[extc.k], writes=[hist_conv[l].k])
                    if do_out:
                        for b in range(B):
                            pt = bank()
                            P.tr(pt[0:3, 0:64], extc.v[:, b, Ls:Ls + 3], identf[0:64, 0:64], reads=[extc.k, cf.k], writes=[pt.k])
                            o_ = ost[n_ost % 2]
                            n_ost += 1
                            P.copy("act", o_.ap[0:3, :], pt[0:3, 0:64], reads=[pt.k], writes=[o_.k])
                            dst = o_pconv[l] if prompt else o_sconv[l, b]
                            P.dma("sp", dst[:, ht * 64:(ht + 1) * 64], o_.ap[0:3, :], reads=[o_.k], is_output=True)
                    caccv = cacc.ap[:, 0:NT].rearrange("p (b t) -> p b t", b=B)
                    P.ts("dve", caccv, extc.v[:, :, 0:Ls], cw[:, ht, 0:1], None, ALU.mult, reads=[extc.k, cw.k], writes=[cacc.k])
                    for j in range(1, 4):
                        P.stt(caccv, extc.v[:, :, j:j + Ls], cw[:, ht, j:j + 1], caccv, ALU.mult, ALU.add,
                              reads=[extc.k, cw.k, cacc.k], writes=[cacc.k])
                    if which == 2:
                        P.act(qkvb[2].v[:, hh, :], cacc.ap[:, 0:NT], AF.Silu, reads=[cacc.k], writes=[qkvb[2].k])
                    else:
                        P.act(cacc.ap[:, 0:NT], cacc.ap[:, 0:NT], AF.Silu, reads=[cacc.k], writes=[cacc.k])
                        P.act(sq.ap[:, 0:NT], cacc.ap[:, 0:NT], AF.Square, reads=[cacc.k], writes=[sq.k])
                        pss = bank()
                        P.mm(pss[0:64, 0:NT], onesf[0:64, 0:64], sq.ap[:, 0:NT], reads=[cf.k, sq.k], writes=[pss.k])
                        P.ts("dve", sq.ap[:, 0:NT], pss[0:64, 0:NT], EPS, None, ALU.add, reads=[pss.k], writes=[sq.k])
                        P.act(sq.ap[:, 0:NT], sq.ap[:, 0:NT], AF.Ln, reads=[sq.k], writes=[sq.k])
                        P.act(sq.ap[:, 0:NT], sq.ap[:, 0:NT], AF.Exp, scale=-0.5, reads=[sq.k], writes=[sq.k])
                        if l == 0 and hg == 0 and which == 1 and hh == 0:
                            dbg_store("rs", sq.ap[:, 0:NT], [sq.k])
                            dbg_store("cs", cacc.ap[:, 0:NT], [cacc.k])
                        if which == 0:
                            P.stt(qkvb[0].v[:, hh, :], cacc.ap[:, 0:NT], 0.125, sq.ap[:, 0:NT], ALU.mult, ALU.mult,
                                  reads=[cacc.k, sq.k], writes=[qkvb[0].k])
                        else:
                            P.tt("dve", qkvb[1].v[:, hh, :], cacc.ap[:, 0:NT], sq.ap[:, 0:NT], ALU.mult,
                                 reads=[cacc.k, sq.k], writes=[qkvb[1].k])
            load_w_in(WB, l, C_ZDN + hg * HW_, HW_)
            load_w_in(WA, l, C_BETA, 16, dcol=0)
            for hh in range(HG):
                pz = bank()
                fm_proj(pz[0:64, 0:NT], WB, hh * 64, 64, NT, WB.k, pz.k)
                P.act(yc.v[:, hg * HG + hh, :], pz[0:64, 0:NT], AF.Silu, reads=[pz.k], writes=[yc.k])
            Gb, dgb, E, E1, kbg, qg = (tmp[n] for n in ("Gb", "dgb", "E", "E1", "kbg", "qg"))
            Q0, qkT, P0, TT_, Qb, Pb = (tmp[n] for n in ("Q0", "qkT", "P0", "TT", "Qb", "Pb"))
            vb, kd, R, vnw, osq = (tmp[n] for n in ("vb", "kd", "R", "vnw", "osq"))
            for s in range(NSUB if DN_NSUB is None else DN_NSUB):
                cs = slice(s * C, (s + 1) * C)
                bseq = s
                if not prompt:
                    P.dma("sp", Ssm.v, sdelta[l, bseq, hsl].rearrange("h k v -> k h v"), writes=[Ssm.k])
                    Sv, Sk = Ssm.v, Ssm.k
                else:
                    Sv, Sk = S_p[l][:, hsl, :], S_p[l].k
                P.tag = 'dn.pre'
                pbg = bank()
                for kt in range(8):
                    P.mm(pbg[0:C, 0:16], xnT[:, kt, cs], WA[:, kt, 0:16], start=(kt == 0), stop=(kt == 7),
                         reads=[xnT.k, WA.k], writes=[pbg.k])
                P.act(beta.ap[0:C, :], pbg[0:C, hg * HG:(hg + 1) * HG], AF.Sigmoid, reads=[pbg.k], writes=[beta.k])
                P.tt("dve", gg.ap[0:C, :], pbg[0:C, 8 + hg * HG: 8 + (hg + 1) * HG], dtb_bc[0:C, hsl], ALU.add, reads=[pbg.k, dtb_bc.k], writes=[gg.k])
                P.act(gg.ap[0:C, :], gg.ap[0:C, :], AF.Exp, reads=[gg.k], writes=[gg.k])
                P.ts("dve", gg.ap[0:C, :], gg.ap[0:C, :], 1.0, None, ALU.add, reads=[gg.k], writes=[gg.k])
                P.act(gg.ap[0:C, :], gg.ap[0:C, :], AF.Ln, reads=[gg.k], writes=[gg.k])
                P.tt("dve", gg.ap[0:C, :], gg.ap[0:C, :], nega.ap[0:C, hsl], ALU.mult, reads=[gg.k, nega.k], writes=[gg.k])
                pg1 = bank()
                P.mm(pg1[0:C, 0:HG], m_incl[0:C, 0:C], gg.ap[0:C, :], reads=[cf.k, gg.k], writes=[pg1.k])
                P.mm(pg1[0:64, 8:8 + HG], onesf[0:C, 0:64], gg.ap[0:C, :], reads=[cf.k, gg.k], writes=[pg1.k])
                P.copy("act", gc.ap[0:C, :], pg1[0:C, 0:HG], reads=[pg1.k], writes=[gc.k])
                P.act(elast.ap[:, :], pg1[0:64, 8:8 + HG], AF.Exp, reads=[pg1.k], writes=[elast.k])
                P.tt("dve", edl.ap[0:C, :], pg1[0:C, 8:8 + HG], gc.ap[0:C, :], ALU.subtract, reads=[pg1.k, gc.k], writes=[edl.k])
                P.act(edl.ap[0:C, :], edl.ap[0:C, :], AF.Exp, reads=[edl.k], writes=[edl.k])
                if DN_CUT <= 1:
                    continue
                P.copy("act", Gb.v[0:C, :, :], gg.ap[0:C, :].unsqueeze(2).to_broadcast([C, HG, 64]), reads=[gg.k], writes=[Gb.k])
                if DN_CUT <= 1.2:
                    continue
                P.tt("pool", dgb.v[0:C, :, 0:C], identf[0:C, 0:C].unsqueeze(1).to_broadcast([C, HG, C]),
                     beta.ap[0:C, :].unsqueeze(2).to_broadcast([C, HG, C]), ALU.mult, reads=[cf.k, beta.k], writes=[dgb.k])
                if DN_CUT <= 1.4:
                    continue
                pgcb = bank()
                pbb = bank()
                for hh in range(HG):
                    P.mm(pgcb[0:64, hh * 64: hh * 64 + C], Gb.v[0:C, hh, :], m_incl[0:C, 0:C], reads=[Gb.k, cf.k], writes=[pgcb.k])
                    P.mm(pbb[0:64, hh * 64: hh * 64 + C], onesf[0:C, 0:64], dgb.v[0:C, hh, 0:C], reads=[cf.k, dgb.k], writes=[pbb.k])
                if DN_CUT <= 1.6:
                    continue
                gcbv = hv(pgcb[0:64, 0:HW_], HG)
                pbbv = hv(pbb[0:64, 0:HW_], HG)
                P.act(E.v[:, :, 0:C], gcbv[:, :, 0:C], AF.Exp, reads=[pgcb.k], writes=[E.k])
                if DN_CUT <= 1.8:
                    continue
                P.op("act", lambda e, o=dgb.v, i=Gb.v, c=C: e.mul(out=o[0:c, :, :], in_=i[0:c, :, :], mul=-1.0), [Gb.k, dgb.k], [dgb.k])
                pdf = bank()
                for hh in range(HG):
                    P.mm(pdf[0:C, hh * 64: hh * 64 + C], Gb.v[0:C, hh, 0:C], m_incl[0:C, 0:C], start=True, stop=False,
                         reads=[Gb.k, cf.k], writes=[pdf.k])
                    P.mm(pdf[0:C, hh * 64: hh * 64 + C], m_incl[0:C, 0:C], dgb.v[0:C, hh, 0:C], start=False, stop=True,
                         reads=[dgb.k, cf.k], writes=[pdf.k])
                P.ts("dve", E1.v[0:C, :, 0:C], hv(pdf[0:64, 0:HW_], HG)[0:C, :, 0:C], 0.0, None, ALU.min, reads=[pdf.k], writes=[E1.k])
                if DN_CUT <= 1.9:
                    continue
                P.act(E1.v[0:C, :, 0:C], E1.v[0:C, :, 0:C], AF.Exp, reads=[E1.k], writes=[E1.k])
                if DN_CUT <= 2:
                    continue
                kTs = qkvb[1].v[:, :, cs]
                qTs = qkvb[0].v[:, :, cs]
                P.tt("dve", kbg.v[:, :, 0:C], kTs, pbbv[:, :, 0:C], ALU.mult, reads=[qkvb[1].k, pbb.k], writes=[kbg.k])
                P.copy("act", kbT.v[:, :, 0:C], kbg.v[:, :, 0:C], reads=[kbg.k], writes=[kbT.k])
                P.tt("dve", kbg.v[:, :, 0:C], kbg.v[:, :, 0:C], E.v[:, :, 0:C], ALU.mult, reads=[kbg.k, E.k, kbT.k], writes=[kbg.k])
                P.tt("pool", qg.v[:, :, 0:C], qTs, E.v[:, :, 0:C], ALU.mult, reads=[qkvb[0].k, E.k], writes=[qg.k])
                pkk = bank()
                pqk = bank()
                for hh in range(HG):
                    P.mm(pkk[0:C, hh * 64: hh * 64 + C], qkvb[1].v[:, hh, cs], kbT.v[:, hh, 0:C], reads=[qkvb[1].k, kbT.k], writes=[pkk.k])
                    P.mm(pqk[0:C, hh * 64: hh * 64 + C], qkvb[1].v[:, hh, cs], qkvb[0].v[:, hh, cs], reads=[qkvb[1].k, qkvb[0].k], writes=[pqk.k])
                P.tt("dve", Q0.v[0:C, :, 0:C], hv(pkk[0:64, 0:HW_], HG)[0:C, :, 0:C], E1.v[0:C, :, 0:C], ALU.mult, reads=[pkk.k, E1.k], writes=[Q0.k])
                P.tt("pool", Q0.v[0:C, :, 0:C], Q0.v[0:C, :, 0:C], m_nstrict[0:C, 0:C].unsqueeze(1).to_broadcast([C, HG, C]), ALU.mult,
                     reads=[Q0.k, cf.k], writes=[Q0.k])
                P.tt("dve", qkT.v[0:C, :, 0:C], hv(pqk[0:64, 0:HW_], HG)[0:C, :, 0:C], E1.v[0:C, :, 0:C], ALU.mult, reads=[pqk.k, E1.k], writes=[qkT.k])
                P.tt("pool", qkT.v[0:C, :, 0:C], qkT.v[0:C, :, 0:C], m_incl[0:C, 0:C].unsqueeze(1).to_broadcast([C, HG, C]), ALU.mult,
                     reads=[qkT.k, cf.k], writes=[qkT.k])
                if DN_CUT <= 3:
                    continue
                ptp = bank()
                for hh in range(HG):
                    P.tr(ptp[0:C, hh * 64: hh * 64 + C], Q0.v[0:C, hh, 0:C], identf[0:C, 0:C], reads=[Q0.k, cf.k], writes=[ptp.k])
                P.copy("act", P0.v[0:C, :, 0:C], hv(ptp[0:64, 0:HW_], HG)[0:C, :, 0:C], reads=[ptp.k], writes=[P0.k])
                P.tt("dve", TT_.v[0:C, :, 0:C], Q0.v[0:C, :, 0:C], identf[0:C, 0:C].unsqueeze(1).to_broadcast([C, HG, C]), ALU.add,
                     reads=[Q0.k, cf.k], writes=[TT_.k])
                if DN_CUT <= 4:
                    continue
                P.tag = 'dn.neu'
                Qa, Pa, Qn_, Pn_ = Q0, P0, Qb, Pb
                for lv in range(LV - 1):
                    pq2 = bank()
                    pp2 = bank()
                    lastlv = (lv == LV - 2)
                    for hh in range(HG):
                        if not lastlv:
                            P.mm(pq2[0:C, hh * 64: hh * 64 + C], Pa.v[0:C, hh, 0:C], Qa.v[0:C, hh, 0:C], reads=[Pa.k, Qa.k], writes=[pq2.k])
                        P.mm(pp2[0:C, hh * 64: hh * 64 + C], Qa.v[0:C, hh, 0:C], Pa.v[0:C, hh, 0:C], reads=[Pa.k, Qa.k], writes=[pp2.k])
                    P.copy("act", Pn_.v[0:C, :, 0:C], hv(pp2[0:64, 0:HW_], HG)[0:C, :, 0:C], reads=[pp2.k], writes=[Pn_.k])
                    if not lastlv:
                        P.copy("dve", Qn_.v[0:C, :, 0:C], hv(pq2[0:64, 0:HW_], HG)[0:C, :, 0:C], reads=[pq2.k], writes=[Qn_.k])
                    pt2 = bank()
                    for hh in range(HG):
                        P.mm(pt2[0:C, hh * 64: hh * 64 + C], Pn_.v[0:C, hh, 0:C], TT_.v[0:C, hh, 0:C], reads=[Pn_.k, TT_.k], writes=[pt2.k])
                    P.tt("dve", TT_.v[0:C, :, 0:C], TT_.v[0:C, :, 0:C], hv(pt2[0:64, 0:HW_], HG)[0:C, :, 0:C], ALU.add,
                         reads=[pt2.k, TT_.k], writes=[TT_.k])
                    Qa, Qn_ = Qn_, Qa
                    Pa, Pn_ = Pn_, Pa
                if DN_CUT <= 5:
                    continue
                P.tag = 'dn.scan'
                for hh in range(HG):
                    P.tr(bankb[0:C, hh * 64:(hh + 1) * 64], qkvb[2].v[:, hh, cs], identb[0:64, 0:64], reads=[qkvb[2].k, identb.k], writes=[bankb.k])
                    P.tr(bankb[0:C, HW_ + hh * 64: HW_ + (hh + 1) * 64], qkvb[1].v[:, hh, cs], identb[0:64, 0:64], reads=[qkvb[1].k, identb.k], writes=[bankb.k])
                P.tt("dve", vb.v[0:C, :, :], hv(bankb[0:C, 0:HW_], HG),
                     beta.ap[0:C, :].unsqueeze(2).to_broadcast([C, HG, 64]), ALU.mult, reads=[bankb.k, beta.k], writes=[vb.k])
                P.tt("dve", kd.v[0:C, :, :], hv(bankb[0:C, HW_:2 * HW_], HG),
                     edl.ap[0:C, :].unsqueeze(2).to_broadcast([C, HG, 64]), ALU.mult, reads=[bankb.k, edl.k], writes=[kd.k])
                if DN_CUT <= 6:
                    continue
                if l == 0 and hg == 0 and s == DBG_S:
                    dbg_store("gg", gg.ap[0:C, :], [gg.k])
                    dbg_store("beta", beta.ap[0:C, :], [beta.k])
                    dbg_store("gc", gc.ap[0:C, :], [gc.k])
                    dbg_store("E1", E1.v[0:C, :, 0:C], [E1.k])
                    dbg_store("Q0", Q0.v[0:C, :, 0:C], [Q0.k])
                    dbg_store("qkT", qkT.v[0:C, :, 0:C], [qkT.k])
                    dbg_store("TT", TT_.v[0:C, :, 0:C], [TT_.k])
                    dbg_store("kbg", kbg.v[:, :, 0:C], [kbg.k])
                    dbg_store("qg", qg.v[:, :, 0:C], [qg.k])
                    dbg_store("vb", vb.v[0:C, :, :], [vb.k])
                    dbg_store("kd", kd.v[0:C, :, :], [kd.k])
                pR = bank()
                for hh in range(HG):
                    P.mm(pR[0:C, hh * 64:(hh + 1) * 64], kbg.v[:, hh, 0:C], Sv[:, hh, :], reads=[kbg.k, Sk], writes=[pR.k])
                P.tt("dve", R.v[0:C, :, :], vb.v[0:C, :, :], hv(pR[0:C, 0:HW_], HG), ALU.subtract,
                     reads=[vb.k, pR.k], writes=[R.k])
                pvn = bank()
                for hh in range(HG):
                    P.mm(pvn[0:C, hh * 64:(hh + 1) * 64], TT_.v[0:C, hh, 0:C], R.v[0:C, hh, :], reads=[TT_.k, R.k], writes=[pvn.k])
                P.copy("act", vnw.v[0:C, :, :], hv(pvn[0:C, 0:HW_], HG), reads=[pvn.k], writes=[vnw.k])
                if DN_CUT <= 7:
                    continue
                po_ = bank()
                for hh in range(HG):
                    P.mm(po_[0:64, hh * 64: hh * 64 + C], Sv[:, hh, :], qg.v[:, hh, 0:C], start=True, stop=False,
                         reads=[Sk, qg.k], writes=[po_.k])
                    P.mm(po_[0:64, hh * 64: hh * 64 + C], vnw.v[0:C, hh, :], qkT.v[0:C, hh, 0:C], start=False, stop=True,
                         reads=[vnw.k, qkT.k], writes=[po_.k])
                pS = bank()
                for hh in range(HG):
                    P.mm(pS[0:64, hh * 64:(hh + 1) * 64], kd.v[0:C, hh, :], vnw.v[0:C, hh, :], reads=[kd.k, vnw.k], writes=[pS.k])
                for hh in range(HG):
                    P.ts("dve", Sv[:, hh, :], Sv[:, hh, :], elast.ap[:, hh:hh + 1], None, ALU.mult, reads=[Sk, elast.k], writes=[Sk])
                P.tt("dve", Sv, Sv, hv(pS[0:64, 0:HW_], HG), ALU.add, reads=[pS.k, Sk], writes=[Sk])
                if DN_CUT <= 8:
                    continue
                if l == 0 and hg == 0 and s == DBG_S:
                    dbg_store("R", R.v[0:C, :, :], [R.k])
                    dbg_store("vnw", vnw.v[0:C, :, :], [vnw.k])
                    dbg_store("Snew", Sv, [Sk])
                ov = hv(po_[0:64, 0:HW_], HG)
                P.act(osq.v[:, :, 0:C], ov[:, :, 0:C], AF.Square, reads=[po_.k], writes=[osq.k])
                pn2 = bank()
                for hh in range(HG):
                    P.mm(pn2[0:64, hh * 64: hh * 64 + C], onesf[0:64, 0:64], osq.v[:, hh, 0:C], reads=[cf.k, osq.k], writes=[pn2.k])
                P.ts("dve", osq.v[:, :, 0:C], hv(pn2[0:64, 0:HW_], HG)[:, :, 0:C], 1.0 / 64, EPS, ALU.mult, ALU.add, reads=[pn2.k], writes=[osq.k])
                P.act(osq.v[:, :, 0:C], osq.v[:, :, 0:C], AF.Ln, reads=[osq.k], writes=[osq.k])
                P.act(osq.v[:, :, 0:C], osq.v[:, :, 0:C], AF.Exp, scale=-0.5, reads=[osq.k], writes=[osq.k])
                P.stt(osq.v[:, :, 0:C], ov[:, :, 0:C], gdn[:, 0:1], osq.v[:, :, 0:C], ALU.mult, ALU.mult,
                      reads=[po_.k, gdn.k, osq.k], writes=[osq.k])
                P.tt("dve", yc.v[:, hsl, cs], osq.v[:, :, 0:C], yc.v[:, hsl, cs], ALU.mult, reads=[osq.k, yc.k], writes=[yc.k])
                if not prompt:
                    P.dma("sp", o_sdelta[l, bseq, hsl].rearrange("h k v -> k h v"), Sv, reads=[Sk], is_output=True)
            if prompt and ck == NCH - 1:
                P.dma("sp", o_pdelta[l, hsl].rearrange("h k v -> k h v"), S_p[l][:, hsl, :],
                      reads=[S_p[l].k], is_output=True)
        dbg_store(f"yc{l}", yc.v, [yc.k])
        new_stage(reset_b=False)
        merge_branch(cfg, l, 2, yc.v, yc.k, w_br_dn, per_head=True)

    def stage_final(cfg):
        new_stage()
        P.tag = 'final'
        NT, TT, NTI, prompt, ck = cfg["NT"], cfg["TT"], cfg["NTI"], cfg["prompt"], cfg["ck"]
        P.dma("sp", gn_bc[:, :], final_norm_g.partition_broadcast(128), writes=[gn_bc.k])
        junk = af([128, D], "junkf")
        ssq = af([128, 4], "ssqf")
        yo = [af([128, D], f"yo{i}") for i in range(2)]
        for ti in range(NTI):
            xt = x_sb[0:TT, ti, :]
            P.act(junk.ap[0:TT, :], xt, AF.Square, accum_out=ssq.ap[0:TT, ti:ti + 1], reads=[x_sb.k], writes=[junk.k, ssq.k])
            rstd_inplace(ssq.ap[0:TT, ti:ti + 1], 1.0 / D, [ssq.k])
            y = yo[ti % 2]
            P.stt(y.ap[0:TT, :], xt, ssq.ap[0:TT, ti:ti + 1], gn_bc[0:TT, :], ALU.mult, ALU.mult,
                  reads=[x_sb.k, ssq.k, gn_bc.k], writes=[y.k])
            if prompt:
                r0 = ck * CH + ti * 128
                P.dma("sp", y_p[r0:r0 + 128, :], y.ap[0:128, :], reads=[y.k], is_output=True)
            else:
                P.dma("sp", y_s, y.ap[0:32, :], reads=[y.k], is_output=True)

    cfgs = []
    if not sample_only:
        for ck in range(prompt_chunks):
            cfgs.append(dict(NT=CH, TT=128, NTI=4, B=1, Ls=CH, prompt=True, ck=ck, C=64))
    if not prompt_only:
        cfgs.append(dict(NT=32, TT=32, NTI=1, B=4, Ls=8, prompt=False, ck=0, C=8))
    for cfg in cfgs:
        new_stage()
        if cfg["prompt"]:
            r0 = cfg["ck"] * CH
            P.dma("sp", x_sb[:, :, :], xp[r0:r0 + CH, :].rearrange("(t p) d -> p t d", p=128), writes=[x_sb.k])
        else:
            P.dma("sp", x_sb[0:32, 0, :], xs, writes=[x_sb.k])
        for l in range(DEPTH):
            if "norm" not in skip:
                stage_norm(cfg, l)
            if "pool" in stages:
                stage_pool(cfg, l)
            if "mla" in stages:
                stage_mla(cfg, l)
            if "dn" in stages:
                stage_dn(cfg, l)
            if "out" not in skip:
                stage_out(cfg, l)
        if "final" not in skip:
            stage_final(cfg)
    P.fence()
    P.emit(sems, slot_sems)
    es.close()
    return nc, P


def _prep_inputs(inp):
    global _CONST
    if _CONST is None:
        _CONST = _consts()
    f32 = np.float32
    w_uq = np.asarray(inp["w_uq"], f32).reshape(DEPTH, 384, H, 96)
    rope = w_uq[..., 64:96]
    rope_sw = np.concatenate([rope[..., 16:32], rope[..., 0:16]], -1)
    w_uq_ext = np.ascontiguousarray(np.concatenate([w_uq, rope_sw], -1).reshape(DEPTH, 384, H * 128))
    shared = {
        "ckv": np.asarray(inp["cache_kv_latent"], f32),
        "ckr": np.asarray(inp["cache_k_rope"], f32),
        "norm_g": np.asarray(inp["norm_g"], f32),
        "w_in": np.asarray(inp["w_in"], f32),
        "pool_mix": np.ascontiguousarray(np.asarray(inp["pool_mix"], f32).transpose(0, 2, 1, 3)),
        "pool_scale": np.ascontiguousarray(np.asarray(inp["pool_scale"], f32).reshape(DEPTH, 4, 128).transpose(0, 2, 1)),
        "q_norm_g": np.asarray(inp["q_norm_g"], f32),
        "w_uq": w_uq_ext,
        "kv_norm_g": np.asarray(inp["kv_norm_g"], f32),
        "w_ukT": np.ascontiguousarray(np.asarray(inp["w_uk"], f32).transpose(0, 3, 2, 1)),
        "w_uv": np.ascontiguousarray(np.asarray(inp["w_uv"], f32).reshape(DEPTH, 256, H * 64)),
        "conv_w": np.ascontiguousarray(np.asarray(inp["conv_w"], f32).reshape(DEPTH, 4, 24, 64).transpose(0, 3, 2, 1)),
        "a_log": np.asarray(inp["a_log"], f32),
        "dt_bias": np.asarray(inp["dt_bias"], f32),
        "dn_norm_g": np.ascontiguousarray(np.asarray(inp["dn_norm_g"], f32).reshape(DEPTH, 64, 1)),
        "w_br_pool": np.asarray(inp["w_br_pool"], f32),
        "w_br_mla": np.asarray(inp["w_br_mla"], f32),
        "w_br_dn": np.asarray(inp["w_br_dn"], f32),
        "w_out": np.asarray(inp["w_out"], f32),
        "final_norm_g": np.asarray(inp["final_norm_g"], f32),
        "cf": _CONST["cf"], "ropeq": _CONST["ropeq"], "ropek": _CONST["ropek"],
    }
    xp = np.asarray(inp["x_prompt"], f32)
    xs = np.asarray(inp["x_sample"], f32)
    sp = np.asarray(inp["state_pool"], f32)
    sc = np.asarray(inp["state_conv"], f32)
    sd = np.asarray(inp["state_delta"], f32)
    pt = np.asarray(inp["page_table"], np.int32)
    in_maps = []
    for c in range(NCORE):
        m = dict(shared)
        m["xp"] = np.ascontiguousarray(xp[c])
        m["xs"] = np.ascontiguousarray(xs[4 * c:4 * c + 4].reshape(32, D))
        m["spool"] = np.ascontiguousarray(sp[:, 4 * c:4 * c + 4])
        m["sconv"] = np.ascontiguousarray(sc[:, 4 * c:4 * c + 4])
        m["sdelta"] = np.ascontiguousarray(sd[:, 4 * c:4 * c + 4])
        m["ptab"] = np.ascontiguousarray(pt[4 * c:4 * c + 4])
        in_maps.append(m)
    return in_maps


_NC = None


def kernel(**inputs):
    global _NC
    in_maps = _prep_inputs(inputs)
    if _NC is None:
        _NC = build()[0]
    res = run_bass_kernel_spmd(_NC, in_maps, core_ids=list(range(NCORE)))
    r = res.results
    cat = lambda k: np.stack([r[c][k] for c in range(NCORE)], 0)
    y_p = cat("y_p")
    y_s = np.concatenate([r[c]["y_s"].reshape(4, 8, D) for c in range(NCORE)], 0)
    p_kv = np.stack([r[c]["o_pkv"] for c in range(NCORE)], 1)
    p_kr = np.stack([r[c]["o_pkr"] for c in range(NCORE)], 1)
    p_pool = np.stack([r[c]["o_ppool"] for c in range(NCORE)], 1)
    p_conv = np.stack([r[c]["o_pconv"] for c in range(NCORE)], 1)
    p_delta = np.stack([r[c]["o_pdelta"] for c in range(NCORE)], 1)
    s_kv = np.concatenate([r[c]["o_skv"].reshape(DEPTH, 4, 8, 256) for c in range(NCORE)], 1)
    s_kr = np.concatenate([r[c]["o_skr"].reshape(DEPTH, 4, 8, 32) for c in range(NCORE)], 1)
    s_pool = np.concatenate([r[c]["o_spool"] for c in range(NCORE)], 1)
    s_conv = np.concatenate([r[c]["o_sconv"] for c in range(NCORE)], 1)
    s_delta = np.concatenate([r[c]["o_sdelta"] for c in range(NCORE)], 1)
    outs = (y_p, y_s, p_kv, p_kr, p_pool, p_conv, p_delta, s_kv, s_kr, s_pool, s_conv, s_delta)
    return tuple(np.ascontiguousarray(o, dtype=np.float32) for o in outs)
```

```python
import contextlib
import numpy as np
import concourse.bass as bass
import concourse.mybir as mybir
from concourse.bass_utils import run_bass_kernel_spmd

F32 = mybir.dt.float32
BF16 = mybir.dt.bfloat16
I32 = mybir.dt.int32
AF = mybir.ActivationFunctionType
ALU = mybir.AluOpType

D = 1024
SEQ = 2048
DEPTH = 2
EPS = 1e-6
NPAGE = 128
H = 8
MLA_SCALE = 96 ** -0.5
NCORE = 8
CH = 512
NCH = SEQ // CH
C_POOL, C_ZPOOL, C_Q, C_KV, C_KR, C_ZMLA, C_QKV, C_ZDN, C_BETA, C_ALPHA, C_GATE = (
    0, 512, 1024, 1408, 1664, 1696, 2208, 3744, 4256, 4264, 4272)
INW = 7344


class Tk:
    __slots__ = ("w", "r", "name")

    def __init__(self, name=""):
        self.w = {}
        self.r = {}
        self.name = name


class Op:
    __slots__ = ("fn", "waits", "signal", "dma", "tag")

    def __init__(self, fn, dma=None):
        self.fn = fn
        self.waits = []
        self.signal = False
        self.dma = dma
        self.tag = None


STREAMS = ("pe", "act", "dve", "pool", "sp")
NSLOT = {"sp": 28, "act": 8, "pool": 24}


class Prog:
    def __init__(self, nc):
        self.nc = nc
        self.ops = {s: [] for s in STREAMS}
        self.seen_c = {s: {} for s in STREAMS}
        self.seen_d = {s: {} for s in STREAMS}
        self.slot_next = {s: 0 for s in NSLOT}
        self.slot_val = {}
        self.out_dma_events = []
        self.pending_dma = {}
        self.last_c = {s: -1 for s in STREAMS}
        self.tag = None
        self.annotate = False

    def _need(self, stream, ev, waits, force_same=False):
        if ev[0] == "c":
            _, e2, idx = ev
            if idx < 0:
                return
            if e2 == stream and stream == "pe" and not force_same:
                return
            if self.seen_c[stream].get(e2, -1) >= idx:
                return
            self.seen_c[stream][e2] = idx
            self.ops[e2][idx].signal = True
            waits.append(ev)
        else:
            _, slot, val = ev
            if self.seen_d[stream].get(slot, 0) >= val:
                return
            self.seen_d[stream][slot] = val
            waits.append(ev)

    def _deps(self, stream, reads, writes, force_same=False):
        waits = []
        for t in reads:
            for ev in t.w.values():
                self._need(stream, ev, waits, force_same)
        for t in writes:
            for ev in t.w.values():
                self._need(stream, ev, waits, force_same)
            for ev in t.r.values():
                self._need(stream, ev, waits, force_same)
        return waits

    def _commit(self, ev, reads, writes):
        key = ev[:2]
        for t in reads:
            t.r[key] = ev
        for t in writes:
            t.w = {key: ev}
            t.r = {}

    def op(self, stream, fn, reads=(), writes=()):
        o = Op(fn)
        o.waits = self._deps(stream, reads, writes)
        idx = len(self.ops[stream])
        o.tag = self.tag
        self.ops[stream].append(o)
        self.last_c[stream] = idx
        self._commit(("c", stream, idx), reads, writes)
        return o

    def dma(self, stream, out, in_, reads=(), writes=(), is_output=False, **kw):
        n = NSLOT[stream]
        k = self.slot_next[stream]
        self.slot_next[stream] = k + 1
        slot = (stream, k % n)
        prev = self.slot_val.get(slot, 0)
        val = prev + 16
        self.slot_val[slot] = val
        o = Op(lambda e: e.dma_start(out=out, in_=in_, **kw), dma=(slot, val))
        o.waits = self._deps(stream, reads, writes, force_same=True)
        o.tag = self.tag
        if prev > 0:
            self._need(stream, ("d", slot, prev), o.waits)
        self.ops[stream].append(o)
        ev = ("d", slot, val)
        self._commit(ev, reads, writes)
        self.pending_dma[slot] = ev
        if is_output:
            self.out_dma_events.append(ev)
        return o

    def idma(self, out, in_, idx_ap, reads=(), writes=()):
        stream = "pool"
        n = NSLOT[stream]
        k = self.slot_next[stream]
        self.slot_next[stream] = k + 1
        slot = (stream, k % n)
        prev = self.slot_val.get(slot, 0)
        val = prev + 16
        self.slot_val[slot] = val
        o = Op(lambda e: e.indirect_dma_start(out=out, out_offset=None, in_=in_,
                                              in_offset=bass.IndirectOffsetOnAxis(ap=idx_ap, axis=0)), dma=(slot, val))
        o.waits = self._deps(stream, reads, writes, force_same=True)
        if prev > 0:
            self._need(stream, ("d", slot, prev), o.waits)
        self.ops[stream].append(o)
        ev = ("d", slot, val)
        self._commit(ev, reads, writes)
        self.pending_dma[slot] = ev
        return o

    def fence(self):
        last = dict(self.last_c)
        pend = list(self.pending_dma.values())
        self.pending_dma = {}
        self._fence_waits = {}
        for a in STREAMS:
            waits = []
            for b in STREAMS:
                if b != a:
                    self._need(a, ("c", b, last[b]), waits)
            for ev in pend:
                self._need(a, ev, waits)
            if waits:
                o = Op(None)
                o.waits = waits
                self.ops[a].append(o)

    def mm(self, out, lhsT, rhs, start=True, stop=True, reads=(), writes=(), **kw):
        return self.op("pe", lambda e: e.matmul(out, lhsT, rhs, start=start, stop=stop, **kw), reads, writes)

    def tr(self, out, in_, ident, reads=(), writes=()):
        return self.op("pe", lambda e: e.transpose(out, in_, ident), reads, writes)

    def act(self, out, in_, func, reads=(), writes=(), **kw):
        return self.op("act", lambda e: e.activation(out=out, in_=in_, func=func, **kw), reads, writes)

    def tt(self, stream, out, in0, in1, op, reads=(), writes=()):
        return self.op(stream, lambda e: e.tensor_tensor(out=out, in0=in0, in1=in1, op=op), reads, writes)

    def ts(self, stream, out, in0, s1, s2, op0, op1=None, reads=(), writes=(), **kw):
        if op1 is None:
            return self.op(stream, lambda e: e.tensor_scalar(out=out, in0=in0, scalar1=s1, scalar2=None, op0=op0, **kw), reads, writes)
        return self.op(stream, lambda e: e.tensor_scalar(out=out, in0=in0, scalar1=s1, scalar2=s2, op0=op0, op1=op1, **kw), reads, writes)

    def stt(self, out, in0, scalar, in1, op0, op1, reads=(), writes=(), **kw):
        return self.op("dve", lambda e: e.scalar_tensor_tensor(out=out, in0=in0, scalar=scalar, in1=in1, op0=op0, op1=op1, **kw), reads, writes)

    def copy(self, stream, out, in_, reads=(), writes=()):
        if stream == "act":
            return self.op("act", lambda e: e.copy(out=out, in_=in_), reads, writes)
        return self.op(stream, lambda e: e.tensor_copy(out=out, in_=in_), reads, writes)

    def memset(self, stream, ap, val, writes=()):
        return self.op(stream, lambda e: e.memset(ap, val), (), writes)

    def emit(self, sems, slot_sems):
        nc = self.nc
        cum = {}
        for s in STREAMS:
            c = 0
            arr = []
            for o in self.ops[s]:
                if o.signal and o.dma is None and o.fn is not None:
                    c += 1
                arr.append(c)
            cum[s] = arr
        final_waits = []
        for ev in self.out_dma_events:
            self._need("sp", ev, final_waits)
        self.n_instr = {s: len(self.ops[s]) for s in STREAMS}

        def run(stream, eng):
            for o in self.ops[stream]:
                for ev in o.waits:
                    if ev[0] == "c":
                        eng.wait_ge(sems[ev[1]], cum[ev[1]][ev[2]])
                    else:
                        eng.wait_ge(slot_sems[ev[1]], ev[2])
                if o.fn is None:
                    continue
                ins = o.fn(eng)
                if self.annotate and o.tag:
                    ins.annotate(o.tag)
                if o.dma is not None:
                    ins.then_inc(slot_sems[o.dma[0]], 16)
                elif o.signal:
                    ins.then_inc(sems[stream], 1)
            if stream == "sp":
                for ev in final_waits:
                    eng.wait_ge(slot_sems[ev[1]], ev[2])

        with nc.Block() as block:
            @block.tensor
            def _(e):
                run("pe", e)

            @block.scalar
            def _(e):
                run("act", e)

            @block.vector
            def _(e):
                run("dve", e)

            @block.gpsimd
            def _(e):
                run("pool", e)

            @block.sync
            def _(e):
                run("sp", e)


class Buf:
    def __init__(self, t, name):
        self.t = t
        self.k = Tk(name)

    def __getitem__(self, key):
        return self.t[key]


def _consts():
    c = {}
    half = 16
    inv = np.power(10000.0, -np.arange(half, dtype=np.float32) / half).astype(np.float32)

    def tabs(pos):
        ang = pos.astype(np.float32)[:, None] * inv[None, :]
        return np.cos(ang).astype(np.float32), np.sin(ang).astype(np.float32)

    posp = np.arange(SEQ)
    poss = 16384 + np.arange(8)
    cp, sp_ = tabs(posp)
    cs, ss = tabs(poss)
    ropeq = np.zeros((32, 2, SEQ + 32), np.float32)
    ropeq[:, 0, :SEQ] = np.concatenate([cp.T, cp.T], 0)
    ropeq[:, 1, :SEQ] = np.concatenate([-sp_.T, sp_.T], 0)
    cs4 = np.tile(cs, (4, 1))
    ss4 = np.tile(ss, (4, 1))
    ropeq[:, 0, SEQ:] = np.concatenate([cs4.T, cs4.T], 0)
    ropeq[:, 1, SEQ:] = np.concatenate([-ss4.T, ss4.T], 0)
    c["ropeq"] = ropeq
    ropek = np.zeros((128, 17, 32), np.float32)
    ropek[:, :16, :16] = cp.reshape(16, 128, 16).transpose(1, 0, 2)
    ropek[:, :16, 16:] = sp_.reshape(16, 128, 16).transpose(1, 0, 2)
    ropek[:32, 16, :16] = cs4
    ropek[:32, 16, 16:] = ss4
    c["ropek"] = ropek
    f = np.zeros((128, 1024), np.float32)
    f[:, 0:128] = np.eye(128)
    f[:, 128:256] = 1.0
    ii = np.arange(64)
    f[:64, 256:320] = (ii[None, :] >= ii[:, None])
    f[:64, 320:384] = -(ii[None, :] > ii[:, None]).astype(np.float32)
    jj = np.arange(128)
    f[:, 384:512] = (jj[:, None] <= jj[None, :])
    t15 = np.arange(15)
    for gi, w in enumerate((2, 4, 8, 16)):
        f[:, 512 + gi * 15: 512 + (gi + 1) * 15] = 1.0 / np.minimum(t15 + 1, w)
    f[:8, 576:584] = (np.arange(8)[:, None] <= np.arange(8)[None, :])
    f[:, 600] = np.arange(128)
    f[:, 601] = np.arange(128) + 5120 * 128
    f[:, 610:626] = np.arange(16)[None, :]
    f[:, 626:642] = np.arange(16)[None, :] + 5120 * 16
    c["cf"] = f
    return c


_CONST = None


DBG_S = 0
DN_NSUB = None
DN_CUT = 99


def build(sample_only=False, prompt_chunks=NCH, dbg=None, stages=("pool", "mla", "dn"), npool=5120, skip=(), prompt_only=False, annotate=False):
    nc = bass.Bass("TRN2", target_bir_lowering=False)
    es = contextlib.ExitStack()

    def din(name, shape, dt=F32):
        return nc.dram_tensor(name, list(shape), dt, kind="ExternalInput").ap()

    def dout(name, shape, dt=F32):
        return nc.dram_tensor(name, list(shape), dt, kind="ExternalOutput").ap()

    xp = din("xp", [SEQ, D])
    xs = din("xs", [32, D])
    ckv = din("ckv", [DEPTH, npool, 128, 256])
    ckr = din("ckr", [DEPTH, npool, 128, 32])
    spool = din("spool", [DEPTH, 4, 15, 512])
    sconv = din("sconv", [DEPTH, 4, 3, 1536])
    sdelta = din("sdelta", [DEPTH, 4, 8, 64, 64])
    ptab = din("ptab", [4, 128], I32)
    norm_g = din("norm_g", [DEPTH, D])
    w_in = din("w_in", [DEPTH, D, INW])
    pool_mix = din("pool_mix", [DEPTH, 128, 4, 128])
    pool_scale = din("pool_scale", [DEPTH, 128, 4])
    q_norm_g = din("q_norm_g", [DEPTH, 384])
    w_uq = din("w_uq", [DEPTH, 384, H * 128])
    kv_norm_g = din("kv_norm_g", [DEPTH, 256])
    w_ukT = din("w_ukT", [DEPTH, 64, H, 256])
    w_uv = din("w_uv", [DEPTH, 256, H * 64])
    conv_w = din("conv_w", [DEPTH, 64, 24, 4])
    a_log = din("a_log", [DEPTH, H])
    dt_bias = din("dt_bias", [DEPTH, H])
    dn_norm_g = din("dn_norm_g", [DEPTH, 64, 1])
    w_br_pool = din("w_br_pool", [DEPTH, 512, D])
    w_br_mla = din("w_br_mla", [DEPTH, 512, D])
    w_br_dn = din("w_br_dn", [DEPTH, 512, D])
    w_out = din("w_out", [DEPTH, D, D])
    final_norm_g = din("final_norm_g", [D])
    cf_d = din("cf", [128, 1024])
    ropeq_d = din("ropeq", [32, 2, SEQ + 32])
    ropek_d = din("ropek", [128, 17, 32])

    y_p = dout("y_p", [SEQ, D])
    y_s = dout("y_s", [32, D])
    o_pkv = dout("o_pkv", [DEPTH, SEQ, 256])
    o_pkr = dout("o_pkr", [DEPTH, SEQ, 32])
    o_ppool = dout("o_ppool", [DEPTH, 15, 512])
    o_pconv = dout("o_pconv", [DEPTH, 3, 1536])
    o_pdelta = dout("o_pdelta", [DEPTH, H, 64, 64])
    o_skv = dout("o_skv", [DEPTH, 32, 256])
    o_skr = dout("o_skr", [DEPTH, 32, 32])
    o_spool = dout("o_spool", [DEPTH, 4, 15, 512])
    o_sconv = dout("o_sconv", [DEPTH, 4, 3, 1536])
    o_sdelta = dout("o_sdelta", [DEPTH, 4, H, 64, 64])
    dbg_out = {}
    if dbg:
        for name, shape in dbg.items():
            dbg_out[name] = dout("dbg_" + name, shape)

    def sb(name, shape, dt=F32):
        return Buf(es.enter_context(nc.sbuf_tensor(name, list(shape), dt)), name)

    def pstile(name, shape, dt=F32):
        return Buf(es.enter_context(nc.psum_tensor(name, list(shape), dt)), name)

    P = Prog(nc)
    P.annotate = annotate

    x_sb = sb("x_sb", [128, 4, D])
    xnT = sb("xnT", [128, 8, CH], BF16)
    mrg = sb("mrg", [128, 8, CH])
    kTc = [sb(f"kTc{l}", [128, 3, SEQ], BF16) for l in range(DEPTH)]
    Vc = [sb(f"Vc{l}", [128, 16, 256], BF16) for l in range(DEPTH)]
    hist_pool = [sb(f"hpool{l}", [128, 4, 15]) for l in range(DEPTH)]
    hist_conv = [sb(f"hconv{l}", [64, 24, 3]) for l in range(DEPTH)]
    S_p = [sb(f"S_p{l}", [64, H, 64]) for l in range(DEPTH)]
    WA = sb("WA", [128, 8, 672], BF16)
    WB = sb("WB", [128, 8, 512], BF16)
    WBR = sb("WBR", [128, 8 * D], BF16)
    mixw = sb("mixw", [128, 4, 128], BF16)
    cw = sb("cw", [64, 24, 4])
    psc = sb("psc", [128, 4])
    gdn = sb("gdn", [64, 1])
    gn_bc = sb("gn_bc", [128, D])
    a_bc = sb("a_bc", [64, H])
    dtb_bc = sb("dtb_bc", [64, H])
    cf = sb("cf_sb", [128, 1024])
    identb = sb("identb", [128, 128], BF16)
    onesb = sb("onesb", [128, 128], BF16)
    ropek = sb("ropek_sb", [128, 17, 32])
    ptb = sb("ptb", [128, 128], I32)
    ridx = sb("ridx", [128, 128], I32)
    AF_N = 10368
    AB_N = 17408
    arena_f = sb("arena_f", [128, AF_N])
    arena_b = sb("arena_b", [128, AB_N], BF16)
    banks = [pstile(f"psf{i}", [128, 512]) for i in range(6)]
    bankb = pstile("psb", [128, 1024], BF16)
    bankb2 = pstile("psb2", [128, 1024], BF16)

    globals()["_SBUF_LEFT"] = nc.sbuf_bytes_remaining
    sems = {s: es.enter_context(nc.semaphore("sem_" + s)) for s in ("pe", "act", "dve", "pool", "sp")}
    slot_sems = {}
    for s, n in NSLOT.items():
        for i in range(n):
            slot_sems[(s, i)] = es.enter_context(nc.semaphore(f"ds_{s}_{i}"))

    identf = cf[:, 0:128]
    onesf = cf[:, 128:256]
    m_incl = cf[0:64, 256:320]
    m_nstrict = cf[0:64, 320:384]
    m_causal = cf[:, 384:512]
    rc15 = cf[:, 512:572]
    m_causal8 = cf[0:8, 576:584]
    iota_p = cf[:, 600:601]

    st = {"af": 0, "ab": 0, "n": 0, "rot": list(range(6)), "ri": 0}

    class AB:
        pass

    def _arena(ar, key, cap, shape, name, even):
        n = int(np.prod(shape[1:]))
        na = (n + 1) // 2 * 2 if even else n
        off = st[key]
        st[key] = off + na
        assert st[key] <= cap, ("arena overflow", key, name, st[key], cap)
        st["n"] += 1
        b = AB()
        b.k = Tk(name or f"{key}{st['n']}")
        b.shape = list(shape)
        flat = ar.t[0:shape[0], off:off + n]
        b.ap = flat
        sh = shape
        if len(sh) == 2:
            b.v = flat
        elif len(sh) == 3:
            b.v = flat.rearrange("p (a b) -> p a b", b=sh[2])
        elif len(sh) == 4:
            b.v = flat.rearrange("p (a b c) -> p a b c", b=sh[2], c=sh[3])
        else:
            raise ValueError
        return b

    def af(shape, name=None):
        return _arena(arena_f, "af", AF_N, shape, name, False)

    def ab(shape, name=None):
        return _arena(arena_b, "ab", AB_N, shape, name, True)

    def afb(shape, name=None):
        n = int(np.prod(shape[1:]))
        nf = (n + 1) // 2
        off = st["af"]
        st["af"] = off + nf
        assert st["af"] <= AF_N, ("arena overflow", "afb", name, st["af"], AF_N)
        b = AB()
        b.k = Tk(name or "afb")
        b.shape = list(shape)
        flat = arena_f.t[0:shape[0], off:off + nf].bitcast(BF16)[:, 0:n]
        b.ap = flat
        sh = shape
        if len(sh) == 2:
            b.v = flat
        elif len(sh) == 3:
            b.v = flat.rearrange("p (a b) -> p a b", b=sh[2])
        else:
            b.v = flat.rearrange("p (a b c) -> p a b c", b=sh[2], c=sh[3])
        return b

    def new_stage(reset_b=True):
        P.fence()
        st["af"] = 0
        if reset_b:
            st["ab"] = 0

    def set_rot(lst):
        st["rot"] = list(lst)
        st["ri"] = 0

    def bank():
        b = banks[st["rot"][st["ri"] % len(st["rot"])]]
        st["ri"] += 1
        return b

    def hv(ap, n, t=None):
        return ap.rearrange("p (h t) -> p h t", h=n)

    P.dma("sp", cf[:, :], cf_d, writes=[cf.k])
    P.dma("sp", ropek[:, :, :], ropek_d, writes=[ropek.k])
    P.copy("dve", identb[:, :], cf[:, 0:128], reads=[cf.k], writes=[identb.k])
    P.copy("dve", onesb[:, :], cf[:, 128:256], reads=[cf.k], writes=[onesb.k])
    for l in range(DEPTH):
        P.memset("pool", hist_pool[l][:, :, :], 0.0, writes=[hist_pool[l].k])
        P.memset("pool", hist_conv[l][:, :, :], 0.0, writes=[hist_conv[l].k])
        P.memset("pool", S_p[l][:, :, :], 0.0, writes=[S_p[l].k])

    def dbg_store(name, ap, reads):
        if name in dbg_out:
            P.dma("pool", dbg_out[name], ap, reads=reads, is_output=True)

    w_in_v = [w_in[l].rearrange("(kt p) n -> p kt n", p=128) for l in range(DEPTH)]

    def load_w_in(dst, l, c0, ncol, dcol=0):
        P.dma("pool", dst[:, :, dcol:dcol + ncol], w_in_v[l][:, :, c0:c0 + ncol], writes=[dst.k])

    def fm_proj(ps_ap, Wb, wcol, M, NT, wk, psk):
        for kt in range(8):
            P.mm(ps_ap, Wb[:, kt, wcol:wcol + M], xnT[:, kt, 0:NT], start=(kt == 0), stop=(kt == 7),
                 reads=[wk, xnT.k], writes=[psk])

    def rstd_inplace(a, mult, keys):
        P.ts("dve", a, a, mult, EPS, ALU.mult, ALU.add, reads=keys, writes=keys)
        P.act(a, a, AF.Ln, reads=keys, writes=keys)
        P.act(a, a, AF.Exp, scale=-0.5, reads=keys, writes=keys)

    def stage_norm(cfg, l):
        new_stage()
        P.tag = 'norm'
        NT, TT, NTI = cfg["NT"], cfg["TT"], cfg["NTI"]
        P.dma("sp", gn_bc[:, :], norm_g[l].partition_broadcast(128), writes=[gn_bc.k])
        junk = af([128, D], "junk")
        ssq = af([128, 4], "ssq")
        xn = ab([128, D], "xn")
        for ti in range(NTI):
            xt = x_sb[0:TT, ti, :]
            P.act(junk.ap[0:TT, :], xt, AF.Square, accum_out=ssq.ap[0:TT, ti:ti + 1],
                  reads=[x_sb.k], writes=[junk.k, ssq.k])
            rstd_inplace(ssq.ap[0:TT, ti:ti + 1], 1.0 / D, [ssq.k])
            P.stt(xn.ap[0:TT, :], xt, ssq.ap[0:TT, ti:ti + 1], gn_bc[0:TT, :], ALU.mult, ALU.mult,
                  reads=[x_sb.k, ssq.k, gn_bc.k], writes=[xn.k])
            for kt in range(8):
                P.tr(bankb[:, kt * 128: kt * 128 + TT], xn.ap[0:TT, kt * 128:(kt + 1) * 128], identb[0:TT, 0:TT],
                     reads=[xn.k, identb.k], writes=[bankb.k])
            P.copy("act", xnT[:, :, ti * TT:(ti + 1) * TT], hv(bankb[:, :], 8)[:, :, 0:TT],
                   reads=[bankb.k], writes=[xnT.k])
        P.memset("pool", mrg[:, :, 0:NT], 0.0, writes=[mrg.k])
        dbg_store("xnT", xnT[:, :, 0:NT], [xnT.k])

    def merge_branch(cfg, l, bi, yv, yk, w_br, per_head):
        P.tag = 'merge'
        NT = cfg["NT"]
        if per_head:
            wv = WBR[0:64, :].rearrange("p (h n) -> p h n", h=8)
            P.dma("pool", wv, w_br[l].rearrange("(h p) n -> p h n", p=64), writes=[WBR.k])
        else:
            wv = WBR[:, 0:4 * D].rearrange("p (h n) -> p h n", h=4)
            P.dma("pool", wv, w_br[l].rearrange("(kt p) n -> p kt n", p=128), writes=[WBR.k])
        gs = af([128, CH], "gsig")
        Ws = [WB, WA]
        for half in range(2):
            load_w_in(Ws[half], l, C_GATE + bi * D + half * 512, 512)
        for half in range(2):
            Wc = Ws[half]
            for jj in range(4):
                j = half * 4 + jj
                pg = bank()
                fm_proj(pg[:, 0:NT], Wc, jj * 128, 128, NT, Wc.k, pg.k)
                P.act(gs.ap[:, 0:NT], pg[:, 0:NT], AF.Sigmoid, reads=[pg.k], writes=[gs.k])
                pb = bank()
                nk = 8 if per_head else 4
                for kk in range(nk):
                    P.mm(pb[:, 0:NT], wv[:, kk, j * 128:(j + 1) * 128], yv[:, kk, 0:NT],
                         start=(kk == 0), stop=(kk == nk - 1), reads=[WBR.k, yk], writes=[pb.k])
                P.tt("dve", gs.ap[:, 0:NT], gs.ap[:, 0:NT], pb[:, 0:NT], ALU.mult, reads=[gs.k, pb.k], writes=[gs.k])
                P.tt("pool", mrg[:, j, 0:NT], mrg[:, j, 0:NT], gs.ap[:, 0:NT], ALU.add, reads=[gs.k, mrg.k], writes=[mrg.k])

    def stage_out(cfg, l):
        new_stage()
        P.tag = 'out'
        NT, TT, NTI = cfg["NT"], cfg["TT"], cfg["NTI"]
        dbg_store(f"mrg{l}", mrg[:, :, 0:NT], [mrg.k])
        mb = ab([128, 8, NT], "mrgb")
        P.copy("dve", mb.v, mrg[:, :, 0:NT], reads=[mrg.k], writes=[mb.k])
        wo = w_out[l].rearrange("(kt p) n -> p kt n", p=128)
        Ws = [WB, WA]
        for half in range(2):
            P.dma("pool", Ws[half][:, :, 0:512], wo[:, :, half * 512:(half + 1) * 512], writes=[Ws[half].k])
        for half in range(2):
            Wc = Ws[half]
            for ti in range(NTI):
                pb = bank()
                for kt in range(8):
                    P.mm(pb[0:TT, :], mb.v[:, kt, ti * TT:(ti + 1) * TT], Wc[:, kt, 0:512], start=(kt == 0), stop=(kt == 7),
                         reads=[mb.k, Wc.k], writes=[pb.k])
                xsl = x_sb[0:TT, ti, half * 512:(half + 1) * 512]
                P.tt("dve", xsl, xsl, pb[0:TT, :], ALU.add, reads=[pb.k, x_sb.k], writes=[x_sb.k])

    def stage_pool(cfg, l):
        new_stage()
        P.tag = 'pool'
        NT, B, Ls, prompt, ck = cfg["NT"], cfg["B"], cfg["Ls"], cfg["prompt"], cfg["ck"]
        W = 15 + Ls
        load_w_in(WA, l, C_POOL, 512)
        load_w_in(WB, l, C_ZPOOL, 512)
        P.dma("pool", mixw[:, :, :], pool_mix[l], writes=[mixw.k])
        P.dma("sp", psc[:, :], pool_scale[l], writes=[psc.k])
        ext = af([128, 4, B, W], "ext")
        if prompt:
            P.copy("pool", ext.v[:, :, 0, 0:15], hist_pool[l][:, :, :], reads=[hist_pool[l].k], writes=[ext.k])
        else:
            stg = af([15, 4 * 512], "stg")
            for b in range(4):
                P.dma("sp", stg.ap[:, b * 512:(b + 1) * 512], spool[l, b], writes=[stg.k])
            pt = bank()
            for b in range(4):
                for g in range(4):
                    P.tr(pt[:, (b * 4 + g) * 15:(b * 4 + g + 1) * 15], stg.ap[0:15, b * 512 + g * 128: b * 512 + (g + 1) * 128],
                         identf[0:15, 0:15], reads=[stg.k, cf.k], writes=[pt.k])
            P.copy("act", ext.v[:, :, :, 0:15], pt[:, 0:240].rearrange("p (b g t) -> p g b t", b=4, g=4),
                   reads=[pt.k], writes=[ext.k])
        for g in range(4):
            pu = bank()
            fm_proj(pu[:, 0:NT], WA, g * 128, 128, NT, WA.k, pu.k)
            P.copy("act", ext.v[:, g, :, 15:W], pu[:, 0:NT].rearrange("p (b t) -> p b t", b=B), reads=[pu.k], writes=[ext.k])
        if prompt:
            P.copy("pool", hist_pool[l][:, :, :], ext.v[:, :, 0, Ls:Ls + 15], reads=[ext.k], writes=[hist_pool[l].k])
        if (not prompt) or ck == NCH - 1:
            ostg = af([15, 512], "ostg")
            for b in range(B):
                pt = bank()
                for g in range(4):
                    P.tr(pt[0:15, g * 128:(g + 1) * 128], ext.v[:, g, b, Ls:Ls + 15], identf[:, :],
                         reads=[ext.k, cf.k], writes=[pt.k])
                P.copy("act", ostg.ap[0:15, :], pt[0:15, 0:512], reads=[pt.k], writes=[ostg.k])
                dst = o_ppool[l] if prompt else o_spool[l, b]
                P.dma("sp", dst, ostg.ap[0:15, :], reads=[ostg.k], is_output=True)
        wa = af([128, B, W], "wa")
        wb_ = af([128, B, W], "wb")
        dT = ab([128, 4, B, Ls], "dT")
        ya = ab([128, 4, NT], "ya")
        zs = af([128, CH], "zs")
        fx = af([128, 15], "fx")
        for g, wdw in enumerate((2, 4, 8, 16)):
            cur, curk, n = ext.v[:, g, :, :], ext.k, W
            sh = 1
            bufs = [wa, wb_]
            bi = 0
            while sh < wdw:
                o = bufs[bi]
                P.tt("pool", o.v[:, :, 0:n - sh], cur[:, :, sh:n], cur[:, :, 0:n - sh], ALU.add, reads=[curk], writes=[o.k])
                cur, curk, n = o.v, o.k, n - sh
                sh *= 2
                bi ^= 1
            o0 = n - Ls
            P.stt(dT.v[:, g, :, :], cur[:, :, o0:o0 + Ls], 1.0 / wdw, ext.v[:, g, :, 15:W], ALU.mult, ALU.subtract,
                  reads=[curk, ext.k], writes=[dT.k])
            if prompt and ck == 0:
                P.tt("pool", fx.ap, cur[:, 0, o0:o0 + 15], rc15[:, g * 15:(g + 1) * 15], ALU.mult,
                     reads=[curk, cf.k], writes=[fx.k])
                P.tt("dve", dT.v[:, g, 0, 0:15], fx.ap, ext.v[:, g, 0, 15:30], ALU.subtract,
                     reads=[fx.k, ext.k, dT.k], writes=[dT.k])
        for g in range(4):
            p1 = bank()
            P.mm(p1[:, 0:NT], mixw[:, g, :], dT.ap[:, g * NT:(g + 1) * NT], reads=[mixw.k, dT.k], writes=[p1.k])
            p2 = bank()
            fm_proj(p2[:, 0:NT], WB, g * 128, 128, NT, WB.k, p2.k)
            P.act(zs.ap[:, 0:NT], p2[:, 0:NT], AF.Silu, reads=[p2.k], writes=[zs.k])
            P.stt(ya.v[:, g, 0:NT], p1[:, 0:NT], psc[:, g:g + 1], zs.ap[:, 0:NT], ALU.mult, ALU.mult,
                  reads=[p1.k, psc.k, zs.k], writes=[ya.k])
        dbg_store(f"ya{l}", ya.v, [ya.k])
        merge_branch(cfg, l, 0, ya.v, ya.k, w_br_pool, per_head=False)

    def stage_mla(cfg, l):
        new_stage()
        P.tag = 'mla.pre'
        NT, TT, NTI, B, Ls, prompt, ck = cfg["NT"], cfg["TT"], cfg["NTI"], cfg["B"], cfg["Ls"], cfg["prompt"], cfg["ck"]
        tok0 = ck * CH if prompt else 0
        wuq = ab([128, 3, H * 128], "wuq")
        wuk = ab([64, H, 256], "wuk")
        wuv = ab([128, 2, H * 64], "wuv")
        ropeq = af([32, 2, NT], "ropeq")
        gq_bc = af([128, 384], "gq_bc")
        gkv_bc = af([128, 256], "gkv_bc")
        load_w_in(WA, l, C_Q, 672)
        load_w_in(WB, l, C_ZMLA, 512)
        P.dma("pool", wuq.v[:, :, :], w_uq[l].rearrange("(kt p) n -> p kt n", p=128), writes=[wuq.k])
        P.dma("pool", wuk.v[:, :, :], w_ukT[l], writes=[wuk.k])
        P.dma("pool", wuv.v[:, :, :], w_uv[l].rearrange("(kt p) n -> p kt n", p=128), writes=[wuv.k])
        P.dma("sp", gq_bc.v[:, :], q_norm_g[l].partition_broadcast(128), writes=[gq_bc.k])
        P.dma("sp", gkv_bc.v[:, :], kv_norm_g[l].partition_broadcast(128), writes=[gkv_bc.k])
        rq0 = tok0 if prompt else SEQ
        P.dma("sp", ropeq.v[:, :, 0:NT], ropeq_d[:, :, rq0:rq0 + NT], writes=[ropeq.k])
        yb = ab([64, 8, NT], "yb")
        for h in range(8):
            pz = bank()
            fm_proj(pz[0:64, 0:NT], WB, h * 64, 64, NT, WB.k, pz.k)
            P.act(yb.v[:, h, 0:NT], pz[0:64, 0:NT], AF.Silu, reads=[pz.k], writes=[yb.k])
        ckvf = af([128, 288], "ckvf")
        ssq = af([128, 2], "ssq2")
        junk = af([128, 384], "junk2")
        t1 = af([128, 64], "ropetmp")
        qr = af([32, 2, 8, TT], "qr")
        rden = af([128, 4 * TT], "rden")
        ckvb = ab([128, 288], "ckvb")
        cqb = ab([128, 384], "cqb")
        cqT = ab([128, 3, TT], "cqT")
        qn = ab([64, 8, TT], "qn")
        qrT = ab([32, 8, TT], "qrT")
        qlT = ab([128, 2, 8, TT], "qlT")
        if prompt:
            pT = [ab([128, 4, TT], f"pT{i}") for i in range(2)]
        else:
            knT = ab([128, 3, 32], "knT")
            vn = ab([32, 256], "vn")
            kvb = [afb([128, 8, 256], f"kvb{i}") for i in range(4)]
            krb = [afb([128, 8, 32], f"krb{i}") for i in range(4)]
            kTp = [ab([128, 2, 3, 128], f"kTp{i}") for i in range(4)]
            pts = [ab([128, 512], f"pts{i}") for i in range(2)]
            ptn = ab([8, 64], "ptn")
            vb8 = ab([8, 256], "vb8")
            qc = ab([128, 2, 64], "qc")
            qrc = ab([32, 64], "qrc")
            ols = ab([128, 2, 64], "ols")
        for ti in range(NTI):
            set_rot(range(6))
            ktile = (tok0 // 128 + ti) if prompt else 16
            tsl = slice(ti * TT, (ti + 1) * TT)
            P.tag = 'mla.proj'
            pk = bank()
            for kt in range(8):
                P.mm(pk[0:TT, 0:288], xnT[:, kt, tsl], WA[:, kt, 384:672], start=(kt == 0), stop=(kt == 7),
                     reads=[xnT.k, WA.k], writes=[pk.k])
            P.act(junk.ap[0:TT, 0:256], pk[0:TT, 0:256], AF.Square, accum_out=ssq.ap[0:TT, 0:1],
                  reads=[pk.k], writes=[junk.k, ssq.k])
            rstd_inplace(ssq.ap[0:TT, 0:1], 1.0 / 256, [ssq.k])
            P.stt(ckvf.ap[0:TT, 0:256], pk[0:TT, 0:256], ssq.ap[0:TT, 0:1], gkv_bc.v[0:TT, :], ALU.mult, ALU.mult,
                  reads=[pk.k, ssq.k, gkv_bc.k], writes=[ckvf.k])
            cosk = ropek[0:TT, ktile, 0:16]
            sink = ropek[0:TT, ktile, 16:32]
            P.tt("dve", t1.ap[0:TT, 0:16], pk[0:TT, 256:272], cosk, ALU.mult, reads=[pk.k, ropek.k], writes=[t1.k])
            P.tt("dve", t1.ap[0:TT, 16:32], pk[0:TT, 272:288], sink, ALU.mult, reads=[pk.k, ropek.k], writes=[t1.k])
            P.tt("dve", t1.ap[0:TT, 32:48], pk[0:TT, 272:288], cosk, ALU.mult, reads=[pk.k, ropek.k], writes=[t1.k])
            P.tt("dve", t1.ap[0:TT, 48:64], pk[0:TT, 256:272], sink, ALU.mult, reads=[pk.k, ropek.k], writes=[t1.k])
            P.tt("dve", ckvf.ap[0:TT, 256:272], t1.ap[0:TT, 0:16], t1.ap[0:TT, 16:32], ALU.subtract, reads=[t1.k], writes=[ckvf.k])
            P.tt("dve", ckvf.ap[0:TT, 272:288], t1.ap[0:TT, 32:48], t1.ap[0:TT, 48:64], ALU.add, reads=[t1.k], writes=[ckvf.k])
            if prompt:
                r0 = tok0 + ti * 128
                P.dma("sp", o_pkv[l, r0:r0 + 128, :], ckvf.ap[0:128, 0:256], reads=[ckvf.k], is_output=True)
                P.dma("sp", o_pkr[l, r0:r0 + 128, :], ckvf.ap[0:128, 256:288], reads=[ckvf.k], is_output=True)
            else:
                P.dma("sp", o_skv[l], ckvf.ap[0:32, 0:256], reads=[ckvf.k], is_output=True)
                P.dma("sp", o_skr[l], ckvf.ap[0:32, 256:288], reads=[ckvf.k], is_output=True)
            P.copy("pool", ckvb.ap[0:TT, :], ckvf.ap[0:TT, :], reads=[ckvf.k], writes=[ckvb.k])
            for j, (c0, cn) in enumerate(((0, 128), (128, 128), (256, 32))):
                P.tr(bankb[0:cn, j * 128: j * 128 + TT], ckvb.ap[0:TT, c0:c0 + cn], identb[0:TT, 0:TT],
                     reads=[ckvb.k, identb.k], writes=[bankb.k])
            if prompt:
                P.copy("pool", Vc[l][:, ktile, :], ckvb.ap[:, 0:256], reads=[ckvb.k], writes=[Vc[l].k])
                P.copy("act", kTc[l][:, 0:2, ktile * 128:(ktile + 1) * 128], hv(bankb[:, 0:256], 2),
                       reads=[bankb.k], writes=[kTc[l].k])
                P.copy("act", kTc[l][0:32, 2, ktile * 128:(ktile + 1) * 128], bankb[0:32, 256:384],
                       reads=[bankb.k], writes=[kTc[l].k])
            else:
                P.copy("pool", vn.ap[0:32, :], ckvb.ap[0:32, 0:256], reads=[ckvb.k], writes=[vn.k])
                P.copy("act", knT.v[:, 0:2, :], hv(bankb[:, 0:256], 2)[:, :, 0:32], reads=[bankb.k], writes=[knT.k])
                P.copy("act", knT.v[0:32, 2, :], bankb[0:32, 256:288], reads=[bankb.k], writes=[knT.k])
            pq = bank()
            for kt in range(8):
                P.mm(pq[0:TT, 0:384], xnT[:, kt, tsl], WA[:, kt, 0:384], start=(kt == 0), stop=(kt == 7),
                     reads=[xnT.k, WA.k], writes=[pq.k])
            P.act(junk.ap[0:TT, 0:384], pq[0:TT, 0:384], AF.Square, accum_out=ssq.ap[0:TT, 1:2],
                  reads=[pq.k], writes=[junk.k, ssq.k])
            rstd_inplace(ssq.ap[0:TT, 1:2], 1.0 / 384, [ssq.k])
            P.stt(cqb.ap[0:TT, :], pq[0:TT, 0:384], ssq.ap[0:TT, 1:2], gq_bc.v[0:TT, :], ALU.mult, ALU.mult,
                  reads=[pq.k, ssq.k, gq_bc.k], writes=[cqb.k])
            for j in range(3):
                P.tr(bankb[:, 384 + j * 128: 384 + j * 128 + TT], cqb.ap[0:TT, j * 128:(j + 1) * 128], identb[0:TT, 0:TT],
                     reads=[cqb.k, identb.k], writes=[bankb.k])
            P.copy("act", cqT.v[:, :, 0:TT], hv(bankb[:, 384:768], 3)[:, :, 0:TT], reads=[bankb.k], writes=[cqT.k])
            for hg in range(2):
                pn = bank()
                for hh in range(4):
                    h = hg * 4 + hh
                    for j in range(3):
                        P.mm(pn[0:64, hh * 128: hh * 128 + TT], wuq.v[:, j, h * 128: h * 128 + 64], cqT.v[:, j, 0:TT],
                             start=(j == 0), stop=(j == 2), reads=[wuq.k, cqT.k], writes=[pn.k])
                P.copy("act", qn.v[:, hg * 4:(hg + 1) * 4, :], hv(pn[0:64, :], 4)[:, :, 0:TT], reads=[pn.k], writes=[qn.k])
            for v in range(2):
                for hg in range(2):
                    pr = bank()
                    for hh in range(4):
                        h = hg * 4 + hh
                        c0 = h * 128 + 64 + v * 32
                        for j in range(3):
                            P.mm(pr[0:32, hh * 128: hh * 128 + TT], wuq.v[:, j, c0:c0 + 32],
                                 cqT.v[:, j, 0:TT], start=(j == 0), stop=(j == 2), reads=[wuq.k, cqT.k], writes=[pr.k])
                    tab = ropeq.v[:, v, tsl]
                    P.tt("dve", qr.v[:, v, hg * 4:(hg + 1) * 4, :], hv(pr[0:32, :], 4)[:, :, 0:TT],
                         tab.unsqueeze(1).to_broadcast([32, 4, TT]), ALU.mult, reads=[pr.k, ropeq.k], writes=[qr.k])
            P.tt("pool", qrT.v, qr.v[:, 0, :, :], qr.v[:, 1, :, :], ALU.add, reads=[qr.k], writes=[qrT.k])
            for j in range(2):
                for hg in range(2):
                    pl = bank()
                    for hh in range(4):
                        h = hg * 4 + hh
                        P.mm(pl[:, hh * 128: hh * 128 + TT], wuk.v[:, h, j * 128:(j + 1) * 128], qn.v[:, h, :],
                             reads=[wuk.k, qn.k], writes=[pl.k])
                    P.copy("act" if hg == 0 else "dve", qlT.v[:, j, hg * 4:(hg + 1) * 4, :], hv(pl[:, :], 4)[:, :, 0:TT],
                           reads=[pl.k], writes=[qlT.k])
            dbg_store(f"qlT{l}", qlT.v, [qlT.k])
            dbg_store(f"qrT{l}", qrT.v, [qrT.k])
            P.tag = 'mla.attn'
            po = [banks[0], banks[1]]
            pd = banks[2]
            set_rot([3, 4, 5])
            if prompt:
                nkt = ktile + 1
                for hg in range(2):
                    qsl = slice(hg * 4, (hg + 1) * 4)
                    def att_S(kt):
                        pscr = bank()
                        ksl = slice(kt * 128, (kt + 1) * 128)
                        P.mm(pscr[:, :], kTc[l][:, 0, ksl], qlT.v[:, 0, qsl, :], start=True, stop=False,
                             reads=[kTc[l].k, qlT.k], writes=[pscr.k])
                        P.mm(pscr[:, :], kTc[l][:, 1, ksl], qlT.v[:, 1, qsl, :], start=False, stop=False,
                             reads=[kTc[l].k, qlT.k], writes=[pscr.k])
                        P.mm(pscr[:, :], kTc[l][0:32, 2, ksl], qrT.v[0:32, qsl, :], start=False, stop=True,
                             reads=[kTc[l].k, qrT.k], writes=[pscr.k])
                        pt_ = pT[kt % 2]
                        P.act(pt_.ap[:, :], pscr[:, :], AF.Exp, scale=MLA_SCALE, reads=[pscr.k], writes=[pt_.k])
                        if kt == ktile:
                            P.tt("pool", pt_.v, pt_.v, m_causal.unsqueeze(1).to_broadcast([128, 4, 128]), ALU.mult,
                                 reads=[cf.k, pt_.k], writes=[pt_.k])

                    def att_PV(kt):
                        pt_ = pT[kt % 2]
                        for j in range(2):
                            P.mm(po[j][:, :], Vc[l][:, kt, j * 128:(j + 1) * 128], pt_.ap[:, :], start=(kt == 0), stop=(kt == nkt - 1),
                                 reads=[Vc[l].k, pt_.k], writes=[po[j].k])
                        P.mm(pd[:, :], onesb[:, :], pt_.ap[:, :], start=(kt == 0), stop=(kt == nkt - 1),
                             reads=[onesb.k, pt_.k], writes=[pd.k])

                    att_S(0)
                    for kt in range(nkt):
                        if kt + 1 < nkt:
                            att_S(kt + 1)
                        att_PV(kt)
                    P.act(rden.ap[:, :], pd[:, :], AF.Ln, reads=[pd.k], writes=[rden.k])
                    P.act(rden.ap[:, :], rden.ap[:, :], AF.Exp, scale=-1.0, reads=[rden.k], writes=[rden.k])
                    for j in range(2):
                        P.tt("dve", qlT.v[:, j, qsl, :], hv(po[j][:, :], 4), hv(rden.ap[:, :], 4), ALU.mult,
                             reads=[po[j].k, rden.k, qlT.k], writes=[qlT.k])
                for hg in range(2):
                    pm = bank()
                    for hh in range(4):
                        h = hg * 4 + hh
                        for j in range(2):
                            P.mm(pm[0:64, hh * 128:(hh + 1) * 128], wuv.v[:, j, h * 64:(h + 1) * 64], qlT.v[:, j, h, :],
                                 start=(j == 0), stop=(j == 1), reads=[wuv.k, qlT.k], writes=[pm.k])
                    P.tt("dve", yb.v[:, hg * 4:(hg + 1) * 4, tsl], hv(pm[0:64, :], 4), yb.v[:, hg * 4:(hg + 1) * 4, tsl], ALU.mult,
                         reads=[pm.k, yb.k], writes=[yb.k])
            else:
                ckv8 = ckv.rearrange("l n (a t) c -> (l n a) (t c)", t=8)
                ckr8 = ckr.rearrange("l n (a t) c -> (l n a) (t c)", t=8)
                NG = 16
                set_rot([3, 4, 5])
                trbufs = [(bankb, bankb[:, 0:768]), (bankb2, bankb2[:, 0:768])]
                for b in range(4):
                    P.dma("sp", ptb[:, 0:1], ptab[b].rearrange("(p o) -> p o", o=1), writes=[ptb.k])
                    P.stt(ridx[:, 0:16], ptb[:, 0:1].to_broadcast([128, 16]), 16.0, cf[:, 610 + 16 * l:626 + 16 * l], ALU.mult, ALU.add,
                          reads=[ptb.k, cf.k], writes=[ridx.k])
                    P.copy("dve", qc.v.rearrange("p j (h t) -> p j h t", t=8), qlT.v[:, :, :, b * 8:(b + 1) * 8],
                           reads=[qlT.k], writes=[qc.k])
                    P.copy("dve", qrc.ap.rearrange("p (h t) -> p h t", t=8), qrT.v[:, :, b * 8:(b + 1) * 8],
                           reads=[qrT.k], writes=[qrc.k])

                    def issue_dma(g):
                        kb, kr_ = kvb[g % 4], krb[g % 4]
                        P.idma(kb.ap, ckv8, ridx[:, g:g + 1], reads=[ridx.k], writes=[kb.k])
                        P.idma(kr_.ap, ckr8, ridx[:, g:g + 1], reads=[ridx.k], writes=[kr_.k])

                    def T_S(g):
                        kb, kr_ = kvb[g % 4], krb[g % 4]
                        pscr = bank()
                        for pr in range(4):
                            kt_ = kTp[pr]
                            trb, trv = trbufs[pr % 2]
                            for t2 in range(2):
                                t = pr * 2 + t2
                                for j in range(3):
                                    cc = (t2 * 3 + j) * 128
                                    src = kb.v[:, t, j * 128:(j + 1) * 128] if j < 2 else kr_.v[:, t, :]
                                    cn = 128 if j < 2 else 32
                                    P.tr(trv[0:cn, cc:cc + 128], src, identb[:, :],
                                         reads=[kb.k, kr_.k, identb.k], writes=[trb.k])
                            view = trv.rearrange("p (t j c) -> p t j c", t=2, j=3)
                            P.copy("dve", kt_.v[:, :, 0:2, :], view[:, :, 0:2, :], reads=[trb.k], writes=[kt_.k])
                            P.copy("act", kt_.v[0:32, :, 2, :], view[0:32, :, 2, :], reads=[trb.k], writes=[kt_.k])
                            for t2 in range(2):
                                t = pr * 2 + t2
                                osl = slice(t * 64, (t + 1) * 64)
                                P.mm(pscr[:, osl], kt_.v[:, t2, 0, :], qc.v[:, 0, :], start=True, stop=False, reads=[kt_.k, qc.k], writes=[pscr.k])
                                P.mm(pscr[:, osl], kt_.v[:, t2, 1, :], qc.v[:, 1, :], start=False, stop=False, reads=[kt_.k, qc.k], writes=[pscr.k])
                                P.mm(pscr[:, osl], kt_.v[0:32, t2, 2, :], qrc.ap[0:32, :], start=False, stop=True, reads=[kt_.k, qrc.k], writes=[pscr.k])
                        pt_ = pts[g % 2]
                        P.act(pt_.ap[:, :], pscr[:, :], AF.Exp, scale=MLA_SCALE, reads=[pscr.k], writes=[pt_.k])

                    def PVg(g):
                        kb = kvb[g % 4]
                        pt_ = pts[g % 2]
                        for t in range(8):
                            first = (g == 0 and t == 0)
                            osl = slice(t * 64, (t + 1) * 64)
                            for j in range(2):
                                P.mm(po[j][:, 0:64], kb.v[:, t, j * 128:(j + 1) * 128], pt_.ap[:, osl], start=first, stop=False,
                                     reads=[kb.k, pt_.k], writes=[po[j].k])
                            P.mm(pd[:, 0:64], onesb[:, :], pt_.ap[:, osl], start=first, stop=False,
                                 reads=[onesb.k, pt_.k], writes=[pd.k])

                    for g in range(3):
                        issue_dma(g)
                    if l == 0 and b == 0:
                        dbg_store("ridx", ridx[:, 0:16], [ridx.k])
                        dbg_store("kvb0", kvb[0].v, [kvb[0].k])
                    for g in range(NG):
                        T_S(g)
                        if l == 0 and b == 0 and g == 0:
                            dbg_store("pts0", pts[0].ap, [pts[0].k])
                            dbg_store("kTp0", kTp[0].v, [kTp[0].k])
                        if g > 0:
                            PVg(g - 1)
                        if g + 3 < NG:
                            issue_dma(g + 3)
                    PVg(NG - 1)
                    pscr = bank()
                    k0, k1, k2 = knT.v[:, 0, b * 8:(b + 1) * 8], knT.v[:, 1, b * 8:(b + 1) * 8], knT.v[0:32, 2, b * 8:(b + 1) * 8]
                    P.dma("sp", vb8.ap[0:8, :], vn.ap[b * 8:(b + 1) * 8, :], reads=[vn.k], writes=[vb8.k])
                    P.mm(pscr[0:8, 0:64], k0, qc.v[:, 0, :], start=True, stop=False, reads=[knT.k, qc.k], writes=[pscr.k])
                    P.mm(pscr[0:8, 0:64], k1, qc.v[:, 1, :], start=False, stop=False, reads=[knT.k, qc.k], writes=[pscr.k])
                    P.mm(pscr[0:8, 0:64], k2, qrc.ap[0:32, :], start=False, stop=True, reads=[knT.k, qrc.k], writes=[pscr.k])
                    P.act(ptn.ap[0:8, :], pscr[0:8, 0:64], AF.Exp, scale=MLA_SCALE, reads=[pscr.k], writes=[ptn.k])
                    P.tt("pool", hv(ptn.ap[0:8, :], 8), hv(ptn.ap[0:8, :], 8),
                         m_causal8.unsqueeze(1).to_broadcast([8, 8, 8]), ALU.mult, reads=[cf.k, ptn.k], writes=[ptn.k])
                    for j in range(2):
                        P.mm(po[j][:, 0:64], vb8.ap[0:8, j * 128:(j + 1) * 128], ptn.ap[0:8, :], start=False, stop=True,
                             reads=[vb8.k, ptn.k], writes=[po[j].k])
                    P.mm(pd[:, 0:64], onesb[0:8, :], ptn.ap[0:8, :], start=False, stop=True,
                         reads=[onesb.k, ptn.k], writes=[pd.k])
                    P.act(rden.ap[:, 0:64], pd[:, 0:64], AF.Ln, reads=[pd.k], writes=[rden.k])
                    P.act(rden.ap[:, 0:64], rden.ap[:, 0:64], AF.Exp, scale=-1.0, reads=[rden.k], writes=[rden.k])
                    for j in range(2):
                        P.tt("dve", ols.v[:, j, :], po[j][:, 0:64], rden.ap[:, 0:64], ALU.mult,
                             reads=[po[j].k, rden.k], writes=[ols.k])
                    pm = bank()
                    for h in range(8):
                        for j in range(2):
                            P.mm(pm[0:64, h * 8:(h + 1) * 8], wuv.v[:, j, h * 64:(h + 1) * 64], ols.v[:, j, h * 8:(h + 1) * 8],
                                 start=(j == 0), stop=(j == 1), reads=[wuv.k, ols.k], writes=[pm.k])
                    P.tt("dve", yb.v[:, :, b * 8:(b + 1) * 8], hv(pm[0:64, 0:64], 8), yb.v[:, :, b * 8:(b + 1) * 8], ALU.mult,
                         reads=[pm.k, yb.k], writes=[yb.k])
        set_rot(range(6))
        dbg_store(f"yb{l}", yb.v, [yb.k])
        merge_branch(cfg, l, 1, yb.v, yb.k, w_br_mla, per_head=True)

    def stage_dn(cfg, l):
        new_stage()
        P.tag = 'dn.d1'
        NT, B, Ls, prompt, ck, C = cfg["NT"], cfg["B"], cfg["Ls"], cfg["prompt"], cfg["ck"], cfg["C"]
        NSUB = NT // C
        LV = int(np.log2(C))
        W = 3 + Ls
        HG = 8
        NHG = 8 // HG
        HW_ = HG * 64
        do_out = (not prompt) or ck == NCH - 1
        P.dma("sp", cw[:, :, :], conv_w[l], writes=[cw.k])
        P.dma("sp", gdn[:, :], dn_norm_g[l], writes=[gdn.k])
        P.dma("sp", a_bc[:, :], a_log[l].partition_broadcast(64), writes=[a_bc.k])
        P.dma("sp", dtb_bc[:, :], dt_bias[l].partition_broadcast(64), writes=[dtb_bc.k])
        nega = af([64, 8], "nega")
        P.act(nega.ap, a_bc[:, :], AF.Exp, reads=[a_bc.k], writes=[nega.k])
        P.ts("dve", nega.ap, nega.ap, -1.0, None, ALU.mult, reads=[nega.k], writes=[nega.k])
        yc = ab([64, 8, NT], "yc")
        extc2 = [af([64, B, W], f"extc{i}") for i in range(2)]
        if not prompt:
            hs = af([64, 24, 4, 3], "hs")
            stgc = af([3, 1536], "stgc")
            for b in range(4):
                P.dma("sp", stgc.ap[0:3, :], sconv[l, b], writes=[stgc.k])
                pt = bank()
                for ht in range(24):
                    P.tr(pt[0:64, ht * 3:(ht + 1) * 3], stgc.ap[0:3, ht * 64:(ht + 1) * 64], identf[0:3, 0:3],
                         reads=[stgc.k, cf.k], writes=[pt.k])
                P.copy("act", hs.v[:, :, b, :], hv(pt[0:64, 0:72], 24), reads=[pt.k], writes=[hs.k])
        ost = [af([3, 64], f"ost{i}") for i in range(2)]
        qkvb = [ab([64, HG, NT], f"qkvb{i}") for i in range(3)]
        cacc2 = [af([64, NT], f"cacc{i}") for i in range(2)]
        sq2 = [af([64, NT], f"sq{i}") for i in range(2)]
        names = ["Gb", "dgb", "E", "E1", "kbg", "qg", "Q0", "qkT", "P0", "TT", "Qb", "Pb"]
        tmp = {n: af([64, HG, 64], n) for n in names}
        tmp["vb"] = tmp["Gb"]
        tmp["kd"] = tmp["dgb"]
        tmp["R"] = tmp["E1"]
        tmp["vnw"] = tmp["Qb"]
        tmp["osq"] = tmp["Q0"]
        kbT = ab([64, HG, 64], "kbT")
        beta = af([64, HG], "beta")
        gg = af([64, HG], "gg")
        gc = af([64, HG], "gc")
        elast = af([64, HG], "elast")
        edl = af([64, HG], "edl")
        Ssm = af([64, HG, 64], "Ssm") if not prompt else None
        n_ost = 0
        for hg in range(NHG):
            hsl = slice(hg * HG, (hg + 1) * HG)
            P.tag = 'dn.d1'
            Wq = [WA, WB, WA]
            load_w_in(WA, l, C_QKV + 0 * 512 + hg * HW_, HW_)
            load_w_in(WB, l, C_QKV + 1 * 512 + hg * HW_, HW_)
            for which in range(3):
                WA_ = Wq[which]
                if which == 2:
                    load_w_in(WA, l, C_QKV + 2 * 512 + hg * HW_, HW_)
                for hh in range(HG):
                    h = hg * HG + hh
                    ht = which * 8 + h
                    extc, cacc, sq = extc2[hh % 2], cacc2[hh % 2], sq2[hh % 2]
                    pp = bank()
                    fm_proj(pp[0:64, 0:NT], WA_, hh * 64, 64, NT, WA_.k, pp.k)
                    if prompt:
                        P.copy("pool", extc.v[:, 0, 0:3], hist_conv[l][:, ht, :], reads=[hist_conv[l].k], writes=[extc.k])
                    else:
                        P.copy("pool", extc.v[:, :, 0:3], hs.v[:, ht, :, :], reads=[hs.k], writes=[extc.k])
                    P.copy("act", extc.v[:, :, 3:W], pp[0:64, 0:NT].rearrange("p (b t) -> p b t", b=B), reads=[pp.k], writes=[extc.k])
                    if prompt:
                        P.copy("pool", hist_conv[l][:, ht, :], extc.v[:, 0, Ls:Ls + 3], reads=[extc.k], writes=[hist_conv[l].k])
                    if do_out:
                        for b in range(B):
                            pt = bank()
                            P.tr(pt[0:3, 0:64], extc.v[:, b, Ls:Ls + 3], identf[0:64, 0:64], reads=[extc.k, cf.k], writes=[pt.k])
                            o_ = ost[n_ost % 2]
                            n_ost += 1
                            P.copy("act", o_.ap[0:3, :], pt[0:3, 0:64], reads=[pt.k], writes=[o_.k])
                            dst = o_pconv[l] if prompt else o_sconv[l, b]
                            P.dma("sp", dst[:, ht * 64:(ht + 1) * 64], o_.ap[0:3, :], reads=[o_.k], is_output=True)
                    caccv = cacc.ap[:, 0:NT].rearrange("p (b t) -> p b t", b=B)
                    P.ts("dve", caccv, extc.v[:, :, 0:Ls], cw[:, ht, 0:1], None, ALU.mult, reads=[extc.k, cw.k], writes=[cacc.k])
                    for j in range(1, 4):
                        P.stt(caccv, extc.v[:, :, j:j + Ls], cw[:, ht, j:j + 1], caccv, ALU.mult, ALU.add,
                              reads=[extc.k, cw.k, cacc.k], writes=[cacc.k])
                    if which == 2:
                        P.act(qkvb[2].v[:, hh, :], cacc.ap[:, 0:NT], AF.Silu, reads=[cacc.k], writes=[qkvb[2].k])
                    else:
                        P.act(cacc.ap[:, 0:NT], cacc.ap[:, 0:NT], AF.Silu, reads=[cacc.k], writes=[cacc.k])
                        P.act(sq.ap[:, 0:NT], cacc.ap[:, 0:NT], AF.Square, reads=[cacc.k], writes=[sq.k])
                        pss = bank()
                        P.mm(pss[0:64, 0:NT], onesf[0:64, 0:64], sq.ap[:, 0:NT], reads=[cf.k, sq.k], writes=[pss.k])
                        P.ts("dve", sq.ap[:, 0:NT], pss[0:64, 0:NT], EPS, None, ALU.add, reads=[pss.k], writes=[sq.k])
                        P.act(sq.ap[:, 0:NT], sq.ap[:, 0:NT], AF.Ln, reads=[sq.k], writes=[sq.k])
                        P.act(sq.ap[:, 0:NT], sq.ap[:, 0:NT], AF.Exp, scale=-0.5, reads=[sq.k], writes=[sq.k])
                        if l == 0 and hg == 0 and which == 1 and hh == 0:
                            dbg_store("rs", sq.ap[:, 0:NT], [sq.k])
                            dbg_store("cs", cacc.ap[:, 0:NT], [cacc.k])
                        if which == 0:
                            P.stt(qkvb[0].v[:, hh, :], cacc.ap[:, 0:NT], 0.125, sq.ap[:, 0:NT], ALU.mult, ALU.mult,
                                  reads=[cacc.k, sq.k], writes=[qkvb[0].k])
                        else:
                            P.tt("dve", qkvb[1].v[:, hh, :], cacc.ap[:, 0:NT], sq.ap[:, 0:NT], ALU.mult,
                                 reads=[cacc.k, sq.k], writes=[qkvb[1].k])
            load_w_in(WB, l, C_ZDN + hg * HW_, HW_)
            load_w_in(WA, l, C_BETA, 16, dcol=0)
            for hh in range(HG):
                pz = bank()
                fm_proj(pz[0:64, 0:NT], WB, hh * 64, 64, NT, WB.k, pz.k)
                P.act(yc.v[:, hg * HG + hh, :], pz[0:64, 0:NT], AF.Silu, reads=[pz.k], writes=[yc.k])
            Gb, dgb, E, E1, kbg, qg = (tmp[n] for n in ("Gb", "dgb", "E", "E1", "kbg", "qg"))
            Q0, qkT, P0, TT_, Qb, Pb = (tmp[n] for n in ("Q0", "qkT", "P0", "TT", "Qb", "Pb"))
            vb, kd, R, vnw, osq = (tmp[n] for n in ("vb", "kd", "R", "vnw", "osq"))
            for s in range(NSUB if DN_NSUB is None else DN_NSUB):
                cs = slice(s * C, (s + 1) * C)
                bseq = s
                if not prompt:
                    P.dma("sp", Ssm.v, sdelta[l, bseq, hsl].rearrange("h k v -> k h v"), writes=[Ssm.k])
                    Sv, Sk = Ssm.v, Ssm.k
                else:
                    Sv, Sk = S_p[l][:, hsl, :], S_p[l].k
                P.tag = 'dn.pre'
                pbg = bank()
                for kt in range(8):
                    P.mm(pbg[0:C, 0:16], xnT[:, kt, cs], WA[:, kt, 0:16], start=(kt == 0), stop=(kt == 7),
                         reads=[xnT.k, WA.k], writes=[pbg.k])
                P.act(beta.ap[0:C, :], pbg[0:C, hg * HG:(hg + 1) * HG], AF.Sigmoid, reads=[pbg.k], writes=[beta.k])
                P.tt("dve", gg.ap[0:C, :], pbg[0:C, 8 + hg * HG: 8 + (hg + 1) * HG], dtb_bc[0:C, hsl], ALU.add, reads=[pbg.k, dtb_bc.k], writes=[gg.k])
                P.act(gg.ap[0:C, :], gg.ap[0:C, :], AF.Exp, reads=[gg.k], writes=[gg.k])
                P.ts("dve", gg.ap[0:C, :], gg.ap[0:C, :], 1.0, None, ALU.add, reads=[gg.k], writes=[gg.k])
                P.act(gg.ap[0:C, :], gg.ap[0:C, :], AF.Ln, reads=[gg.k], writes=[gg.k])
                P.tt("dve", gg.ap[0:C, :], gg.ap[0:C, :], nega.ap[0:C, hsl], ALU.mult, reads=[gg.k, nega.k], writes=[gg.k])
                pg1 = bank()
                P.mm(pg1[0:C, 0:HG], m_incl[0:C, 0:C], gg.ap[0:C, :], reads=[cf.k, gg.k], writes=[pg1.k])
                P.mm(pg1[0:64, 8:8 + HG], onesf[0:C, 0:64], gg.ap[0:C, :], reads=[cf.k, gg.k], writes=[pg1.k])
                P.copy("act", gc.ap[0:C, :], pg1[0:C, 0:HG], reads=[pg1.k], writes=[gc.k])
                P.act(elast.ap[:, :], pg1[0:64, 8:8 + HG], AF.Exp, reads=[pg1.k], writes=[elast.k])
                P.tt("dve", edl.ap[0:C, :], pg1[0:C, 8:8 + HG], gc.ap[0:C, :], ALU.subtract, reads=[pg1.k, gc.k], writes=[edl.k])
                P.act(edl.ap[0:C, :], edl.ap[0:C, :], AF.Exp, reads=[edl.k], writes=[edl.k])
                if DN_CUT <= 1:
                    continue
                P.copy("act", Gb.v[0:C, :, :], gg.ap[0:C, :].unsqueeze(2).to_broadcast([C, HG, 64]), reads=[gg.k], writes=[Gb.k])
                if DN_CUT <= 1.2:
                    continue
                P.tt("pool", dgb.v[0:C, :, 0:C], identf[0:C, 0:C].unsqueeze(1).to_broadcast([C, HG, C]),
                     beta.ap[0:C, :].unsqueeze(2).to_broadcast([C, HG, C]), ALU.mult, reads=[cf.k, beta.k], writes=[dgb.k])
                if DN_CUT <= 1.4:
                    continue
                pgcb = bank()
                pbb = bank()
                for hh in range(HG):
                    P.mm(pgcb[0:64, hh * 64: hh * 64 + C], Gb.v[0:C, hh, :], m_incl[0:C, 0:C], reads=[Gb.k, cf.k], writes=[pgcb.k])
                    P.mm(pbb[0:64, hh * 64: hh * 64 + C], onesf[0:C, 0:64], dgb.v[0:C, hh, 0:C], reads=[cf.k, dgb.k], writes=[pbb.k])
                if DN_CUT <= 1.6:
                    continue
                gcbv = hv(pgcb[0:64, 0:HW_], HG)
                pbbv = hv(pbb[0:64, 0:HW_], HG)
                P.act(E.v[:, :, 0:C], gcbv[:, :, 0:C], AF.Exp, reads=[pgcb.k], writes=[E.k])
                if DN_CUT <= 1.8:
                    continue
                P.op("act", lambda e, o=dgb.v, i=Gb.v, c=C: e.mul(out=o[0:c, :, :], in_=i[0:c, :, :], mul=-1.0), [Gb.k, dgb.k], [dgb.k])
                pdf = bank()
                for hh in range(HG):
                    P.mm(pdf[0:C, hh * 64: hh * 64 + C], Gb.v[0:C, hh, 0:C], m_incl[0:C, 0:C], start=True, stop=False,
                         reads=[Gb.k, cf.k], writes=[pdf.k])
                    P.mm(pdf[0:C, hh * 64: hh * 64 + C], m_incl[0:C, 0:C], dgb.v[0:C, hh, 0:C], start=False, stop=True,
                         reads=[dgb.k, cf.k], writes=[pdf.k])
                P.ts("dve", E1.v[0:C, :, 0:C], hv(pdf[0:64, 0:HW_], HG)[0:C, :, 0:C], 0.0, None, ALU.min, reads=[pdf.k], writes=[E1.k])
                if DN_CUT <= 1.9:
                    continue
                P.act(E1.v[0:C, :, 0:C], E1.v[0:C, :, 0:C], AF.Exp, reads=[E1.k], writes=[E1.k])
                if DN_CUT <= 2:
                    continue
                kTs = qkvb[1].v[:, :, cs]
                qTs = qkvb[0].v[:, :, cs]
                P.tt("dve", kbg.v[:, :, 0:C], kTs, pbbv[:, :, 0:C], ALU.mult, reads=[qkvb[1].k, pbb.k], writes=[kbg.k])
                P.copy("act", kbT.v[:, :, 0:C], kbg.v[:, :, 0:C], reads=[kbg.k], writes=[kbT.k])
                P.tt("dve", kbg.v[:, :, 0:C], kbg.v[:, :, 0:C], E.v[:, :, 0:C], ALU.mult, reads=[kbg.k, E.k, kbT.k], writes=[kbg.k])
                P.tt("pool", qg.v[:, :, 0:C], qTs, E.v[:, :, 0:C], ALU.mult, reads=[qkvb[0].k, E.k], writes=[qg.k])
                pkk = bank()
                pqk = bank()
                for hh in range(HG):
                    P.mm(pkk[0:C, hh * 64: hh * 64 + C], qkvb[1].v[:, hh, cs], kbT.v[:, hh, 0:C], reads=[qkvb[1].k, kbT.k], writes=[pkk.k])
                    P.mm(pqk[0:C, hh * 64: hh * 64 + C], qkvb[1].v[:, hh, cs], qkvb[0].v[:, hh, cs], reads=[qkvb[1].k, qkvb[0].k], writes=[pqk.k])
                P.tt("dve", Q0.v[0:C, :, 0:C], hv(pkk[0:64, 0:HW_], HG)[0:C, :, 0:C], E1.v[0:C, :, 0:C], ALU.mult, reads=[pkk.k, E1.k], writes=[Q0.k])
                P.tt("pool", Q0.v[0:C, :, 0:C], Q0.v[0:C, :, 0:C], m_nstrict[0:C, 0:C].unsqueeze(1).to_broadcast([C, HG, C]), ALU.mult,
                     reads=[Q0.k, cf.k], writes=[Q0.k])
                P.tt("dve", qkT.v[0:C, :, 0:C], hv(pqk[0:64, 0:HW_], HG)[0:C, :, 0:C], E1.v[0:C, :, 0:C], ALU.mult, reads=[pqk.k, E1.k], writes=[qkT.k])
                P.tt("pool", qkT.v[0:C, :, 0:C], qkT.v[0:C, :, 0:C], m_incl[0:C, 0:C].unsqueeze(1).to_broadcast([C, HG, C]), ALU.mult,
                     reads=[qkT.k, cf.k], writes=[qkT.k])
                if DN_CUT <= 3:
                    continue
                ptp = bank()
                for hh in range(HG):
                    P.tr(ptp[0:C, hh * 64: hh * 64 + C], Q0.v[0:C, hh, 0:C], identf[0:C, 0:C], reads=[Q0.k, cf.k], writes=[ptp.k])
                P.copy("act", P0.v[0:C, :, 0:C], hv(ptp[0:64, 0:HW_], HG)[0:C, :, 0:C], reads=[ptp.k], writes=[P0.k])
                P.tt("dve", TT_.v[0:C, :, 0:C], Q0.v[0:C, :, 0:C], identf[0:C, 0:C].unsqueeze(1).to_broadcast([C, HG, C]), ALU.add,
                     reads=[Q0.k, cf.k], writes=[TT_.k])
                if DN_CUT <= 4:
                    continue
                P.tag = 'dn.neu'
                Qa, Pa, Qn_, Pn_ = Q0, P0, Qb, Pb
                for lv in range(LV - 1):
                    pq2 = bank()
                    pp2 = bank()
                    lastlv = (lv == LV - 2)
                    for hh in range(HG):
                        if not lastlv:
                            P.mm(pq2[0:C, hh * 64: hh * 64 + C], Pa.v[0:C, hh, 0:C], Qa.v[0:C, hh, 0:C], reads=[Pa.k, Qa.k], writes=[pq2.k])
                        P.mm(pp2[0:C, hh * 64: hh * 64 + C], Qa.v[0:C, hh, 0:C], Pa.v[0:C, hh, 0:C], reads=[Pa.k, Qa.k], writes=[pp2.k])
                    P.copy("act", Pn_.v[0:C, :, 0:C], hv(pp2[0:64, 0:HW_], HG)[0:C, :, 0:C], reads=[pp2.k], writes=[Pn_.k])
                    if not lastlv:
                        P.copy("dve", Qn_.v[0:C, :, 0:C], hv(pq2[0:64, 0:HW_], HG)[0:C, :, 0:C], reads=[pq2.k], writes=[Qn_.k])
                    pt2 = bank()
                    for hh in range(HG):
                        P.mm(pt2[0:C, hh * 64: hh * 64 + C], Pn_.v[0:C, hh, 0:C], TT_.v[0:C, hh, 0:C], reads=[Pn_.k, TT_.k], writes=[pt2.k])
                    P.tt("dve", TT_.v[0:C, :, 0:C], TT_.v[0:C, :, 0:C], hv(pt2[0:64, 0:HW_], HG)[0:C, :, 0:C], ALU.add,
                         reads=[pt2.k, TT_.k], writes=[TT_.k])
                    Qa, Qn_ = Qn_, Qa
                    Pa, Pn_ = Pn_, Pa
                if DN_CUT <= 5:
                    continue
                P.tag = 'dn.scan'
                for hh in range(HG):
                    P.tr(bankb[0:C, hh * 64:(hh + 1) * 64], qkvb[2].v[:, hh, cs], identb[0:64, 0:64], reads=[qkvb[2].k, identb.k], writes=[bankb.k])
                    P.tr(bankb[0:C, HW_ + hh * 64: HW_ + (hh + 1) * 64], qkvb[1].v[:, hh, cs], identb[0:64, 0:64], reads=[qkvb[1].k, identb.k], writes=[bankb.k])
                P.tt("dve", vb.v[0:C, :, :], hv(bankb[0:C, 0:HW_], HG),
                     beta.ap[0:C, :].unsqueeze(2).to_broadcast([C, HG, 64]), ALU.mult, reads=[bankb.k, beta.k], writes=[vb.k])
                P.tt("dve", kd.v[0:C, :, :], hv(bankb[0:C, HW_:2 * HW_], HG),
                     edl.ap[0:C, :].unsqueeze(2).to_broadcast([C, HG, 64]), ALU.mult, reads=[bankb.k, edl.k], writes=[kd.k])
                if DN_CUT <= 6:
                    continue
                if l == 0 and hg == 0 and s == DBG_S:
                    dbg_store("gg", gg.ap[0:C, :], [gg.k])
                    dbg_store("beta", beta.ap[0:C, :], [beta.k])
                    dbg_store("gc", gc.ap[0:C, :], [gc.k])
                    dbg_store("E1", E1.v[0:C, :, 0:C], [E1.k])
                    dbg_store("Q0", Q0.v[0:C, :, 0:C], [Q0.k])
                    dbg_store("qkT", qkT.v[0:C, :, 0:C], [qkT.k])
                    dbg_store("TT", TT_.v[0:C, :, 0:C], [TT_.k])
                    dbg_store("kbg", kbg.v[:, :, 0:C], [kbg.k])
                    dbg_store("qg", qg.v[:, :, 0:C], [qg.k])
                    dbg_store("vb", vb.v[0:C, :, :], [vb.k])
                    dbg_store("kd", kd.v[0:C, :, :], [kd.k])
                pR = bank()
                for hh in range(HG):
                    P.mm(pR[0:C, hh * 64:(hh + 1) * 64], kbg.v[:, hh, 0:C], Sv[:, hh, :], reads=[kbg.k, Sk], writes=[pR.k])
                P.tt("dve", R.v[0:C, :, :], vb.v[0:C, :, :], hv(pR[0:C, 0:HW_], HG), ALU.subtract,
                     reads=[vb.k, pR.k], writes=[R.k])
                pvn = bank()
                for hh in range(HG):
                    P.mm(pvn[0:C, hh * 64:(hh + 1) * 64], TT_.v[0:C, hh, 0:C], R.v[0:C, hh, :], reads=[TT_.k, R.k], writes=[pvn.k])
                P.copy("act", vnw.v[0:C, :, :], hv(pvn[0:C, 0:HW_], HG), reads=[pvn.k], writes=[vnw.k])
                if DN_CUT <= 7:
                    continue
                po_ = bank()
                for hh in range(HG):
                    P.mm(po_[0:64, hh * 64: hh * 64 + C], Sv[:, hh, :], qg.v[:, hh, 0:C], start=True, stop=False,
                         reads=[Sk, qg.k], writes=[po_.k])
                    P.mm(po_[0:64, hh * 64: hh * 64 + C], vnw.v[0:C, hh, :], qkT.v[0:C, hh, 0:C], start=False, stop=True,
                         reads=[vnw.k, qkT.k], writes=[po_.k])
                pS = bank()
                for hh in range(HG):
                    P.mm(pS[0:64, hh * 64:(hh + 1) * 64], kd.v[0:C, hh, :], vnw.v[0:C, hh, :], reads=[kd.k, vnw.k], writes=[pS.k])
                for hh in range(HG):
                    P.ts("dve", Sv[:, hh, :], Sv[:, hh, :], elast.ap[:, hh:hh + 1], None, ALU.mult, reads=[Sk, elast.k], writes=[Sk])
                P.tt("dve", Sv, Sv, hv(pS[0:64, 0:HW_], HG), ALU.add, reads=[pS.k, Sk], writes=[Sk])
                if DN_CUT <= 8:
                    continue
                if l == 0 and hg == 0 and s == DBG_S:
                    dbg_store("R", R.v[0:C, :, :], [R.k])
                    dbg_store("vnw", vnw.v[0:C, :, :], [vnw.k])
                    dbg_store("Snew", Sv, [Sk])
                ov = hv(po_[0:64, 0:HW_], HG)
                P.act(osq.v[:, :, 0:C], ov[:, :, 0:C], AF.Square, reads=[po_.k], writes=[osq.k])
                pn2 = bank()
                for hh in range(HG):
                    P.mm(pn2[0:64, hh * 64: hh * 64 + C], onesf[0:64, 0:64], osq.v[:, hh, 0:C], reads=[cf.k, osq.k], writes=[pn2.k])
                P.ts("dve", osq.v[:, :, 0:C], hv(pn2[0:64, 0:HW_], HG)[:, :, 0:C], 1.0 / 64, EPS, ALU.mult, ALU.add, reads=[pn2.k], writes=[osq.k])
                P.act(osq.v[:, :, 0:C], osq.v[:, :, 0:C], AF.Ln, reads=[osq.k], writes=[osq.k])
                P.act(osq.v[:, :, 0:C], osq.v[:, :, 0:C], AF.Exp, scale=-0.5, reads=[osq.k], writes=[osq.k])
                P.stt(osq.v[:, :, 0:C], ov[:, :, 0:C], gdn[:, 0:1], osq.v[:, :, 0:C], ALU.mult, ALU.mult,
                      reads=[po_.k, gdn.k, osq.k], writes=[osq.k])
                P.tt("dve", yc.v[:, hsl, cs], osq.v[:, :, 0:C], yc.v[:, hsl, cs], ALU.mult, reads=[osq.k, yc.k], writes=[yc.k])
                if not prompt:
                    P.dma("sp", o_sdelta[l, bseq, hsl].rearrange("h k v -> k h v"), Sv, reads=[Sk], is_output=True)
            if prompt and ck == NCH - 1:
                P.dma("sp", o_pdelta[l, hsl].rearrange("h k v -> k h v"), S_p[l][:, hsl, :],
                      reads=[S_p[l].k], is_output=True)
        dbg_store(f"yc{l}", yc.v, [yc.k])
        new_stage(reset_b=False)
        merge_branch(cfg, l, 2, yc.v, yc.k, w_br_dn, per_head=True)

    def stage_final(cfg):
        new_stage()
        P.tag = 'final'
        NT, TT, NTI, prompt, ck = cfg["NT"], cfg["TT"], cfg["NTI"], cfg["prompt"], cfg["ck"]
        P.dma("sp", gn_bc[:, :], final_norm_g.partition_broadcast(128), writes=[gn_bc.k])
        junk = af([128, D], "junkf")
        ssq = af([128, 4], "ssqf")
        yo = [af([128, D], f"yo{i}") for i in range(2)]
        for ti in range(NTI):
            xt = x_sb[0:TT, ti, :]
            P.act(junk.ap[0:TT, :], xt, AF.Square, accum_out=ssq.ap[0:TT, ti:ti + 1], reads=[x_sb.k], writes=[junk.k, ssq.k])
            rstd_inplace(ssq.ap[0:TT, ti:ti + 1], 1.0 / D, [ssq.k])
            y = yo[ti % 2]
            P.stt(y.ap[0:TT, :], xt, ssq.ap[0:TT, ti:ti + 1], gn_bc[0:TT, :], ALU.mult, ALU.mult,
                  reads=[x_sb.k, ssq.k, gn_bc.k], writes=[y.k])
            if prompt:
                r0 = ck * CH + ti * 128
                P.dma("sp", y_p[r0:r0 + 128, :], y.ap[0:128, :], reads=[y.k], is_output=True)
            else:
                P.dma("sp", y_s, y.ap[0:32, :], reads=[y.k], is_output=True)

    cfgs = []
    if not sample_only:
        for ck in range(prompt_chunks):
            cfgs.append(dict(NT=CH, TT=128, NTI=4, B=1, Ls=CH, prompt=True, ck=ck, C=64))
    if not prompt_only:
        cfgs.append(dict(NT=32, TT=32, NTI=1, B=4, Ls=8, prompt=False, ck=0, C=8))
    for cfg in cfgs:
        new_stage()
        if cfg["prompt"]:
            r0 = cfg["ck"] * CH
            P.dma("sp", x_sb[:, :, :], xp[r0:r0 + CH, :].rearrange("(t p) d -> p t d", p=128), writes=[x_sb.k])
        else:
            P.dma("sp", x_sb[0:32, 0, :], xs, writes=[x_sb.k])
        for l in range(DEPTH):
            if "norm" not in skip:
                stage_norm(cfg, l)
            if "pool" in stages:
                stage_pool(cfg, l)
            if "mla" in stages:
                stage_mla(cfg, l)
            if "dn" in stages:
                stage_dn(cfg, l)
            if "out" not in skip:
                stage_out(cfg, l)
        if "final" not in skip:
            stage_final(cfg)
    P.fence()
    P.emit(sems, slot_sems)
    es.close()
    return nc, P


def _prep_inputs(inp):
    global _CONST
    if _CONST is None:
        _CONST = _consts()
    f32 = np.float32
    w_uq = np.asarray(inp["w_uq"], f32).reshape(DEPTH, 384, H, 96)
    rope = w_uq[..., 64:96]
    rope_sw = np.concatenate([rope[..., 16:32], rope[..., 0:16]], -1)
    w_uq_ext = np.ascontiguousarray(np.concatenate([w_uq, rope_sw], -1).reshape(DEPTH, 384, H * 128))
    shared = {
        "ckv": np.asarray(inp["cache_kv_latent"], f32),
        "ckr": np.asarray(inp["cache_k_rope"], f32),
        "norm_g": np.asarray(inp["norm_g"], f32),
        "w_in": np.asarray(inp["w_in"], f32),
        "pool_mix": np.ascontiguousarray(np.asarray(inp["pool_mix"], f32).transpose(0, 2, 1, 3)),
        "pool_scale": np.ascontiguousarray(np.asarray(inp["pool_scale"], f32).reshape(DEPTH, 4, 128).transpose(0, 2, 1)),
        "q_norm_g": np.asarray(inp["q_norm_g"], f32),
        "w_uq": w_uq_ext,
        "kv_norm_g": np.asarray(inp["kv_norm_g"], f32),
        "w_ukT": np.ascontiguousarray(np.asarray(inp["w_uk"], f32).transpose(0, 3, 2, 1)),
        "w_uv": np.ascontiguousarray(np.asarray(inp["w_uv"], f32).reshape(DEPTH, 256, H * 64)),
        "conv_w": np.ascontiguousarray(np.asarray(inp["conv_w"], f32).reshape(DEPTH, 4, 24, 64).transpose(0, 3, 2, 1)),
        "a_log": np.asarray(inp["a_log"], f32),
        "dt_bias": np.asarray(inp["dt_bias"], f32),
        "dn_norm_g": np.ascontiguousarray(np.asarray(inp["dn_norm_g"], f32).reshape(DEPTH, 64, 1)),
        "w_br_pool": np.asarray(inp["w_br_pool"], f32),
        "w_br_mla": np.asarray(inp["w_br_mla"], f32),
        "w_br_dn": np.asarray(inp["w_br_dn"], f32),
        "w_out": np.asarray(inp["w_out"], f32),
        "final_norm_g": np.asarray(inp["final_norm_g"], f32),
        "cf": _CONST["cf"], "ropeq": _CONST["ropeq"], "ropek": _CONST["ropek"],
    }
    xp = np.asarray(inp["x_prompt"], f32)
    xs = np.asarray(inp["x_sample"], f32)
    sp = np.asarray(inp["state_pool"], f32)
    sc = np.asarray(inp["state_conv"], f32)
    sd = np.asarray(inp["state_delta"], f32)
    pt = np.asarray(inp["page_table"], np.int32)
    in_maps = []
    for c in range(NCORE):
        m = dict(shared)
        m["xp"] = np.ascontiguousarray(xp[c])
        m["xs"] = np.ascontiguousarray(xs[4 * c:4 * c + 4].reshape(32, D))
        m["spool"] = np.ascontiguousarray(sp[:, 4 * c:4 * c + 4])
        m["sconv"] = np.ascontiguousarray(sc[:, 4 * c:4 * c + 4])
        m["sdelta"] = np.ascontiguousarray(sd[:, 4 * c:4 * c + 4])
        m["ptab"] = np.ascontiguousarray(pt[4 * c:4 * c + 4])
        in_maps.append(m)
    return in_maps


_NC = None


def kernel(**inputs):
    global _NC
    in_maps = _prep_inputs(inputs)
    if _NC is None:
        _NC = build()[0]
    res = run_bass_kernel_spmd(_NC, in_maps, core_ids=list(range(NCORE)))
    r = res.results
    cat = lambda k: np.stack([r[c][k] for c in range(NCORE)], 0)
    y_p = cat("y_p")
    y_s = np.concatenate([r[c]["y_s"].reshape(4, 8, D) for c in range(NCORE)], 0)
    p_kv = np.stack([r[c]["o_pkv"] for c in range(NCORE)], 1)
    p_kr = np.stack([r[c]["o_pkr"] for c in range(NCORE)], 1)
    p_pool = np.stack([r[c]["o_ppool"] for c in range(NCORE)], 1)
    p_conv = np.stack([r[c]["o_pconv"] for c in range(NCORE)], 1)
    p_delta = np.stack([r[c]["o_pdelta"] for c in range(NCORE)], 1)
    s_kv = np.concatenate([r[c]["o_skv"].reshape(DEPTH, 4, 8, 256) for c in range(NCORE)], 1)
    s_kr = np.concatenate([r[c]["o_skr"].reshape(DEPTH, 4, 8, 32) for c in range(NCORE)], 1)
    s_pool = np.concatenate([r[c]["o_spool"] for c in range(NCORE)], 1)
    s_conv = np.concatenate([r[c]["o_sconv"] for c in range(NCORE)], 1)
    s_delta = np.concatenate([r[c]["o_sdelta"] for c in range(NCORE)], 1)
    outs = (y_p, y_s, p_kv, p_kr, p_pool, p_conv, p_delta, s_kv, s_kr, s_pool, s_conv, s_delta)
    return tuple(np.ascontiguousarray(o, dtype=np.float32) for o in outs)
```
